# Optimizing a Trainium2 kernel written in Bass

```python
import math
import jax, jax.numpy as jnp
from jax import lax
import numpy as np

D_MODEL = 1024
BATCH = 8
SEQ = 8192
DEPTH = 1

CHUNK = 64
EPS = 1e-6
ATT_HEADS = 8
ATT_KV_HEADS = 2
ATT_HEAD_DIM = 64
IDX_HEADS = 8
IDX_HEAD_DIM = 64
TOPK_MAX = 256
Q_BLOCK = 128
MLSTM_HEADS = 4
MLSTM_QK_DIM = 64
MLSTM_V_DIM = 128
CONV_WIDTH = 4
MIX_A = ATT_HEADS * ATT_HEAD_DIM
MIX_B = MLSTM_HEADS * MLSTM_V_DIM
D_FF = 2816
SPLIT_WIDTHS = (
    ATT_HEADS * ATT_HEAD_DIM,
    ATT_KV_HEADS * ATT_HEAD_DIM,
    ATT_KV_HEADS * ATT_HEAD_DIM,
    IDX_HEADS * IDX_HEAD_DIM,
    IDX_HEAD_DIM,
    IDX_HEADS,
    MIX_B,
    MIX_B,
    MLSTM_HEADS,
    MLSTM_HEADS,
    MIX_B,
    D_MODEL,
    D_MODEL,
)
D_IN = sum(SPLIT_WIDTHS)

kernel_name = "hybrid_dsa_mlstm_macaron_block"


def rms_norm(x, g):
    xf = x.astype(jnp.float32)
    y = xf * lax.rsqrt(jnp.mean(xf * xf, axis=-1, keepdims=True) + EPS)
    return (y * g.astype(jnp.float32)).astype(x.dtype)


def swiglu_ffn(x, g, w_gate, w_up, w_down):
    h = rms_norm(x, g)
    return (jax.nn.silu(h @ w_gate) * (h @ w_up)) @ w_down


def split_columns(p):
    parts, off = [], 0
    for w in SPLIT_WIDTHS:
        parts.append(p[..., off:off + w])
        off += w
    return parts


def dsa_attention(q, k, v, iq, ik, iw):
    B, S = q.shape[0], q.shape[1]
    k_sel = min(TOPK_MAX, S // 4)
    nb = S // Q_BLOCK
    reps = ATT_HEADS // ATT_KV_HEADS
    idx_scale = (IDX_HEAD_DIM ** -0.5) * (IDX_HEADS ** -0.5)
    att_scale = ATT_HEAD_DIM ** -0.5
    key_pos = jnp.arange(S)

    qb = q.reshape(B, nb, Q_BLOCK, ATT_KV_HEADS, reps, ATT_HEAD_DIM).transpose(1, 0, 2, 3, 4, 5)
    iqb = iq.reshape(B, nb, Q_BLOCK, IDX_HEADS, IDX_HEAD_DIM).transpose(1, 0, 2, 3, 4)
    iwb = iw.reshape(B, nb, Q_BLOCK, IDX_HEADS).transpose(1, 0, 2, 3)
    ids = jnp.arange(nb, dtype=jnp.int32)

    def block(args):
        q_blk, iq_blk, iw_blk, bi = args
        t = bi * Q_BLOCK + jnp.arange(Q_BLOCK)
        logits = jnp.einsum('bqhd,bsd->bqhs', iq_blk, ik)
        score = jnp.einsum('bqhs,bqh->bqs', jax.nn.relu(logits), iw_blk).astype(jnp.float32) * idx_scale
        limit = (t // CHUNK + 1) * CHUNK
        allowed = key_pos[None, :] < limit[:, None]
        score = jnp.where(allowed[None], score, -jnp.inf)
        top_s, top_i = lax.top_k(score, k_sel)
        valid = jnp.isfinite(top_s)
        kg = jax.vmap(lambda kb, ib: kb[ib])(k, top_i)
        vg = jax.vmap(lambda vb, ib: vb[ib])(v, top_i)
        s = jnp.einsum('bqgrd,bqkgd->bqgrk', q_blk, kg).astype(jnp.float32) * att_scale
        s = jnp.where(valid[:, :, None, None, :], s, -jnp.inf)
        p = jax.nn.softmax(s, axis=-1).astype(vg.dtype)
        return jnp.einsum('bqgrk,bqkgd->bqgrd', p, vg)

    o = lax.map(block, (qb, iqb, iwb, ids))
    return o.transpose(1, 0, 2, 3, 4, 5).reshape(B, S, MIX_A)


def mlstm_branch(xm, vm, ig, fg, o_pre, conv_w, conv_b, w_mq, w_mk, b_i, b_f, head_g):
    B, S, _ = xm.shape
    H, L = MLSTM_HEADS, CHUNK
    nc = S // L
    xpad = jnp.pad(xm, ((0, 0), (CONV_WIDTH - 1, 0), (0, 0)))
    xc = conv_b + xpad[:, 0:S] * conv_w[0]
    for j in range(1, CONV_WIDTH):
        xc = xc + xpad[:, j:j + S] * conv_w[j]
    xc = jax.nn.silu(xc).reshape(B, S, H, MLSTM_V_DIM)
    q = jnp.einsum('bshc,hcd->bshd', xc, w_mq).astype(jnp.float32) * (MLSTM_QK_DIM ** -0.5)
    k = jnp.einsum('bshc,hcd->bshd', xc, w_mk).astype(jnp.float32)
    v = vm.reshape(B, S, H, MLSTM_V_DIM).astype(jnp.float32)
    log_i = (ig + b_i).astype(jnp.float32)
    log_f = jax.nn.log_sigmoid((fg + b_f).astype(jnp.float32))

    def to_chunks(a):
        a = a.reshape((B, nc, L) + a.shape[2:])
        perm = (1, 0, 3, 2) + tuple(range(4, a.ndim))
        return a.transpose(perm)

    tril = jnp.tril(jnp.ones((L, L), dtype=bool))

    def step(carry, inp):
        C, n, m = carry
        qc, kc, vc, ic, fc = inp
        b = jnp.cumsum(fc, axis=-1)
        D = b[..., :, None] - b[..., None, :] + ic[..., None, :]
        D = jnp.where(tril, D, -jnp.inf)
        inter = b + m[..., None]
        m_t = jnp.maximum(jnp.max(D, axis=-1), inter)
        W = jnp.einsum('bhtd,bhsd->bhts', qc, kc) * jnp.exp(D - m_t[..., None])
        w_int = jnp.exp(inter - m_t)
        num = jnp.einsum('bhts,bhsv->bhtv', W, vc) + w_int[..., None] * jnp.einsum('bhtd,bhdv->bhtv', qc, C)
        den = jnp.sum(W, axis=-1) + w_int * jnp.einsum('bhtd,bhd->bht', qc, n)
        h = num / jnp.maximum(jnp.abs(den), jnp.exp(-m_t))[..., None]
        bL = b[..., -1]
        wk = bL[..., None] - b + ic
        m_new = jnp.maximum(bL + m, jnp.max(wk, axis=-1))
        ws = jnp.exp(wk - m_new[..., None])
        wc = jnp.exp(bL + m - m_new)
        C_new = wc[..., None, None] * C + jnp.einsum('bhs,bhsd,bhsv->bhdv', ws, kc, vc)
        n_new = wc[..., None] * n + jnp.einsum('bhs,bhsd->bhd', ws, kc)
        return (C_new, n_new, m_new), h

    init = (jnp.zeros((B, H, MLSTM_QK_DIM, MLSTM_V_DIM), jnp.float32),
            jnp.zeros((B, H, MLSTM_QK_DIM), jnp.float32),
            jnp.zeros((B, H), jnp.float32))
    _, hs = lax.scan(step, init, (to_chunks(q), to_chunks(k), to_chunks(v), to_chunks(log_i), to_chunks(log_f)))
    h = hs.transpose(1, 0, 3, 2, 4).reshape(B, S, H, MLSTM_V_DIM)
    h = rms_norm(h, head_g.reshape(H, MLSTM_V_DIM)).reshape(B, S, MIX_B).astype(xm.dtype)
    return h * jax.nn.sigmoid(o_pre)


def setup_inputs(seed: int = 0) -> dict:
    key = jax.random.key(seed)
    ks = jax.random.split(key, 32)
    f32 = jnp.float32

    def nrm(k, shape, fan_in):
        return jax.random.normal(k, shape, f32) * (fan_in ** -0.5)

    def gain(k, shape):
        return 1.0 + 0.02 * jax.random.normal(k, shape, f32)

    Ld = DEPTH
    return {
        "x": jax.random.normal(ks[0], (BATCH, SEQ, D_MODEL), f32),
        "ffn1_norm": gain(ks[1], (Ld, D_MODEL)),
        "ffn1_w_gate": nrm(ks[2], (Ld, D_MODEL, D_FF), D_MODEL),
        "ffn1_w_up": nrm(ks[3], (Ld, D_MODEL, D_FF), D_MODEL),
        "ffn1_w_down": nrm(ks[4], (Ld, D_FF, D_MODEL), D_FF),
        "mix_norm": gain(ks[5], (Ld, D_MODEL)),
        "w_in": nrm(ks[6], (Ld, D_MODEL, D_IN), D_MODEL),
        "q_norm": gain(ks[7], (Ld, ATT_HEAD_DIM)),
        "k_norm": gain(ks[8], (Ld, ATT_HEAD_DIM)),
        "idx_k_norm": gain(ks[9], (Ld, IDX_HEAD_DIM)),
        "conv_w": nrm(ks[10], (Ld, CONV_WIDTH, MIX_B), CONV_WIDTH),
        "conv_b": 0.02 * jax.random.normal(ks[11], (Ld, MIX_B), f32),
        "w_mq": nrm(ks[12], (Ld, MLSTM_HEADS, MLSTM_V_DIM, MLSTM_QK_DIM), MLSTM_V_DIM),
        "w_mk": nrm(ks[13], (Ld, MLSTM_HEADS, MLSTM_V_DIM, MLSTM_QK_DIM), MLSTM_V_DIM),
        "b_i": 0.1 * jax.random.normal(ks[14], (Ld, MLSTM_HEADS), f32),
        "b_f": 3.0 + 0.1 * jax.random.normal(ks[15], (Ld, MLSTM_HEADS), f32),
        "m_head_norm": gain(ks[16], (Ld, MIX_B)),
        "w_proj_a": nrm(ks[17], (Ld, MIX_A, D_MODEL), MIX_A),
        "w_proj_b": nrm(ks[18], (Ld, MIX_B, D_MODEL), MIX_B),
        "w_out": nrm(ks[19], (Ld, D_MODEL, D_MODEL), D_MODEL),
        "ffn2_norm": gain(ks[20], (Ld, D_MODEL)),
        "ffn2_w_gate": nrm(ks[21], (Ld, D_MODEL, D_FF), D_MODEL),
        "ffn2_w_up": nrm(ks[22], (Ld, D_MODEL, D_FF), D_MODEL),
        "ffn2_w_down": nrm(ks[23], (Ld, D_FF, D_MODEL), D_FF),
    }


def reference(x, ffn1_norm, ffn1_w_gate, ffn1_w_up, ffn1_w_down, mix_norm, w_in,
              q_norm, k_norm, idx_k_norm, conv_w, conv_b, w_mq, w_mk, b_i, b_f,
              m_head_norm, w_proj_a, w_proj_b, w_out, ffn2_norm, ffn2_w_gate,
              ffn2_w_up, ffn2_w_down):
    B, S, _ = x.shape
    for l in range(DEPTH):
        x = x + 0.5 * swiglu_ffn(x, ffn1_norm[l], ffn1_w_gate[l], ffn1_w_up[l], ffn1_w_down[l])
        h = rms_norm(x, mix_norm[l])
        (aq, ak, av, iq, ik, iw, mx, mv, mi, mf, mo, ga, gb) = split_columns(h @ w_in[l])
        q = rms_norm(aq.reshape(B, S, ATT_HEADS, ATT_HEAD_DIM), q_norm[l])
        k = rms_norm(ak.reshape(B, S, ATT_KV_HEADS, ATT_HEAD_DIM), k_norm[l])
        v = av.reshape(B, S, ATT_KV_HEADS, ATT_HEAD_DIM)
        iq = iq.reshape(B, S, IDX_HEADS, IDX_HEAD_DIM)
        ik = rms_norm(ik, idx_k_norm[l])
        ya = dsa_attention(q, k, v, iq, ik, iw)
        yb = mlstm_branch(mx, mv, mi, mf, mo, conv_w[l], conv_b[l], w_mq[l], w_mk[l],
                          b_i[l], b_f[l], m_head_norm[l])
        merged = jax.nn.sigmoid(ga) * (ya @ w_proj_a[l]) + jax.nn.sigmoid(gb) * (yb @ w_proj_b[l])
        x = x + merged @ w_out[l]
        x = x + 0.5 * swiglu_ffn(x, ffn2_norm[l], ffn2_w_gate[l], ffn2_w_up[l], ffn2_w_down[l])
    return x
```

```python
import numpy as np
from contextlib import ExitStack
import ml_dtypes
import concourse.bass as bass
import concourse.mybir as mybir
from concourse.bass_utils import run_bass_kernel_spmd

F32 = mybir.dt.float32
BF16 = mybir.dt.bfloat16
ALU = mybir.AluOpType
AF = mybir.ActivationFunctionType
AX = mybir.AxisListType

D = 1024
DFF = 2816
NFF = DFF // 128
DIN = 4944
EPS = 1e-6
NEG = -30000.0


class Buf:
    __slots__ = ("name", "w", "r", "psum")

    def __init__(self, name):
        self.name = name
        self.w = {}
        self.r = {}
        self.psum = False


class Group:
    def __init__(self, sem, final=False):
        self.sem = sem
        self.n = 0
        self.final = final


class Prog:
    ENG = ("pe", "act", "dve", "pool", "sp")

    def __init__(self, nc):
        self.nc = nc
        self.ops = {e: [] for e in self.ENG}

    def op(self, eng, fn, reads=(), writes=(), group=None):
        ops = self.ops[eng]
        idx = len(ops)
        rec = {"fn": fn, "waits": {}, "sig": False, "group": group}
        is_dma = group is not None

        def need(key, val, raw):
            if is_dma and key[0] == "g" and key[1] is group:
                return
            if key[0] == "e" and key[1] == eng and not is_dma:
                if eng == "pe" or not raw:
                    return
            w = rec["waits"]
            if w.get(key, -1) < val:
                w[key] = val
            if key[0] == "e":
                self.ops[key[1]][val]["sig"] = True

        for b in reads:
            for k, v in b.w.items():
                need(k, v, True)
            if b.psum:
                for k, v in b.r.items():
                    need(k, v, False)
        for b in writes:
            for k, v in b.w.items():
                need(k, v, False)
            for k, v in b.r.items():
                need(k, v, False)
        if is_dma:
            group.n += 1
            me = (("g", group), group.n)
        else:
            me = (("e", eng), idx)
        for b in reads:
            if b.r.get(me[0], -1) < me[1]:
                b.r[me[0]] = me[1]
        for b in writes:
            b.w = {me[0]: me[1]}
            b.r = {}
        ops.append(rec)
        return rec

    def emit(self, block, sems, final_waits, base=None):
        if base is None:
            base = {e: 0 for e in self.ENG}
        sigcount = {}
        for e in self.ENG:
            c = base[e]
            lst = []
            for rec in self.ops[e]:
                if rec["sig"]:
                    c += 1
                lst.append(c)
            sigcount[e] = lst

        def run(engname, engine):
            waited = {}
            for rec in self.ops[engname]:
                for key, val in rec["waits"].items():
                    if key[0] == "e":
                        sem = sems[key[1]]
                        v = sigcount[key[1]][val]
                    else:
                        g = key[1]
                        sem = g.sem
                        v = 16 * (g.n if g.final else val)
                    if waited.get(id(sem), -1) >= v:
                        continue
                    waited[id(sem)] = v
                    engine.wait_ge(sem, v)
                ins = rec["fn"](engine)
                if rec["group"] is not None:
                    ins.then_inc(rec["group"].sem, 16)
                elif rec["sig"]:
                    ins.then_inc(sems[engname], 1)
            for g in final_waits.get(engname, ()):
                engine.wait_ge(g.sem, 16 * g.n)

        @block.tensor
        def _(e):
            run("pe", e)

        @block.scalar
        def _(e):
            run("act", e)

        @block.vector
        def _(e):
            run("dve", e)

        @block.gpsimd
        def _(e):
            run("pool", e)

        @block.sync
        def _(e):
            run("sp", e)

        for e in self.ENG:
            base[e] = sigcount[e][-1] if sigcount[e] else base[e]


class _Alloc:
    def __init__(self, nc, es):
        self._nc = nc
        self._es = es

    def alloc_sbuf_tensor(self, name, shape, dt):
        return self._es.enter_context(self._nc.sbuf_tensor(name, list(shape), dt))

    def alloc_psum_tensor(self, name, shape, dt):
        return self._es.enter_context(self._nc.psum_tensor(name, list(shape), dt))


class TL:
    def __init__(self, t, name):
        self.t = t
        self.b = Buf(name)


class Ctx:
    def __init__(self, nc, es, tag, debug=False):
        self.rnc = nc
        self.nc = _Alloc(nc, es)
        self.es = es
        self.tag = tag
        self.P = Prog(nc)
        self.groups = []
        self.debug = debug

    def group(self, final=False):
        sem = self.rnc.alloc_semaphore(f"{self.tag}g{len(self.groups)}")
        g = Group(sem, final)
        self.groups.append(g)
        return g

    def sb(self, name, shape, dt):
        return TL(self.nc.alloc_sbuf_tensor(self.tag + name, list(shape), dt), name)

    def ps(self, name, shape, dt):
        t = TL(self.nc.alloc_psum_tensor(self.tag + name, list(shape), dt), name)
        t.b.psum = True
        return t


_GSTATE = {}


def run_phase(nc, tag, fn, debug=False):
    with ExitStack() as es:
        cx = Ctx(nc, es, tag, debug)
        import os
        for _i in range(int(os.environ.get('DUMMYSEM', '0'))):
            es.enter_context(nc.semaphore(f"{tag}dummy{_i}"))
        finals = fn(cx)
        st = _GSTATE.setdefault(id(nc), {})
        if "sems" not in st:
            st["sems"] = {e: nc.alloc_semaphore(f"s_{e}") for e in Prog.ENG}
            st["base"] = {e: 0 for e in Prog.ENG}
        with nc.Block() as block:
            cx.P.emit(block, st["sems"], finals, st["base"])


def _rr(lst, i):
    return lst[i % len(lst)]


def ffn_phase(cx, S, xin, xout, gvec, wg, wu, wd, identd):
    nc, P = cx.nc, cx.P
    tag = cx.tag
    xin_buf, xout_buf = Buf("xin"), Buf("xout")
    ident = nc.alloc_sbuf_tensor(f"{tag}ident", [128, 128], BF16)
    identb = Buf("ident")
    cg0 = cx.group(final=True)
    P.op("sp", lambda e: e.dma_start(out=ident[:], in_=identd), writes=[identb], group=cg0)
    TT = 256
    NT = S // TT
    WG = nc.alloc_sbuf_tensor(f"{tag}WG", [128, 8, DFF], BF16)
    WU = nc.alloc_sbuf_tensor(f"{tag}WU", [128, 8, DFF], BF16)
    WD = nc.alloc_sbuf_tensor(f"{tag}WD", [128, NFF, D], BF16)
    stg = [nc.alloc_sbuf_tensor(f"{tag}stg{i}", [128, DFF // 2], F32) for i in range(2)]
    stg_b = [Buf(f"stg{i}") for i in range(2)]
    stg_g = [cx.group() for _ in range(2)]
    wbuf = Buf("W")
    G = nc.alloc_sbuf_tensor(f"{tag}G", [128, D], F32)
    gb = Buf("G")
    cg = cx.group(final=True)
    P.op("sp", lambda e: e.dma_start(out=G[:], in_=gvec.partition_broadcast(128)), writes=[gb], group=cg)
    nhalf = nc.alloc_sbuf_tensor(f"{tag}nh", [128, 2], F32)
    nhb = Buf("nh")
    P.op("pool", lambda e: e.memset(nhalf[:], -0.5), writes=[nhb])

    conv_eng = ["act", "dve", "pool"]
    k = 0

    def load_conv(src_ap, dst_ap, width):
        nonlocal k
        s = k % 2
        st, sb_, sg = stg[s], stg_b[s], stg_g[s]
        P.op("sp", lambda e: e.dma_start(out=st[:, 0:width], in_=src_ap), writes=[sb_], group=sg)
        ce = conv_eng[k % 3]
        if ce == "act":
            P.op("act", lambda e: e.copy(out=dst_ap, in_=st[:, 0:width]), reads=[sb_], writes=[wbuf])
        else:
            P.op(ce, lambda e: e.tensor_copy(out=dst_ap, in_=st[:, 0:width]), reads=[sb_], writes=[wbuf])
        k += 1

    HF = DFF // 2
    for kc in range(8):
        for hh in range(2):
            load_conv(wg[kc * 128:(kc + 1) * 128, hh * HF:(hh + 1) * HF], WG[:, kc, hh * HF:(hh + 1) * HF], HF)
            load_conv(wu[kc * 128:(kc + 1) * 128, hh * HF:(hh + 1) * HF], WU[:, kc, hh * HF:(hh + 1) * HF], HF)
    for f in range(NFF):
        load_conv(wd[f * 128:(f + 1) * 128, :], WD[:, f, :], D)

    xt = [nc.alloc_sbuf_tensor(f"{tag}xt{i}", [128, 2, D], F32) for i in range(2)]
    xt_b = [Buf(f"xt{i}") for i in range(2)]
    xt_g = [cx.group() for _ in range(2)]
    st_g = [cx.group() for _ in range(2)]
    junk = nc.alloc_sbuf_tensor(f"{tag}junk", [128, D], BF16)
    junk_b = Buf("junk")
    ss = [nc.alloc_sbuf_tensor(f"{tag}ss{i}", [128, 2], F32) for i in range(2)]
    ss_b = [Buf(f"ss{i}") for i in range(2)]
    ms = [nc.alloc_sbuf_tensor(f"{tag}ms{i}", [128, 2], F32) for i in range(2)]
    ms_b = [Buf(f"ms{i}") for i in range(2)]
    rs = [nc.alloc_sbuf_tensor(f"{tag}rs{i}", [128, 2], F32) for i in range(2)]
    rs_b = [Buf(f"rs{i}") for i in range(2)]
    hb = [nc.alloc_sbuf_tensor(f"{tag}h{i}", [128, 2, D], BF16) for i in range(2)]
    hb_b = [Buf(f"h{i}") for i in range(2)]
    hT = [nc.alloc_sbuf_tensor(f"{tag}hT{i}", [128, 8, TT], BF16) for i in range(2)]
    hT_b = [Buf(f"hT{i}") for i in range(2)]
    actT = [nc.alloc_sbuf_tensor(f"{tag}actT{i}", [128, NFF, TT], BF16) for i in range(2)]
    actT_b = [Buf(f"actT{i}") for i in range(2)]
    sg = [nc.alloc_sbuf_tensor(f"{tag}sg{i}", [128, 512], F32) for i in range(2)]
    sg_b = [Buf(f"sg{i}") for i in range(2)]
    pT = [nc.alloc_psum_tensor(f"{tag}pT{i}", [128, 1024], BF16) for i in range(2)]
    pT_b = [Buf(f"pT{i}") for i in range(2)]
    pG = [nc.alloc_psum_tensor(f"{tag}pG{i}", [128, 512], F32) for i in range(2)]
    pG_b = [Buf(f"pG{i}") for i in range(2)]
    pU = [nc.alloc_psum_tensor(f"{tag}pU{i}", [128, 512], F32) for i in range(2)]
    pU_b = [Buf(f"pU{i}") for i in range(2)]
    pD = [nc.alloc_psum_tensor(f"{tag}pD{i}", [128, 512], F32) for i in range(2)]
    pD_b = [Buf(f"pD{i}") for i in range(2)]

    def stage_load(i):
        s = i % 2
        t0 = i * TT
        P.op("sp", lambda e: e.dma_start(
            out=xt[s][:], in_=xin[t0:t0 + TT, :].rearrange("(s p) d -> p s d", p=128)),
            reads=[xin_buf], writes=[xt_b[s]], group=xt_g[s])

    def stage_norm(i):
        s = i % 2
        for sub in range(2):
            P.op("act", lambda e, sub=sub: e.activation(
                out=junk[:], in_=xt[s][:, sub, :], func=AF.Square, accum_out=ss[s][:, sub:sub + 1]),
                reads=[xt_b[s]], writes=[junk_b, ss_b[s]])
        P.op("dve", lambda e: e.tensor_scalar(out=ms[s][:], in0=ss[s][:], scalar1=1.0 / D, scalar2=EPS,
                                              op0=ALU.mult, op1=ALU.add), reads=[ss_b[s]], writes=[ms_b[s]])
        P.op("pool", lambda e: e.tensor_tensor(out=rs[s][:], in0=ms[s][:], in1=nhalf[:], op=ALU.pow),
             reads=[ms_b[s], nhb], writes=[rs_b[s]])
        for sub in range(2):
            P.op("dve", lambda e, sub=sub: e.scalar_tensor_tensor(
                out=hb[s][:, sub, :], in0=xt[s][:, sub, :], scalar=rs[s][:, sub:sub + 1], in1=G[:],
                op0=ALU.mult, op1=ALU.mult), reads=[xt_b[s], rs_b[s], gb], writes=[hb_b[s]])

    def stage_T(i):
        s = i % 2
        for half in range(2):
            pb, pbb = pT[half], pT_b[half]
            for kk in range(4):
                kc = half * 4 + kk
                for sub in range(2):
                    P.op("pe", lambda e, kc=kc, sub=sub, kk=kk, pb=pb: e.transpose(
                        out=pb[:, kk * TT + sub * 128: kk * TT + (sub + 1) * 128],
                        in_=hb[s][:, sub, kc * 128:(kc + 1) * 128], identity=ident[:]),
                        reads=[hb_b[s], identb], writes=[pbb])
            if half == 0:
                P.op("act", lambda e, pb=pb: e.copy(out=hT[s][:, 0:4, :], in_=pb[:].rearrange("p (k t) -> p k t", k=4)),
                     reads=[pbb], writes=[hT_b[s]])
            else:
                P.op("dve", lambda e, pb=pb: e.tensor_copy(out=hT[s][:, 4:8, :], in_=pb[:].rearrange("p (k t) -> p k t", k=4)),
                     reads=[pbb], writes=[hT_b[s]])

    def stage_GU(i):
        s = i % 2
        for f2 in range(NFF // 2):
            q = f2 % 2
            for (W_, pp, ppb) in ((WG, pG[q], pG_b[q]), (WU, pU[q], pU_b[q])):
                for c in range(2):
                    f = 2 * f2 + c
                    for kc in range(8):
                        P.op("pe", lambda e, W_=W_, pp=pp, c=c, f=f, kc=kc: e.matmul(
                            out=pp[:, c * TT:(c + 1) * TT], lhsT=W_[:, kc, f * 128:(f + 1) * 128],
                            rhs=hT[s][:, kc, :], start=(kc == 0), stop=(kc == 7)),
                            reads=[wbuf, hT_b[s]], writes=[ppb])
            P.op("act", lambda e, q=q: e.activation(out=sg[q][:], in_=pG[q][:], func=AF.Silu),
                 reads=[pG_b[q]], writes=[sg_b[q]])
            P.op("dve", lambda e, q=q, f2=f2: e.tensor_tensor(
                out=actT[s][:, 2 * f2:2 * f2 + 2, :], in0=sg[q][:].rearrange("p (c t) -> p c t", c=2),
                in1=pU[q][:].rearrange("p (c t) -> p c t", c=2), op=ALU.mult),
                reads=[sg_b[q], pU_b[q]], writes=[actT_b[s]])

    def stage_D(i):
        s = i % 2
        t0 = i * TT
        n = 0
        for sub in range(2):
            for dh in range(2):
                q = n % 2
                n += 1
                for f in range(NFF):
                    P.op("pe", lambda e, q=q, f=f, sub=sub, dh=dh: e.matmul(
                        out=pD[q][:], lhsT=actT[s][:, f, sub * 128:(sub + 1) * 128],
                        rhs=WD[:, f, dh * 512:(dh + 1) * 512], start=(f == 0), stop=(f == NFF - 1)),
                        reads=[wbuf, actT_b[s]], writes=[pD_b[q]])
                P.op("dve", lambda e, q=q, sub=sub, dh=dh: e.scalar_tensor_tensor(
                    out=xt[s][:, sub, dh * 512:(dh + 1) * 512], in0=pD[q][:], scalar=0.5,
                    in1=xt[s][:, sub, dh * 512:(dh + 1) * 512], op0=ALU.mult, op1=ALU.add),
                    reads=[pD_b[q], xt_b[s]], writes=[xt_b[s]])
        P.op("pool", lambda e: e.dma_start(
            out=xout[t0:t0 + TT, :].rearrange("(s p) d -> p s d", p=128), in_=xt[s][:]),
            reads=[xt_b[s]], writes=[xout_buf], group=st_g[s])

    stage_load(0)
    if NT > 1:
        stage_load(1)
    stage_norm(0)
    stage_T(0)
    for i in range(NT):
        stage_GU(i)
        if i + 1 < NT:
            stage_norm(i + 1)
            stage_T(i + 1)
        stage_D(i)
        if i + 2 < NT:
            stage_load(i + 2)
    return {"pool": st_g}


def _b(L):
    return [x.b for x in L]


def mm(P, out, lhsT, rhs, st, sp, R, W):
    P.op("pe", lambda e: e.matmul(out=out, lhsT=lhsT, rhs=rhs, start=st, stop=sp), _b(R), _b(W))


def tr(P, out, in_, ident, R, W):
    P.op("pe", lambda e: e.transpose(out=out, in_=in_, identity=ident), _b(R), _b(W))


def act(P, out, in_, func, R, W, **kw):
    P.op("act", lambda e: e.activation(out=out, in_=in_, func=func, **kw), _b(R), _b(W))


def ts(P, eng, out, in0, s1, s2, op0, op1, R, W, accum=None):
    if accum is None:
        P.op(eng, lambda e: e.tensor_scalar(out=out, in0=in0, scalar1=s1, scalar2=s2, op0=op0, op1=op1), _b(R), _b(W))
    else:
        P.op(eng, lambda e: e.tensor_scalar(out=out, in0=in0, scalar1=s1, scalar2=s2, op0=op0, op1=op1,
                                            accum_out=accum), _b(R), _b(W))


def tt(P, eng, out, in0, in1, op, R, W):
    P.op(eng, lambda e: e.tensor_tensor(out=out, in0=in0, in1=in1, op=op), _b(R), _b(W))


def stt(P, out, in0, scalar, in1, op0, op1, R, W):
    P.op("dve", lambda e: e.scalar_tensor_tensor(out=out, in0=in0, scalar=scalar, in1=in1, op0=op0, op1=op1),
         _b(R), _b(W))


def cp(P, eng, out, in_, R, W):
    if eng == "act":
        P.op("act", lambda e: e.copy(out=out, in_=in_), _b(R), _b(W))
    else:
        P.op(eng, lambda e: e.tensor_copy(out=out, in_=in_), _b(R), _b(W))


def red(P, out, in_, op, R, W, **kw):
    P.op("dve", lambda e: e.tensor_reduce(out=out, in_=in_, axis=AX.X, op=op, **kw), _b(R), _b(W))


def dma(P, eng, out, in_, R, W, g, **kw):
    P.op(eng, lambda e: e.dma_start(out=out, in_=in_, **kw), _b(R), _b(W), group=g)


def mset(P, eng, ap, val, W):
    P.op(eng, lambda e: e.memset(ap, val), [], _b(W))


def rstd_ops(cx, ssq, n, rs, nh, R):
    P = cx.P
    ts(P, "dve", ssq.t[:], ssq.t[:], 1.0 / n, EPS, ALU.mult, ALU.add, R + [ssq], [ssq])
    tt(P, "pool", rs.t[:], ssq.t[:], nh.t[:, 0:ssq.t.shape[1]], ALU.pow, [ssq, nh], [rs])


IDX_SCALE = (64 ** -0.5) * (8 ** -0.5)
ATT_SCALE = 64 ** -0.5
MQ_SCALE = 64 ** -0.5


def load_ident(cx, Cd, cg):
    ident = cx.sb("ident", [128, 128], BF16)
    dma(cx.P, "sp", ident.t[:], Cd["c_ident"], [], [ident], cg)
    return ident


def norm_transpose(cx, xt, G, ident, nh, hb, hT, pT, ss, rs, junk, ntok_sub=2):
    P = cx.P
    for sub in range(ntok_sub):
        act(P, junk.t[:], xt.t[:, sub, :], AF.Square, [xt], [junk, ss], accum_out=ss.t[:, sub:sub + 1])
    rstd_ops(cx, ss, D, rs, nh, [])
    for sub in range(ntok_sub):
        stt(P, hb.t[:, sub, :], xt.t[:, sub, :], rs.t[:, sub:sub + 1], G.t[:], ALU.mult, ALU.mult,
            [xt, rs, G], [hb])
    for half in range(2):
        pb = pT[half]
        for kk in range(4):
            kc = half * 4 + kk
            for sub in range(ntok_sub):
                tr(P, pb.t[:, kk * 256 + sub * 128: kk * 256 + (sub + 1) * 128],
                   hb.t[:, sub, kc * 128:(kc + 1) * 128], ident.t[:], [hb, ident], [pb])
        cp(P, "act" if half == 0 else "dve", hT.t[:, half * 4:half * 4 + 4, :],
           pb.t[:].rearrange("p (k t) -> p k t", k=4), [pb], [hT])


def proj_phase(cx, S, X1, A, Wd, Cd):
    nc, P = cx.nc, cx.P
    TT = 256
    NT = S // TT
    cg = cx.group(final=True)
    ident = load_ident(cx, Cd, cg)
    WIN = cx.sb("WIN", [128, 8, DIN], BF16)
    stg = [cx.sb(f"stg{i}", [128, 824], F32) for i in range(2)]
    stg_g = [cx.group() for _ in range(2)]
    k = 0
    for kc in range(8):
        for part in range(6):
            s_ = k % 2
            c0 = part * 824
            dma(P, "sp", stg[s_].t[:], Wd["w_in"][kc * 128:(kc + 1) * 128, c0:c0 + 824], [], [stg[s_]], stg_g[s_])
            cp(P, ["act", "dve", "pool"][k % 3], WIN.t[:, kc, c0:c0 + 824], stg[s_].t[:], [stg[s_]], [WIN])
            k += 1
    import os
    SL = int(os.environ.get('SL', '99'))
    G = cx.sb("G", [128, D], F32)
    dma(P, "sp", G.t[:], Wd["mix_norm"].partition_broadcast(128), [], [G], cg)
    GQ = cx.sb("GQ", [128, 512], F32)
    for h in range(8):
        if SL >= 2:
            dma(P, "sp", GQ.t[:, h * 64:(h + 1) * 64], Wd["q_norm"].partition_broadcast(128), [], [GQ], cg)
    if SL >= 2:
        ts(P, "dve", GQ.t[:], GQ.t[:], ATT_SCALE, None, ALU.mult, ALU.bypass, [GQ], [GQ])
    GK = cx.sb("GK", [128, 128], F32)
    for h in range(2):
        if SL >= 3:
            dma(P, "sp", GK.t[:, h * 64:(h + 1) * 64], Wd["k_norm"].partition_broadcast(128), [], [GK], cg)
    GI = cx.sb("GI", [128, 64], F32)
    if SL >= 3:
        dma(P, "sp", GI.t[:], Wd["idx_k_norm"].partition_broadcast(128), [], [GI], cg)
    CW = cx.sb("CW", [128, 4, 4], F32)
    CB = cx.sb("CB", [128, 4], F32)
    if SL >= 4:
        dma(P, "sp", CB.t[:], Wd["conv_b"], [], [CB], cg)
    if SL >= 4:
      dma(P, "sp", CW.t[:], Wd["conv_w"].rearrange("p (c j) -> p c j", c=4), [], [CW], cg)
    wst = cx.sb("wst", [128, 2, 4, 64], F32)
    WM = cx.sb("WM", [128, 2, 4, 64], BF16)
    if SL >= 5:
        dma(P, "sp", wst.t[:, 0], Wd["w_mq"].rearrange("c (h d) -> c h d", h=4), [], [wst], cg)
        dma(P, "sp", wst.t[:, 1], Wd["w_mk"].rearrange("c (h d) -> c h d", h=4), [], [wst], cg)
        cp(P, "dve", WM.t[:], wst.t[:], [wst], [WM])
    BIF = cx.sb("BIF", [128, 8], F32)
    if SL >= 6:
        dma(P, "sp", BIF.t[:, 0:4], Wd["b_i"].partition_broadcast(128), [], [BIF], cg)
        dma(P, "sp", BIF.t[:, 4:8], Wd["b_f"].partition_broadcast(128), [], [BIF], cg)
    NH = cx.sb("NH", [128, 8], F32)
    mset(P, "pool", NH.t[:], -0.5, [NH])

    R2 = range(2)
    xt = [cx.sb(f"xt{i}", [128, 2, D], F32) for i in R2]
    xt_g = [cx.group() for _ in R2]
    junk = cx.sb("junk", [128, D], BF16)
    ss = [cx.sb(f"ss{i}", [128, 2], F32) for i in R2]
    rs = [cx.sb(f"rs{i}", [128, 2], F32) for i in R2]
    hb = [cx.sb("hb0", [128, 2, D], BF16)] * 2
    hT = [cx.sb(f"hT{i}", [128, 8, TT], BF16) for i in R2]
    pT = [cx.ps(f"pT{i}", [128, 1024], BF16) for i in R2]
    pX = [cx.ps(f"pX{i}", [128, 1024], BF16) for i in R2]
    pM = [cx.ps(f"pM{i}", [128, 512], F32) for i in range(4)]
    nbank = [0]

    def bank():
        b = pM[nbank[0] % 4]
        nbank[0] += 1
        return b

    sqt = [cx.sb(f"sqt{i}", [128, 512], F32) for i in R2]
    t1 = [cx.sb(f"t1{i}", [128, 512], F32) for i in R2]
    qn = [cx.sb(f"qn{i}", [128, 512], BF16) for i in R2]
    kn = [cx.sb(f"kn{i}", [128, 128], BF16) for i in R2]
    ikn = [cx.sb(f"ikn{i}", [128, 64], BF16) for i in R2]
    iqs = [cx.sb(f"iqs{i}", [128, 512], BF16) for i in R2]
    s8 = [cx.sb(f"s8{i}", [128, 8], F32) for i in R2]
    r8 = [cx.sb(f"r8{i}", [128, 8], F32) for i in R2]
    s2 = [cx.sb(f"s2{i}", [128, 2], F32) for i in R2]
    r2 = [cx.sb(f"r2{i}", [128, 2], F32) for i in R2]
    s1 = [cx.sb(f"s1{i}", [128, 1], F32) for i in R2]
    r1 = [cx.sb(f"r1{i}", [128, 1], F32) for i in R2]
    absw = [cx.sb(f"absw{i}", [128, 8], F32) for i in R2]
    zt = [cx.sb(f"zt{i}", [128, 8], F32) for i in R2]
    et = [cx.sb(f"et{i}", [128, 4], F32) for i in R2]
    def outs(name, shape, dt):
        return [cx.sb(f"{name}{i}", shape, dt) for i in R2], [cx.group() for _ in R2]
    QTt, QT_g = outs("QTt", [128, 4, TT], BF16)
    KTt, KT_g = outs("KTt", [128, TT], BF16)
    IQTt, IQT_g = outs("IQTt", [128, 4, TT], BF16)
    IKTt, IKT_g = outs("IKTt", [64, TT], BF16)
    vb, V_g = outs("vb", [128, 2, 128], BF16)
    sgn, SGN_g = outs("sgn", [128, 2, 8], F32)
    mvb, MV_g = outs("mvb", [128, 2, 512], BF16)
    lif, LIF_g = outs("lif", [128, 2, 8], F32)
    mob, MO_g = outs("mob", [128, 2, 512], BF16)
    gat, GAT_g = outs("gat", [128, 8, TT], BF16)
    gbt, GBT_g = outs("gbt", [128, 8, TT], BF16)
    mqt, MQT_g = outs("mqt", [64, 4, TT], BF16)
    mkt, MKT_g = outs("mkt", [64, 4, TT], BF16)
    mkb, MK_g = outs("mkb", [128, 2, 256], BF16)
    XM = [cx.sb(f"XM{i}", [128, 4, TT + 3], F32) for i in R2]
    mset(P, "pool", XM[0].t[:, :, 0:3], 0.0, [XM[0]])
    acc = [cx.sb("acc0", [128, 4, TT], F32)] * 2
    sig = [cx.sb("sig0", [128, 4, TT], F32)] * 2
    xct = [cx.sb(f"xct{i}", [128, 4, TT], BF16) for i in R2]
    X1b = TL(None, "X1")
    allg = []

    def load(i):
        s_ = i % 2
        t0 = i * TT
        dma(P, "sp", xt[s_].t[:], X1[t0:t0 + TT, :].rearrange("(s p) d -> p s d", p=128), [X1b], [xt[s_]], xt_g[s_])

    def tile(i):
        import os
        LV = int(os.environ.get('LV', '99'))
        if LV < 1:
            return
        s_ = i % 2
        t0 = i * TT
        tsl = slice(t0, t0 + TT)
        norm_transpose(cx, xt[s_], G, ident, NH, hb[s_], hT[s_], pT, ss[s_], rs[s_], junk)
        H = hT[s_]

        def tok_group(sub, c0, width):
            b = bank()
            for kc in range(8):
                mm(P, b.t[:, 0:width], H.t[:, kc, sub * 128:(sub + 1) * 128], WIN.t[:, kc, c0:c0 + width],
                   kc == 0, kc == 7, [H, WIN], [b])
            return b

        pq, pi = pX[0], pX[1]
        ssl = None
        for sub in range(2):
            u = sub
            if LV < 2:
                continue
            b = tok_group(sub, 0, 512)
            act(P, sqt[u].t[:], b.t[:], AF.Square, [b], [sqt[u]])
            red(P, s8[u].t[:], sqt[u].t[:].rearrange("p (h d) -> p h d", h=8), ALU.add, [sqt[u]], [s8[u]])
            rstd_ops(cx, s8[u], 64, r8[u], NH, [])
            tt(P, "dve", t1[u].t[:].rearrange("p (h d) -> p h d", h=8), b.t[:].rearrange("p (h d) -> p h d", h=8),
               r8[u].t[:].unsqueeze(2).broadcast_to([128, 8, 64]), ALU.mult, [b, r8[u]], [t1[u]])
            tt(P, "pool", qn[u].t[:].rearrange("p (r g d) -> p g r d", r=4, g=2),
               t1[u].t[:].rearrange("p (g r d) -> p g r d", g=2, r=4),
               GQ.t[:].rearrange("p (g r d) -> p g r d", g=2, r=4), ALU.mult, [t1[u], GQ], [qn[u]])
            ssl = slice(sub * 128, (sub + 1) * 128)
            for r in range(4):
                tr(P, pq.t[:, r * 128:(r + 1) * 128], qn[u].t[:, r * 128:(r + 1) * 128],
                   ident.t[:], [qn[u], ident], [pq])
            cp(P, "act", QTt[s_].t[:, :, ssl], pq.t[:, 0:512].rearrange("p (r t) -> p r t", r=4), [pq], [QTt[s_]])
            if sub == 1:
                dma(P, "sp", A["QT"][:, :, tsl].rearrange("r p t -> p r t"), QTt[s_].t[:], [QTt[s_]], [], QT_g[s_])
            if LV < 3:
                continue
            b = tok_group(sub, 512, 256)
            act(P, sqt[u].t[:, 0:128], b.t[:, 0:128], AF.Square, [b], [sqt[u]])
            red(P, s2[u].t[:], sqt[u].t[:, 0:128].rearrange("p (h d) -> p h d", h=2), ALU.add, [sqt[u]], [s2[u]])
            rstd_ops(cx, s2[u], 64, r2[u], NH, [])
            tt(P, "dve", t1[u].t[:, 0:128].rearrange("p (h d) -> p h d", h=2),
               b.t[:, 0:128].rearrange("p (h d) -> p h d", h=2),
               r2[u].t[:].unsqueeze(2).broadcast_to([128, 2, 64]), ALU.mult, [b, r2[u]], [t1[u]])
            tt(P, "pool", kn[u].t[:], t1[u].t[:, 0:128], GK.t[:], ALU.mult, [t1[u], GK], [kn[u]])
            tr(P, pq.t[:, 512:640], kn[u].t[:], ident.t[:], [kn[u], ident], [pq])
            cp(P, "act", vb[s_].t[:, sub, :], b.t[:, 128:256], [b], [vb[s_]])
            cp(P, "dve", KTt[s_].t[:, ssl], pq.t[:, 512:640], [pq], [KTt[s_]])
            if sub == 1:
                dma(P, "sp", A["KT"][:, tsl], KTt[s_].t[:], [KTt[s_]], [], KT_g[s_])
                dma(P, "sp", A["V"][tsl, :].rearrange("(s p) d -> p s d", p=128), vb[s_].t[:], [vb[s_]], [], V_g[s_])
            if LV < 4:
                continue
            bw = tok_group(sub, 1280, 72)
            ts(P, "dve", zt[u].t[:], bw.t[:, 64:72], -IDX_SCALE, None, ALU.mult, ALU.bypass, [bw], [zt[u]])
            stt(P, absw[u].t[:], bw.t[:, 64:72], IDX_SCALE, zt[u].t[:], ALU.mult, ALU.max, [bw, zt[u]], [absw[u]])
            ts(P, "dve", sgn[s_].t[:, sub, :], bw.t[:, 64:72], 0.0, 2.0, ALU.is_ge, ALU.mult, [bw], [sgn[s_]])
            ts(P, "dve", sgn[s_].t[:, sub, :], sgn[s_].t[:, sub, :], -1.0, None, ALU.add, ALU.bypass, [sgn[s_]], [sgn[s_]])
            act(P, sqt[u].t[:, 0:64], bw.t[:, 0:64], AF.Square, [bw], [sqt[u], s1[u]], accum_out=s1[u].t[:])
            rstd_ops(cx, s1[u], 64, r1[u], NH, [])
            stt(P, ikn[u].t[:], bw.t[:, 0:64], r1[u].t[:], GI.t[:], ALU.mult, ALU.mult, [bw, r1[u], GI], [ikn[u]])
            b = tok_group(sub, 768, 512)
            tt(P, "dve", iqs[u].t[:].rearrange("p (h d) -> p h d", h=8), b.t[:].rearrange("p (h d) -> p h d", h=8),
               absw[u].t[:].unsqueeze(2).broadcast_to([128, 8, 64]), ALU.mult, [b, absw[u]], [iqs[u]])
            for r in range(4):
                tr(P, pi.t[:, r * 128:(r + 1) * 128], iqs[u].t[:, r * 128:(r + 1) * 128],
                   ident.t[:], [iqs[u], ident], [pi])
            tr(P, pi.t[0:64, 512:640], ikn[u].t[:], ident.t[:], [ikn[u], ident], [pi])
            cp(P, "act", IQTt[s_].t[:, :, ssl], pi.t[:, 0:512].rearrange("p (r t) -> p r t", r=4), [pi], [IQTt[s_]])
            cp(P, "dve", IKTt[s_].t[:, ssl], pi.t[0:64, 512:640], [pi], [IKTt[s_]])
            if sub == 1:
                dma(P, "sp", A["IQT"][:, :, tsl].rearrange("r p t -> p r t"), IQTt[s_].t[:], [IQTt[s_]], [], IQT_g[s_])
                dma(P, "sp", A["IKT"][:, tsl], IKTt[s_].t[:], [IKTt[s_]], [], IKT_g[s_])
                dma(P, "sp", A["SGN"][tsl, :].rearrange("(s p) h -> p s h", p=128), sgn[s_].t[:], [sgn[s_]], [], SGN_g[s_])
            if LV < 5:
                continue
            b = tok_group(sub, 1864, 512)
            cp(P, "dve", mvb[s_].t[:, sub, :], b.t[:], [b], [mvb[s_]])
            b = tok_group(sub, 2376, 8)
            tt(P, "dve", zt[u].t[:], b.t[:, 0:8], BIF.t[:], ALU.add, [b, BIF], [zt[u]])
            cp(P, "pool", lif[s_].t[:, sub, 0:4], zt[u].t[:, 0:4], [zt[u]], [lif[s_]])
            act(P, et[u].t[:], zt[u].t[:, 4:8], AF.Exp, [zt[u]], [et[u]], scale=-1.0)
            act(P, et[u].t[:], et[u].t[:], AF.Ln, [et[u]], [et[u]], bias=1.0)
            ts(P, "dve", lif[s_].t[:, sub, 4:8], et[u].t[:], -1.0, None, ALU.mult, ALU.bypass, [et[u]], [lif[s_]])
            b = tok_group(sub, 2384, 512)
            act(P, mob[s_].t[:, sub, :], b.t[:], AF.Sigmoid, [b], [mob[s_]])
        if LV < 6:
            return
        dma(P, "sp", A["MV"][tsl, :].rearrange("(s p) d -> p s d", p=128), mvb[s_].t[:], [mvb[s_]], [], MV_g[s_])
        dma(P, "sp", A["LIF"][tsl, :].rearrange("(s p) d -> p s d", p=128), lif[s_].t[:], [lif[s_]], [], LIF_g[s_])
        dma(P, "sp", A["MO"][tsl, :].rearrange("(s p) d -> p s d", p=128), mob[s_].t[:], [mob[s_]], [], MO_g[s_])

        def feat_pair(c0):
            b = bank()
            for c in range(2):
                for kc in range(8):
                    mm(P, b.t[:, c * TT:(c + 1) * TT], WIN.t[:, kc, c0 + c * 128:c0 + (c + 1) * 128], H.t[:, kc, :],
                       kc == 0, kc == 7, [H, WIN], [b])
            return b

        if LV < 7:
            return
        for (base, dst) in ((2896, gat[s_]), (3920, gbt[s_])):
            for pr in range(4):
                b = feat_pair(base + pr * 256)
                act(P, dst.t[:, 2 * pr:2 * pr + 2, :], b.t[:].rearrange("p (c t) -> p c t", c=2), AF.Sigmoid, [b], [dst])
        dma(P, "sp", A["GAT"][:, :, tsl].rearrange("c p t -> p c t"), gat[s_].t[:], [gat[s_]], [], GAT_g[s_])
        dma(P, "sp", A["GBT"][:, :, tsl].rearrange("c p t -> p c t"), gbt[s_].t[:], [gbt[s_]], [], GBT_g[s_])
        if LV < 8:
            return
        xm, xn = XM[s_], XM[1 - s_]
        for pr in range(2):
            b = feat_pair(1352 + pr * 256)
            cp(P, "dve", xm.t[:, 2 * pr:2 * pr + 2, 3:3 + TT], b.t[:].rearrange("p (c t) -> p c t", c=2), [b], [xm])
        cp(P, "pool", xn.t[:, :, 0:3], xm.t[:, :, TT:TT + 3], [xm], [xn])
        ac = acc[s_]
        for c in range(4):
            ts(P, "dve", ac.t[:, c, :], xm.t[:, c, 0:TT], CW.t[:, c, 0:1], CB.t[:, c:c + 1], ALU.mult, ALU.add,
               [xm, CW, CB], [ac])
            for j in range(1, 4):
                stt(P, ac.t[:, c, :], xm.t[:, c, j:j + TT], CW.t[:, c, j:j + 1], ac.t[:, c, :], ALU.mult, ALU.add,
                    [xm, CW, ac], [ac])
        act(P, sig[s_].t[:], ac.t[:], AF.Sigmoid, [ac], [sig[s_]])
        tt(P, "pool", xct[s_].t[:], ac.t[:], sig[s_].t[:], ALU.mult, [ac, sig[s_]], [xct[s_]])
        if LV < 9:
            return
        xc = xct[s_]
        for (wi, dst, scale, name, grp) in ((0, mqt[s_], MQ_SCALE, "MQT", MQT_g[s_]), (1, mkt[s_], 1.0, "MKT", MKT_g[s_])):
            for hp in range(2):
                b = bank()
                for c in range(2):
                    h = hp * 2 + c
                    mm(P, b.t[0:64, c * TT:(c + 1) * TT], WM.t[:, wi, h, :], xc.t[:, h, :], True, True, [WM, xc], [b])
                act(P, dst.t[:, 2 * hp:2 * hp + 2, :], b.t[0:64, :].rearrange("p (c t) -> p c t", c=2), AF.Copy,
                    [b], [dst], scale=scale)
            dma(P, "sp", A[name][:, :, tsl].rearrange("h d t -> d h t"), dst.t[:], [dst], [], grp)
        for sub in range(2):
            b = bank()
            for h in range(4):
                mm(P, b.t[:, h * 64:(h + 1) * 64], xc.t[:, h, sub * 128:(sub + 1) * 128], WM.t[:, 1, h, :], True, True,
                   [xc, WM], [b])
            cp(P, "dve", mkb[s_].t[:, sub, :], b.t[:, 0:256], [b], [mkb[s_]])
        dma(P, "sp", A["MK"][tsl, :].rearrange("(s p) d -> p s d", p=128), mkb[s_].t[:], [mkb[s_]], [], MK_g[s_])

    load(0)
    if NT > 1:
        load(1)
    for i in range(NT):
        tile(i)
        if i + 2 < NT:
            load(i + 2)
    fin = QT_g + KT_g + IQT_g + IKT_g + V_g + SGN_g + MV_g + LIF_g + MO_g + GAT_g + GBT_g + MQT_g + MKT_g + MK_g
    return {"sp": fin}


NIT = 14


def dsa_phase(cx, S, A, Cd):
    nc, P = cx.nc, cx.P
    NB = S // 512
    NJ = S // 128
    cg = cx.group(final=True)
    ident = load_ident(cx, Cd, cg)
    IK2 = cx.sb("IK2", [128, S], BF16)
    dma(P, "sp", IK2.t[0:64, :], A["IKT"], [], [IK2], cg)
    dma(P, "sp", IK2.t[64:128, :], A["IKT"], [], [IK2], cg)
    KT = cx.sb("KT", [128, S], BF16)
    dma(P, "sp", KT.t[:], A["KT"], [], [KT], cg)
    VA = cx.sb("VA", [128, NJ, 2, 65], BF16)
    mset(P, "pool", VA.t[:], 1.0, [VA])
    for g_ in range(2):
        dma(P, "sp", VA.t[:, :, g_, 0:64], A["V"][:, g_ * 64:(g_ + 1) * 64].rearrange("(j p) d -> p j d", p=128),
            [], [VA], cg)
    CM = cx.sb("CM", [128, 4, 512], F32)
    dma(P, "sp", CM.t[:], Cd["c_cmask"].rearrange("i p s -> p i s"), [], [CM], cg)
    CROW = cx.sb("CROW", [128, NIT], F32)
    dma(P, "sp", CROW.t[:], Cd["c_steps"], [], [CROW], cg)
    ONES = cx.sb("ONES", [128, 64], F32)
    mset(P, "pool", ONES.t[:], 1.0, [ONES])

    SC = cx.sb("SC", [128, S], F32)
    MK = cx.sb("MK", [128, S], BF16)
    MT = cx.sb("MT", [128, NJ, 512], BF16)
    IQb = cx.sb("IQb", [128, 4, 512], BF16)
    Qb = cx.sb("Qb", [128, 4, 512], BF16)
    SGb = cx.sb("SGb", [128, 4, 8], F32)
    ld_g1, ld_g2, ld_g3 = cx.group(), cx.group(), cx.group()
    DG = [cx.sb(f"DG{i}", [128, 8, 128], BF16) for i in range(2)]
    Rt = [cx.sb(f"Rt{i}", [128, 512], BF16) for i in range(3)]
    Et = [cx.sb(f"Et{i}", [128, 512], BF16) for i in range(3)]
    Pt = [cx.sb(f"Pt{i}", [128, 512], BF16) for i in range(3)]
    M1 = cx.sb("M1", [128, 1], F32)
    LO = cx.sb("LO", [128, 1], F32)
    W0 = cx.sb("W0", [128, 1], F32)
    STEP = cx.sb("STEP", [128, NIT], F32)
    MID = cx.sb("MID", [128, 1], F32)
    CNT = cx.sb("CNT", [128, 1], F32)
    DT = cx.sb("DT", [128, 1], F32)
    RC = cx.sb("RC", [128, 512], F32)
    OS = cx.sb("OS", [64, 512], F32)
    YT = [cx.sb(f"YT{i}", [64, 512], BF16) for i in range(2)]
    YT_g = [cx.group() for _ in range(2)]
    pL = [cx.ps(f"pL{i}", [128, 512], F32) for i in range(3)]
    SCp = cx.ps("SCp", [128, 512], F32)
    pTr = cx.ps("pTr", [128, 1024], BF16)
    pS = [cx.ps(f"pS{i}", [128, 512], F32) for i in range(2)]
    pO = cx.ps("pO", [128, 512], F32)
    cnt = {"l": 0, "r": 0, "s": 0, "e": 0, "p": 0, "y": 0, "d": 0, "m": 0}

    def nxt(key, lst):
        v = lst[cnt[key] % len(lst)]
        cnt[key] += 1
        return v

    for b in range(NB):
        Nb = 512 * (b + 1)
        njb = 4 * (b + 1)
        bs = slice(b * 512, (b + 1) * 512)
        dma(P, "sp", IQb.t[:], A["IQT"][:, :, bs].rearrange("r p t -> p r t"), [], [IQb], ld_g1)
        dma(P, "sp", Qb.t[:], A["QT"][:, :, bs].rearrange("r p t -> p r t"), [], [Qb], ld_g2)
        dma(P, "sp", SGb.t[:], A["SGN"][bs, :].rearrange("(i p) h -> p i h", p=128), [], [SGb], ld_g3)
        for ii in range(4):
            tq = slice(ii * 128, (ii + 1) * 128)
            dg = nxt("d", DG)
            for h in range(8):
                ts(P, "dve", dg.t[:, h, :], ident.t[:], SGb.t[:, ii, h:h + 1], None, ALU.mult, ALU.bypass,
                   [ident, SGb], [dg])
            for kb in range(b + 1):
                ks = slice(kb * 512, (kb + 1) * 512)
                for h in range(8):
                    pr, half = h // 2, h % 2
                    ps_ = slice(half * 64, half * 64 + 64)
                    L = nxt("l", pL)
                    mm(P, L.t[:], IQb.t[ps_, pr, tq], IK2.t[ps_, ks], True, True, [IQb, IK2], [L])
                    R = nxt("r", Rt)
                    act(P, R.t[:], L.t[:], AF.Relu, [L], [R])
                    mm(P, SCp.t[:], dg.t[:, h, :], R.t[:], h == 0, h == 7, [dg, R], [SCp])
                cp(P, "act", SC.t[:, ks], SCp.t[:], [SCp], [SC])
            red(P, M1.t[:], SC.t[:, 0:Nb], ALU.max, [SC], [M1], apply_absolute_value=True)
            tt(P, "pool", SC.t[:, bs], SC.t[:, bs], CM.t[:, ii, :], ALU.add, [SC, CM], [SC])
            ts(P, "dve", LO.t[:], M1.t[:], -1.001, -1e-6, ALU.mult, ALU.add, [M1], [LO])
            ts(P, "dve", W0.t[:], M1.t[:], 2.002, 2e-6, ALU.mult, ALU.add, [M1], [W0])
            ts(P, "dve", STEP.t[:], CROW.t[:], W0.t[:], None, ALU.mult, ALU.bypass, [CROW, W0], [STEP])
            for k in range(NIT):
                tt(P, "dve", MID.t[:], LO.t[:], STEP.t[:, k:k + 1], ALU.add, [LO, STEP], [MID])
                ts(P, "dve", MK.t[:, 0:Nb], SC.t[:, 0:Nb], MID.t[:], None, ALU.is_ge, ALU.add, [SC, MID], [MK, CNT],
                   accum=CNT.t[:])
                stt(P, DT.t[:], CNT.t[:], 255.5, STEP.t[:, k:k + 1], ALU.is_ge, ALU.mult, [CNT, STEP], [DT])
                tt(P, "dve", LO.t[:], LO.t[:], DT.t[:], ALU.add, [LO, DT], [LO])
            ts(P, "dve", MK.t[:, 0:Nb], SC.t[:, 0:Nb], LO.t[:], None, ALU.is_ge, ALU.bypass, [SC, LO], [MK])
            for j0 in range(0, njb, 8):
                n = min(8, njb - j0)
                for jj in range(n):
                    j = j0 + jj
                    tr(P, pTr.t[:, jj * 128:(jj + 1) * 128], MK.t[:, j * 128:(j + 1) * 128], ident.t[:], [MK, ident], [pTr])
                cp(P, "act" if (cnt["m"] % 2 == 0) else "dve", MT.t[:, j0:j0 + n, tq],
                   pTr.t[:, 0:n * 128].rearrange("p (j t) -> p j t", t=128), [pTr], [MT])
                cnt["m"] += 1
        for h in range(8):
            g, r = h // 4, h % 4
            ps_ = slice(g * 64, g * 64 + 64)
            for j in range(njb):
                ST = nxt("s", pS)
                mm(P, ST.t[:], KT.t[ps_, j * 128:(j + 1) * 128], Qb.t[ps_, r, :], True, True, [KT, Qb], [ST])
                E = nxt("e", Et)
                act(P, E.t[:], ST.t[:], AF.Exp, [ST], [E])
                PT = nxt("p", Pt)
                tt(P, "pool", PT.t[:], E.t[:], MT.t[:, j, :], ALU.mult, [E, MT], [PT])
                mm(P, pO.t[0:65, :], VA.t[:, j, g, :], PT.t[:], j == 0, j == njb - 1, [VA, PT], [pO])
            P.op("dve", lambda e: e.reciprocal(out=RC.t[64:65, :], in_=pO.t[64:65, :]), [pO.b], [RC.b])
            BC = nxt("s", pS)
            mm(P, BC.t[0:64, :], ONES.t[64:65, 0:64], RC.t[64:65, :], True, True, [ONES, RC], [BC])
            cp(P, "act", OS.t[:], pO.t[0:64, :], [pO], [OS])
            y = cnt["y"] % 2
            cnt["y"] += 1
            tt(P, "dve", YT[y].t[:], OS.t[:], BC.t[0:64, :], ALU.mult, [OS, BC], [YT[y]])
            dma(P, "sp", A["YAT"][h, :, bs], YT[y].t[:], [YT[y]], [], YT_g[y])
    return {"sp": YT_g}


def mlstm_phase(cx, S, A, Wd, Cd):
    nc, P = cx.nc, cx.P
    NT = S // 128
    cg = cx.group(final=True)
    ident = load_ident(cx, Cd, cg)
    T2 = cx.sb("T2", [128, 128], F32)
    dma(P, "sp", T2.t[:], Cd["c_tri2"], [], [T2], cg)
    NEGM = cx.sb("NEGM", [128, 512], BF16)
    dma(P, "sp", NEGM.t[:], Cd["c_negm"], [], [NEGM], cg)
    HG = cx.sb("HG", [128, 512], F32)
    dma(P, "sp", HG.t[:], Wd["m_head_norm"].partition_broadcast(128), [], [HG], cg)
    NH = cx.sb("NH", [128, 8], F32)
    mset(P, "pool", NH.t[:], -0.5, [NH])
    CN = [cx.sb(f"CN{i}", [64, 4, 129], F32) for i in range(3)]
    mset(P, "pool", CN[0].t[:], 0.0, [CN[0]])
    R2 = range(2)
    LIF = [cx.sb(f"LIF{i}", [128, 8], F32) for i in R2]
    MQ = [cx.sb(f"MQ{i}", [64, 4, 128], BF16) for i in R2]
    MKt = [cx.sb(f"MKt{i}", [64, 4, 128], BF16) for i in R2]
    MKb = [cx.sb(f"MKb{i}", [128, 256], BF16) for i in R2]
    VA = [cx.sb(f"VA{i}", [128, 4, 129], BF16) for i in R2]
    MOt = [cx.sb(f"MOt{i}", [128, 512], BF16) for i in R2]
    lg = [[cx.group() for _ in range(6)] for _ in R2]
    for i in R2:
        mset(P, "pool", VA[i].t[:], 1.0, [VA[i]])
    LFm = cx.sb("LFm", [128, 4, 128], F32)
    bias = cx.sb("bias", [128, 4], F32)
    AT = cx.sb("AT", [128, 4, 128], F32)
    EB = cx.sb("EB", [128, 4, 128], F32)
    BL = cx.sb("BL", [128, 4], F32)
    WS = cx.sb("WS", [128, 4], F32)
    WT = cx.sb("WT", [128, 4, 128], BF16)
    QP = cx.sb("QP", [64, 4, 128], F32)
    KW = cx.sb("KW", [128, 4, 64], BF16)
    dn = cx.sb("dn", [128, 4], F32)
    rec = cx.sb("rec", [128, 4], F32)
    hh = cx.sb("hh", [128, 4, 128], F32)
    junk = cx.sb("junk", [128, 128], BF16)
    ssq = cx.sb("ssq", [128, 4], F32)
    rs4 = cx.sb("rs4", [128, 4], F32)
    yb = cx.sb("yb", [128, 512], BF16)
    YBt = [cx.sb(f"YBt{i}", [128, 4, 512], BF16) for i in R2]
    YB_g = [cx.group() for _ in R2]
    pX = cx.ps("pX", [128, 512], F32)
    pY = cx.ps("pY", [128, 512], F32)
    pZ = cx.ps("pZ", [128, 512], F32)
    pQ = cx.ps("pQ", [128, 512], F32)
    pD = cx.ps("pD", [128, 512], F32)
    pN = [cx.ps(f"pN{i}", [128, 512], F32) for i in R2]
    pTr = cx.ps("pTr", [128, 1024], BF16)

    def load(i):
        s_ = i % 2
        tsl = slice(i * 128, (i + 1) * 128)
        g = lg[s_]
        dma(P, "sp", LIF[s_].t[:], A["LIF"][tsl, :], [], [LIF[s_]], g[0])
        dma(P, "sp", MQ[s_].t[:], A["MQT"][:, :, tsl].rearrange("h d t -> d h t"), [], [MQ[s_]], g[1])
        dma(P, "sp", MKt[s_].t[:], A["MKT"][:, :, tsl].rearrange("h d t -> d h t"), [], [MKt[s_]], g[2])
        dma(P, "sp", MKb[s_].t[:], A["MK"][tsl, :], [], [MKb[s_]], g[3])
        dma(P, "sp", VA[s_].t[:, :, 0:128], A["MV"][tsl, :].rearrange("p (h d) -> p h d", h=4), [], [VA[s_]], g[4])
        dma(P, "sp", MOt[s_].t[:], A["MO"][tsl, :], [], [MOt[s_]], g[5])

    def tile(i):
        s_ = i % 2
        lif, mq, mkt, mkb, va, mo = LIF[s_], MQ[s_], MKt[s_], MKb[s_], VA[s_], MOt[s_]
        import os
        ML = int(os.environ.get('ML', '99'))
        cc, cm, cn = CN[(2 * i) % 3], CN[(2 * i + 1) % 3], CN[(2 * i + 2) % 3]
        cp(P, "pool", LFm.t[:], lif.t[:, 4:8].unsqueeze(2).broadcast_to([128, 4, 128]), [lif], [LFm])
        for h in range(4):
            mm(P, pX.t[:, h * 128:(h + 1) * 128], LFm.t[:, h, :], T2.t[:], True, True, [LFm, T2], [pX])
        mm(P, pY.t[:], ident.t[:], NEGM.t[:], True, False, [ident, NEGM], [pY])
        for h in range(4):
            mm(P, pY.t[:, h * 128:(h + 1) * 128], LFm.t[:, h, :], T2.t[:], False, h == 3, [LFm, T2], [pY])
        mm(P, pZ.t[:, 0:4], T2.t[:], lif.t[:, 4:8], True, True, [T2, lif], [pZ])
        tt(P, "dve", bias.t[:], lif.t[:, 0:4], pZ.t[:, 0:4], ALU.subtract, [lif, pZ], [bias])
        for h in range(4):
            act(P, AT.t[:, h, :], pY.t[:, h * 128:(h + 1) * 128], AF.Exp, [pY, bias], [AT], bias=bias.t[:, h:h + 1])
        act(P, EB.t[:], pX.t[:].rearrange("p (h t) -> p h t", h=4), AF.Exp, [pX], [EB])
        if ML < 2:
            return
        pXv = pX.t[:].rearrange("p (h t) -> p h t", h=4)
        cp(P, "dve", BL.t[0:64, :], pXv[0:64, :, 63], [pX], [BL])
        cp(P, "dve", BL.t[64:128, :], pXv[64:128, :, 127], [pX], [BL])
        tt(P, "dve", BL.t[:], BL.t[:], bias.t[:], ALU.add, [BL, bias], [BL])
        act(P, WS.t[:], BL.t[:], AF.Exp, [BL], [WS])
        for h in range(4):
            mm(P, pQ.t[:, h * 128:(h + 1) * 128], mkt.t[:, h, :], mq.t[:, h, :], True, True, [mkt, mq], [pQ])
        tt(P, "dve", WT.t[:], pQ.t[:].rearrange("p (h t) -> p h t", h=4), AT.t[:], ALU.mult, [pQ, AT], [WT])
        tt(P, "pool", QP.t[:], mq.t[:], EB.t[0:64, :, :], ALU.mult, [mq, EB], [QP])
        tt(P, "pool", KW.t[:], mkb.t[:].rearrange("p (h d) -> p h d", h=4),
           WS.t[:].unsqueeze(2).broadcast_to([128, 4, 64]), ALU.mult, [mkb, WS], [KW])
        if ML < 3:
            return
        for (rows, src, dst, col) in ((slice(0, 64), cc, cm, 63), (slice(64, 128), cm, cn, 127)):
            for hp in range(2):
                for c in range(2):
                    h = 2 * hp + c
                    mm(P, pD.t[0:64, c * 129:(c + 1) * 129], KW.t[rows, h, :], va.t[rows, h, :], True, True,
                       [KW, va], [pD])
                for c in range(2):
                    h = 2 * hp + c
                    stt(P, dst.t[:, h, :], src.t[:, h, :], EB.t[0:64, h, col:col + 1], pD.t[0:64, c * 129:(c + 1) * 129],
                        ALU.mult, ALU.add, [src, EB, pD], [dst])
        if ML < 4:
            return
        for h in range(4):
            bk = pN[h // 2]
            osl = slice((h % 2) * 129, (h % 2) * 129 + 129)
            mm(P, bk.t[:, osl], WT.t[:, h, :], va.t[:, h, :], True, False, [WT, va], [bk])
            mm(P, bk.t[0:64, osl], QP.t[:, h, 0:64], cc.t[:, h, :], False, False, [QP, cc], [bk])
            mm(P, bk.t[64:128, osl], QP.t[:, h, 64:128], cm.t[:, h, :], False, True, [QP, cm], [bk])
        if ML < 5:
            return
        for q in range(2):
            act(P, dn.t[:, 2 * q:2 * q + 2], pN[q].t[:, 0:258].rearrange("p (h c) -> p h c", c=129)[:, :, 128],
                AF.Abs, [pN[q]], [dn])
        ts(P, "dve", dn.t[:], dn.t[:], 1.0, None, ALU.max, ALU.bypass, [dn], [dn])
        P.op("dve", lambda e: e.reciprocal(out=rec.t[:], in_=dn.t[:]), [dn.b], [rec.b])
        for h in range(4):
            o0 = (h % 2) * 129
            act(P, hh.t[:, h, :], pN[h // 2].t[:, o0:o0 + 128], AF.Copy, [pN[h // 2], rec], [hh], scale=rec.t[:, h:h + 1])
        if ML < 6:
            return
        for h in range(4):
            act(P, junk.t[:], hh.t[:, h, :], AF.Square, [hh], [junk, ssq], accum_out=ssq.t[:, h:h + 1])
        rstd_ops(cx, ssq, 128, rs4, NH, [])
        if ML < 7:
            return
        tt(P, "dve", hh.t[:], hh.t[:], rs4.t[:].unsqueeze(2).broadcast_to([128, 4, 128]), ALU.mult, [hh, rs4], [hh])
        hf = hh.t[:].rearrange("p h d -> p (h d)")
        tt(P, "pool", hf, hf, HG.t[:], ALU.mult, [hh, HG], [hh])
        tt(P, "dve", yb.t[:], hf, mo.t[:], ALU.mult, [hh, mo], [yb])
        if ML < 8:
            return
        for c in range(4):
            tr(P, pTr.t[:, c * 128:(c + 1) * 128], yb.t[:, c * 128:(c + 1) * 128], ident.t[:], [yb, ident], [pTr])
        yt = YBt[(i // 4) % 2]
        cp(P, "act", yt.t[:, :, (i % 4) * 128:(i % 4 + 1) * 128], pTr.t[:, 0:512].rearrange("p (c t) -> p c t", c=4),
           [pTr], [yt])
        if i % 4 == 3:
            b0 = (i // 4) * 512
            dma(P, "sp", A["YBT"][:, :, b0:b0 + 512].rearrange("c p t -> p c t"), yt.t[:], [yt], [], YB_g[(i // 4) % 2])

    load(0)
    for i in range(NT):
        if i + 1 < NT:
            load(i + 1)
        tile(i)
    return {"sp": YB_g}


def merge_phase(cx, S, X1, X2, A, Wd, Cd):
    nc, P = cx.nc, cx.P
    TT = 256
    NT = S // TT
    WA = cx.sb("WA", [128, 4, D], BF16)
    WB = cx.sb("WB", [128, 4, D], BF16)
    WO = cx.sb("WO", [128, 8, D], BF16)
    stg = [cx.sb(f"stg{i}", [128, D], F32) for i in range(2)]
    stg_g = [cx.group() for _ in range(2)]
    k = 0
    for (src, dst, n) in ((Wd["w_proj_a"], WA, 4), (Wd["w_proj_b"], WB, 4), (Wd["w_out"], WO, 8)):
        for c in range(n):
            s_ = k % 2
            dma(P, "sp", stg[s_].t[:], src[c * 128:(c + 1) * 128, :], [], [stg[s_]], stg_g[s_])
            cp(P, ["act", "dve", "pool"][k % 3], dst.t[:, c, :], stg[s_].t[:], [stg[s_]], [dst])
            k += 1
    R2 = range(2)
    YA = [cx.sb(f"YA{i}", [128, 4, TT], BF16) for i in R2]
    YB = [cx.sb(f"YB{i}", [128, 4, TT], BF16) for i in R2]
    GA = [cx.sb(f"GA{i}", [128, 8, TT], BF16) for i in R2]
    GB = [cx.sb(f"GB{i}", [128, 8, TT], BF16) for i in R2]
    xt = [cx.sb(f"xt{i}", [128, 2, D], F32) for i in R2]
    lg = [[cx.group() for _ in range(5)] for _ in R2]
    st_g = [cx.group() for _ in R2]
    m1 = [cx.sb(f"m1{i}", [128, 512], F32) for i in R2]
    m2 = [cx.sb(f"m2{i}", [128, 512], F32) for i in R2]
    MG = [cx.sb(f"MG{i}", [128, 8, TT], BF16) for i in R2]
    pA = [cx.ps(f"pA{i}", [128, 512], F32) for i in R2]
    pB = [cx.ps(f"pB{i}", [128, 512], F32) for i in R2]
    pO = [cx.ps(f"pO{i}", [128, 512], F32) for i in R2]

    def load(i):
        s_ = i % 2
        tsl = slice(i * TT, (i + 1) * TT)
        g = lg[s_]
        dma(P, "sp", YA[s_].t[:], A["YAT"][:, :, tsl].rearrange("(c w) d t -> (w d) c t", w=2), [], [YA[s_]], g[0])
        dma(P, "sp", YB[s_].t[:], A["YBT"][:, :, tsl].rearrange("c p t -> p c t"), [], [YB[s_]], g[1])
        dma(P, "sp", GA[s_].t[:], A["GAT"][:, :, tsl].rearrange("c p t -> p c t"), [], [GA[s_]], g[2])
        dma(P, "sp", GB[s_].t[:], A["GBT"][:, :, tsl].rearrange("c p t -> p c t"), [], [GB[s_]], g[3])
        dma(P, "sp", xt[s_].t[:], X1[tsl, :].rearrange("(s p) d -> p s d", p=128), [], [xt[s_]], g[4])

    def tile(i):
        s_ = i % 2
        tsl = slice(i * TT, (i + 1) * TT)
        for dcp in range(4):
            q = dcp % 2
            for (W_, Y_, bk) in ((WA, YA[s_], pA[q]), (WB, YB[s_], pB[q])):
                for c2 in range(2):
                    dc = 2 * dcp + c2
                    for c in range(4):
                        mm(P, bk.t[:, c2 * TT:(c2 + 1) * TT], W_.t[:, c, dc * 128:(dc + 1) * 128], Y_.t[:, c, :],
                           c == 0, c == 3, [W_, Y_], [bk])
            gsl = slice(2 * dcp, 2 * dcp + 2)
            tt(P, "dve", m1[q].t[:].rearrange("p (c t) -> p c t", c=2), pA[q].t[:].rearrange("p (c t) -> p c t", c=2),
               GA[s_].t[:, gsl, :], ALU.mult, [pA[q], GA[s_]], [m1[q]])
            tt(P, "dve", m2[q].t[:].rearrange("p (c t) -> p c t", c=2), pB[q].t[:].rearrange("p (c t) -> p c t", c=2),
               GB[s_].t[:, gsl, :], ALU.mult, [pB[q], GB[s_]], [m2[q]])
            tt(P, "pool", MG[s_].t[:, gsl, :], m1[q].t[:].rearrange("p (c t) -> p c t", c=2),
               m2[q].t[:].rearrange("p (c t) -> p c t", c=2), ALU.add, [m1[q], m2[q]], [MG[s_]])
        n = 0
        for sub in range(2):
            for dh in range(2):
                bk = pO[n % 2]
                n += 1
                for dc in range(8):
                    mm(P, bk.t[:], MG[s_].t[:, dc, sub * 128:(sub + 1) * 128], WO.t[:, dc, dh * 512:(dh + 1) * 512],
                       dc == 0, dc == 7, [MG[s_], WO], [bk])
                xs = xt[s_].t[:, sub, dh * 512:(dh + 1) * 512]
                tt(P, "dve", xs, bk.t[:], xs, ALU.add, [bk, xt[s_]], [xt[s_]])
        dma(P, "pool", X2[tsl, :].rearrange("(s p) d -> p s d", p=128), xt[s_].t[:], [xt[s_]], [], st_g[s_])

    load(0)
    if NT > 1:
        load(1)
    for i in range(NT):
        tile(i)
        if i + 2 < NT:
            load(i + 2)
    return {"pool": st_g}


WSPEC = {
    "ffn1_norm": [1, D], "ffn1_w_gate": [D, DFF], "ffn1_w_up": [D, DFF], "ffn1_w_down": [DFF, D],
    "mix_norm": [1, D], "w_in": [D, DIN], "q_norm": [1, 64], "k_norm": [1, 64], "idx_k_norm": [1, 64],
    "conv_w": [128, 16], "conv_b": [128, 4], "w_mq": [128, 256], "w_mk": [128, 256], "b_i": [1, 4],
    "b_f": [1, 4], "m_head_norm": [1, 512], "w_proj_a": [512, D], "w_proj_b": [512, D], "w_out": [D, D],
    "ffn2_norm": [1, D], "ffn2_w_gate": [D, DFF], "ffn2_w_up": [D, DFF], "ffn2_w_down": [DFF, D],
}


def consts():
    c = {}
    c["c_ident"] = np.eye(128, dtype=np.float32).astype(ml_dtypes.bfloat16)
    cm = np.zeros((4, 128, 512), np.float32)
    for ii in range(4):
        for p in range(128):
            lim = ((ii * 128 + p) // 64 + 1) * 64
            cm[ii, p, lim:] = -1e30
    c["c_cmask"] = cm
    c["c_steps"] = np.tile((2.0 ** -(np.arange(NIT) + 1.0)).astype(np.float32)[None, :], (128, 1))
    j = np.arange(128)
    t2 = ((j[:, None] // 64 == j[None, :] // 64) & (j[:, None] <= j[None, :]))
    c["c_tri2"] = t2.astype(np.float32)
    c["c_negm"] = np.tile(np.where(t2, 0.0, NEG).astype(np.float32), (1, 4)).astype(ml_dtypes.bfloat16)
    return c


CSPEC = {"c_ident": ([128, 128], BF16), "c_cmask": ([4, 128, 512], F32), "c_steps": ([128, NIT], F32),
         "c_tri2": ([128, 128], F32), "c_negm": ([128, 512], BF16)}


def build_nc(S, debug=False, upto=9):
    nc = bass.Bass("TRN2", target_bir_lowering=False)

    def din(name, shape, dt=F32):
        return nc.dram_tensor(name, list(shape), dt, kind="ExternalInput").ap()

    x = din("x", [S, D])
    Wd = {k: din(k, v) for k, v in WSPEC.items()}
    Cd = {k: din(k, v[0], v[1]) for k, v in CSPEC.items()}
    out = nc.dram_tensor("out", [S, D], F32, kind="ExternalOutput").ap()
    kind = "ExternalOutput" if debug else "Internal"

    def scr(name, shape, dt):
        return nc.dram_tensor(name, list(shape), dt, kind=kind).ap()

    X1 = scr("X1", [S, D], F32)
    X2 = scr("X2", [S, D], F32)
    A = {
        "QT": scr("QT", [4, 128, S], BF16), "KT": scr("KT", [128, S], BF16), "V": scr("V", [S, 128], BF16),
        "IQT": scr("IQT", [4, 128, S], BF16), "IKT": scr("IKT", [64, S], BF16), "SGN": scr("SGN", [S, 8], F32),
        "MV": scr("MV", [S, 512], BF16), "LIF": scr("LIF", [S, 8], F32), "MO": scr("MO", [S, 512], BF16),
        "GAT": scr("GAT", [8, 128, S], BF16), "GBT": scr("GBT", [8, 128, S], BF16),
        "MQT": scr("MQT", [4, 64, S], BF16), "MKT": scr("MKT", [4, 64, S], BF16), "MK": scr("MK", [S, 256], BF16),
        "YAT": scr("YAT", [8, 64, S], BF16), "YBT": scr("YBT", [4, 128, S], BF16),
    }
    ph = [
        ("f1", lambda cx: ffn_phase(cx, S, x, X1, Wd["ffn1_norm"], Wd["ffn1_w_gate"], Wd["ffn1_w_up"],
                                    Wd["ffn1_w_down"], Cd["c_ident"])),
        ("pj", lambda cx: proj_phase(cx, S, X1, A, Wd, Cd)),
        ("ds", lambda cx: dsa_phase(cx, S, A, Cd)),
        ("ml", lambda cx: mlstm_phase(cx, S, A, Wd, Cd)),
        ("mg", lambda cx: merge_phase(cx, S, X1, X2, A, Wd, Cd)),
        ("f2", lambda cx: ffn_phase(cx, S, X2, out, Wd["ffn2_norm"], Wd["ffn2_w_gate"], Wd["ffn2_w_up"],
                                    Wd["ffn2_w_down"], Cd["c_ident"])),
    ]
    import os
    lo_ = int(os.environ.get('FROM', '0'))
    for n, (tag, fn) in enumerate(ph):
        if lo_ <= n < upto:
            run_phase(nc, tag, fn, debug)
    return nc


def layout_weights(inputs):
    sh = {}
    for k, v in WSPEC.items():
        a = np.asarray(inputs[k], dtype=np.float32)[0]
        if k == "conv_w":
            a = a.reshape(4, 4, 128).transpose(2, 1, 0)
        elif k == "conv_b":
            a = a.reshape(4, 128).T
        elif k in ("w_mq", "w_mk"):
            a = a.transpose(1, 0, 2)
        sh[k] = np.ascontiguousarray(a).reshape(v)
    return sh


_NC_CACHE = {}


def kernel(**inputs):
    x = np.ascontiguousarray(np.asarray(inputs["x"], dtype=np.float32))
    B, S, _ = x.shape
    if S not in _NC_CACHE:
        _NC_CACHE[S] = build_nc(S)
    nc = _NC_CACHE[S]
    shared = layout_weights(inputs)
    shared.update(consts())
    in_maps = []
    for b in range(B):
        m = dict(shared)
        m["x"] = x[b]
        in_maps.append(m)
    res = run_bass_kernel_spmd(nc, in_maps, core_ids=list(range(B)))
    return np.stack([np.asarray(r["out"], dtype=np.float32) for r in res.results], axis=0)
```

```python
import numpy as np
from contextlib import ExitStack
import ml_dtypes
import concourse.bass as bass
import concourse.mybir as mybir
from concourse.bass_utils import run_bass_kernel_spmd

F32 = mybir.dt.float32
BF16 = mybir.dt.bfloat16
ALU = mybir.AluOpType
AF = mybir.ActivationFunctionType
AX = mybir.AxisListType

D = 1024
DFF = 2816
NFF = DFF // 128
DIN = 4944
EPS = 1e-6
NEG = -30000.0


class Buf:
    __slots__ = ("name", "w", "r", "psum")

    def __init__(self, name):
        self.name = name
        self.w = {}
        self.r = {}
        self.psum = False


class Group:
    def __init__(self, sem, final=False):
        self.sem = sem
        self.n = 0
        self.final = final


class Prog:
    ENG = ("pe", "act", "dve", "pool", "sp")

    def __init__(self, nc):
        self.nc = nc
        self.ops = {e: [] for e in self.ENG}

    def op(self, eng, fn, reads=(), writes=(), group=None):
        ops = self.ops[eng]
        idx = len(ops)
        rec = {"fn": fn, "waits": {}, "sig": False, "group": group}
        is_dma = group is not None

        def need(key, val, raw):
            if is_dma and key[0] == "g" and key[1] is group:
                return
            if key[0] == "e" and key[1] == eng and not is_dma:
                if eng == "pe" or not raw:
                    return
            w = rec["waits"]
            if w.get(key, -1) < val:
                w[key] = val
            if key[0] == "e":
                self.ops[key[1]][val]["sig"] = True

        for b in reads:
            for k, v in b.w.items():
                need(k, v, True)
            if b.psum:
                for k, v in b.r.items():
                    need(k, v, False)
        for b in writes:
            for k, v in b.w.items():
                need(k, v, False)
            for k, v in b.r.items():
                need(k, v, False)
        if is_dma:
            group.n += 1
            me = (("g", group), group.n)
        else:
            me = (("e", eng), idx)
        for b in reads:
            if b.r.get(me[0], -1) < me[1]:
                b.r[me[0]] = me[1]
        for b in writes:
            b.w = {me[0]: me[1]}
            b.r = {}
        ops.append(rec)
        return rec

    def emit(self, block, sems, final_waits, base=None):
        if base is None:
            base = {e: 0 for e in self.ENG}
        sigcount = {}
        for e in self.ENG:
            c = base[e]
            lst = []
            for rec in self.ops[e]:
                if rec["sig"]:
                    c += 1
                lst.append(c)
            sigcount[e] = lst

        def run(engname, engine):
            waited = {}
            for rec in self.ops[engname]:
                for key, val in rec["waits"].items():
                    if key[0] == "e":
                        sem = sems[key[1]]
                        v = sigcount[key[1]][val]
                    else:
                        g = key[1]
                        sem = g.sem
                        v = 16 * (g.n if g.final else val)
                    if waited.get(id(sem), -1) >= v:
                        continue
                    waited[id(sem)] = v
                    engine.wait_ge(sem, v)
                ins = rec["fn"](engine)
                if rec["group"] is not None:
                    ins.then_inc(rec["group"].sem, 16)
                elif rec["sig"]:
                    ins.then_inc(sems[engname], 1)
            for g in final_waits.get(engname, ()):
                engine.wait_ge(g.sem, 16 * g.n)

        @block.tensor
        def _(e):
            run("pe", e)

        @block.scalar
        def _(e):
            run("act", e)

        @block.vector
        def _(e):
            run("dve", e)

        @block.gpsimd
        def _(e):
            run("pool", e)

        @block.sync
        def _(e):
            run("sp", e)

        for e in self.ENG:
            base[e] = sigcount[e][-1] if sigcount[e] else base[e]


class _Alloc:
    def __init__(self, nc, es):
        self._nc = nc
        self._es = es

    def alloc_sbuf_tensor(self, name, shape, dt):
        return self._es.enter_context(self._nc.sbuf_tensor(name, list(shape), dt))

    def alloc_psum_tensor(self, name, shape, dt):
        return self._es.enter_context(self._nc.psum_tensor(name, list(shape), dt))


class TL:
    def __init__(self, t, name):
        self.t = t
        self.b = Buf(name)


class Ctx:
    def __init__(self, nc, es, tag, debug=False):
        self.rnc = nc
        self.nc = _Alloc(nc, es)
        self.es = es
        self.tag = tag
        self.P = Prog(nc)
        self.groups = []
        self.debug = debug

    def group(self, final=False):
        sem = self.rnc.alloc_semaphore(f"{self.tag}g{len(self.groups)}")
        g = Group(sem, final)
        self.groups.append(g)
        return g

    def sb(self, name, shape, dt):
        return TL(self.nc.alloc_sbuf_tensor(self.tag + name, list(shape), dt), name)

    def ps(self, name, shape, dt):
        t = TL(self.nc.alloc_psum_tensor(self.tag + name, list(shape), dt), name)
        t.b.psum = True
        return t


_GSTATE = {}


def run_phase(nc, tag, fn, debug=False):
    with ExitStack() as es:
        cx = Ctx(nc, es, tag, debug)
        import os
        for _i in range(int(os.environ.get('DUMMYSEM', '0'))):
            es.enter_context(nc.semaphore(f"{tag}dummy{_i}"))
        finals = fn(cx)
        st = _GSTATE.setdefault(id(nc), {})
        if "sems" not in st:
            st["sems"] = {e: nc.alloc_semaphore(f"s_{e}") for e in Prog.ENG}
            st["base"] = {e: 0 for e in Prog.ENG}
        with nc.Block() as block:
            cx.P.emit(block, st["sems"], finals, st["base"])


def _rr(lst, i):
    return lst[i % len(lst)]


def ffn_phase(cx, S, xin, xout, gvec, wg, wu, wd, identd):
    nc, P = cx.nc, cx.P
    tag = cx.tag
    xin_buf, xout_buf = Buf("xin"), Buf("xout")
    ident = nc.alloc_sbuf_tensor(f"{tag}ident", [128, 128], BF16)
    identb = Buf("ident")
    cg0 = cx.group(final=True)
    P.op("sp", lambda e: e.dma_start(out=ident[:], in_=identd), writes=[identb], group=cg0)
    TT = 256
    NT = S // TT
    WG = nc.alloc_sbuf_tensor(f"{tag}WG", [128, 8, DFF], BF16)
    WU = nc.alloc_sbuf_tensor(f"{tag}WU", [128, 8, DFF], BF16)
    WD = nc.alloc_sbuf_tensor(f"{tag}WD", [128, NFF, D], BF16)
    stg = [nc.alloc_sbuf_tensor(f"{tag}stg{i}", [128, DFF // 2], F32) for i in range(2)]
    stg_b = [Buf(f"stg{i}") for i in range(2)]
    stg_g = [cx.group() for _ in range(2)]
    wbuf = Buf("W")
    G = nc.alloc_sbuf_tensor(f"{tag}G", [128, D], F32)
    gb = Buf("G")
    cg = cx.group(final=True)
    P.op("sp", lambda e: e.dma_start(out=G[:], in_=gvec.partition_broadcast(128)), writes=[gb], group=cg)
    nhalf = nc.alloc_sbuf_tensor(f"{tag}nh", [128, 2], F32)
    nhb = Buf("nh")
    P.op("pool", lambda e: e.memset(nhalf[:], -0.5), writes=[nhb])

    conv_eng = ["act", "dve", "pool"]
    k = 0

    def load_conv(src_ap, dst_ap, width):
        nonlocal k
        s = k % 2
        st, sb_, sg = stg[s], stg_b[s], stg_g[s]
        P.op("sp", lambda e: e.dma_start(out=st[:, 0:width], in_=src_ap), writes=[sb_], group=sg)
        ce = conv_eng[k % 3]
        if ce == "act":
            P.op("act", lambda e: e.copy(out=dst_ap, in_=st[:, 0:width]), reads=[sb_], writes=[wbuf])
        else:
            P.op(ce, lambda e: e.tensor_copy(out=dst_ap, in_=st[:, 0:width]), reads=[sb_], writes=[wbuf])
        k += 1

    HF = DFF // 2
    for kc in range(8):
        for hh in range(2):
            load_conv(wg[kc * 128:(kc + 1) * 128, hh * HF:(hh + 1) * HF], WG[:, kc, hh * HF:(hh + 1) * HF], HF)
            load_conv(wu[kc * 128:(kc + 1) * 128, hh * HF:(hh + 1) * HF], WU[:, kc, hh * HF:(hh + 1) * HF], HF)
    for f in range(NFF):
        load_conv(wd[f * 128:(f + 1) * 128, :], WD[:, f, :], D)

    xt = [nc.alloc_sbuf_tensor(f"{tag}xt{i}", [128, 2, D], F32) for i in range(2)]
    xt_b = [Buf(f"xt{i}") for i in range(2)]
    xt_g = [cx.group() for _ in range(2)]
    st_g = [cx.group() for _ in range(2)]
    junk = nc.alloc_sbuf_tensor(f"{tag}junk", [128, D], BF16)
    junk_b = Buf("junk")
    ss = [nc.alloc_sbuf_tensor(f"{tag}ss{i}", [128, 2], F32) for i in range(2)]
    ss_b = [Buf(f"ss{i}") for i in range(2)]
    ms = [nc.alloc_sbuf_tensor(f"{tag}ms{i}", [128, 2], F32) for i in range(2)]
    ms_b = [Buf(f"ms{i}") for i in range(2)]
    rs = [nc.alloc_sbuf_tensor(f"{tag}rs{i}", [128, 2], F32) for i in range(2)]
    rs_b = [Buf(f"rs{i}") for i in range(2)]
    hb = [nc.alloc_sbuf_tensor(f"{tag}h{i}", [128, 2, D], BF16) for i in range(2)]
    hb_b = [Buf(f"h{i}") for i in range(2)]
    hT = [nc.alloc_sbuf_tensor(f"{tag}hT{i}", [128, 8, TT], BF16) for i in range(2)]
    hT_b = [Buf(f"hT{i}") for i in range(2)]
    actT = [nc.alloc_sbuf_tensor(f"{tag}actT{i}", [128, NFF, TT], BF16) for i in range(2)]
    actT_b = [Buf(f"actT{i}") for i in range(2)]
    sg = [nc.alloc_sbuf_tensor(f"{tag}sg{i}", [128, 512], F32) for i in range(2)]
    sg_b = [Buf(f"sg{i}") for i in range(2)]
    pT = [nc.alloc_psum_tensor(f"{tag}pT{i}", [128, 1024], BF16) for i in range(2)]
    pT_b = [Buf(f"pT{i}") for i in range(2)]
    pG = [nc.alloc_psum_tensor(f"{tag}pG{i}", [128, 512], F32) for i in range(2)]
    pG_b = [Buf(f"pG{i}") for i in range(2)]
    pU = [nc.alloc_psum_tensor(f"{tag}pU{i}", [128, 512], F32) for i in range(2)]
    pU_b = [Buf(f"pU{i}") for i in range(2)]
    pD = [nc.alloc_psum_tensor(f"{tag}pD{i}", [128, 512], F32) for i in range(2)]
    pD_b = [Buf(f"pD{i}") for i in range(2)]

    def stage_load(i):
        s = i % 2
        t0 = i * TT
        P.op("sp", lambda e: e.dma_start(
            out=xt[s][:], in_=xin[t0:t0 + TT, :].rearrange("(s p) d -> p s d", p=128)),
            reads=[xin_buf], writes=[xt_b[s]], group=xt_g[s])

    def stage_norm(i):
        s = i % 2
        for sub in range(2):
            P.op("act", lambda e, sub=sub: e.activation(
                out=junk[:], in_=xt[s][:, sub, :], func=AF.Square, accum_out=ss[s][:, sub:sub + 1]),
                reads=[xt_b[s]], writes=[junk_b, ss_b[s]])
        P.op("dve", lambda e: e.tensor_scalar(out=ms[s][:], in0=ss[s][:], scalar1=1.0 / D, scalar2=EPS,
                                              op0=ALU.mult, op1=ALU.add), reads=[ss_b[s]], writes=[ms_b[s]])
        P.op("pool", lambda e: e.tensor_tensor(out=rs[s][:], in0=ms[s][:], in1=nhalf[:], op=ALU.pow),
             reads=[ms_b[s], nhb], writes=[rs_b[s]])
        for sub in range(2):
            P.op("dve", lambda e, sub=sub: e.scalar_tensor_tensor(
                out=hb[s][:, sub, :], in0=xt[s][:, sub, :], scalar=rs[s][:, sub:sub + 1], in1=G[:],
                op0=ALU.mult, op1=ALU.mult), reads=[xt_b[s], rs_b[s], gb], writes=[hb_b[s]])

    def stage_T(i):
        s = i % 2
        for half in range(2):
            pb, pbb = pT[half], pT_b[half]
            for kk in range(4):
                kc = half * 4 + kk
                for sub in range(2):
                    P.op("pe", lambda e, kc=kc, sub=sub, kk=kk, pb=pb: e.transpose(
                        out=pb[:, kk * TT + sub * 128: kk * TT + (sub + 1) * 128],
                        in_=hb[s][:, sub, kc * 128:(kc + 1) * 128], identity=ident[:]),
                        reads=[hb_b[s], identb], writes=[pbb])
            if half == 0:
                P.op("act", lambda e, pb=pb: e.copy(out=hT[s][:, 0:4, :], in_=pb[:].rearrange("p (k t) -> p k t", k=4)),
                     reads=[pbb], writes=[hT_b[s]])
            else:
                P.op("dve", lambda e, pb=pb: e.tensor_copy(out=hT[s][:, 4:8, :], in_=pb[:].rearrange("p (k t) -> p k t", k=4)),
                     reads=[pbb], writes=[hT_b[s]])

    def stage_GU(i):
        s = i % 2
        for f2 in range(NFF // 2):
            q = f2 % 2
            for (W_, pp, ppb) in ((WG, pG[q], pG_b[q]), (WU, pU[q], pU_b[q])):
                for c in range(2):
                    f = 2 * f2 + c
                    for kc in range(8):
                        P.op("pe", lambda e, W_=W_, pp=pp, c=c, f=f, kc=kc: e.matmul(
                            out=pp[:, c * TT:(c + 1) * TT], lhsT=W_[:, kc, f * 128:(f + 1) * 128],
                            rhs=hT[s][:, kc, :], start=(kc == 0), stop=(kc == 7)),
                            reads=[wbuf, hT_b[s]], writes=[ppb])
            P.op("act", lambda e, q=q: e.activation(out=sg[q][:], in_=pG[q][:], func=AF.Silu),
                 reads=[pG_b[q]], writes=[sg_b[q]])
            P.op("dve", lambda e, q=q, f2=f2: e.tensor_tensor(
                out=actT[s][:, 2 * f2:2 * f2 + 2, :], in0=sg[q][:].rearrange("p (c t) -> p c t", c=2),
                in1=pU[q][:].rearrange("p (c t) -> p c t", c=2), op=ALU.mult),
                reads=[sg_b[q], pU_b[q]], writes=[actT_b[s]])

    def stage_D(i):
        s = i % 2
        t0 = i * TT
        n = 0
        for sub in range(2):
            for dh in range(2):
                q = n % 2
                n += 1
                for f in range(NFF):
                    P.op("pe", lambda e, q=q, f=f, sub=sub, dh=dh: e.matmul(
                        out=pD[q][:], lhsT=actT[s][:, f, sub * 128:(sub + 1) * 128],
                        rhs=WD[:, f, dh * 512:(dh + 1) * 512], start=(f == 0), stop=(f == NFF - 1)),
                        reads=[wbuf, actT_b[s]], writes=[pD_b[q]])
                P.op("dve", lambda e, q=q, sub=sub, dh=dh: e.scalar_tensor_tensor(
                    out=xt[s][:, sub, dh * 512:(dh + 1) * 512], in0=pD[q][:], scalar=0.5,
                    in1=xt[s][:, sub, dh * 512:(dh + 1) * 512], op0=ALU.mult, op1=ALU.add),
                    reads=[pD_b[q], xt_b[s]], writes=[xt_b[s]])
        P.op("pool", lambda e: e.dma_start(
            out=xout[t0:t0 + TT, :].rearrange("(s p) d -> p s d", p=128), in_=xt[s][:]),
            reads=[xt_b[s]], writes=[xout_buf], group=st_g[s])

    stage_load(0)
    if NT > 1:
        stage_load(1)
    stage_norm(0)
    stage_T(0)
    for i in range(NT):
        stage_GU(i)
        if i + 1 < NT:
            stage_norm(i + 1)
            stage_T(i + 1)
        stage_D(i)
        if i + 2 < NT:
            stage_load(i + 2)
    return {"pool": st_g}


def _b(L):
    return [x.b for x in L]


def mm(P, out, lhsT, rhs, st, sp, R, W):
    P.op("pe", lambda e: e.matmul(out=out, lhsT=lhsT, rhs=rhs, start=st, stop=sp), _b(R), _b(W))


def tr(P, out, in_, ident, R, W):
    P.op("pe", lambda e: e.transpose(out=out, in_=in_, identity=ident), _b(R), _b(W))


def act(P, out, in_, func, R, W, **kw):
    P.op("act", lambda e: e.activation(out=out, in_=in_, func=func, **kw), _b(R), _b(W))


def ts(P, eng, out, in0, s1, s2, op0, op1, R, W, accum=None):
    if accum is None:
        P.op(eng, lambda e: e.tensor_scalar(out=out, in0=in0, scalar1=s1, scalar2=s2, op0=op0, op1=op1), _b(R), _b(W))
    else:
        P.op(eng, lambda e: e.tensor_scalar(out=out, in0=in0, scalar1=s1, scalar2=s2, op0=op0, op1=op1,
                                            accum_out=accum), _b(R), _b(W))


def tt(P, eng, out, in0, in1, op, R, W):
    P.op(eng, lambda e: e.tensor_tensor(out=out, in0=in0, in1=in1, op=op), _b(R), _b(W))


def stt(P, out, in0, scalar, in1, op0, op1, R, W):
    P.op("dve", lambda e: e.scalar_tensor_tensor(out=out, in0=in0, scalar=scalar, in1=in1, op0=op0, op1=op1),
         _b(R), _b(W))


def cp(P, eng, out, in_, R, W):
    if eng == "act":
        P.op("act", lambda e: e.copy(out=out, in_=in_), _b(R), _b(W))
    else:
        P.op(eng, lambda e: e.tensor_copy(out=out, in_=in_), _b(R), _b(W))


def red(P, out, in_, op, R, W, **kw):
    P.op("dve", lambda e: e.tensor_reduce(out=out, in_=in_, axis=AX.X, op=op, **kw), _b(R), _b(W))


def dma(P, eng, out, in_, R, W, g, **kw):
    P.op(eng, lambda e: e.dma_start(out=out, in_=in_, **kw), _b(R), _b(W), group=g)


def mset(P, eng, ap, val, W):
    P.op(eng, lambda e: e.memset(ap, val), [], _b(W))


def rstd_ops(cx, ssq, n, rs, nh, R):
    P = cx.P
    ts(P, "dve", ssq.t[:], ssq.t[:], 1.0 / n, EPS, ALU.mult, ALU.add, R + [ssq], [ssq])
    tt(P, "pool", rs.t[:], ssq.t[:], nh.t[:, 0:ssq.t.shape[1]], ALU.pow, [ssq, nh], [rs])


IDX_SCALE = (64 ** -0.5) * (8 ** -0.5)
ATT_SCALE = 64 ** -0.5
MQ_SCALE = 64 ** -0.5


def load_ident(cx, Cd, cg):
    ident = cx.sb("ident", [128, 128], BF16)
    dma(cx.P, "sp", ident.t[:], Cd["c_ident"], [], [ident], cg)
    return ident


def norm_transpose(cx, xt, G, ident, nh, hb, hT, pT, ss, rs, junk, ntok_sub=2):
    P = cx.P
    for sub in range(ntok_sub):
        act(P, junk.t[:], xt.t[:, sub, :], AF.Square, [xt], [junk, ss], accum_out=ss.t[:, sub:sub + 1])
    rstd_ops(cx, ss, D, rs, nh, [])
    for sub in range(ntok_sub):
        stt(P, hb.t[:, sub, :], xt.t[:, sub, :], rs.t[:, sub:sub + 1], G.t[:], ALU.mult, ALU.mult,
            [xt, rs, G], [hb])
    for half in range(2):
        pb = pT[half]
        for kk in range(4):
            kc = half * 4 + kk
            for sub in range(ntok_sub):
                tr(P, pb.t[:, kk * 256 + sub * 128: kk * 256 + (sub + 1) * 128],
                   hb.t[:, sub, kc * 128:(kc + 1) * 128], ident.t[:], [hb, ident], [pb])
        cp(P, "act" if half == 0 else "dve", hT.t[:, half * 4:half * 4 + 4, :],
           pb.t[:].rearrange("p (k t) -> p k t", k=4), [pb], [hT])


def proj_phase(cx, S, X1, A, Wd, Cd):
    nc, P = cx.nc, cx.P
    TT = 256
    NT = S // TT
    cg = cx.group(final=True)
    ident = load_ident(cx, Cd, cg)
    WIN = cx.sb("WIN", [128, 8, DIN], BF16)
    stg = [cx.sb(f"stg{i}", [128, 824], F32) for i in range(2)]
    stg_g = [cx.group() for _ in range(2)]
    k = 0
    for kc in range(8):
        for part in range(6):
            s_ = k % 2
            c0 = part * 824
            dma(P, "sp", stg[s_].t[:], Wd["w_in"][kc * 128:(kc + 1) * 128, c0:c0 + 824], [], [stg[s_]], stg_g[s_])
            cp(P, ["act", "dve", "pool"][k % 3], WIN.t[:, kc, c0:c0 + 824], stg[s_].t[:], [stg[s_]], [WIN])
            k += 1
    import os
    SL = int(os.environ.get('SL', '99'))
    G = cx.sb("G", [128, D], F32)
    dma(P, "sp", G.t[:], Wd["mix_norm"].partition_broadcast(128), [], [G], cg)
    GQ = cx.sb("GQ", [128, 512], F32)
    for h in range(8):
        if SL >= 2:
            dma(P, "sp", GQ.t[:, h * 64:(h + 1) * 64], Wd["q_norm"].partition_broadcast(128), [], [GQ], cg)
    if SL >= 2:
        ts(P, "dve", GQ.t[:], GQ.t[:], ATT_SCALE, None, ALU.mult, ALU.bypass, [GQ], [GQ])
    GK = cx.sb("GK", [128, 128], F32)
    for h in range(2):
        if SL >= 3:
            dma(P, "sp", GK.t[:, h * 64:(h + 1) * 64], Wd["k_norm"].partition_broadcast(128), [], [GK], cg)
    GI = cx.sb("GI", [128, 64], F32)
    if SL >= 3:
        dma(P, "sp", GI.t[:], Wd["idx_k_norm"].partition_broadcast(128), [], [GI], cg)
    CW = cx.sb("CW", [128, 4, 4], F32)
    CB = cx.sb("CB", [128, 4], F32)
    if SL >= 4:
        dma(P, "sp", CB.t[:], Wd["conv_b"], [], [CB], cg)
    if SL >= 4:
      dma(P, "sp", CW.t[:], Wd["conv_w"].rearrange("p (c j) -> p c j", c=4), [], [CW], cg)
    wst = cx.sb("wst", [128, 2, 4, 64], F32)
    WM = cx.sb("WM", [128, 2, 4, 64], BF16)
    if SL >= 5:
        dma(P, "sp", wst.t[:, 0], Wd["w_mq"].rearrange("c (h d) -> c h d", h=4), [], [wst], cg)
        dma(P, "sp", wst.t[:, 1], Wd["w_mk"].rearrange("c (h d) -> c h d", h=4), [], [wst], cg)
        cp(P, "dve", WM.t[:], wst.t[:], [wst], [WM])
    BIF = cx.sb("BIF", [128, 8], F32)
    if SL >= 6:
        dma(P, "sp", BIF.t[:, 0:4], Wd["b_i"].partition_broadcast(128), [], [BIF], cg)
        dma(P, "sp", BIF.t[:, 4:8], Wd["b_f"].partition_broadcast(128), [], [BIF], cg)
    NH = cx.sb("NH", [128, 8], F32)
    mset(P, "pool", NH.t[:], -0.5, [NH])

    R2 = range(2)
    xt = [cx.sb(f"xt{i}", [128, 2, D], F32) for i in R2]
    xt_g = [cx.group() for _ in R2]
    junk = cx.sb("junk", [128, D], BF16)
    ss = [cx.sb(f"ss{i}", [128, 2], F32) for i in R2]
    rs = [cx.sb(f"rs{i}", [128, 2], F32) for i in R2]
    hb = [cx.sb("hb0", [128, 2, D], BF16)] * 2
    hT = [cx.sb(f"hT{i}", [128, 8, TT], BF16) for i in R2]
    pT = [cx.ps(f"pT{i}", [128, 1024], BF16) for i in R2]
    pX = [cx.ps(f"pX{i}", [128, 1024], BF16) for i in R2]
    pM = [cx.ps(f"pM{i}", [128, 512], F32) for i in range(4)]
    nbank = [0]

    def bank():
        b = pM[nbank[0] % 4]
        nbank[0] += 1
        return b

    sqt = [cx.sb(f"sqt{i}", [128, 512], F32) for i in R2]
    t1 = [cx.sb(f"t1{i}", [128, 512], F32) for i in R2]
    qn = [cx.sb(f"qn{i}", [128, 512], BF16) for i in R2]
    kn = [cx.sb(f"kn{i}", [128, 128], BF16) for i in R2]
    ikn = [cx.sb(f"ikn{i}", [128, 64], BF16) for i in R2]
    iqs = [cx.sb(f"iqs{i}", [128, 512], BF16) for i in R2]
    s8 = [cx.sb(f"s8{i}", [128, 8], F32) for i in R2]
    r8 = [cx.sb(f"r8{i}", [128, 8], F32) for i in R2]
    s2 = [cx.sb(f"s2{i}", [128, 2], F32) for i in R2]
    r2 = [cx.sb(f"r2{i}", [128, 2], F32) for i in R2]
    s1 = [cx.sb(f"s1{i}", [128, 1], F32) for i in R2]
    r1 = [cx.sb(f"r1{i}", [128, 1], F32) for i in R2]
    absw = [cx.sb(f"absw{i}", [128, 8], F32) for i in R2]
    zt = [cx.sb(f"zt{i}", [128, 8], F32) for i in R2]
    et = [cx.sb(f"et{i}", [128, 4], F32) for i in R2]
    def outs(name, shape, dt):
        return [cx.sb(f"{name}{i}", shape, dt) for i in R2], [cx.group() for _ in R2]
    QTt, QT_g = outs("QTt", [128, 4, TT], BF16)
    KTt, KT_g = outs("KTt", [128, TT], BF16)
    IQTt, IQT_g = outs("IQTt", [128, 4, TT], BF16)
    IKTt, IKT_g = outs("IKTt", [64, TT], BF16)
    vb, V_g = outs("vb", [128, 2, 128], BF16)
    sgn, SGN_g = outs("sgn", [128, 2, 8], F32)
    mvb, MV_g = outs("mvb", [128, 2, 512], BF16)
    lif, LIF_g = outs("lif", [128, 2, 8], F32)
    mob, MO_g = outs("mob", [128, 2, 512], BF16)
    gat, GAT_g = outs("gat", [128, 8, TT], BF16)
    gbt, GBT_g = outs("gbt", [128, 8, TT], BF16)
    mqt, MQT_g = outs("mqt", [64, 4, TT], BF16)
    mkt, MKT_g = outs("mkt", [64, 4, TT], BF16)
    mkb, MK_g = outs("mkb", [128, 2, 256], BF16)
    XM = [cx.sb(f"XM{i}", [128, 4, TT + 3], F32) for i in R2]
    mset(P, "pool", XM[0].t[:, :, 0:3], 0.0, [XM[0]])
    acc = [cx.sb("acc0", [128, 4, TT], F32)] * 2
    sig = [cx.sb("sig0", [128, 4, TT], F32)] * 2
    xct = [cx.sb(f"xct{i}", [128, 4, TT], BF16) for i in R2]
    X1b = TL(None, "X1")
    allg = []

    def load(i):
        s_ = i % 2
        t0 = i * TT
        dma(P, "sp", xt[s_].t[:], X1[t0:t0 + TT, :].rearrange("(s p) d -> p s d", p=128), [X1b], [xt[s_]], xt_g[s_])

    def tile(i):
        import os
        LV = int(os.environ.get('LV', '99'))
        if LV < 1:
            return
        s_ = i % 2
        t0 = i * TT
        tsl = slice(t0, t0 + TT)
        norm_transpose(cx, xt[s_], G, ident, NH, hb[s_], hT[s_], pT, ss[s_], rs[s_], junk)
        H = hT[s_]

        def tok_group(sub, c0, width):
            b = bank()
            for kc in range(8):
                mm(P, b.t[:, 0:width], H.t[:, kc, sub * 128:(sub + 1) * 128], WIN.t[:, kc, c0:c0 + width],
                   kc == 0, kc == 7, [H, WIN], [b])
            return b

        pq, pi = pX[0], pX[1]
        ssl = None
        for sub in range(2):
            u = sub
            if LV < 2:
                continue
            b = tok_group(sub, 0, 512)
            act(P, sqt[u].t[:], b.t[:], AF.Square, [b], [sqt[u]])
            red(P, s8[u].t[:], sqt[u].t[:].rearrange("p (h d) -> p h d", h=8), ALU.add, [sqt[u]], [s8[u]])
            rstd_ops(cx, s8[u], 64, r8[u], NH, [])
            tt(P, "dve", t1[u].t[:].rearrange("p (h d) -> p h d", h=8), b.t[:].rearrange("p (h d) -> p h d", h=8),
               r8[u].t[:].unsqueeze(2).broadcast_to([128, 8, 64]), ALU.mult, [b, r8[u]], [t1[u]])
            tt(P, "pool", qn[u].t[:].rearrange("p (r g d) -> p g r d", r=4, g=2),
               t1[u].t[:].rearrange("p (g r d) -> p g r d", g=2, r=4),
               GQ.t[:].rearrange("p (g r d) -> p g r d", g=2, r=4), ALU.mult, [t1[u], GQ], [qn[u]])
            ssl = slice(sub * 128, (sub + 1) * 128)
            for r in range(4):
                tr(P, pq.t[:, r * 128:(r + 1) * 128], qn[u].t[:, r * 128:(r + 1) * 128],
                   ident.t[:], [qn[u], ident], [pq])
            cp(P, "act", QTt[s_].t[:, :, ssl], pq.t[:, 0:512].rearrange("p (r t) -> p r t", r=4), [pq], [QTt[s_]])
            if sub == 1:
                dma(P, "sp", A["QT"][:, :, tsl].rearrange("r p t -> p r t"), QTt[s_].t[:], [QTt[s_]], [], QT_g[s_])
            if LV < 3:
                continue
            b = tok_group(sub, 512, 256)
            act(P, sqt[u].t[:, 0:128], b.t[:, 0:128], AF.Square, [b], [sqt[u]])
            red(P, s2[u].t[:], sqt[u].t[:, 0:128].rearrange("p (h d) -> p h d", h=2), ALU.add, [sqt[u]], [s2[u]])
            rstd_ops(cx, s2[u], 64, r2[u], NH, [])
            tt(P, "dve", t1[u].t[:, 0:128].rearrange("p (h d) -> p h d", h=2),
               b.t[:, 0:128].rearrange("p (h d) -> p h d", h=2),
               r2[u].t[:].unsqueeze(2).broadcast_to([128, 2, 64]), ALU.mult, [b, r2[u]], [t1[u]])
            tt(P, "pool", kn[u].t[:], t1[u].t[:, 0:128], GK.t[:], ALU.mult, [t1[u], GK], [kn[u]])
            tr(P, pq.t[:, 512:640], kn[u].t[:], ident.t[:], [kn[u], ident], [pq])
            cp(P, "act", vb[s_].t[:, sub, :], b.t[:, 128:256], [b], [vb[s_]])
            cp(P, "dve", KTt[s_].t[:, ssl], pq.t[:, 512:640], [pq], [KTt[s_]])
            if sub == 1:
                dma(P, "sp", A["KT"][:, tsl], KTt[s_].t[:], [KTt[s_]], [], KT_g[s_])
                dma(P, "sp", A["V"][tsl, :].rearrange("(s p) d -> p s d", p=128), vb[s_].t[:], [vb[s_]], [], V_g[s_])
            if LV < 4:
                continue
            bw = tok_group(sub, 1280, 72)
            ts(P, "dve", zt[u].t[:], bw.t[:, 64:72], -IDX_SCALE, None, ALU.mult, ALU.bypass, [bw], [zt[u]])
            stt(P, absw[u].t[:], bw.t[:, 64:72], IDX_SCALE, zt[u].t[:], ALU.mult, ALU.max, [bw, zt[u]], [absw[u]])
            ts(P, "dve", sgn[s_].t[:, sub, :], bw.t[:, 64:72], 0.0, 2.0, ALU.is_ge, ALU.mult, [bw], [sgn[s_]])
            ts(P, "dve", sgn[s_].t[:, sub, :], sgn[s_].t[:, sub, :], -1.0, None, ALU.add, ALU.bypass, [sgn[s_]], [sgn[s_]])
            act(P, sqt[u].t[:, 0:64], bw.t[:, 0:64], AF.Square, [bw], [sqt[u], s1[u]], accum_out=s1[u].t[:])
            rstd_ops(cx, s1[u], 64, r1[u], NH, [])
            stt(P, ikn[u].t[:], bw.t[:, 0:64], r1[u].t[:], GI.t[:], ALU.mult, ALU.mult, [bw, r1[u], GI], [ikn[u]])
            b = tok_group(sub, 768, 512)
            tt(P, "dve", iqs[u].t[:].rearrange("p (h d) -> p h d", h=8), b.t[:].rearrange("p (h d) -> p h d", h=8),
               absw[u].t[:].unsqueeze(2).broadcast_to([128, 8, 64]), ALU.mult, [b, absw[u]], [iqs[u]])
            for r in range(4):
                tr(P, pi.t[:, r * 128:(r + 1) * 128], iqs[u].t[:, r * 128:(r + 1) * 128],
                   ident.t[:], [iqs[u], ident], [pi])
            tr(P, pi.t[0:64, 512:640], ikn[u].t[:], ident.t[:], [ikn[u], ident], [pi])
            cp(P, "act", IQTt[s_].t[:, :, ssl], pi.t[:, 0:512].rearrange("p (r t) -> p r t", r=4), [pi], [IQTt[s_]])
            cp(P, "dve", IKTt[s_].t[:, ssl], pi.t[0:64, 512:640], [pi], [IKTt[s_]])
            if sub == 1:
                dma(P, "sp", A["IQT"][:, :, tsl].rearrange("r p t -> p r t"), IQTt[s_].t[:], [IQTt[s_]], [], IQT_g[s_])
                dma(P, "sp", A["IKT"][:, tsl], IKTt[s_].t[:], [IKTt[s_]], [], IKT_g[s_])
                dma(P, "sp", A["SGN"][tsl, :].rearrange("(s p) h -> p s h", p=128), sgn[s_].t[:], [sgn[s_]], [], SGN_g[s_])
            if LV < 5:
                continue
            b = tok_group(sub, 1864, 512)
            cp(P, "dve", mvb[s_].t[:, sub, :], b.t[:], [b], [mvb[s_]])
            b = tok_group(sub, 2376, 8)
            tt(P, "dve", zt[u].t[:], b.t[:, 0:8], BIF.t[:], ALU.add, [b, BIF], [zt[u]])
            cp(P, "pool", lif[s_].t[:, sub, 0:4], zt[u].t[:, 0:4], [zt[u]], [lif[s_]])
            act(P, et[u].t[:], zt[u].t[:, 4:8], AF.Exp, [zt[u]], [et[u]], scale=-1.0)
            act(P, et[u].t[:], et[u].t[:], AF.Ln, [et[u]], [et[u]], bias=1.0)
            ts(P, "dve", lif[s_].t[:, sub, 4:8], et[u].t[:], -1.0, None, ALU.mult, ALU.bypass, [et[u]], [lif[s_]])
            b = tok_group(sub, 2384, 512)
            act(P, mob[s_].t[:, sub, :], b.t[:], AF.Sigmoid, [b], [mob[s_]])
        if LV < 6:
            return
        dma(P, "sp", A["MV"][tsl, :].rearrange("(s p) d -> p s d", p=128), mvb[s_].t[:], [mvb[s_]], [], MV_g[s_])
        dma(P, "sp", A["LIF"][tsl, :].rearrange("(s p) d -> p s d", p=128), lif[s_].t[:], [lif[s_]], [], LIF_g[s_])
        dma(P, "sp", A["MO"][tsl, :].rearrange("(s p) d -> p s d", p=128), mob[s_].t[:], [mob[s_]], [], MO_g[s_])

        def feat_pair(c0):
            b = bank()
            for c in range(2):
                for kc in range(8):
                    mm(P, b.t[:, c * TT:(c + 1) * TT], WIN.t[:, kc, c0 + c * 128:c0 + (c + 1) * 128], H.t[:, kc, :],
                       kc == 0, kc == 7, [H, WIN], [b])
            return b

        if LV < 7:
            return
        for (base, dst) in ((2896, gat[s_]), (3920, gbt[s_])):
            for pr in range(4):
                b = feat_pair(base + pr * 256)
                act(P, dst.t[:, 2 * pr:2 * pr + 2, :], b.t[:].rearrange("p (c t) -> p c t", c=2), AF.Sigmoid, [b], [dst])
        dma(P, "sp", A["GAT"][:, :, tsl].rearrange("c p t -> p c t"), gat[s_].t[:], [gat[s_]], [], GAT_g[s_])
        dma(P, "sp", A["GBT"][:, :, tsl].rearrange("c p t -> p c t"), gbt[s_].t[:], [gbt[s_]], [], GBT_g[s_])
        if LV < 8:
            return
        xm, xn = XM[s_], XM[1 - s_]
        for pr in range(2):
            b = feat_pair(1352 + pr * 256)
            cp(P, "dve", xm.t[:, 2 * pr:2 * pr + 2, 3:3 + TT], b.t[:].rearrange("p (c t) -> p c t", c=2), [b], [xm])
        cp(P, "pool", xn.t[:, :, 0:3], xm.t[:, :, TT:TT + 3], [xm], [xn])
        ac = acc[s_]
        for c in range(4):
            ts(P, "dve", ac.t[:, c, :], xm.t[:, c, 0:TT], CW.t[:, c, 0:1], CB.t[:, c:c + 1], ALU.mult, ALU.add,
               [xm, CW, CB], [ac])
            for j in range(1, 4):
                stt(P, ac.t[:, c, :], xm.t[:, c, j:j + TT], CW.t[:, c, j:j + 1], ac.t[:, c, :], ALU.mult, ALU.add,
                    [xm, CW, ac], [ac])
        act(P, sig[s_].t[:], ac.t[:], AF.Sigmoid, [ac], [sig[s_]])
        tt(P, "pool", xct[s_].t[:], ac.t[:], sig[s_].t[:], ALU.mult, [ac, sig[s_]], [xct[s_]])
        if LV < 9:
            return
        xc = xct[s_]
        for (wi, dst, scale, name, grp) in ((0, mqt[s_], MQ_SCALE, "MQT", MQT_g[s_]), (1, mkt[s_], 1.0, "MKT", MKT_g[s_])):
            for hp in range(2):
                b = bank()
                for c in range(2):
                    h = hp * 2 + c
                    mm(P, b.t[0:64, c * TT:(c + 1) * TT], WM.t[:, wi, h, :], xc.t[:, h, :], True, True, [WM, xc], [b])
                act(P, dst.t[:, 2 * hp:2 * hp + 2, :], b.t[0:64, :].rearrange("p (c t) -> p c t", c=2), AF.Copy,
                    [b], [dst], scale=scale)
            dma(P, "sp", A[name][:, :, tsl].rearrange("h d t -> d h t"), dst.t[:], [dst], [], grp)
        for sub in range(2):
            b = bank()
            for h in range(4):
                mm(P, b.t[:, h * 64:(h + 1) * 64], xc.t[:, h, sub * 128:(sub + 1) * 128], WM.t[:, 1, h, :], True, True,
                   [xc, WM], [b])
            cp(P, "dve", mkb[s_].t[:, sub, :], b.t[:, 0:256], [b], [mkb[s_]])
        dma(P, "sp", A["MK"][tsl, :].rearrange("(s p) d -> p s d", p=128), mkb[s_].t[:], [mkb[s_]], [], MK_g[s_])

    load(0)
    if NT > 1:
        load(1)
    for i in range(NT):
        tile(i)
        if i + 2 < NT:
            load(i + 2)
    fin = QT_g + KT_g + IQT_g + IKT_g + V_g + SGN_g + MV_g + LIF_g + MO_g + GAT_g + GBT_g + MQT_g + MKT_g + MK_g
    return {"sp": fin}


NIT = 14


def dsa_phase(cx, S, A, Cd):
    nc, P = cx.nc, cx.P
    NB = S // 512
    NJ = S // 128
    cg = cx.group(final=True)
    ident = load_ident(cx, Cd, cg)
    IKbs = [cx.sb(f"IKb{i}", [128, 512], BF16) for i in range(3)]
    IK_g = [cx.group() for _ in range(3)]
    KT = cx.sb("KT", [128, S], BF16)
    dma(P, "sp", KT.t[:], A["KT"], [], [KT], cg)
    VA = cx.sb("VA", [128, NJ, 2, 65], BF16)
    mset(P, "pool", VA.t[:], 1.0, [VA])
    for g_ in range(2):
        dma(P, "sp", VA.t[:, :, g_, 0:64], A["V"][:, g_ * 64:(g_ + 1) * 64].rearrange("(j p) d -> p j d", p=128),
            [], [VA], cg)
    CM = cx.sb("CM", [128, 4, 512], BF16)
    dma(P, "sp", CM.t[:], Cd["c_cmask"].rearrange("i p s -> p i s"), [], [CM], cg)
    CROW = cx.sb("CROW", [128, NIT], F32)
    dma(P, "sp", CROW.t[:], Cd["c_steps"], [], [CROW], cg)
    ONES = cx.sb("ONES", [128, 64], F32)
    mset(P, "pool", ONES.t[:], 1.0, [ONES])

    SCs = [cx.sb(f"SC{i}", [128, S], F32) for i in range(2)]
    MK = cx.sb("MK", [128, S], BF16)
    MT = cx.sb("MT", [128, NJ, 512], BF16)
    IQb = cx.sb("IQb", [128, 4, 512], BF16)
    Qb = cx.sb("Qb", [128, 4, 512], BF16)
    SGb = cx.sb("SGb", [128, 4, 8], F32)
    ld_g1, ld_g2, ld_g3 = cx.group(), cx.group(), cx.group()
    DG = [cx.sb("DG0", [128, 8, 128], BF16)]
    Rt = [cx.sb(f"Rt{i}", [128, 512], BF16) for i in range(3)]
    Et = [cx.sb(f"Et{i}", [128, 512], BF16) for i in range(2)]
    Pt = [cx.sb(f"Pt{i}", [128, 512], BF16) for i in range(2)]
    M1 = cx.sb("M1", [128, 1], F32)
    LO = cx.sb("LO", [128, 1], F32)
    W0 = cx.sb("W0", [128, 1], F32)
    STEP = cx.sb("STEP", [128, NIT], F32)
    NSTEP = cx.sb("NSTEP", [128, NIT], F32)
    STEP2 = cx.sb("STEP2", [128, NIT], F32)
    MID = cx.sb("MID", [128, 1], F32)
    CNT = cx.sb("CNT", [128, 1], F32)
    DT = cx.sb("DT", [128, 1], F32)
    RC = cx.sb("RC", [128, 512], F32)
    OS = cx.sb("OS", [64, 512], F32)
    YT = [cx.sb(f"YT{i}", [64, 512], BF16) for i in range(2)]
    YT_g = [cx.group() for _ in range(2)]
    pL = [cx.ps(f"pL{i}", [128, 512], F32) for i in range(3)]
    SCp = cx.ps("SCp", [128, 512], F32)
    pTr = cx.ps("pTr", [128, 1024], BF16)
    pS = [cx.ps(f"pS{i}", [128, 512], F32) for i in range(2)]
    pO = cx.ps("pO", [128, 512], F32)
    cnt = {"ik": 0, "l": 0, "r": 0, "s": 0, "e": 0, "p": 0, "y": 0, "d": 0, "m": 0}

    def nxt(key, lst):
        v = lst[cnt[key] % len(lst)]
        cnt[key] += 1
        return v

    tiles = [(b, ii) for b in range(NB) for ii in range(4)]
    dgs = {}

    def indexer(n):
        b, ii = tiles[n]
        bs = slice(b * 512, (b + 1) * 512)
        SC = SCs[n % 2]
        if ii == 0:
            dma(P, "sp", IQb.t[:], A["IQT"][:, :, bs].rearrange("r p t -> p r t"), [], [IQb], ld_g1)
            dma(P, "sp", SGb.t[:], A["SGN"][bs, :].rearrange("(i p) h -> p i h", p=128), [], [SGb], ld_g3)
        tq = slice(ii * 128, (ii + 1) * 128)
        dg = nxt("d", DG)
        for h in range(8):
            ts(P, "pool", dg.t[:, h, :], ident.t[:], SGb.t[:, ii, h:h + 1], 1.0, ALU.mult, ALU.mult,
               [ident, SGb], [dg])
        for kb in range(b + 1):
            ks = slice(kb * 512, (kb + 1) * 512)
            ikq = cnt["ik"] % 3
            cnt["ik"] += 1
            IKb = IKbs[ikq]
            dma(P, "sp", IKb.t[0:64, :], A["IKT"][:, ks], [], [IKb], IK_g[ikq])
            dma(P, "sp", IKb.t[64:128, :], A["IKT"][:, ks], [], [IKb], IK_g[ikq])
            for h in range(8):
                pr, half = h // 2, h % 2
                ps_ = slice(half * 64, half * 64 + 64)
                L = nxt("l", pL)
                mm(P, L.t[:], IQb.t[ps_, pr, tq], IKb.t[ps_, :], True, True, [IQb, IKb], [L])
                R = nxt("r", Rt)
                act(P, R.t[:], L.t[:], AF.Relu, [L], [R])
                mm(P, SCp.t[:], dg.t[:, h, :], R.t[:], h == 0, h == 7, [dg, R], [SCp])
            cp(P, "act", SC.t[:, ks], SCp.t[:], [SCp], [SC])

    def post(n):
        b, ii = tiles[n]
        Nb = 512 * (b + 1)
        njb = 4 * (b + 1)
        bs = slice(b * 512, (b + 1) * 512)
        tq = slice(ii * 128, (ii + 1) * 128)
        SC = SCs[n % 2]
        red(P, M1.t[:], SC.t[:, 0:Nb], ALU.max, [SC], [M1], apply_absolute_value=True)
        tt(P, "pool", SC.t[:, bs], SC.t[:, bs], CM.t[:, ii, :], ALU.add, [SC, CM], [SC])
        ts(P, "dve", W0.t[:], M1.t[:], 2.002, 2e-6, ALU.mult, ALU.add, [M1], [W0])
        ts(P, "dve", STEP.t[:], CROW.t[:], W0.t[:], None, ALU.mult, ALU.bypass, [CROW, W0], [STEP])
        ts(P, "dve", NSTEP.t[:], STEP.t[:], -1.0, None, ALU.mult, ALU.bypass, [STEP], [NSTEP])
        ts(P, "dve", STEP2.t[:], STEP.t[:], 2.0, None, ALU.mult, ALU.bypass, [STEP], [STEP2])
        mset(P, "dve", MID.t[:], 0.0, [MID])
        for k in range(NIT):
            ts(P, "dve", MK.t[:, 0:Nb], SC.t[:, 0:Nb], MID.t[:], None, ALU.is_ge, ALU.add, [SC, MID], [MK, CNT],
               accum=CNT.t[:])
            if k + 1 < NIT:
                stt(P, DT.t[:], CNT.t[:], 255.5, STEP2.t[:, k + 1:k + 2], ALU.is_ge, ALU.mult, [CNT, STEP2], [DT])
                stt(P, MID.t[:], DT.t[:], NSTEP.t[:, k + 1:k + 2], MID.t[:], ALU.add, ALU.add, [DT, NSTEP, MID], [MID])
            else:
                stt(P, DT.t[:], CNT.t[:], 255.5, STEP.t[:, k:k + 1], ALU.is_lt, ALU.mult, [CNT, STEP], [DT])
                tt(P, "dve", LO.t[:], MID.t[:], DT.t[:], ALU.subtract, [MID, DT], [LO])
        ts(P, "dve", MK.t[:, 0:Nb], SC.t[:, 0:Nb], LO.t[:], None, ALU.is_ge, ALU.bypass, [SC, LO], [MK])
        for j0 in range(0, njb, 8):
            nn = min(8, njb - j0)
            for jj in range(nn):
                j = j0 + jj
                tr(P, pTr.t[:, jj * 128:(jj + 1) * 128], MK.t[:, j * 128:(j + 1) * 128], ident.t[:], [MK, ident], [pTr])
            cp(P, "act" if (cnt["m"] % 2 == 0) else "dve", MT.t[:, j0:j0 + nn, tq],
               pTr.t[:, 0:nn * 128].rearrange("p (j t) -> p j t", t=128), [pTr], [MT])
            cnt["m"] += 1

    def attention(b):
        njb = 4 * (b + 1)
        bs = slice(b * 512, (b + 1) * 512)
        dma(P, "sp", Qb.t[:], A["QT"][:, :, bs].rearrange("r p t -> p r t"), [], [Qb], ld_g2)
        for h in range(8):
            g, r = h // 4, h % 4
            ps_ = slice(g * 64, g * 64 + 64)
            for j in range(njb):
                ST = nxt("s", pS)
                mm(P, ST.t[:], KT.t[ps_, j * 128:(j + 1) * 128], Qb.t[ps_, r, :], True, True, [KT, Qb], [ST])
                E = nxt("e", Et)
                act(P, E.t[:], ST.t[:], AF.Exp, [ST], [E])
                PT = nxt("p", Pt)
                tt(P, "pool" if (cnt["p"] % 4 == 0) else "dve", PT.t[:], E.t[:], MT.t[:, j, :], ALU.mult, [E, MT], [PT])
                mm(P, pO.t[0:65, :], VA.t[:, j, g, :], PT.t[:], j == 0, j == njb - 1, [VA, PT], [pO])
            P.op("dve", lambda e: e.reciprocal(out=RC.t[64:65, :], in_=pO.t[64:65, :]), [pO.b], [RC.b])
            BC = nxt("s", pS)
            mm(P, BC.t[0:64, :], ONES.t[64:65, 0:64], RC.t[64:65, :], True, True, [ONES, RC], [BC])
            cp(P, "act", OS.t[:], pO.t[0:64, :], [pO], [OS])
            y = cnt["y"] % 2
            cnt["y"] += 1
            tt(P, "dve", YT[y].t[:], OS.t[:], BC.t[0:64, :], ALU.mult, [OS, BC], [YT[y]])
            dma(P, "sp", A["YAT"][h, :, bs], YT[y].t[:], [YT[y]], [], YT_g[y])

    indexer(0)
    for n in range(len(tiles)):
        if n + 1 < len(tiles):
            indexer(n + 1)
        post(n)
        if tiles[n][1] == 3:
            attention(tiles[n][0])
    return {"sp": YT_g}


def mlstm_phase(cx, S, A, Wd, Cd):
    nc, P = cx.nc, cx.P
    NT = S // 128
    cg = cx.group(final=True)
    ident = load_ident(cx, Cd, cg)
    T2 = cx.sb("T2", [128, 128], F32)
    dma(P, "sp", T2.t[:], Cd["c_tri2"], [], [T2], cg)
    NEGM = cx.sb("NEGM", [128, 512], BF16)
    dma(P, "sp", NEGM.t[:], Cd["c_negm"], [], [NEGM], cg)
    HG = cx.sb("HG", [128, 512], F32)
    dma(P, "sp", HG.t[:], Wd["m_head_norm"].partition_broadcast(128), [], [HG], cg)
    NH = cx.sb("NH", [128, 8], F32)
    mset(P, "pool", NH.t[:], -0.5, [NH])
    CN = [cx.sb(f"CN{i}", [64, 4, 129], F32) for i in range(3)]
    mset(P, "pool", CN[0].t[:], 0.0, [CN[0]])
    R2 = range(2)
    LIF = [cx.sb(f"LIF{i}", [128, 8], F32) for i in R2]
    MQ = [cx.sb(f"MQ{i}", [64, 4, 128], BF16) for i in R2]
    MKt = [cx.sb(f"MKt{i}", [64, 4, 128], BF16) for i in R2]
    MKb = [cx.sb(f"MKb{i}", [128, 256], BF16) for i in R2]
    VA = [cx.sb(f"VA{i}", [128, 4, 129], BF16) for i in R2]
    MOt = [cx.sb(f"MOt{i}", [128, 512], BF16) for i in R2]
    lg = [[cx.group() for _ in range(6)] for _ in R2]
    for i in R2:
        mset(P, "pool", VA[i].t[:], 1.0, [VA[i]])
    LFm = cx.sb("LFm", [128, 4, 128], F32)
    bias = cx.sb("bias", [128, 4], F32)
    AT = cx.sb("AT", [128, 4, 128], F32)
    EB = cx.sb("EB", [128, 4, 128], F32)
    BL = cx.sb("BL", [128, 4], F32)
    WS = cx.sb("WS", [128, 4], F32)
    WT = cx.sb("WT", [128, 4, 128], BF16)
    QP = cx.sb("QP", [64, 4, 128], F32)
    KW = cx.sb("KW", [128, 4, 64], BF16)
    dn = cx.sb("dn", [128, 4], F32)
    rec = cx.sb("rec", [128, 4], F32)
    hh = cx.sb("hh", [128, 4, 128], F32)
    junk = cx.sb("junk", [128, 128], BF16)
    ssq = cx.sb("ssq", [128, 4], F32)
    rs4 = cx.sb("rs4", [128, 4], F32)
    yb = cx.sb("yb", [128, 512], BF16)
    YBt = [cx.sb(f"YBt{i}", [128, 4, 512], BF16) for i in R2]
    YB_g = [cx.group() for _ in R2]
    pX = cx.ps("pX", [128, 512], F32)
    pY = cx.ps("pY", [128, 512], F32)
    pZ = cx.ps("pZ", [128, 512], F32)
    pQ = cx.ps("pQ", [128, 512], F32)
    pD = cx.ps("pD", [128, 512], F32)
    pN = [cx.ps(f"pN{i}", [128, 512], F32) for i in R2]
    pTr = cx.ps("pTr", [128, 1024], BF16)

    def load(i):
        s_ = i % 2
        tsl = slice(i * 128, (i + 1) * 128)
        g = lg[s_]
        dma(P, "sp", LIF[s_].t[:], A["LIF"][tsl, :], [], [LIF[s_]], g[0])
        dma(P, "sp", MQ[s_].t[:], A["MQT"][:, :, tsl].rearrange("h d t -> d h t"), [], [MQ[s_]], g[1])
        dma(P, "sp", MKt[s_].t[:], A["MKT"][:, :, tsl].rearrange("h d t -> d h t"), [], [MKt[s_]], g[2])
        dma(P, "sp", MKb[s_].t[:], A["MK"][tsl, :], [], [MKb[s_]], g[3])
        dma(P, "sp", VA[s_].t[:, :, 0:128], A["MV"][tsl, :].rearrange("p (h d) -> p h d", h=4), [], [VA[s_]], g[4])
        dma(P, "sp", MOt[s_].t[:], A["MO"][tsl, :], [], [MOt[s_]], g[5])

    def tile(i):
        s_ = i % 2
        lif, mq, mkt, mkb, va, mo = LIF[s_], MQ[s_], MKt[s_], MKb[s_], VA[s_], MOt[s_]
        import os
        ML = int(os.environ.get('ML', '99'))
        cc, cm, cn = CN[(2 * i) % 3], CN[(2 * i + 1) % 3], CN[(2 * i + 2) % 3]
        cp(P, "pool", LFm.t[:], lif.t[:, 4:8].unsqueeze(2).broadcast_to([128, 4, 128]), [lif], [LFm])
        for h in range(4):
            mm(P, pX.t[:, h * 128:(h + 1) * 128], LFm.t[:, h, :], T2.t[:], True, True, [LFm, T2], [pX])
        mm(P, pY.t[:], ident.t[:], NEGM.t[:], True, False, [ident, NEGM], [pY])
        for h in range(4):
            mm(P, pY.t[:, h * 128:(h + 1) * 128], LFm.t[:, h, :], T2.t[:], False, h == 3, [LFm, T2], [pY])
        mm(P, pZ.t[:, 0:4], T2.t[:], lif.t[:, 4:8], True, True, [T2, lif], [pZ])
        tt(P, "dve", bias.t[:], lif.t[:, 0:4], pZ.t[:, 0:4], ALU.subtract, [lif, pZ], [bias])
        for h in range(4):
            act(P, AT.t[:, h, :], pY.t[:, h * 128:(h + 1) * 128], AF.Exp, [pY, bias], [AT], bias=bias.t[:, h:h + 1])
        act(P, EB.t[:], pX.t[:].rearrange("p (h t) -> p h t", h=4), AF.Exp, [pX], [EB])
        if ML < 2:
            return
        pXv = pX.t[:].rearrange("p (h t) -> p h t", h=4)
        cp(P, "dve", BL.t[0:64, :], pXv[0:64, :, 63], [pX], [BL])
        cp(P, "dve", BL.t[64:128, :], pXv[64:128, :, 127], [pX], [BL])
        tt(P, "dve", BL.t[:], BL.t[:], bias.t[:], ALU.add, [BL, bias], [BL])
        act(P, WS.t[:], BL.t[:], AF.Exp, [BL], [WS])
        for h in range(4):
            mm(P, pQ.t[:, h * 128:(h + 1) * 128], mkt.t[:, h, :], mq.t[:, h, :], True, True, [mkt, mq], [pQ])
        tt(P, "dve", WT.t[:], pQ.t[:].rearrange("p (h t) -> p h t", h=4), AT.t[:], ALU.mult, [pQ, AT], [WT])
        tt(P, "pool", QP.t[:], mq.t[:], EB.t[0:64, :, :], ALU.mult, [mq, EB], [QP])
        tt(P, "pool", KW.t[:], mkb.t[:].rearrange("p (h d) -> p h d", h=4),
           WS.t[:].unsqueeze(2).broadcast_to([128, 4, 64]), ALU.mult, [mkb, WS], [KW])
        if ML < 3:
            return
        for (rows, src, dst, col) in ((slice(0, 64), cc, cm, 63), (slice(64, 128), cm, cn, 127)):
            for hp in range(2):
                for c in range(2):
                    h = 2 * hp + c
                    mm(P, pD.t[0:64, c * 129:(c + 1) * 129], KW.t[rows, h, :], va.t[rows, h, :], True, True,
                       [KW, va], [pD])
                for c in range(2):
                    h = 2 * hp + c
                    stt(P, dst.t[:, h, :], src.t[:, h, :], EB.t[0:64, h, col:col + 1], pD.t[0:64, c * 129:(c + 1) * 129],
                        ALU.mult, ALU.add, [src, EB, pD], [dst])
        if ML < 4:
            return
        for h in range(4):
            bk = pN[h // 2]
            osl = slice((h % 2) * 129, (h % 2) * 129 + 129)
            mm(P, bk.t[:, osl], WT.t[:, h, :], va.t[:, h, :], True, False, [WT, va], [bk])
            mm(P, bk.t[0:64, osl], QP.t[:, h, 0:64], cc.t[:, h, :], False, False, [QP, cc], [bk])
            mm(P, bk.t[64:128, osl], QP.t[:, h, 64:128], cm.t[:, h, :], False, True, [QP, cm], [bk])
        if ML < 5:
            return
        for q in range(2):
            act(P, dn.t[:, 2 * q:2 * q + 2], pN[q].t[:, 0:258].rearrange("p (h c) -> p h c", c=129)[:, :, 128],
                AF.Abs, [pN[q]], [dn])
        ts(P, "dve", dn.t[:], dn.t[:], 1.0, None, ALU.max, ALU.bypass, [dn], [dn])
        P.op("dve", lambda e: e.reciprocal(out=rec.t[:], in_=dn.t[:]), [dn.b], [rec.b])
        for h in range(4):
            o0 = (h % 2) * 129
            act(P, hh.t[:, h, :], pN[h // 2].t[:, o0:o0 + 128], AF.Copy, [pN[h // 2], rec], [hh], scale=rec.t[:, h:h + 1])
        if ML < 6:
            return
        for h in range(4):
            act(P, junk.t[:], hh.t[:, h, :], AF.Square, [hh], [junk, ssq], accum_out=ssq.t[:, h:h + 1])
        rstd_ops(cx, ssq, 128, rs4, NH, [])
        if ML < 7:
            return
        tt(P, "dve", hh.t[:], hh.t[:], rs4.t[:].unsqueeze(2).broadcast_to([128, 4, 128]), ALU.mult, [hh, rs4], [hh])
        hf = hh.t[:].rearrange("p h d -> p (h d)")
        tt(P, "pool", hf, hf, HG.t[:], ALU.mult, [hh, HG], [hh])
        tt(P, "dve", yb.t[:], hf, mo.t[:], ALU.mult, [hh, mo], [yb])
        if ML < 8:
            return
        for c in range(4):
            tr(P, pTr.t[:, c * 128:(c + 1) * 128], yb.t[:, c * 128:(c + 1) * 128], ident.t[:], [yb, ident], [pTr])
        yt = YBt[(i // 4) % 2]
        cp(P, "act", yt.t[:, :, (i % 4) * 128:(i % 4 + 1) * 128], pTr.t[:, 0:512].rearrange("p (c t) -> p c t", c=4),
           [pTr], [yt])
        if i % 4 == 3:
            b0 = (i // 4) * 512
            dma(P, "sp", A["YBT"][:, :, b0:b0 + 512].rearrange("c p t -> p c t"), yt.t[:], [yt], [], YB_g[(i // 4) % 2])

    load(0)
    for i in range(NT):
        if i + 1 < NT:
            load(i + 1)
        tile(i)
    return {"sp": YB_g}


def merge_phase(cx, S, X1, X2, A, Wd, Cd):
    nc, P = cx.nc, cx.P
    TT = 256
    NT = S // TT
    WA = cx.sb("WA", [128, 4, D], BF16)
    WB = cx.sb("WB", [128, 4, D], BF16)
    WO = cx.sb("WO", [128, 8, D], BF16)
    stg = [cx.sb(f"stg{i}", [128, D], F32) for i in range(2)]
    stg_g = [cx.group() for _ in range(2)]
    k = 0
    for (src, dst, n) in ((Wd["w_proj_a"], WA, 4), (Wd["w_proj_b"], WB, 4), (Wd["w_out"], WO, 8)):
        for c in range(n):
            s_ = k % 2
            dma(P, "sp", stg[s_].t[:], src[c * 128:(c + 1) * 128, :], [], [stg[s_]], stg_g[s_])
            cp(P, ["act", "dve", "pool"][k % 3], dst.t[:, c, :], stg[s_].t[:], [stg[s_]], [dst])
            k += 1
    R2 = range(2)
    YA = [cx.sb(f"YA{i}", [128, 4, TT], BF16) for i in R2]
    YB = [cx.sb(f"YB{i}", [128, 4, TT], BF16) for i in R2]
    GA = [cx.sb(f"GA{i}", [128, 8, TT], BF16) for i in R2]
    GB = [cx.sb(f"GB{i}", [128, 8, TT], BF16) for i in R2]
    xt = [cx.sb(f"xt{i}", [128, 2, D], F32) for i in R2]
    lg = [[cx.group() for _ in range(5)] for _ in R2]
    st_g = [cx.group() for _ in R2]
    m1 = [cx.sb(f"m1{i}", [128, 512], F32) for i in R2]
    m2 = [cx.sb(f"m2{i}", [128, 512], F32) for i in R2]
    MG = [cx.sb(f"MG{i}", [128, 8, TT], BF16) for i in R2]
    pA = [cx.ps(f"pA{i}", [128, 512], F32) for i in R2]
    pB = [cx.ps(f"pB{i}", [128, 512], F32) for i in R2]
    pO = [cx.ps(f"pO{i}", [128, 512], F32) for i in R2]

    def load(i):
        s_ = i % 2
        tsl = slice(i * TT, (i + 1) * TT)
        g = lg[s_]
        dma(P, "sp", YA[s_].t[:], A["YAT"][:, :, tsl].rearrange("(c w) d t -> (w d) c t", w=2), [], [YA[s_]], g[0])
        dma(P, "sp", YB[s_].t[:], A["YBT"][:, :, tsl].rearrange("c p t -> p c t"), [], [YB[s_]], g[1])
        dma(P, "sp", GA[s_].t[:], A["GAT"][:, :, tsl].rearrange("c p t -> p c t"), [], [GA[s_]], g[2])
        dma(P, "sp", GB[s_].t[:], A["GBT"][:, :, tsl].rearrange("c p t -> p c t"), [], [GB[s_]], g[3])
        dma(P, "sp", xt[s_].t[:], X1[tsl, :].rearrange("(s p) d -> p s d", p=128), [], [xt[s_]], g[4])

    def tile(i):
        s_ = i % 2
        tsl = slice(i * TT, (i + 1) * TT)
        for dcp in range(4):
            q = dcp % 2
            for (W_, Y_, bk) in ((WA, YA[s_], pA[q]), (WB, YB[s_], pB[q])):
                for c2 in range(2):
                    dc = 2 * dcp + c2
                    for c in range(4):
                        mm(P, bk.t[:, c2 * TT:(c2 + 1) * TT], W_.t[:, c, dc * 128:(dc + 1) * 128], Y_.t[:, c, :],
                           c == 0, c == 3, [W_, Y_], [bk])
            gsl = slice(2 * dcp, 2 * dcp + 2)
            tt(P, "dve", m1[q].t[:].rearrange("p (c t) -> p c t", c=2), pA[q].t[:].rearrange("p (c t) -> p c t", c=2),
               GA[s_].t[:, gsl, :], ALU.mult, [pA[q], GA[s_]], [m1[q]])
            tt(P, "dve", m2[q].t[:].rearrange("p (c t) -> p c t", c=2), pB[q].t[:].rearrange("p (c t) -> p c t", c=2),
               GB[s_].t[:, gsl, :], ALU.mult, [pB[q], GB[s_]], [m2[q]])
            tt(P, "pool", MG[s_].t[:, gsl, :], m1[q].t[:].rearrange("p (c t) -> p c t", c=2),
               m2[q].t[:].rearrange("p (c t) -> p c t", c=2), ALU.add, [m1[q], m2[q]], [MG[s_]])
        n = 0
        for sub in range(2):
            for dh in range(2):
                bk = pO[n % 2]
                n += 1
                for dc in range(8):
                    mm(P, bk.t[:], MG[s_].t[:, dc, sub * 128:(sub + 1) * 128], WO.t[:, dc, dh * 512:(dh + 1) * 512],
                       dc == 0, dc == 7, [MG[s_], WO], [bk])
                xs = xt[s_].t[:, sub, dh * 512:(dh + 1) * 512]
                tt(P, "dve", xs, bk.t[:], xs, ALU.add, [bk, xt[s_]], [xt[s_]])
        dma(P, "pool", X2[tsl, :].rearrange("(s p) d -> p s d", p=128), xt[s_].t[:], [xt[s_]], [], st_g[s_])

    load(0)
    if NT > 1:
        load(1)
    for i in range(NT):
        tile(i)
        if i + 2 < NT:
            load(i + 2)
    return {"pool": st_g}


WSPEC = {
    "ffn1_norm": [1, D], "ffn1_w_gate": [D, DFF], "ffn1_w_up": [D, DFF], "ffn1_w_down": [DFF, D],
    "mix_norm": [1, D], "w_in": [D, DIN], "q_norm": [1, 64], "k_norm": [1, 64], "idx_k_norm": [1, 64],
    "conv_w": [128, 16], "conv_b": [128, 4], "w_mq": [128, 256], "w_mk": [128, 256], "b_i": [1, 4],
    "b_f": [1, 4], "m_head_norm": [1, 512], "w_proj_a": [512, D], "w_proj_b": [512, D], "w_out": [D, D],
    "ffn2_norm": [1, D], "ffn2_w_gate": [D, DFF], "ffn2_w_up": [D, DFF], "ffn2_w_down": [DFF, D],
}


def consts():
    c = {}
    c["c_ident"] = np.eye(128, dtype=np.float32).astype(ml_dtypes.bfloat16)
    cm = np.zeros((4, 128, 512), np.float32)
    for ii in range(4):
        for p in range(128):
            lim = ((ii * 128 + p) // 64 + 1) * 64
            cm[ii, p, lim:] = -1e30
    c["c_cmask"] = cm.astype(ml_dtypes.bfloat16)
    c["c_steps"] = np.tile((2.0 ** -(np.arange(NIT) + 1.0)).astype(np.float32)[None, :], (128, 1))
    j = np.arange(128)
    t2 = ((j[:, None] // 64 == j[None, :] // 64) & (j[:, None] <= j[None, :]))
    c["c_tri2"] = t2.astype(np.float32)
    c["c_negm"] = np.tile(np.where(t2, 0.0, NEG).astype(np.float32), (1, 4)).astype(ml_dtypes.bfloat16)
    return c


CSPEC = {"c_ident": ([128, 128], BF16), "c_cmask": ([4, 128, 512], BF16), "c_steps": ([128, NIT], F32),
         "c_tri2": ([128, 128], F32), "c_negm": ([128, 512], BF16)}


def build_nc(S, debug=False, upto=9):
    nc = bass.Bass("TRN2", target_bir_lowering=False)

    def din(name, shape, dt=F32):
        return nc.dram_tensor(name, list(shape), dt, kind="ExternalInput").ap()

    x = din("x", [S, D])
    Wd = {k: din(k, v) for k, v in WSPEC.items()}
    Cd = {k: din(k, v[0], v[1]) for k, v in CSPEC.items()}
    out = nc.dram_tensor("out", [S, D], F32, kind="ExternalOutput").ap()
    kind = "ExternalOutput" if debug else "Internal"

    def scr(name, shape, dt):
        return nc.dram_tensor(name, list(shape), dt, kind=kind).ap()

    X1 = scr("X1", [S, D], F32)
    X2 = scr("X2", [S, D], F32)
    A = {
        "QT": scr("QT", [4, 128, S], BF16), "KT": scr("KT", [128, S], BF16), "V": scr("V", [S, 128], BF16),
        "IQT": scr("IQT", [4, 128, S], BF16), "IKT": scr("IKT", [64, S], BF16), "SGN": scr("SGN", [S, 8], F32),
        "MV": scr("MV", [S, 512], BF16), "LIF": scr("LIF", [S, 8], F32), "MO": scr("MO", [S, 512], BF16),
        "GAT": scr("GAT", [8, 128, S], BF16), "GBT": scr("GBT", [8, 128, S], BF16),
        "MQT": scr("MQT", [4, 64, S], BF16), "MKT": scr("MKT", [4, 64, S], BF16), "MK": scr("MK", [S, 256], BF16),
        "YAT": scr("YAT", [8, 64, S], BF16), "YBT": scr("YBT", [4, 128, S], BF16),
    }
    ph = [
        ("f1", lambda cx: ffn_phase(cx, S, x, X1, Wd["ffn1_norm"], Wd["ffn1_w_gate"], Wd["ffn1_w_up"],
                                    Wd["ffn1_w_down"], Cd["c_ident"])),
        ("pj", lambda cx: proj_phase(cx, S, X1, A, Wd, Cd)),
        ("ds", lambda cx: dsa_phase(cx, S, A, Cd)),
        ("ml", lambda cx: mlstm_phase(cx, S, A, Wd, Cd)),
        ("mg", lambda cx: merge_phase(cx, S, X1, X2, A, Wd, Cd)),
        ("f2", lambda cx: ffn_phase(cx, S, X2, out, Wd["ffn2_norm"], Wd["ffn2_w_gate"], Wd["ffn2_w_up"],
                                    Wd["ffn2_w_down"], Cd["c_ident"])),
    ]
    import os
    lo_ = int(os.environ.get('FROM', '0'))
    for n, (tag, fn) in enumerate(ph):
        if lo_ <= n < upto:
            run_phase(nc, tag, fn, debug)
    return nc


def layout_weights(inputs):
    sh = {}
    for k, v in WSPEC.items():
        a = np.asarray(inputs[k], dtype=np.float32)[0]
        if k == "conv_w":
            a = a.reshape(4, 4, 128).transpose(2, 1, 0)
        elif k == "conv_b":
            a = a.reshape(4, 128).T
        elif k in ("w_mq", "w_mk"):
            a = a.transpose(1, 0, 2)
        sh[k] = np.ascontiguousarray(a).reshape(v)
    return sh


_NC_CACHE = {}


def kernel(**inputs):
    x = np.ascontiguousarray(np.asarray(inputs["x"], dtype=np.float32))
    B, S, _ = x.shape
    if S not in _NC_CACHE:
        _NC_CACHE[S] = build_nc(S)
    nc = _NC_CACHE[S]
    shared = layout_weights(inputs)
    shared.update(consts())
    in_maps = []
    for b in range(B):
        m = dict(shared)
        m["x"] = x[b]
        in_maps.append(m)
    res = run_bass_kernel_spmd(nc, in_maps, core_ids=list(range(B)))
    return np.stack([np.asarray(r["out"], dtype=np.float32) for r in res.results], axis=0)
```

```python
import numpy as np
from contextlib import ExitStack
import ml_dtypes
import concourse.bass as bass
import concourse.mybir as mybir
from concourse.bass_utils import run_bass_kernel_spmd

F32 = mybir.dt.float32
BF16 = mybir.dt.bfloat16
ALU = mybir.AluOpType
AF = mybir.ActivationFunctionType
AX = mybir.AxisListType

D = 1024
DFF = 2816
NFF = DFF // 128
DIN = 4944
EPS = 1e-6
NEG = -30000.0


class Buf:
    __slots__ = ("name", "w", "r", "psum")

    def __init__(self, name):
        self.name = name
        self.w = {}
        self.r = {}
        self.psum = False


class Group:
    def __init__(self, sem, final=False):
        self.sem = sem
        self.n = 0
        self.final = final


class Prog:
    ENG = ("pe", "act", "dve", "pool", "sp")

    def __init__(self, nc):
        self.nc = nc
        self.ops = {e: [] for e in self.ENG}

    def op(self, eng, fn, reads=(), writes=(), group=None):
        ops = self.ops[eng]
        idx = len(ops)
        rec = {"fn": fn, "waits": {}, "sig": False, "group": group}
        is_dma = group is not None

        def need(key, val, raw):
            if is_dma and key[0] == "g" and key[1] is group:
                return
            if key[0] == "e" and key[1] == eng and not is_dma:
                if eng == "pe" or not raw:
                    return
            w = rec["waits"]
            if w.get(key, -1) < val:
                w[key] = val
            if key[0] == "e":
                self.ops[key[1]][val]["sig"] = True

        for b in reads:
            for k, v in b.w.items():
                need(k, v, True)
            if b.psum:
                for k, v in b.r.items():
                    need(k, v, False)
        for b in writes:
            for k, v in b.w.items():
                need(k, v, False)
            for k, v in b.r.items():
                need(k, v, False)
        if is_dma:
            group.n += 1
            me = (("g", group), group.n)
        else:
            me = (("e", eng), idx)
        for b in reads:
            if b.r.get(me[0], -1) < me[1]:
                b.r[me[0]] = me[1]
        for b in writes:
            b.w = {me[0]: me[1]}
            b.r = {}
        ops.append(rec)
        return rec

    def emit(self, block, sems, final_waits, base=None):
        if base is None:
            base = {e: 0 for e in self.ENG}
        sigcount = {}
        for e in self.ENG:
            c = base[e]
            lst = []
            for rec in self.ops[e]:
                if rec["sig"]:
                    c += 1
                lst.append(c)
            sigcount[e] = lst

        def run(engname, engine):
            waited = {}
            for rec in self.ops[engname]:
                for key, val in rec["waits"].items():
                    if key[0] == "e":
                        sem = sems[key[1]]
                        v = sigcount[key[1]][val]
                    else:
                        g = key[1]
                        sem = g.sem
                        v = 16 * (g.n if g.final else val)
                    if waited.get(id(sem), -1) >= v:
                        continue
                    waited[id(sem)] = v
                    engine.wait_ge(sem, v)
                ins = rec["fn"](engine)
                if rec["group"] is not None:
                    ins.then_inc(rec["group"].sem, 16)
                elif rec["sig"]:
                    ins.then_inc(sems[engname], 1)
            for g in final_waits.get(engname, ()):
                engine.wait_ge(g.sem, 16 * g.n)

        @block.tensor
        def _(e):
            run("pe", e)

        @block.scalar
        def _(e):
            run("act", e)

        @block.vector
        def _(e):
            run("dve", e)

        @block.gpsimd
        def _(e):
            run("pool", e)

        @block.sync
        def _(e):
            run("sp", e)

        for e in self.ENG:
            base[e] = sigcount[e][-1] if sigcount[e] else base[e]


class _Alloc:
    def __init__(self, nc, es):
        self._nc = nc
        self._es = es

    def alloc_sbuf_tensor(self, name, shape, dt):
        return self._es.enter_context(self._nc.sbuf_tensor(name, list(shape), dt))

    def alloc_psum_tensor(self, name, shape, dt):
        return self._es.enter_context(self._nc.psum_tensor(name, list(shape), dt))


class TL:
    def __init__(self, t, name):
        self.t = t
        self.b = Buf(name)


class Ctx:
    def __init__(self, nc, es, tag, debug=False):
        self.rnc = nc
        self.nc = _Alloc(nc, es)
        self.es = es
        self.tag = tag
        self.P = Prog(nc)
        self.groups = []
        self.debug = debug

    def group(self, final=False):
        sem = self.rnc.alloc_semaphore(f"{self.tag}g{len(self.groups)}")
        g = Group(sem, final)
        self.groups.append(g)
        return g

    def sb(self, name, shape, dt):
        return TL(self.nc.alloc_sbuf_tensor(self.tag + name, list(shape), dt), name)

    def ps(self, name, shape, dt):
        t = TL(self.nc.alloc_psum_tensor(self.tag + name, list(shape), dt), name)
        t.b.psum = True
        return t


_GSTATE = {}


def run_phase(nc, tag, fn, debug=False):
    with ExitStack() as es:
        cx = Ctx(nc, es, tag, debug)
        import os
        for _i in range(int(os.environ.get('DUMMYSEM', '0'))):
            es.enter_context(nc.semaphore(f"{tag}dummy{_i}"))
        finals = fn(cx)
        st = _GSTATE.setdefault(id(nc), {})
        if "sems" not in st:
            st["sems"] = {e: nc.alloc_semaphore(f"s_{e}") for e in Prog.ENG}
            st["base"] = {e: 0 for e in Prog.ENG}
        with nc.Block() as block:
            cx.P.emit(block, st["sems"], finals, st["base"])


def _rr(lst, i):
    return lst[i % len(lst)]


def ffn_phase(cx, S, xin, xout, gvec, wg, wu, wd, identd):
    nc, P = cx.nc, cx.P
    tag = cx.tag
    xin_buf, xout_buf = Buf("xin"), Buf("xout")
    ident = nc.alloc_sbuf_tensor(f"{tag}ident", [128, 128], BF16)
    identb = Buf("ident")
    cg0 = cx.group(final=True)
    P.op("sp", lambda e: e.dma_start(out=ident[:], in_=identd), writes=[identb], group=cg0)
    TT = 256
    NT = S // TT
    WG = nc.alloc_sbuf_tensor(f"{tag}WG", [128, 8, DFF], BF16)
    WU = nc.alloc_sbuf_tensor(f"{tag}WU", [128, 8, DFF], BF16)
    WD = nc.alloc_sbuf_tensor(f"{tag}WD", [128, NFF, D], BF16)
    stg = [nc.alloc_sbuf_tensor(f"{tag}stg{i}", [128, DFF // 2], F32) for i in range(2)]
    stg_b = [Buf(f"stg{i}") for i in range(2)]
    stg_g = [cx.group() for _ in range(2)]
    wbuf = Buf("W")
    G = nc.alloc_sbuf_tensor(f"{tag}G", [128, D], F32)
    gb = Buf("G")
    cg = cx.group(final=True)
    P.op("sp", lambda e: e.dma_start(out=G[:], in_=gvec.partition_broadcast(128)), writes=[gb], group=cg)
    nhalf = nc.alloc_sbuf_tensor(f"{tag}nh", [128, 2], F32)
    nhb = Buf("nh")
    P.op("pool", lambda e: e.memset(nhalf[:], -0.5), writes=[nhb])

    conv_eng = ["act", "dve", "pool"]
    k = 0

    def load_conv(src_ap, dst_ap, width):
        nonlocal k
        s = k % 2
        st, sb_, sg = stg[s], stg_b[s], stg_g[s]
        P.op("sp", lambda e: e.dma_start(out=st[:, 0:width], in_=src_ap), writes=[sb_], group=sg)
        ce = conv_eng[k % 3]
        if ce == "act":
            P.op("act", lambda e: e.copy(out=dst_ap, in_=st[:, 0:width]), reads=[sb_], writes=[wbuf])
        else:
            P.op(ce, lambda e: e.tensor_copy(out=dst_ap, in_=st[:, 0:width]), reads=[sb_], writes=[wbuf])
        k += 1

    HF = DFF // 2
    for kc in range(8):
        for hh in range(2):
            load_conv(wg[kc * 128:(kc + 1) * 128, hh * HF:(hh + 1) * HF], WG[:, kc, hh * HF:(hh + 1) * HF], HF)
            load_conv(wu[kc * 128:(kc + 1) * 128, hh * HF:(hh + 1) * HF], WU[:, kc, hh * HF:(hh + 1) * HF], HF)
    for f in range(NFF):
        load_conv(wd[f * 128:(f + 1) * 128, :], WD[:, f, :], D)

    xt = [nc.alloc_sbuf_tensor(f"{tag}xt{i}", [128, 2, D], F32) for i in range(2)]
    xt_b = [Buf(f"xt{i}") for i in range(2)]
    xt_g = [cx.group() for _ in range(2)]
    st_g = [cx.group() for _ in range(2)]
    junk = nc.alloc_sbuf_tensor(f"{tag}junk", [128, D], BF16)
    junk_b = Buf("junk")
    ss = [nc.alloc_sbuf_tensor(f"{tag}ss{i}", [128, 2], F32) for i in range(2)]
    ss_b = [Buf(f"ss{i}") for i in range(2)]
    ms = [nc.alloc_sbuf_tensor(f"{tag}ms{i}", [128, 2], F32) for i in range(2)]
    ms_b = [Buf(f"ms{i}") for i in range(2)]
    rs = [nc.alloc_sbuf_tensor(f"{tag}rs{i}", [128, 2], F32) for i in range(2)]
    rs_b = [Buf(f"rs{i}") for i in range(2)]
    hb = [nc.alloc_sbuf_tensor(f"{tag}h{i}", [128, 2, D], BF16) for i in range(2)]
    hb_b = [Buf(f"h{i}") for i in range(2)]
    hT = [nc.alloc_sbuf_tensor(f"{tag}hT{i}", [128, 8, TT], BF16) for i in range(2)]
    hT_b = [Buf(f"hT{i}") for i in range(2)]
    actT = [nc.alloc_sbuf_tensor(f"{tag}actT{i}", [128, NFF, TT], BF16) for i in range(2)]
    actT_b = [Buf(f"actT{i}") for i in range(2)]
    sg = [nc.alloc_sbuf_tensor(f"{tag}sg{i}", [128, 512], F32) for i in range(2)]
    sg_b = [Buf(f"sg{i}") for i in range(2)]
    pT = [nc.alloc_psum_tensor(f"{tag}pT{i}", [128, 1024], BF16) for i in range(2)]
    pT_b = [Buf(f"pT{i}") for i in range(2)]
    pG = [nc.alloc_psum_tensor(f"{tag}pG{i}", [128, 512], F32) for i in range(2)]
    pG_b = [Buf(f"pG{i}") for i in range(2)]
    pU = [nc.alloc_psum_tensor(f"{tag}pU{i}", [128, 512], F32) for i in range(2)]
    pU_b = [Buf(f"pU{i}") for i in range(2)]
    pD = [nc.alloc_psum_tensor(f"{tag}pD{i}", [128, 512], F32) for i in range(2)]
    pD_b = [Buf(f"pD{i}") for i in range(2)]

    def stage_load(i):
        s = i % 2
        t0 = i * TT
        P.op("sp", lambda e: e.dma_start(
            out=xt[s][:], in_=xin[t0:t0 + TT, :].rearrange("(s p) d -> p s d", p=128)),
            reads=[xin_buf], writes=[xt_b[s]], group=xt_g[s])

    def stage_norm(i):
        s = i % 2
        for sub in range(2):
            P.op("act", lambda e, sub=sub: e.activation(
                out=junk[:], in_=xt[s][:, sub, :], func=AF.Square, accum_out=ss[s][:, sub:sub + 1]),
                reads=[xt_b[s]], writes=[junk_b, ss_b[s]])
        P.op("dve", lambda e: e.tensor_scalar(out=ms[s][:], in0=ss[s][:], scalar1=1.0 / D, scalar2=EPS,
                                              op0=ALU.mult, op1=ALU.add), reads=[ss_b[s]], writes=[ms_b[s]])
        P.op("pool", lambda e: e.tensor_tensor(out=rs[s][:], in0=ms[s][:], in1=nhalf[:], op=ALU.pow),
             reads=[ms_b[s], nhb], writes=[rs_b[s]])
        for sub in range(2):
            P.op("dve", lambda e, sub=sub: e.scalar_tensor_tensor(
                out=hb[s][:, sub, :], in0=xt[s][:, sub, :], scalar=rs[s][:, sub:sub + 1], in1=G[:],
                op0=ALU.mult, op1=ALU.mult), reads=[xt_b[s], rs_b[s], gb], writes=[hb_b[s]])

    def stage_T(i):
        s = i % 2
        for half in range(2):
            pb, pbb = pT[half], pT_b[half]
            for kk in range(4):
                kc = half * 4 + kk
                for sub in range(2):
                    P.op("pe", lambda e, kc=kc, sub=sub, kk=kk, pb=pb: e.transpose(
                        out=pb[:, kk * TT + sub * 128: kk * TT + (sub + 1) * 128],
                        in_=hb[s][:, sub, kc * 128:(kc + 1) * 128], identity=ident[:]),
                        reads=[hb_b[s], identb], writes=[pbb])
            if half == 0:
                P.op("act", lambda e, pb=pb: e.copy(out=hT[s][:, 0:4, :], in_=pb[:].rearrange("p (k t) -> p k t", k=4)),
                     reads=[pbb], writes=[hT_b[s]])
            else:
                P.op("dve", lambda e, pb=pb: e.tensor_copy(out=hT[s][:, 4:8, :], in_=pb[:].rearrange("p (k t) -> p k t", k=4)),
                     reads=[pbb], writes=[hT_b[s]])

    def stage_GU(i):
        s = i % 2
        for f2 in range(NFF // 2):
            q = f2 % 2
            for (W_, pp, ppb) in ((WG, pG[q], pG_b[q]), (WU, pU[q], pU_b[q])):
                for c in range(2):
                    f = 2 * f2 + c
                    for kc in range(8):
                        P.op("pe", lambda e, W_=W_, pp=pp, c=c, f=f, kc=kc: e.matmul(
                            out=pp[:, c * TT:(c + 1) * TT], lhsT=W_[:, kc, f * 128:(f + 1) * 128],
                            rhs=hT[s][:, kc, :], start=(kc == 0), stop=(kc == 7)),
                            reads=[wbuf, hT_b[s]], writes=[ppb])
            P.op("act", lambda e, q=q: e.activation(out=sg[q][:], in_=pG[q][:], func=AF.Silu),
                 reads=[pG_b[q]], writes=[sg_b[q]])
            P.op("dve", lambda e, q=q, f2=f2: e.tensor_tensor(
                out=actT[s][:, 2 * f2:2 * f2 + 2, :], in0=sg[q][:].rearrange("p (c t) -> p c t", c=2),
                in1=pU[q][:].rearrange("p (c t) -> p c t", c=2), op=ALU.mult),
                reads=[sg_b[q], pU_b[q]], writes=[actT_b[s]])

    def stage_D(i):
        s = i % 2
        t0 = i * TT
        n = 0
        for sub in range(2):
            for dh in range(2):
                q = n % 2
                n += 1
                for f in range(NFF):
                    P.op("pe", lambda e, q=q, f=f, sub=sub, dh=dh: e.matmul(
                        out=pD[q][:], lhsT=actT[s][:, f, sub * 128:(sub + 1) * 128],
                        rhs=WD[:, f, dh * 512:(dh + 1) * 512], start=(f == 0), stop=(f == NFF - 1)),
                        reads=[wbuf, actT_b[s]], writes=[pD_b[q]])
                P.op("dve", lambda e, q=q, sub=sub, dh=dh: e.scalar_tensor_tensor(
                    out=xt[s][:, sub, dh * 512:(dh + 1) * 512], in0=pD[q][:], scalar=0.5,
                    in1=xt[s][:, sub, dh * 512:(dh + 1) * 512], op0=ALU.mult, op1=ALU.add),
                    reads=[pD_b[q], xt_b[s]], writes=[xt_b[s]])
        P.op("pool", lambda e: e.dma_start(
            out=xout[t0:t0 + TT, :].rearrange("(s p) d -> p s d", p=128), in_=xt[s][:]),
            reads=[xt_b[s]], writes=[xout_buf], group=st_g[s])

    stage_load(0)
    if NT > 1:
        stage_load(1)
    stage_norm(0)
    stage_T(0)
    for i in range(NT):
        stage_GU(i)
        if i + 1 < NT:
            stage_norm(i + 1)
            stage_T(i + 1)
        stage_D(i)
        if i + 2 < NT:
            stage_load(i + 2)
    return {"pool": st_g}


def _b(L):
    return [x.b for x in L]


def mm(P, out, lhsT, rhs, st, sp, R, W):
    P.op("pe", lambda e: e.matmul(out=out, lhsT=lhsT, rhs=rhs, start=st, stop=sp), _b(R), _b(W))


def tr(P, out, in_, ident, R, W):
    P.op("pe", lambda e: e.transpose(out=out, in_=in_, identity=ident), _b(R), _b(W))


def act(P, out, in_, func, R, W, **kw):
    P.op("act", lambda e: e.activation(out=out, in_=in_, func=func, **kw), _b(R), _b(W))


def ts(P, eng, out, in0, s1, s2, op0, op1, R, W, accum=None):
    if accum is None:
        P.op(eng, lambda e: e.tensor_scalar(out=out, in0=in0, scalar1=s1, scalar2=s2, op0=op0, op1=op1), _b(R), _b(W))
    else:
        P.op(eng, lambda e: e.tensor_scalar(out=out, in0=in0, scalar1=s1, scalar2=s2, op0=op0, op1=op1,
                                            accum_out=accum), _b(R), _b(W))


def tt(P, eng, out, in0, in1, op, R, W):
    P.op(eng, lambda e: e.tensor_tensor(out=out, in0=in0, in1=in1, op=op), _b(R), _b(W))


def stt(P, out, in0, scalar, in1, op0, op1, R, W):
    P.op("dve", lambda e: e.scalar_tensor_tensor(out=out, in0=in0, scalar=scalar, in1=in1, op0=op0, op1=op1),
         _b(R), _b(W))


def cp(P, eng, out, in_, R, W):
    if eng == "act":
        P.op("act", lambda e: e.copy(out=out, in_=in_), _b(R), _b(W))
    else:
        P.op(eng, lambda e: e.tensor_copy(out=out, in_=in_), _b(R), _b(W))


def red(P, out, in_, op, R, W, **kw):
    P.op("dve", lambda e: e.tensor_reduce(out=out, in_=in_, axis=AX.X, op=op, **kw), _b(R), _b(W))


def dma(P, eng, out, in_, R, W, g, **kw):
    P.op(eng, lambda e: e.dma_start(out=out, in_=in_, **kw), _b(R), _b(W), group=g)


def mset(P, eng, ap, val, W):
    P.op(eng, lambda e: e.memset(ap, val), [], _b(W))


def rstd_ops(cx, ssq, n, rs, nh, R):
    P = cx.P
    ts(P, "dve", ssq.t[:], ssq.t[:], 1.0 / n, EPS, ALU.mult, ALU.add, R + [ssq], [ssq])
    tt(P, "pool", rs.t[:], ssq.t[:], nh.t[:, 0:ssq.t.shape[1]], ALU.pow, [ssq, nh], [rs])


IDX_SCALE = (64 ** -0.5) * (8 ** -0.5)
ATT_SCALE = 64 ** -0.5
MQ_SCALE = 64 ** -0.5


def load_ident(cx, Cd, cg):
    ident = cx.sb("ident", [128, 128], BF16)
    dma(cx.P, "sp", ident.t[:], Cd["c_ident"], [], [ident], cg)
    return ident


def norm_transpose(cx, xt, G, ident, nh, hb, hT, pT, ss, rs, junk, ntok_sub=2):
    P = cx.P
    for sub in range(ntok_sub):
        act(P, junk.t[:], xt.t[:, sub, :], AF.Square, [xt], [junk, ss], accum_out=ss.t[:, sub:sub + 1])
    rstd_ops(cx, ss, D, rs, nh, [])
    for sub in range(ntok_sub):
        stt(P, hb.t[:, sub, :], xt.t[:, sub, :], rs.t[:, sub:sub + 1], G.t[:], ALU.mult, ALU.mult,
            [xt, rs, G], [hb])
    for half in range(2):
        pb = pT[half]
        for kk in range(4):
            kc = half * 4 + kk
            for sub in range(ntok_sub):
                tr(P, pb.t[:, kk * 256 + sub * 128: kk * 256 + (sub + 1) * 128],
                   hb.t[:, sub, kc * 128:(kc + 1) * 128], ident.t[:], [hb, ident], [pb])
        cp(P, "act" if half == 0 else "dve", hT.t[:, half * 4:half * 4 + 4, :],
           pb.t[:].rearrange("p (k t) -> p k t", k=4), [pb], [hT])


def proj_phase(cx, S, X1, A, Wd, Cd):
    nc, P = cx.nc, cx.P
    TT = 256
    NT = S // TT
    cg = cx.group(final=True)
    ident = load_ident(cx, Cd, cg)
    WIN = cx.sb("WIN", [128, 8, DIN], BF16)
    stg = [cx.sb(f"stg{i}", [128, 824], F32) for i in range(2)]
    stg_g = [cx.group() for _ in range(2)]
    k = 0
    for kc in range(8):
        for part in range(6):
            s_ = k % 2
            c0 = part * 824
            dma(P, "sp", stg[s_].t[:], Wd["w_in"][kc * 128:(kc + 1) * 128, c0:c0 + 824], [], [stg[s_]], stg_g[s_])
            cp(P, ["act", "dve", "pool"][k % 3], WIN.t[:, kc, c0:c0 + 824], stg[s_].t[:], [stg[s_]], [WIN])
            k += 1
    import os
    SL = int(os.environ.get('SL', '99'))
    G = cx.sb("G", [128, D], F32)
    dma(P, "sp", G.t[:], Wd["mix_norm"].partition_broadcast(128), [], [G], cg)
    GQ = cx.sb("GQ", [128, 512], F32)
    for h in range(8):
        if SL >= 2:
            dma(P, "sp", GQ.t[:, h * 64:(h + 1) * 64], Wd["q_norm"].partition_broadcast(128), [], [GQ], cg)
    if SL >= 2:
        ts(P, "dve", GQ.t[:], GQ.t[:], ATT_SCALE, None, ALU.mult, ALU.bypass, [GQ], [GQ])
    GK = cx.sb("GK", [128, 128], F32)
    for h in range(2):
        if SL >= 3:
            dma(P, "sp", GK.t[:, h * 64:(h + 1) * 64], Wd["k_norm"].partition_broadcast(128), [], [GK], cg)
    GI = cx.sb("GI", [128, 64], F32)
    if SL >= 3:
        dma(P, "sp", GI.t[:], Wd["idx_k_norm"].partition_broadcast(128), [], [GI], cg)
    CW = cx.sb("CW", [128, 4, 4], F32)
    CB = cx.sb("CB", [128, 4], F32)
    if SL >= 4:
        dma(P, "sp", CB.t[:], Wd["conv_b"], [], [CB], cg)
    if SL >= 4:
      dma(P, "sp", CW.t[:], Wd["conv_w"].rearrange("p (c j) -> p c j", c=4), [], [CW], cg)
    wst = cx.sb("wst", [128, 2, 4, 64], F32)
    WM = cx.sb("WM", [128, 2, 4, 64], BF16)
    if SL >= 5:
        dma(P, "sp", wst.t[:, 0], Wd["w_mq"].rearrange("c (h d) -> c h d", h=4), [], [wst], cg)
        dma(P, "sp", wst.t[:, 1], Wd["w_mk"].rearrange("c (h d) -> c h d", h=4), [], [wst], cg)
        cp(P, "dve", WM.t[:], wst.t[:], [wst], [WM])
    BIF = cx.sb("BIF", [128, 8], F32)
    if SL >= 6:
        dma(P, "sp", BIF.t[:, 0:4], Wd["b_i"].partition_broadcast(128), [], [BIF], cg)
        dma(P, "sp", BIF.t[:, 4:8], Wd["b_f"].partition_broadcast(128), [], [BIF], cg)
    NH = cx.sb("NH", [128, 8], F32)
    mset(P, "pool", NH.t[:], -0.5, [NH])

    R2 = range(2)
    xt = [cx.sb(f"xt{i}", [128, 2, D], F32) for i in R2]
    xt_g = [cx.group() for _ in R2]
    junk = cx.sb("junk", [128, D], BF16)
    ss = [cx.sb(f"ss{i}", [128, 2], F32) for i in R2]
    rs = [cx.sb(f"rs{i}", [128, 2], F32) for i in R2]
    hb = [cx.sb("hb0", [128, 2, D], BF16)] * 2
    hT = [cx.sb(f"hT{i}", [128, 8, TT], BF16) for i in R2]
    pT = [cx.ps(f"pT{i}", [128, 1024], BF16) for i in R2]
    pX = [cx.ps(f"pX{i}", [128, 1024], BF16) for i in R2]
    pM = [cx.ps(f"pM{i}", [128, 512], F32) for i in range(4)]
    nbank = [0]

    def bank():
        b = pM[nbank[0] % 4]
        nbank[0] += 1
        return b

    sqt = [cx.sb(f"sqt{i}", [128, 512], F32) for i in R2]
    t1 = [cx.sb(f"t1{i}", [128, 512], F32) for i in R2]
    qn = [cx.sb(f"qn{i}", [128, 512], BF16) for i in R2]
    kn = [cx.sb(f"kn{i}", [128, 128], BF16) for i in R2]
    ikn = [cx.sb(f"ikn{i}", [128, 64], BF16) for i in R2]
    iqs = [cx.sb(f"iqs{i}", [128, 512], BF16) for i in R2]
    s8 = [cx.sb(f"s8{i}", [128, 8], F32) for i in R2]
    r8 = [cx.sb(f"r8{i}", [128, 8], F32) for i in R2]
    s2 = [cx.sb(f"s2{i}", [128, 2], F32) for i in R2]
    r2 = [cx.sb(f"r2{i}", [128, 2], F32) for i in R2]
    s1 = [cx.sb(f"s1{i}", [128, 1], F32) for i in R2]
    r1 = [cx.sb(f"r1{i}", [128, 1], F32) for i in R2]
    absw = [cx.sb(f"absw{i}", [128, 8], F32) for i in R2]
    zt = [cx.sb(f"zt{i}", [128, 8], F32) for i in R2]
    et = [cx.sb(f"et{i}", [128, 4], F32) for i in R2]
    def outs(name, shape, dt):
        return [cx.sb(f"{name}{i}", shape, dt) for i in R2], [cx.group() for _ in R2]
    QTt, QT_g = outs("QTt", [128, 4, TT], BF16)
    KTt, KT_g = outs("KTt", [128, TT], BF16)
    IQTt, IQT_g = outs("IQTt", [128, 4, TT], BF16)
    IKTt, IKT_g = outs("IKTt", [64, TT], BF16)
    vb, V_g = outs("vb", [128, 2, 128], BF16)
    sgn, SGN_g = outs("sgn", [128, 2, 8], F32)
    mvb, MV_g = outs("mvb", [128, 2, 512], BF16)
    lif, LIF_g = outs("lif", [128, 2, 8], F32)
    mob, MO_g = outs("mob", [128, 2, 512], BF16)
    gat, GAT_g = outs("gat", [128, 8, TT], BF16)
    gbt, GBT_g = outs("gbt", [128, 8, TT], BF16)
    mqt, MQT_g = outs("mqt", [64, 4, TT], BF16)
    mkt, MKT_g = outs("mkt", [64, 4, TT], BF16)
    mkb, MK_g = outs("mkb", [128, 2, 256], BF16)
    XM = [cx.sb(f"XM{i}", [128, 4, TT + 3], F32) for i in R2]
    mset(P, "pool", XM[0].t[:, :, 0:3], 0.0, [XM[0]])
    acc = [cx.sb("acc0", [128, 4, TT], F32)] * 2
    sig = [cx.sb("sig0", [128, 4, TT], F32)] * 2
    xct = [cx.sb(f"xct{i}", [128, 4, TT], BF16) for i in R2]
    X1b = TL(None, "X1")
    allg = []

    def load(i):
        s_ = i % 2
        t0 = i * TT
        dma(P, "sp", xt[s_].t[:], X1[t0:t0 + TT, :].rearrange("(s p) d -> p s d", p=128), [X1b], [xt[s_]], xt_g[s_])

    def tile(i):
        import os
        LV = int(os.environ.get('LV', '99'))
        if LV < 1:
            return
        s_ = i % 2
        t0 = i * TT
        tsl = slice(t0, t0 + TT)
        norm_transpose(cx, xt[s_], G, ident, NH, hb[s_], hT[s_], pT, ss[s_], rs[s_], junk)
        H = hT[s_]

        def tok_group(sub, c0, width):
            b = bank()
            for kc in range(8):
                mm(P, b.t[:, 0:width], H.t[:, kc, sub * 128:(sub + 1) * 128], WIN.t[:, kc, c0:c0 + width],
                   kc == 0, kc == 7, [H, WIN], [b])
            return b

        pq, pi = pX[0], pX[1]
        ssl = None
        for sub in range(2):
            u = sub
            if LV < 2:
                continue
            b = tok_group(sub, 0, 512)
            act(P, sqt[u].t[:], b.t[:], AF.Square, [b], [sqt[u]])
            red(P, s8[u].t[:], sqt[u].t[:].rearrange("p (h d) -> p h d", h=8), ALU.add, [sqt[u]], [s8[u]])
            rstd_ops(cx, s8[u], 64, r8[u], NH, [])
            tt(P, "dve", t1[u].t[:].rearrange("p (h d) -> p h d", h=8), b.t[:].rearrange("p (h d) -> p h d", h=8),
               r8[u].t[:].unsqueeze(2).broadcast_to([128, 8, 64]), ALU.mult, [b, r8[u]], [t1[u]])
            tt(P, "pool", qn[u].t[:].rearrange("p (r g d) -> p g r d", r=4, g=2),
               t1[u].t[:].rearrange("p (g r d) -> p g r d", g=2, r=4),
               GQ.t[:].rearrange("p (g r d) -> p g r d", g=2, r=4), ALU.mult, [t1[u], GQ], [qn[u]])
            ssl = slice(sub * 128, (sub + 1) * 128)
            for r in range(4):
                tr(P, pq.t[:, r * 128:(r + 1) * 128], qn[u].t[:, r * 128:(r + 1) * 128],
                   ident.t[:], [qn[u], ident], [pq])
            cp(P, "act", QTt[s_].t[:, :, ssl], pq.t[:, 0:512].rearrange("p (r t) -> p r t", r=4), [pq], [QTt[s_]])
            if sub == 1:
                dma(P, "sp", A["QT"][:, :, tsl].rearrange("r p t -> p r t"), QTt[s_].t[:], [QTt[s_]], [], QT_g[s_])
            if LV < 3:
                continue
            b = tok_group(sub, 512, 256)
            act(P, sqt[u].t[:, 0:128], b.t[:, 0:128], AF.Square, [b], [sqt[u]])
            red(P, s2[u].t[:], sqt[u].t[:, 0:128].rearrange("p (h d) -> p h d", h=2), ALU.add, [sqt[u]], [s2[u]])
            rstd_ops(cx, s2[u], 64, r2[u], NH, [])
            tt(P, "dve", t1[u].t[:, 0:128].rearrange("p (h d) -> p h d", h=2),
               b.t[:, 0:128].rearrange("p (h d) -> p h d", h=2),
               r2[u].t[:].unsqueeze(2).broadcast_to([128, 2, 64]), ALU.mult, [b, r2[u]], [t1[u]])
            tt(P, "pool", kn[u].t[:], t1[u].t[:, 0:128], GK.t[:], ALU.mult, [t1[u], GK], [kn[u]])
            tr(P, pq.t[:, 512:640], kn[u].t[:], ident.t[:], [kn[u], ident], [pq])
            cp(P, "act", vb[s_].t[:, sub, :], b.t[:, 128:256], [b], [vb[s_]])
            cp(P, "dve", KTt[s_].t[:, ssl], pq.t[:, 512:640], [pq], [KTt[s_]])
            if sub == 1:
                dma(P, "sp", A["KT"][:, tsl], KTt[s_].t[:], [KTt[s_]], [], KT_g[s_])
                dma(P, "sp", A["V"][tsl, :].rearrange("(s p) d -> p s d", p=128), vb[s_].t[:], [vb[s_]], [], V_g[s_])
            if LV < 4:
                continue
            bw = tok_group(sub, 1280, 72)
            ts(P, "dve", zt[u].t[:], bw.t[:, 64:72], -IDX_SCALE, None, ALU.mult, ALU.bypass, [bw], [zt[u]])
            stt(P, absw[u].t[:], bw.t[:, 64:72], IDX_SCALE, zt[u].t[:], ALU.mult, ALU.max, [bw, zt[u]], [absw[u]])
            ts(P, "dve", sgn[s_].t[:, sub, :], bw.t[:, 64:72], 0.0, 2.0, ALU.is_ge, ALU.mult, [bw], [sgn[s_]])
            ts(P, "dve", sgn[s_].t[:, sub, :], sgn[s_].t[:, sub, :], -1.0, None, ALU.add, ALU.bypass, [sgn[s_]], [sgn[s_]])
            act(P, sqt[u].t[:, 0:64], bw.t[:, 0:64], AF.Square, [bw], [sqt[u], s1[u]], accum_out=s1[u].t[:])
            rstd_ops(cx, s1[u], 64, r1[u], NH, [])
            stt(P, ikn[u].t[:], bw.t[:, 0:64], r1[u].t[:], GI.t[:], ALU.mult, ALU.mult, [bw, r1[u], GI], [ikn[u]])
            b = tok_group(sub, 768, 512)
            tt(P, "dve", iqs[u].t[:].rearrange("p (h d) -> p h d", h=8), b.t[:].rearrange("p (h d) -> p h d", h=8),
               absw[u].t[:].unsqueeze(2).broadcast_to([128, 8, 64]), ALU.mult, [b, absw[u]], [iqs[u]])
            for r in range(4):
                tr(P, pi.t[:, r * 128:(r + 1) * 128], iqs[u].t[:, r * 128:(r + 1) * 128],
                   ident.t[:], [iqs[u], ident], [pi])
            tr(P, pi.t[0:64, 512:640], ikn[u].t[:], ident.t[:], [ikn[u], ident], [pi])
            cp(P, "act", IQTt[s_].t[:, :, ssl], pi.t[:, 0:512].rearrange("p (r t) -> p r t", r=4), [pi], [IQTt[s_]])
            cp(P, "dve", IKTt[s_].t[:, ssl], pi.t[0:64, 512:640], [pi], [IKTt[s_]])
            if sub == 1:
                dma(P, "sp", A["IQT"][:, :, tsl].rearrange("r p t -> p r t"), IQTt[s_].t[:], [IQTt[s_]], [], IQT_g[s_])
                dma(P, "sp", A["IKT"][:, tsl], IKTt[s_].t[:], [IKTt[s_]], [], IKT_g[s_])
                dma(P, "sp", A["SGN"][tsl, :].rearrange("(s p) h -> p s h", p=128), sgn[s_].t[:], [sgn[s_]], [], SGN_g[s_])
            if LV < 5:
                continue
            b = tok_group(sub, 1864, 512)
            cp(P, "dve", mvb[s_].t[:, sub, :], b.t[:], [b], [mvb[s_]])
            b = tok_group(sub, 2376, 8)
            tt(P, "dve", zt[u].t[:], b.t[:, 0:8], BIF.t[:], ALU.add, [b, BIF], [zt[u]])
            cp(P, "pool", lif[s_].t[:, sub, 0:4], zt[u].t[:, 0:4], [zt[u]], [lif[s_]])
            act(P, et[u].t[:], zt[u].t[:, 4:8], AF.Exp, [zt[u]], [et[u]], scale=-1.0)
            act(P, et[u].t[:], et[u].t[:], AF.Ln, [et[u]], [et[u]], bias=1.0)
            ts(P, "dve", lif[s_].t[:, sub, 4:8], et[u].t[:], -1.0, None, ALU.mult, ALU.bypass, [et[u]], [lif[s_]])
            b = tok_group(sub, 2384, 512)
            act(P, mob[s_].t[:, sub, :], b.t[:], AF.Sigmoid, [b], [mob[s_]])
        if LV < 6:
            return
        dma(P, "sp", A["MV"][tsl, :].rearrange("(s p) d -> p s d", p=128), mvb[s_].t[:], [mvb[s_]], [], MV_g[s_])
        dma(P, "sp", A["LIF"][tsl, :].rearrange("(s p) d -> p s d", p=128), lif[s_].t[:], [lif[s_]], [], LIF_g[s_])
        dma(P, "sp", A["MO"][tsl, :].rearrange("(s p) d -> p s d", p=128), mob[s_].t[:], [mob[s_]], [], MO_g[s_])

        def feat_pair(c0):
            b = bank()
            for c in range(2):
                for kc in range(8):
                    mm(P, b.t[:, c * TT:(c + 1) * TT], WIN.t[:, kc, c0 + c * 128:c0 + (c + 1) * 128], H.t[:, kc, :],
                       kc == 0, kc == 7, [H, WIN], [b])
            return b

        if LV < 7:
            return
        for (base, dst) in ((2896, gat[s_]), (3920, gbt[s_])):
            for pr in range(4):
                b = feat_pair(base + pr * 256)
                act(P, dst.t[:, 2 * pr:2 * pr + 2, :], b.t[:].rearrange("p (c t) -> p c t", c=2), AF.Sigmoid, [b], [dst])
        dma(P, "sp", A["GAT"][:, :, tsl].rearrange("c p t -> p c t"), gat[s_].t[:], [gat[s_]], [], GAT_g[s_])
        dma(P, "sp", A["GBT"][:, :, tsl].rearrange("c p t -> p c t"), gbt[s_].t[:], [gbt[s_]], [], GBT_g[s_])
        if LV < 8:
            return
        xm, xn = XM[s_], XM[1 - s_]
        for pr in range(2):
            b = feat_pair(1352 + pr * 256)
            cp(P, "dve", xm.t[:, 2 * pr:2 * pr + 2, 3:3 + TT], b.t[:].rearrange("p (c t) -> p c t", c=2), [b], [xm])
        cp(P, "pool", xn.t[:, :, 0:3], xm.t[:, :, TT:TT + 3], [xm], [xn])
        ac = acc[s_]
        for c in range(4):
            ts(P, "dve", ac.t[:, c, :], xm.t[:, c, 0:TT], CW.t[:, c, 0:1], CB.t[:, c:c + 1], ALU.mult, ALU.add,
               [xm, CW, CB], [ac])
            for j in range(1, 4):
                stt(P, ac.t[:, c, :], xm.t[:, c, j:j + TT], CW.t[:, c, j:j + 1], ac.t[:, c, :], ALU.mult, ALU.add,
                    [xm, CW, ac], [ac])
        act(P, sig[s_].t[:], ac.t[:], AF.Sigmoid, [ac], [sig[s_]])
        tt(P, "pool", xct[s_].t[:], ac.t[:], sig[s_].t[:], ALU.mult, [ac, sig[s_]], [xct[s_]])
        if LV < 9:
            return
        xc = xct[s_]
        for (wi, dst, scale, name, grp) in ((0, mqt[s_], MQ_SCALE, "MQT", MQT_g[s_]), (1, mkt[s_], 1.0, "MKT", MKT_g[s_])):
            for hp in range(2):
                b = bank()
                for c in range(2):
                    h = hp * 2 + c
                    mm(P, b.t[0:64, c * TT:(c + 1) * TT], WM.t[:, wi, h, :], xc.t[:, h, :], True, True, [WM, xc], [b])
                act(P, dst.t[:, 2 * hp:2 * hp + 2, :], b.t[0:64, :].rearrange("p (c t) -> p c t", c=2), AF.Copy,
                    [b], [dst], scale=scale)
            dma(P, "sp", A[name][:, :, tsl].rearrange("h d t -> d h t"), dst.t[:], [dst], [], grp)
        for sub in range(2):
            b = bank()
            for h in range(4):
                mm(P, b.t[:, h * 64:(h + 1) * 64], xc.t[:, h, sub * 128:(sub + 1) * 128], WM.t[:, 1, h, :], True, True,
                   [xc, WM], [b])
            cp(P, "dve", mkb[s_].t[:, sub, :], b.t[:, 0:256], [b], [mkb[s_]])
        dma(P, "sp", A["MK"][tsl, :].rearrange("(s p) d -> p s d", p=128), mkb[s_].t[:], [mkb[s_]], [], MK_g[s_])

    load(0)
    if NT > 1:
        load(1)
    for i in range(NT):
        tile(i)
        if i + 2 < NT:
            load(i + 2)
    fin = QT_g + KT_g + IQT_g + IKT_g + V_g + SGN_g + MV_g + LIF_g + MO_g + GAT_g + GBT_g + MQT_g + MKT_g + MK_g
    return {"sp": fin}


NIT = 14


def dsa_phase(cx, S, A, Cd):
    nc, P = cx.nc, cx.P
    NB = S // 512
    NJ = S // 128
    cg = cx.group(final=True)
    ident = load_ident(cx, Cd, cg)
    IKbs = [cx.sb(f"IKb{i}", [128, 512], BF16) for i in range(3)]
    IK_g = [cx.group() for _ in range(3)]
    KT = cx.sb("KT", [128, S], BF16)
    dma(P, "sp", KT.t[:], A["KT"], [], [KT], cg)
    VA = cx.sb("VA", [128, NJ, 2, 65], BF16)
    mset(P, "pool", VA.t[:], 1.0, [VA])
    for g_ in range(2):
        dma(P, "sp", VA.t[:, :, g_, 0:64], A["V"][:, g_ * 64:(g_ + 1) * 64].rearrange("(j p) d -> p j d", p=128),
            [], [VA], cg)
    CM = cx.sb("CM", [128, 4, 512], BF16)
    dma(P, "sp", CM.t[:], Cd["c_cmask"].rearrange("i p s -> p i s"), [], [CM], cg)
    CROW = cx.sb("CROW", [128, NIT], F32)
    dma(P, "sp", CROW.t[:], Cd["c_steps"], [], [CROW], cg)
    ONES = cx.sb("ONES", [128, 64], F32)
    mset(P, "pool", ONES.t[:], 1.0, [ONES])

    SCs = [cx.sb(f"SC{i}", [128, S], F32) for i in range(2)]
    MK = cx.sb("MK", [128, S], BF16)
    MT = cx.sb("MT", [128, NJ, 512], BF16)
    IQb = cx.sb("IQb", [128, 4, 512], BF16)
    Qb = cx.sb("Qb", [128, 4, 512], BF16)
    SGb = cx.sb("SGb", [128, 4, 8], F32)
    ld_g1, ld_g2, ld_g3 = cx.group(), cx.group(), cx.group()
    DG = [cx.sb("DG0", [128, 8, 128], BF16)]
    Rt = [cx.sb(f"Rt{i}", [128, 512], BF16) for i in range(3)]
    Et = [cx.sb(f"Et{i}", [128, 512], BF16) for i in range(2)]
    Pt = [cx.sb(f"Pt{i}", [128, 512], BF16) for i in range(3)]
    M1 = cx.sb("M1", [128, 1], F32)
    LO = cx.sb("LO", [128, 1], F32)
    W0 = cx.sb("W0", [128, 1], F32)
    STEP = cx.sb("STEP", [128, NIT], F32)
    NSTEP = cx.sb("NSTEP", [128, NIT], F32)
    STEP2 = cx.sb("STEP2", [128, NIT], F32)
    MID = cx.sb("MID", [128, 1], F32)
    CNT = cx.sb("CNT", [128, 1], F32)
    DT = cx.sb("DT", [128, 1], F32)
    RC = cx.sb("RC", [128, 512], F32)
    OS = cx.sb("OS", [64, 512], F32)
    YT = [cx.sb("YT0", [64, 512], BF16)] * 2
    YT_g = [cx.group() for _ in range(2)]
    pW = [cx.ps(f"pW{i}", [128, 512], F32) for i in range(4)]
    SCp = cx.ps("SCp", [128, 512], F32)
    pTr = cx.ps("pTr", [128, 1024], BF16)
    pOs = [cx.ps(f"pO{i}", [128, 512], F32) for i in range(2)]
    cnt = {"ik": 0, "l": 0, "r": 0, "s": 0, "e": 0, "p": 0, "y": 0, "d": 0, "m": 0}

    def nxt(key, lst):
        v = lst[cnt[key] % len(lst)]
        cnt[key] += 1
        return v

    tiles = [(b, ii) for b in range(NB) for ii in range(4)]
    dgs = {}

    def indexer(n):
        b, ii = tiles[n]
        bs = slice(b * 512, (b + 1) * 512)
        SC = SCs[n % 2]
        if ii == 0:
            dma(P, "sp", IQb.t[:], A["IQT"][:, :, bs].rearrange("r p t -> p r t"), [], [IQb], ld_g1)
            dma(P, "sp", SGb.t[:], A["SGN"][bs, :].rearrange("(i p) h -> p i h", p=128), [], [SGb], ld_g3)
        tq = slice(ii * 128, (ii + 1) * 128)
        dg = nxt("d", DG)
        for h in range(8):
            ts(P, "pool", dg.t[:, h, :], ident.t[:], SGb.t[:, ii, h:h + 1], 1.0, ALU.mult, ALU.mult,
               [ident, SGb], [dg])
        items = [(kb, h) for kb in range(b + 1) for h in range(8)]
        LA = 2
        hold = {}

        def front(it):
            kb, h = it
            ks = slice(kb * 512, (kb + 1) * 512)
            if h == 0:
                ikq = cnt["ik"] % 3
                cnt["ik"] += 1
                IKb = IKbs[ikq]
                dma(P, "sp", IKb.t[0:64, :], A["IKT"][:, ks], [], [IKb], IK_g[ikq])
                dma(P, "sp", IKb.t[64:128, :], A["IKT"][:, ks], [], [IKb], IK_g[ikq])
                hold[("ik", kb)] = IKb
            IKb = hold[("ik", kb)]
            pr, half = h // 2, h % 2
            ps_ = slice(half * 64, half * 64 + 64)
            L = nxt("l", pW)
            mm(P, L.t[:], IQb.t[ps_, pr, tq], IKb.t[ps_, :], True, True, [IQb, IKb], [L])
            R = nxt("r", Rt)
            act(P, R.t[:], L.t[:], AF.Relu, [L], [R])
            hold[it] = R

        def back(it):
            kb, h = it
            ks = slice(kb * 512, (kb + 1) * 512)
            R = hold.pop(it)
            mm(P, SCp.t[:], dg.t[:, h, :], R.t[:], h == 0, h == 7, [dg, R], [SCp])
            if h == 7:
                cp(P, "act", SC.t[:, ks], SCp.t[:], [SCp], [SC])

        for idx in range(len(items) + LA):
            if idx < len(items):
                front(items[idx])
            if idx - LA >= 0:
                back(items[idx - LA])

    def post(n):
        b, ii = tiles[n]
        Nb = 512 * (b + 1)
        njb = 4 * (b + 1)
        bs = slice(b * 512, (b + 1) * 512)
        tq = slice(ii * 128, (ii + 1) * 128)
        SC = SCs[n % 2]
        red(P, M1.t[:], SC.t[:, 0:Nb], ALU.max, [SC], [M1], apply_absolute_value=True)
        tt(P, "pool", SC.t[:, bs], SC.t[:, bs], CM.t[:, ii, :], ALU.add, [SC, CM], [SC])
        ts(P, "dve", W0.t[:], M1.t[:], 2.002, 2e-6, ALU.mult, ALU.add, [M1], [W0])
        ts(P, "dve", STEP.t[:], CROW.t[:], W0.t[:], None, ALU.mult, ALU.bypass, [CROW, W0], [STEP])
        ts(P, "dve", NSTEP.t[:], STEP.t[:], -1.0, None, ALU.mult, ALU.bypass, [STEP], [NSTEP])
        ts(P, "dve", STEP2.t[:], STEP.t[:], 2.0, None, ALU.mult, ALU.bypass, [STEP], [STEP2])
        mset(P, "dve", MID.t[:], 0.0, [MID])
        for k in range(NIT):
            ts(P, "dve", MK.t[:, 0:Nb], SC.t[:, 0:Nb], MID.t[:], None, ALU.is_ge, ALU.add, [SC, MID], [MK, CNT],
               accum=CNT.t[:])
            if k + 1 < NIT:
                stt(P, DT.t[:], CNT.t[:], 255.5, STEP2.t[:, k + 1:k + 2], ALU.is_ge, ALU.mult, [CNT, STEP2], [DT])
                stt(P, MID.t[:], DT.t[:], NSTEP.t[:, k + 1:k + 2], MID.t[:], ALU.add, ALU.add, [DT, NSTEP, MID], [MID])
            else:
                stt(P, DT.t[:], CNT.t[:], 255.5, STEP.t[:, k:k + 1], ALU.is_lt, ALU.mult, [CNT, STEP], [DT])
                tt(P, "dve", LO.t[:], MID.t[:], DT.t[:], ALU.subtract, [MID, DT], [LO])
        ts(P, "dve", MK.t[:, 0:Nb], SC.t[:, 0:Nb], LO.t[:], None, ALU.is_ge, ALU.bypass, [SC, LO], [MK])
        for j0 in range(0, njb, 8):
            nn = min(8, njb - j0)
            for jj in range(nn):
                j = j0 + jj
                tr(P, pTr.t[:, jj * 128:(jj + 1) * 128], MK.t[:, j * 128:(j + 1) * 128], ident.t[:], [MK, ident], [pTr])
            cp(P, "act" if (cnt["m"] % 2 == 0) else "dve", MT.t[:, j0:j0 + nn, tq],
               pTr.t[:, 0:nn * 128].rearrange("p (j t) -> p j t", t=128), [pTr], [MT])
            cnt["m"] += 1

    def attention(b):
        njb = 4 * (b + 1)
        bs = slice(b * 512, (b + 1) * 512)
        dma(P, "sp", Qb.t[:], A["QT"][:, :, bs].rearrange("r p t -> p r t"), [], [Qb], ld_g2)
        items = [(h, j) for h in range(8) for j in range(njb)]
        LA = 2
        hold = {}

        def front(it):
            h, j = it
            g, r = h // 4, h % 4
            ps_ = slice(g * 64, g * 64 + 64)
            ST = nxt("l", pW)
            mm(P, ST.t[:], KT.t[ps_, j * 128:(j + 1) * 128], Qb.t[ps_, r, :], True, True, [KT, Qb], [ST])
            E = nxt("e", Et)
            act(P, E.t[:], ST.t[:], AF.Exp, [ST], [E])
            PT = nxt("p", Pt)
            tt(P, "pool" if (cnt["p"] % 4 == 0) else "dve", PT.t[:], E.t[:], MT.t[:, j, :], ALU.mult, [E, MT], [PT])
            hold[it] = PT

        def back(it):
            h, j = it
            g, r = h // 4, h % 4
            pO = pOs[h % 2]
            PT = hold.pop(it)
            mm(P, pO.t[0:65, :], VA.t[:, j, g, :], PT.t[:], j == 0, j == njb - 1, [VA, PT], [pO])
            if j == njb - 1:
                P.op("dve", lambda e: e.reciprocal(out=RC.t[64:65, :], in_=pO.t[64:65, :]), [pO.b], [RC.b])
                BC = nxt("l", pW)
                mm(P, BC.t[0:64, :], ONES.t[64:65, 0:64], RC.t[64:65, :], True, True, [ONES, RC], [BC])
                cp(P, "act", OS.t[:], pO.t[0:64, :], [pO], [OS])
                y = cnt["y"] % 2
                cnt["y"] += 1
                tt(P, "dve", YT[y].t[:], OS.t[:], BC.t[0:64, :], ALU.mult, [OS, BC], [YT[y]])
                dma(P, "sp", A["YAT"][h, :, bs], YT[y].t[:], [YT[y]], [], YT_g[y])

        for idx in range(len(items) + LA):
            if idx < len(items):
                front(items[idx])
            if idx - LA >= 0:
                back(items[idx - LA])

    indexer(0)
    for n in range(len(tiles)):
        if n + 1 < len(tiles):
            indexer(n + 1)
        post(n)
        if tiles[n][1] == 3:
            attention(tiles[n][0])
    return {"sp": YT_g}


def mlstm_phase(cx, S, A, Wd, Cd):
    nc, P = cx.nc, cx.P
    NT = S // 128
    cg = cx.group(final=True)
    ident = load_ident(cx, Cd, cg)
    T2 = cx.sb("T2", [128, 128], F32)
    dma(P, "sp", T2.t[:], Cd["c_tri2"], [], [T2], cg)
    NEGM = cx.sb("NEGM", [128, 512], BF16)
    dma(P, "sp", NEGM.t[:], Cd["c_negm"], [], [NEGM], cg)
    HG = cx.sb("HG", [128, 512], F32)
    dma(P, "sp", HG.t[:], Wd["m_head_norm"].partition_broadcast(128), [], [HG], cg)
    NH = cx.sb("NH", [128, 8], F32)
    mset(P, "pool", NH.t[:], -0.5, [NH])
    CN = [cx.sb(f"CN{i}", [64, 4, 129], F32) for i in range(3)]
    mset(P, "pool", CN[0].t[:], 0.0, [CN[0]])
    R2 = range(2)
    LIF = [cx.sb(f"LIF{i}", [128, 8], F32) for i in R2]
    MQ = [cx.sb(f"MQ{i}", [64, 4, 128], BF16) for i in R2]
    MKt = [cx.sb(f"MKt{i}", [64, 4, 128], BF16) for i in R2]
    MKb = [cx.sb(f"MKb{i}", [128, 256], BF16) for i in R2]
    VA = [cx.sb(f"VA{i}", [128, 4, 129], BF16) for i in R2]
    MOt = [cx.sb(f"MOt{i}", [128, 512], BF16) for i in R2]
    lg = [[cx.group() for _ in range(6)] for _ in R2]
    for i in R2:
        mset(P, "pool", VA[i].t[:], 1.0, [VA[i]])
    LFm = cx.sb("LFm", [128, 4, 128], F32)
    bias = cx.sb("bias", [128, 4], F32)
    AT = cx.sb("AT", [128, 4, 128], F32)
    EB = cx.sb("EB", [128, 4, 128], F32)
    BL = cx.sb("BL", [128, 4], F32)
    WS = cx.sb("WS", [128, 4], F32)
    WT = cx.sb("WT", [128, 4, 128], BF16)
    QP = cx.sb("QP", [64, 4, 128], F32)
    KW = cx.sb("KW", [128, 4, 64], BF16)
    dn = cx.sb("dn", [128, 4], F32)
    rec = cx.sb("rec", [128, 4], F32)
    hh = cx.sb("hh", [128, 4, 128], F32)
    junk = cx.sb("junk", [128, 128], BF16)
    ssq = cx.sb("ssq", [128, 4], F32)
    rs4 = cx.sb("rs4", [128, 4], F32)
    yb = cx.sb("yb", [128, 512], BF16)
    YBt = [cx.sb(f"YBt{i}", [128, 4, 512], BF16) for i in R2]
    YB_g = [cx.group() for _ in R2]
    pX = cx.ps("pX", [128, 512], F32)
    pY = cx.ps("pY", [128, 512], F32)
    pZ = cx.ps("pZ", [128, 512], F32)
    pQ = cx.ps("pQ", [128, 512], F32)
    pD = cx.ps("pD", [128, 512], F32)
    pN = [cx.ps(f"pN{i}", [128, 512], F32) for i in R2]
    pTr = cx.ps("pTr", [128, 1024], BF16)

    def load(i):
        s_ = i % 2
        tsl = slice(i * 128, (i + 1) * 128)
        g = lg[s_]
        dma(P, "sp", LIF[s_].t[:], A["LIF"][tsl, :], [], [LIF[s_]], g[0])
        dma(P, "sp", MQ[s_].t[:], A["MQT"][:, :, tsl].rearrange("h d t -> d h t"), [], [MQ[s_]], g[1])
        dma(P, "sp", MKt[s_].t[:], A["MKT"][:, :, tsl].rearrange("h d t -> d h t"), [], [MKt[s_]], g[2])
        dma(P, "sp", MKb[s_].t[:], A["MK"][tsl, :], [], [MKb[s_]], g[3])
        dma(P, "sp", VA[s_].t[:, :, 0:128], A["MV"][tsl, :].rearrange("p (h d) -> p h d", h=4), [], [VA[s_]], g[4])
        dma(P, "sp", MOt[s_].t[:], A["MO"][tsl, :], [], [MOt[s_]], g[5])

    def tile(i):
        s_ = i % 2
        lif, mq, mkt, mkb, va, mo = LIF[s_], MQ[s_], MKt[s_], MKb[s_], VA[s_], MOt[s_]
        import os
        ML = int(os.environ.get('ML', '99'))
        cc, cm, cn = CN[(2 * i) % 3], CN[(2 * i + 1) % 3], CN[(2 * i + 2) % 3]
        cp(P, "pool", LFm.t[:], lif.t[:, 4:8].unsqueeze(2).broadcast_to([128, 4, 128]), [lif], [LFm])
        for h in range(4):
            mm(P, pX.t[:, h * 128:(h + 1) * 128], LFm.t[:, h, :], T2.t[:], True, True, [LFm, T2], [pX])
        mm(P, pY.t[:], ident.t[:], NEGM.t[:], True, False, [ident, NEGM], [pY])
        for h in range(4):
            mm(P, pY.t[:, h * 128:(h + 1) * 128], LFm.t[:, h, :], T2.t[:], False, h == 3, [LFm, T2], [pY])
        mm(P, pZ.t[:, 0:4], T2.t[:], lif.t[:, 4:8], True, True, [T2, lif], [pZ])
        tt(P, "dve", bias.t[:], lif.t[:, 0:4], pZ.t[:, 0:4], ALU.subtract, [lif, pZ], [bias])
        for h in range(4):
            act(P, AT.t[:, h, :], pY.t[:, h * 128:(h + 1) * 128], AF.Exp, [pY, bias], [AT], bias=bias.t[:, h:h + 1])
        act(P, EB.t[:], pX.t[:].rearrange("p (h t) -> p h t", h=4), AF.Exp, [pX], [EB])
        if ML < 2:
            return
        pXv = pX.t[:].rearrange("p (h t) -> p h t", h=4)
        cp(P, "dve", BL.t[0:64, :], pXv[0:64, :, 63], [pX], [BL])
        cp(P, "dve", BL.t[64:128, :], pXv[64:128, :, 127], [pX], [BL])
        tt(P, "dve", BL.t[:], BL.t[:], bias.t[:], ALU.add, [BL, bias], [BL])
        act(P, WS.t[:], BL.t[:], AF.Exp, [BL], [WS])
        for h in range(4):
            mm(P, pQ.t[:, h * 128:(h + 1) * 128], mkt.t[:, h, :], mq.t[:, h, :], True, True, [mkt, mq], [pQ])
        tt(P, "dve", WT.t[:], pQ.t[:].rearrange("p (h t) -> p h t", h=4), AT.t[:], ALU.mult, [pQ, AT], [WT])
        tt(P, "pool", QP.t[:], mq.t[:], EB.t[0:64, :, :], ALU.mult, [mq, EB], [QP])
        tt(P, "pool", KW.t[:], mkb.t[:].rearrange("p (h d) -> p h d", h=4),
           WS.t[:].unsqueeze(2).broadcast_to([128, 4, 64]), ALU.mult, [mkb, WS], [KW])
        if ML < 3:
            return
        for (rows, src, dst, col) in ((slice(0, 64), cc, cm, 63), (slice(64, 128), cm, cn, 127)):
            for hp in range(2):
                for c in range(2):
                    h = 2 * hp + c
                    mm(P, pD.t[0:64, c * 129:(c + 1) * 129], KW.t[rows, h, :], va.t[rows, h, :], True, True,
                       [KW, va], [pD])
                for c in range(2):
                    h = 2 * hp + c
                    stt(P, dst.t[:, h, :], src.t[:, h, :], EB.t[0:64, h, col:col + 1], pD.t[0:64, c * 129:(c + 1) * 129],
                        ALU.mult, ALU.add, [src, EB, pD], [dst])
        if ML < 4:
            return
        for h in range(4):
            bk = pN[h // 2]
            osl = slice((h % 2) * 129, (h % 2) * 129 + 129)
            mm(P, bk.t[:, osl], WT.t[:, h, :], va.t[:, h, :], True, False, [WT, va], [bk])
            mm(P, bk.t[0:64, osl], QP.t[:, h, 0:64], cc.t[:, h, :], False, False, [QP, cc], [bk])
            mm(P, bk.t[64:128, osl], QP.t[:, h, 64:128], cm.t[:, h, :], False, True, [QP, cm], [bk])
        if ML < 5:
            return
        for q in range(2):
            act(P, dn.t[:, 2 * q:2 * q + 2], pN[q].t[:, 0:258].rearrange("p (h c) -> p h c", c=129)[:, :, 128],
                AF.Abs, [pN[q]], [dn])
        ts(P, "dve", dn.t[:], dn.t[:], 1.0, None, ALU.max, ALU.bypass, [dn], [dn])
        P.op("dve", lambda e: e.reciprocal(out=rec.t[:], in_=dn.t[:]), [dn.b], [rec.b])
        for h in range(4):
            o0 = (h % 2) * 129
            act(P, hh.t[:, h, :], pN[h // 2].t[:, o0:o0 + 128], AF.Copy, [pN[h // 2], rec], [hh], scale=rec.t[:, h:h + 1])
        if ML < 6:
            return
        for h in range(4):
            act(P, junk.t[:], hh.t[:, h, :], AF.Square, [hh], [junk, ssq], accum_out=ssq.t[:, h:h + 1])
        rstd_ops(cx, ssq, 128, rs4, NH, [])
        if ML < 7:
            return
        tt(P, "dve", hh.t[:], hh.t[:], rs4.t[:].unsqueeze(2).broadcast_to([128, 4, 128]), ALU.mult, [hh, rs4], [hh])
        hf = hh.t[:].rearrange("p h d -> p (h d)")
        tt(P, "pool", hf, hf, HG.t[:], ALU.mult, [hh, HG], [hh])
        tt(P, "dve", yb.t[:], hf, mo.t[:], ALU.mult, [hh, mo], [yb])
        if ML < 8:
            return
        for c in range(4):
            tr(P, pTr.t[:, c * 128:(c + 1) * 128], yb.t[:, c * 128:(c + 1) * 128], ident.t[:], [yb, ident], [pTr])
        yt = YBt[(i // 4) % 2]
        cp(P, "act", yt.t[:, :, (i % 4) * 128:(i % 4 + 1) * 128], pTr.t[:, 0:512].rearrange("p (c t) -> p c t", c=4),
           [pTr], [yt])
        if i % 4 == 3:
            b0 = (i // 4) * 512
            dma(P, "sp", A["YBT"][:, :, b0:b0 + 512].rearrange("c p t -> p c t"), yt.t[:], [yt], [], YB_g[(i // 4) % 2])

    load(0)
    for i in range(NT):
        if i + 1 < NT:
            load(i + 1)
        tile(i)
    return {"sp": YB_g}


def merge_phase(cx, S, X1, X2, A, Wd, Cd):
    nc, P = cx.nc, cx.P
    TT = 256
    NT = S // TT
    WA = cx.sb("WA", [128, 4, D], BF16)
    WB = cx.sb("WB", [128, 4, D], BF16)
    WO = cx.sb("WO", [128, 8, D], BF16)
    stg = [cx.sb(f"stg{i}", [128, D], F32) for i in range(2)]
    stg_g = [cx.group() for _ in range(2)]
    k = 0
    for (src, dst, n) in ((Wd["w_proj_a"], WA, 4), (Wd["w_proj_b"], WB, 4), (Wd["w_out"], WO, 8)):
        for c in range(n):
            s_ = k % 2
            dma(P, "sp", stg[s_].t[:], src[c * 128:(c + 1) * 128, :], [], [stg[s_]], stg_g[s_])
            cp(P, ["act", "dve", "pool"][k % 3], dst.t[:, c, :], stg[s_].t[:], [stg[s_]], [dst])
            k += 1
    R2 = range(2)
    YA = [cx.sb(f"YA{i}", [128, 4, TT], BF16) for i in R2]
    YB = [cx.sb(f"YB{i}", [128, 4, TT], BF16) for i in R2]
    GA = [cx.sb(f"GA{i}", [128, 8, TT], BF16) for i in R2]
    GB = [cx.sb(f"GB{i}", [128, 8, TT], BF16) for i in R2]
    xt = [cx.sb(f"xt{i}", [128, 2, D], F32) for i in R2]
    lg = [[cx.group() for _ in range(5)] for _ in R2]
    st_g = [cx.group() for _ in R2]
    m1 = [cx.sb(f"m1{i}", [128, 512], F32) for i in R2]
    m2 = [cx.sb(f"m2{i}", [128, 512], F32) for i in R2]
    MG = [cx.sb(f"MG{i}", [128, 8, TT], BF16) for i in R2]
    pA = [cx.ps(f"pA{i}", [128, 512], F32) for i in R2]
    pB = [cx.ps(f"pB{i}", [128, 512], F32) for i in R2]
    pO = [cx.ps(f"pO{i}", [128, 512], F32) for i in R2]

    def load(i):
        s_ = i % 2
        tsl = slice(i * TT, (i + 1) * TT)
        g = lg[s_]
        dma(P, "sp", YA[s_].t[:], A["YAT"][:, :, tsl].rearrange("(c w) d t -> (w d) c t", w=2), [], [YA[s_]], g[0])
        dma(P, "sp", YB[s_].t[:], A["YBT"][:, :, tsl].rearrange("c p t -> p c t"), [], [YB[s_]], g[1])
        dma(P, "sp", GA[s_].t[:], A["GAT"][:, :, tsl].rearrange("c p t -> p c t"), [], [GA[s_]], g[2])
        dma(P, "sp", GB[s_].t[:], A["GBT"][:, :, tsl].rearrange("c p t -> p c t"), [], [GB[s_]], g[3])
        dma(P, "sp", xt[s_].t[:], X1[tsl, :].rearrange("(s p) d -> p s d", p=128), [], [xt[s_]], g[4])

    def tile(i):
        s_ = i % 2
        tsl = slice(i * TT, (i + 1) * TT)
        for dcp in range(4):
            q = dcp % 2
            for (W_, Y_, bk) in ((WA, YA[s_], pA[q]), (WB, YB[s_], pB[q])):
                for c2 in range(2):
                    dc = 2 * dcp + c2
                    for c in range(4):
                        mm(P, bk.t[:, c2 * TT:(c2 + 1) * TT], W_.t[:, c, dc * 128:(dc + 1) * 128], Y_.t[:, c, :],
                           c == 0, c == 3, [W_, Y_], [bk])
            gsl = slice(2 * dcp, 2 * dcp + 2)
            tt(P, "dve", m1[q].t[:].rearrange("p (c t) -> p c t", c=2), pA[q].t[:].rearrange("p (c t) -> p c t", c=2),
               GA[s_].t[:, gsl, :], ALU.mult, [pA[q], GA[s_]], [m1[q]])
            tt(P, "dve", m2[q].t[:].rearrange("p (c t) -> p c t", c=2), pB[q].t[:].rearrange("p (c t) -> p c t", c=2),
               GB[s_].t[:, gsl, :], ALU.mult, [pB[q], GB[s_]], [m2[q]])
            tt(P, "pool", MG[s_].t[:, gsl, :], m1[q].t[:].rearrange("p (c t) -> p c t", c=2),
               m2[q].t[:].rearrange("p (c t) -> p c t", c=2), ALU.add, [m1[q], m2[q]], [MG[s_]])
        n = 0
        for sub in range(2):
            for dh in range(2):
                bk = pO[n % 2]
                n += 1
                for dc in range(8):
                    mm(P, bk.t[:], MG[s_].t[:, dc, sub * 128:(sub + 1) * 128], WO.t[:, dc, dh * 512:(dh + 1) * 512],
                       dc == 0, dc == 7, [MG[s_], WO], [bk])
                xs = xt[s_].t[:, sub, dh * 512:(dh + 1) * 512]
                tt(P, "dve", xs, bk.t[:], xs, ALU.add, [bk, xt[s_]], [xt[s_]])
        dma(P, "pool", X2[tsl, :].rearrange("(s p) d -> p s d", p=128), xt[s_].t[:], [xt[s_]], [], st_g[s_])

    load(0)
    if NT > 1:
        load(1)
    for i in range(NT):
        tile(i)
        if i + 2 < NT:
            load(i + 2)
    return {"pool": st_g}


WSPEC = {
    "ffn1_norm": [1, D], "ffn1_w_gate": [D, DFF], "ffn1_w_up": [D, DFF], "ffn1_w_down": [DFF, D],
    "mix_norm": [1, D], "w_in": [D, DIN], "q_norm": [1, 64], "k_norm": [1, 64], "idx_k_norm": [1, 64],
    "conv_w": [128, 16], "conv_b": [128, 4], "w_mq": [128, 256], "w_mk": [128, 256], "b_i": [1, 4],
    "b_f": [1, 4], "m_head_norm": [1, 512], "w_proj_a": [512, D], "w_proj_b": [512, D], "w_out": [D, D],
    "ffn2_norm": [1, D], "ffn2_w_gate": [D, DFF], "ffn2_w_up": [D, DFF], "ffn2_w_down": [DFF, D],
}


def consts():
    c = {}
    c["c_ident"] = np.eye(128, dtype=np.float32).astype(ml_dtypes.bfloat16)
    cm = np.zeros((4, 128, 512), np.float32)
    for ii in range(4):
        for p in range(128):
            lim = ((ii * 128 + p) // 64 + 1) * 64
            cm[ii, p, lim:] = -1e30
    c["c_cmask"] = cm.astype(ml_dtypes.bfloat16)
    c["c_steps"] = np.tile((2.0 ** -(np.arange(NIT) + 1.0)).astype(np.float32)[None, :], (128, 1))
    j = np.arange(128)
    t2 = ((j[:, None] // 64 == j[None, :] // 64) & (j[:, None] <= j[None, :]))
    c["c_tri2"] = t2.astype(np.float32)
    c["c_negm"] = np.tile(np.where(t2, 0.0, NEG).astype(np.float32), (1, 4)).astype(ml_dtypes.bfloat16)
    return c


CSPEC = {"c_ident": ([128, 128], BF16), "c_cmask": ([4, 128, 512], BF16), "c_steps": ([128, NIT], F32),
         "c_tri2": ([128, 128], F32), "c_negm": ([128, 512], BF16)}


def build_nc(S, debug=False, upto=9):
    nc = bass.Bass("TRN2", target_bir_lowering=False)

    def din(name, shape, dt=F32):
        return nc.dram_tensor(name, list(shape), dt, kind="ExternalInput").ap()

    x = din("x", [S, D])
    Wd = {k: din(k, v) for k, v in WSPEC.items()}
    Cd = {k: din(k, v[0], v[1]) for k, v in CSPEC.items()}
    out = nc.dram_tensor("out", [S, D], F32, kind="ExternalOutput").ap()
    kind = "ExternalOutput" if debug else "Internal"

    def scr(name, shape, dt):
        return nc.dram_tensor(name, list(shape), dt, kind=kind).ap()

    X1 = scr("X1", [S, D], F32)
    X2 = scr("X2", [S, D], F32)
    A = {
        "QT": scr("QT", [4, 128, S], BF16), "KT": scr("KT", [128, S], BF16), "V": scr("V", [S, 128], BF16),
        "IQT": scr("IQT", [4, 128, S], BF16), "IKT": scr("IKT", [64, S], BF16), "SGN": scr("SGN", [S, 8], F32),
        "MV": scr("MV", [S, 512], BF16), "LIF": scr("LIF", [S, 8], F32), "MO": scr("MO", [S, 512], BF16),
        "GAT": scr("GAT", [8, 128, S], BF16), "GBT": scr("GBT", [8, 128, S], BF16),
        "MQT": scr("MQT", [4, 64, S], BF16), "MKT": scr("MKT", [4, 64, S], BF16), "MK": scr("MK", [S, 256], BF16),
        "YAT": scr("YAT", [8, 64, S], BF16), "YBT": scr("YBT", [4, 128, S], BF16),
    }
    ph = [
        ("f1", lambda cx: ffn_phase(cx, S, x, X1, Wd["ffn1_norm"], Wd["ffn1_w_gate"], Wd["ffn1_w_up"],
                                    Wd["ffn1_w_down"], Cd["c_ident"])),
        ("pj", lambda cx: proj_phase(cx, S, X1, A, Wd, Cd)),
        ("ds", lambda cx: dsa_phase(cx, S, A, Cd)),
        ("ml", lambda cx: mlstm_phase(cx, S, A, Wd, Cd)),
        ("mg", lambda cx: merge_phase(cx, S, X1, X2, A, Wd, Cd)),
        ("f2", lambda cx: ffn_phase(cx, S, X2, out, Wd["ffn2_norm"], Wd["ffn2_w_gate"], Wd["ffn2_w_up"],
                                    Wd["ffn2_w_down"], Cd["c_ident"])),
    ]
    import os
    lo_ = int(os.environ.get('FROM', '0'))
    for n, (tag, fn) in enumerate(ph):
        if lo_ <= n < upto:
            run_phase(nc, tag, fn, debug)
    return nc


def layout_weights(inputs):
    sh = {}
    for k, v in WSPEC.items():
        a = np.asarray(inputs[k], dtype=np.float32)[0]
        if k == "conv_w":
            a = a.reshape(4, 4, 128).transpose(2, 1, 0)
        elif k == "conv_b":
            a = a.reshape(4, 128).T
        elif k in ("w_mq", "w_mk"):
            a = a.transpose(1, 0, 2)
        sh[k] = np.ascontiguousarray(a).reshape(v)
    return sh


_NC_CACHE = {}


def kernel(**inputs):
    x = np.ascontiguousarray(np.asarray(inputs["x"], dtype=np.float32))
    B, S, _ = x.shape
    if S not in _NC_CACHE:
        _NC_CACHE[S] = build_nc(S)
    nc = _NC_CACHE[S]
    shared = layout_weights(inputs)
    shared.update(consts())
    in_maps = []
    for b in range(B):
        m = dict(shared)
        m["x"] = x[b]
        in_maps.append(m)
    res = run_bass_kernel_spmd(nc, in_maps, core_ids=list(range(B)))
    return np.stack([np.asarray(r["out"], dtype=np.float32) for r in res.results], axis=0)
```

```python
import numpy as np
from contextlib import ExitStack
import ml_dtypes
import concourse.bass as bass
import concourse.mybir as mybir
from concourse.bass_utils import run_bass_kernel_spmd

F32 = mybir.dt.float32
BF16 = mybir.dt.bfloat16
ALU = mybir.AluOpType
AF = mybir.ActivationFunctionType
AX = mybir.AxisListType

D = 1024
DFF = 2816
NFF = DFF // 128
DIN = 4944
EPS = 1e-6
NEG = -30000.0


class Buf:
    __slots__ = ("name", "w", "r", "psum")

    def __init__(self, name):
        self.name = name
        self.w = {}
        self.r = {}
        self.psum = False


class Group:
    def __init__(self, sem, final=False):
        self.sem = sem
        self.n = 0
        self.final = final


class Prog:
    ENG = ("pe", "act", "dve", "pool", "sp")

    def __init__(self, nc):
        self.nc = nc
        self.ops = {e: [] for e in self.ENG}

    def op(self, eng, fn, reads=(), writes=(), group=None):
        ops = self.ops[eng]
        idx = len(ops)
        rec = {"fn": fn, "waits": {}, "sig": False, "group": group}
        is_dma = group is not None

        def need(key, val, raw):
            if is_dma and key[0] == "g" and key[1] is group:
                return
            if key[0] == "e" and key[1] == eng and not is_dma:
                if eng == "pe" or not raw:
                    return
            w = rec["waits"]
            if w.get(key, -1) < val:
                w[key] = val
            if key[0] == "e":
                self.ops[key[1]][val]["sig"] = True

        for b in reads:
            for k, v in b.w.items():
                need(k, v, True)
            if b.psum:
                for k, v in b.r.items():
                    need(k, v, False)
        for b in writes:
            for k, v in b.w.items():
                need(k, v, False)
            for k, v in b.r.items():
                need(k, v, False)
        if is_dma:
            group.n += 1
            me = (("g", group), group.n)
        else:
            me = (("e", eng), idx)
        for b in reads:
            if b.r.get(me[0], -1) < me[1]:
                b.r[me[0]] = me[1]
        for b in writes:
            b.w = {me[0]: me[1]}
            b.r = {}
        ops.append(rec)
        return rec

    def emit(self, block, sems, final_waits, base=None):
        if base is None:
            base = {e: 0 for e in self.ENG}
        sigcount = {}
        for e in self.ENG:
            c = base[e]
            lst = []
            for rec in self.ops[e]:
                if rec["sig"]:
                    c += 1
                lst.append(c)
            sigcount[e] = lst

        def run(engname, engine):
            waited = {}
            for rec in self.ops[engname]:
                for key, val in rec["waits"].items():
                    if key[0] == "e":
                        sem = sems[key[1]]
                        v = sigcount[key[1]][val]
                    else:
                        g = key[1]
                        sem = g.sem
                        v = 16 * (g.n if g.final else val)
                    if waited.get(id(sem), -1) >= v:
                        continue
                    waited[id(sem)] = v
                    engine.wait_ge(sem, v)
                ins = rec["fn"](engine)
                if rec["group"] is not None:
                    ins.then_inc(rec["group"].sem, 16)
                elif rec["sig"]:
                    ins.then_inc(sems[engname], 1)
            for g in final_waits.get(engname, ()):
                engine.wait_ge(g.sem, 16 * g.n)

        @block.tensor
        def _(e):
            run("pe", e)

        @block.scalar
        def _(e):
            run("act", e)

        @block.vector
        def _(e):
            run("dve", e)

        @block.gpsimd
        def _(e):
            run("pool", e)

        @block.sync
        def _(e):
            run("sp", e)

        for e in self.ENG:
            base[e] = sigcount[e][-1] if sigcount[e] else base[e]


class _Alloc:
    def __init__(self, nc, es):
        self._nc = nc
        self._es = es

    def alloc_sbuf_tensor(self, name, shape, dt):
        return self._es.enter_context(self._nc.sbuf_tensor(name, list(shape), dt))

    def alloc_psum_tensor(self, name, shape, dt):
        return self._es.enter_context(self._nc.psum_tensor(name, list(shape), dt))


class TL:
    def __init__(self, t, name):
        self.t = t
        self.b = Buf(name)


class Ctx:
    def __init__(self, nc, es, tag, debug=False):
        self.rnc = nc
        self.nc = _Alloc(nc, es)
        self.es = es
        self.tag = tag
        self.P = Prog(nc)
        self.groups = []
        self.debug = debug

    def group(self, final=False):
        sem = self.rnc.alloc_semaphore(f"{self.tag}g{len(self.groups)}")
        g = Group(sem, final)
        self.groups.append(g)
        return g

    def sb(self, name, shape, dt):
        return TL(self.nc.alloc_sbuf_tensor(self.tag + name, list(shape), dt), name)

    def ps(self, name, shape, dt):
        t = TL(self.nc.alloc_psum_tensor(self.tag + name, list(shape), dt), name)
        t.b.psum = True
        return t


_GSTATE = {}


def run_phase(nc, tag, fn, debug=False):
    with ExitStack() as es:
        cx = Ctx(nc, es, tag, debug)
        import os
        for _i in range(int(os.environ.get('DUMMYSEM', '0'))):
            es.enter_context(nc.semaphore(f"{tag}dummy{_i}"))
        finals = fn(cx)
        st = _GSTATE.setdefault(id(nc), {})
        if "sems" not in st:
            st["sems"] = {e: nc.alloc_semaphore(f"s_{e}") for e in Prog.ENG}
            st["base"] = {e: 0 for e in Prog.ENG}
        with nc.Block() as block:
            cx.P.emit(block, st["sems"], finals, st["base"])


def _rr(lst, i):
    return lst[i % len(lst)]


def ffn_phase(cx, S, xin, xout, gvec, wg, wu, wd, identd):
    nc, P = cx.nc, cx.P
    tag = cx.tag
    xin_buf, xout_buf = Buf("xin"), Buf("xout")
    ident = nc.alloc_sbuf_tensor(f"{tag}ident", [128, 128], BF16)
    identb = Buf("ident")
    cg0 = cx.group(final=True)
    P.op("sp", lambda e: e.dma_start(out=ident[:], in_=identd), writes=[identb], group=cg0)
    TT = 256
    NT = S // TT
    WG = nc.alloc_sbuf_tensor(f"{tag}WG", [128, 8, DFF], BF16)
    WU = nc.alloc_sbuf_tensor(f"{tag}WU", [128, 8, DFF], BF16)
    WD = nc.alloc_sbuf_tensor(f"{tag}WD", [128, NFF, D], BF16)
    stg = [nc.alloc_sbuf_tensor(f"{tag}stg{i}", [128, DFF // 2], F32) for i in range(2)]
    stg_b = [Buf(f"stg{i}") for i in range(2)]
    stg_g = [cx.group() for _ in range(2)]
    wbuf = Buf("W")
    G = nc.alloc_sbuf_tensor(f"{tag}G", [128, D], F32)
    gb = Buf("G")
    cg = cx.group(final=True)
    P.op("sp", lambda e: e.dma_start(out=G[:], in_=gvec.partition_broadcast(128)), writes=[gb], group=cg)
    nhalf = nc.alloc_sbuf_tensor(f"{tag}nh", [128, 2], F32)
    nhb = Buf("nh")
    P.op("pool", lambda e: e.memset(nhalf[:], -0.5), writes=[nhb])

    conv_eng = ["act", "dve", "pool"]
    k = 0

    def load_conv(src_ap, dst_ap, width):
        nonlocal k
        s = k % 2
        st, sb_, sg = stg[s], stg_b[s], stg_g[s]
        P.op("sp", lambda e: e.dma_start(out=st[:, 0:width], in_=src_ap), writes=[sb_], group=sg)
        ce = conv_eng[k % 3]
        if ce == "act":
            P.op("act", lambda e: e.copy(out=dst_ap, in_=st[:, 0:width]), reads=[sb_], writes=[wbuf])
        else:
            P.op(ce, lambda e: e.tensor_copy(out=dst_ap, in_=st[:, 0:width]), reads=[sb_], writes=[wbuf])
        k += 1

    HF = DFF // 2
    for kc in range(8):
        for hh in range(2):
            load_conv(wg[kc * 128:(kc + 1) * 128, hh * HF:(hh + 1) * HF], WG[:, kc, hh * HF:(hh + 1) * HF], HF)
            load_conv(wu[kc * 128:(kc + 1) * 128, hh * HF:(hh + 1) * HF], WU[:, kc, hh * HF:(hh + 1) * HF], HF)
    for f in range(NFF):
        load_conv(wd[f * 128:(f + 1) * 128, :], WD[:, f, :], D)

    xt = [nc.alloc_sbuf_tensor(f"{tag}xt{i}", [128, 2, D], F32) for i in range(2)]
    xt_b = [Buf(f"xt{i}") for i in range(2)]
    xt_g = [cx.group() for _ in range(2)]
    st_g = [cx.group() for _ in range(2)]
    junk = nc.alloc_sbuf_tensor(f"{tag}junk", [128, D], BF16)
    junk_b = Buf("junk")
    ss = [nc.alloc_sbuf_tensor(f"{tag}ss{i}", [128, 2], F32) for i in range(2)]
    ss_b = [Buf(f"ss{i}") for i in range(2)]
    ms = [nc.alloc_sbuf_tensor(f"{tag}ms{i}", [128, 2], F32) for i in range(2)]
    ms_b = [Buf(f"ms{i}") for i in range(2)]
    rs = [nc.alloc_sbuf_tensor(f"{tag}rs{i}", [128, 2], F32) for i in range(2)]
    rs_b = [Buf(f"rs{i}") for i in range(2)]
    hb = [nc.alloc_sbuf_tensor(f"{tag}h{i}", [128, 2, D], BF16) for i in range(2)]
    hb_b = [Buf(f"h{i}") for i in range(2)]
    hT = [nc.alloc_sbuf_tensor(f"{tag}hT{i}", [128, 8, TT], BF16) for i in range(2)]
    hT_b = [Buf(f"hT{i}") for i in range(2)]
    actT = [nc.alloc_sbuf_tensor(f"{tag}actT{i}", [128, NFF, TT], BF16) for i in range(2)]
    actT_b = [Buf(f"actT{i}") for i in range(2)]
    sg = [nc.alloc_sbuf_tensor(f"{tag}sg{i}", [128, 512], F32) for i in range(2)]
    sg_b = [Buf(f"sg{i}") for i in range(2)]
    pT = [nc.alloc_psum_tensor(f"{tag}pT{i}", [128, 1024], BF16) for i in range(2)]
    pT_b = [Buf(f"pT{i}") for i in range(2)]
    pG = [nc.alloc_psum_tensor(f"{tag}pG{i}", [128, 512], F32) for i in range(2)]
    pG_b = [Buf(f"pG{i}") for i in range(2)]
    pU = [nc.alloc_psum_tensor(f"{tag}pU{i}", [128, 512], F32) for i in range(2)]
    pU_b = [Buf(f"pU{i}") for i in range(2)]
    pD = [nc.alloc_psum_tensor(f"{tag}pD{i}", [128, 512], F32) for i in range(2)]
    pD_b = [Buf(f"pD{i}") for i in range(2)]

    def stage_load(i):
        s = i % 2
        t0 = i * TT
        P.op("sp", lambda e: e.dma_start(
            out=xt[s][:], in_=xin[t0:t0 + TT, :].rearrange("(s p) d -> p s d", p=128)),
            reads=[xin_buf], writes=[xt_b[s]], group=xt_g[s])

    def stage_norm(i):
        s = i % 2
        for sub in range(2):
            P.op("act", lambda e, sub=sub: e.activation(
                out=junk[:], in_=xt[s][:, sub, :], func=AF.Square, accum_out=ss[s][:, sub:sub + 1]),
                reads=[xt_b[s]], writes=[junk_b, ss_b[s]])
        P.op("dve", lambda e: e.tensor_scalar(out=ms[s][:], in0=ss[s][:], scalar1=1.0 / D, scalar2=EPS,
                                              op0=ALU.mult, op1=ALU.add), reads=[ss_b[s]], writes=[ms_b[s]])
        P.op("pool", lambda e: e.tensor_tensor(out=rs[s][:], in0=ms[s][:], in1=nhalf[:], op=ALU.pow),
             reads=[ms_b[s], nhb], writes=[rs_b[s]])
        for sub in range(2):
            P.op("dve", lambda e, sub=sub: e.scalar_tensor_tensor(
                out=hb[s][:, sub, :], in0=xt[s][:, sub, :], scalar=rs[s][:, sub:sub + 1], in1=G[:],
                op0=ALU.mult, op1=ALU.mult), reads=[xt_b[s], rs_b[s], gb], writes=[hb_b[s]])

    def stage_T(i):
        s = i % 2
        for half in range(2):
            pb, pbb = pT[half], pT_b[half]
            for kk in range(4):
                kc = half * 4 + kk
                for sub in range(2):
                    P.op("pe", lambda e, kc=kc, sub=sub, kk=kk, pb=pb: e.transpose(
                        out=pb[:, kk * TT + sub * 128: kk * TT + (sub + 1) * 128],
                        in_=hb[s][:, sub, kc * 128:(kc + 1) * 128], identity=ident[:]),
                        reads=[hb_b[s], identb], writes=[pbb])
            if half == 0:
                P.op("act", lambda e, pb=pb: e.copy(out=hT[s][:, 0:4, :], in_=pb[:].rearrange("p (k t) -> p k t", k=4)),
                     reads=[pbb], writes=[hT_b[s]])
            else:
                P.op("dve", lambda e, pb=pb: e.tensor_copy(out=hT[s][:, 4:8, :], in_=pb[:].rearrange("p (k t) -> p k t", k=4)),
                     reads=[pbb], writes=[hT_b[s]])

    def stage_GU(i):
        s = i % 2
        for f2 in range(NFF // 2):
            q = f2 % 2
            for (W_, pp, ppb) in ((WG, pG[q], pG_b[q]), (WU, pU[q], pU_b[q])):
                for c in range(2):
                    f = 2 * f2 + c
                    for kc in range(8):
                        P.op("pe", lambda e, W_=W_, pp=pp, c=c, f=f, kc=kc: e.matmul(
                            out=pp[:, c * TT:(c + 1) * TT], lhsT=W_[:, kc, f * 128:(f + 1) * 128],
                            rhs=hT[s][:, kc, :], start=(kc == 0), stop=(kc == 7)),
                            reads=[wbuf, hT_b[s]], writes=[ppb])
            P.op("act", lambda e, q=q: e.activation(out=sg[q][:], in_=pG[q][:], func=AF.Silu),
                 reads=[pG_b[q]], writes=[sg_b[q]])
            P.op("dve", lambda e, q=q, f2=f2: e.tensor_tensor(
                out=actT[s][:, 2 * f2:2 * f2 + 2, :], in0=sg[q][:].rearrange("p (c t) -> p c t", c=2),
                in1=pU[q][:].rearrange("p (c t) -> p c t", c=2), op=ALU.mult),
                reads=[sg_b[q], pU_b[q]], writes=[actT_b[s]])

    def stage_D(i):
        s = i % 2
        t0 = i * TT
        n = 0
        for sub in range(2):
            for dh in range(2):
                q = n % 2
                n += 1
                for f in range(NFF):
                    P.op("pe", lambda e, q=q, f=f, sub=sub, dh=dh: e.matmul(
                        out=pD[q][:], lhsT=actT[s][:, f, sub * 128:(sub + 1) * 128],
                        rhs=WD[:, f, dh * 512:(dh + 1) * 512], start=(f == 0), stop=(f == NFF - 1)),
                        reads=[wbuf, actT_b[s]], writes=[pD_b[q]])
                P.op("dve", lambda e, q=q, sub=sub, dh=dh: e.scalar_tensor_tensor(
                    out=xt[s][:, sub, dh * 512:(dh + 1) * 512], in0=pD[q][:], scalar=0.5,
                    in1=xt[s][:, sub, dh * 512:(dh + 1) * 512], op0=ALU.mult, op1=ALU.add),
                    reads=[pD_b[q], xt_b[s]], writes=[xt_b[s]])
        P.op("pool", lambda e: e.dma_start(
            out=xout[t0:t0 + TT, :].rearrange("(s p) d -> p s d", p=128), in_=xt[s][:]),
            reads=[xt_b[s]], writes=[xout_buf], group=st_g[s])

    stage_load(0)
    if NT > 1:
        stage_load(1)
    stage_norm(0)
    stage_T(0)
    for i in range(NT):
        stage_GU(i)
        if i + 1 < NT:
            stage_norm(i + 1)
            stage_T(i + 1)
        stage_D(i)
        if i + 2 < NT:
            stage_load(i + 2)
    return {"pool": st_g}


def _b(L):
    return [x.b for x in L]


def mm(P, out, lhsT, rhs, st, sp, R, W):
    P.op("pe", lambda e: e.matmul(out=out, lhsT=lhsT, rhs=rhs, start=st, stop=sp), _b(R), _b(W))


def tr(P, out, in_, ident, R, W):
    P.op("pe", lambda e: e.transpose(out=out, in_=in_, identity=ident), _b(R), _b(W))


def act(P, out, in_, func, R, W, **kw):
    P.op("act", lambda e: e.activation(out=out, in_=in_, func=func, **kw), _b(R), _b(W))


def ts(P, eng, out, in0, s1, s2, op0, op1, R, W, accum=None):
    if accum is None:
        P.op(eng, lambda e: e.tensor_scalar(out=out, in0=in0, scalar1=s1, scalar2=s2, op0=op0, op1=op1), _b(R), _b(W))
    else:
        P.op(eng, lambda e: e.tensor_scalar(out=out, in0=in0, scalar1=s1, scalar2=s2, op0=op0, op1=op1,
                                            accum_out=accum), _b(R), _b(W))


def tt(P, eng, out, in0, in1, op, R, W):
    P.op(eng, lambda e: e.tensor_tensor(out=out, in0=in0, in1=in1, op=op), _b(R), _b(W))


def stt(P, out, in0, scalar, in1, op0, op1, R, W):
    P.op("dve", lambda e: e.scalar_tensor_tensor(out=out, in0=in0, scalar=scalar, in1=in1, op0=op0, op1=op1),
         _b(R), _b(W))


def cp(P, eng, out, in_, R, W):
    if eng == "act":
        P.op("act", lambda e: e.copy(out=out, in_=in_), _b(R), _b(W))
    else:
        P.op(eng, lambda e: e.tensor_copy(out=out, in_=in_), _b(R), _b(W))


def red(P, out, in_, op, R, W, **kw):
    P.op("dve", lambda e: e.tensor_reduce(out=out, in_=in_, axis=AX.X, op=op, **kw), _b(R), _b(W))


def dma(P, eng, out, in_, R, W, g, **kw):
    P.op(eng, lambda e: e.dma_start(out=out, in_=in_, **kw), _b(R), _b(W), group=g)


def mset(P, eng, ap, val, W):
    P.op(eng, lambda e: e.memset(ap, val), [], _b(W))


def rstd_ops(cx, ssq, n, rs, nh, R):
    P = cx.P
    ts(P, "dve", ssq.t[:], ssq.t[:], 1.0 / n, EPS, ALU.mult, ALU.add, R + [ssq], [ssq])
    tt(P, "pool", rs.t[:], ssq.t[:], nh.t[:, 0:ssq.t.shape[1]], ALU.pow, [ssq, nh], [rs])


IDX_SCALE = (64 ** -0.5) * (8 ** -0.5)
ATT_SCALE = 64 ** -0.5
MQ_SCALE = 64 ** -0.5


def load_ident(cx, Cd, cg):
    ident = cx.sb("ident", [128, 128], BF16)
    dma(cx.P, "sp", ident.t[:], Cd["c_ident"], [], [ident], cg)
    return ident


def norm_transpose(cx, xt, G, ident, nh, hb, hT, pT, ss, rs, junk, ntok_sub=2):
    P = cx.P
    for sub in range(ntok_sub):
        act(P, junk.t[:], xt.t[:, sub, :], AF.Square, [xt], [junk, ss], accum_out=ss.t[:, sub:sub + 1])
    rstd_ops(cx, ss, D, rs, nh, [])
    for sub in range(ntok_sub):
        stt(P, hb.t[:, sub, :], xt.t[:, sub, :], rs.t[:, sub:sub + 1], G.t[:], ALU.mult, ALU.mult,
            [xt, rs, G], [hb])
    for half in range(2):
        pb = pT[half]
        for kk in range(4):
            kc = half * 4 + kk
            for sub in range(ntok_sub):
                tr(P, pb.t[:, kk * 256 + sub * 128: kk * 256 + (sub + 1) * 128],
                   hb.t[:, sub, kc * 128:(kc + 1) * 128], ident.t[:], [hb, ident], [pb])
        cp(P, "act" if half == 0 else "dve", hT.t[:, half * 4:half * 4 + 4, :],
           pb.t[:].rearrange("p (k t) -> p k t", k=4), [pb], [hT])


def proj_phase(cx, S, X1, A, Wd, Cd):
    nc, P = cx.nc, cx.P
    TT = 256
    NT = S // TT
    cg = cx.group(final=True)
    ident = load_ident(cx, Cd, cg)
    WIN = cx.sb("WIN", [128, 8, DIN], BF16)
    stg = [cx.sb(f"stg{i}", [128, 824], F32) for i in range(2)]
    stg_g = [cx.group() for _ in range(2)]
    k = 0
    for kc in range(8):
        for part in range(6):
            s_ = k % 2
            c0 = part * 824
            dma(P, "sp", stg[s_].t[:], Wd["w_in"][kc * 128:(kc + 1) * 128, c0:c0 + 824], [], [stg[s_]], stg_g[s_])
            cp(P, ["act", "dve", "pool"][k % 3], WIN.t[:, kc, c0:c0 + 824], stg[s_].t[:], [stg[s_]], [WIN])
            k += 1
    import os
    SL = int(os.environ.get('SL', '99'))
    G = cx.sb("G", [128, D], F32)
    dma(P, "sp", G.t[:], Wd["mix_norm"].partition_broadcast(128), [], [G], cg)
    GQ = cx.sb("GQ", [128, 512], F32)
    for h in range(8):
        if SL >= 2:
            dma(P, "sp", GQ.t[:, h * 64:(h + 1) * 64], Wd["q_norm"].partition_broadcast(128), [], [GQ], cg)
    if SL >= 2:
        ts(P, "dve", GQ.t[:], GQ.t[:], ATT_SCALE, None, ALU.mult, ALU.bypass, [GQ], [GQ])
    GK = cx.sb("GK", [128, 128], F32)
    for h in range(2):
        if SL >= 3:
            dma(P, "sp", GK.t[:, h * 64:(h + 1) * 64], Wd["k_norm"].partition_broadcast(128), [], [GK], cg)
    GI = cx.sb("GI", [128, 64], F32)
    if SL >= 3:
        dma(P, "sp", GI.t[:], Wd["idx_k_norm"].partition_broadcast(128), [], [GI], cg)
    CW = cx.sb("CW", [128, 4, 4], F32)
    CB = cx.sb("CB", [128, 4], F32)
    if SL >= 4:
        dma(P, "sp", CB.t[:], Wd["conv_b"], [], [CB], cg)
    if SL >= 4:
      dma(P, "sp", CW.t[:], Wd["conv_w"].rearrange("p (c j) -> p c j", c=4), [], [CW], cg)
    wst = cx.sb("wst", [128, 2, 4, 64], F32)
    WM = cx.sb("WM", [128, 2, 4, 64], BF16)
    if SL >= 5:
        dma(P, "sp", wst.t[:, 0], Wd["w_mq"].rearrange("c (h d) -> c h d", h=4), [], [wst], cg)
        dma(P, "sp", wst.t[:, 1], Wd["w_mk"].rearrange("c (h d) -> c h d", h=4), [], [wst], cg)
        cp(P, "dve", WM.t[:], wst.t[:], [wst], [WM])
    BIF = cx.sb("BIF", [128, 8], F32)
    if SL >= 6:
        dma(P, "sp", BIF.t[:, 0:4], Wd["b_i"].partition_broadcast(128), [], [BIF], cg)
        dma(P, "sp", BIF.t[:, 4:8], Wd["b_f"].partition_broadcast(128), [], [BIF], cg)
    NH = cx.sb("NH", [128, 8], F32)
    mset(P, "pool", NH.t[:], -0.5, [NH])

    R2 = range(2)
    xt = [cx.sb(f"xt{i}", [128, 2, D], F32) for i in R2]
    xt_g = [cx.group() for _ in R2]
    junk = cx.sb("junk", [128, D], BF16)
    ss = [cx.sb(f"ss{i}", [128, 2], F32) for i in R2]
    rs = [cx.sb(f"rs{i}", [128, 2], F32) for i in R2]
    hb = [cx.sb("hb0", [128, 2, D], BF16)] * 2
    hT = [cx.sb(f"hT{i}", [128, 8, TT], BF16) for i in R2]
    pT = [cx.ps(f"pT{i}", [128, 1024], BF16) for i in R2]
    pX = [cx.ps(f"pX{i}", [128, 1024], BF16) for i in R2]
    pM = [cx.ps(f"pM{i}", [128, 512], F32) for i in range(4)]
    nbank = [0]

    def bank():
        b = pM[nbank[0] % 4]
        nbank[0] += 1
        return b

    sqt = [cx.sb(f"sqt{i}", [128, 512], F32) for i in R2]
    t1 = [cx.sb(f"t1{i}", [128, 512], F32) for i in R2]
    qn = [cx.sb(f"qn{i}", [128, 512], BF16) for i in R2]
    kn = [cx.sb(f"kn{i}", [128, 128], BF16) for i in R2]
    ikn = [cx.sb(f"ikn{i}", [128, 64], BF16) for i in R2]
    iqs = [cx.sb(f"iqs{i}", [128, 512], BF16) for i in R2]
    s8 = [cx.sb(f"s8{i}", [128, 8], F32) for i in R2]
    r8 = [cx.sb(f"r8{i}", [128, 8], F32) for i in R2]
    s2 = [cx.sb(f"s2{i}", [128, 2], F32) for i in R2]
    r2 = [cx.sb(f"r2{i}", [128, 2], F32) for i in R2]
    s1 = [cx.sb(f"s1{i}", [128, 1], F32) for i in R2]
    r1 = [cx.sb(f"r1{i}", [128, 1], F32) for i in R2]
    absw = [cx.sb(f"absw{i}", [128, 8], F32) for i in R2]
    zt = [cx.sb(f"zt{i}", [128, 8], F32) for i in R2]
    et = [cx.sb(f"et{i}", [128, 4], F32) for i in R2]
    def outs(name, shape, dt):
        return [cx.sb(f"{name}{i}", shape, dt) for i in R2], [cx.group() for _ in R2]
    QTt, QT_g = outs("QTt", [128, 4, TT], BF16)
    KTt, KT_g = outs("KTt", [128, TT], BF16)
    IQTt, IQT_g = outs("IQTt", [128, 4, TT], BF16)
    IKTt, IKT_g = outs("IKTt", [64, TT], BF16)
    vb, V_g = outs("vb", [128, 2, 128], BF16)
    sgn, SGN_g = outs("sgn", [128, 2, 8], F32)
    mvb, MV_g = outs("mvb", [128, 2, 512], BF16)
    lif, LIF_g = outs("lif", [128, 2, 8], F32)
    mob, MO_g = outs("mob", [128, 2, 512], BF16)
    gat, GAT_g = outs("gat", [128, 8, TT], BF16)
    gbt, GBT_g = outs("gbt", [128, 8, TT], BF16)
    mqt, MQT_g = outs("mqt", [64, 4, TT], BF16)
    mkt, MKT_g = outs("mkt", [64, 4, TT], BF16)
    mkb, MK_g = outs("mkb", [128, 2, 256], BF16)
    XM = [cx.sb(f"XM{i}", [128, 4, TT + 3], F32) for i in R2]
    mset(P, "pool", XM[0].t[:, :, 0:3], 0.0, [XM[0]])
    acc = [cx.sb("acc0", [128, 4, TT], F32)] * 2
    sig = [cx.sb("sig0", [128, 4, TT], F32)] * 2
    xct = [cx.sb(f"xct{i}", [128, 4, TT], BF16) for i in R2]
    X1b = TL(None, "X1")
    allg = []

    def load(i):
        s_ = i % 2
        t0 = i * TT
        dma(P, "sp", xt[s_].t[:], X1[t0:t0 + TT, :].rearrange("(s p) d -> p s d", p=128), [X1b], [xt[s_]], xt_g[s_])

    def tile(i):
        import os
        LV = int(os.environ.get('LV', '99'))
        if LV < 1:
            return
        s_ = i % 2
        t0 = i * TT
        tsl = slice(t0, t0 + TT)
        H = hT[s_]

        def tok_group(sub, c0, width):
            b = bank()
            for kc in range(8):
                mm(P, b.t[:, 0:width], H.t[:, kc, sub * 128:(sub + 1) * 128], WIN.t[:, kc, c0:c0 + width],
                   kc == 0, kc == 7, [H, WIN], [b])
            return b

        pq, pi = pX[0], pX[1]
        ssl = None
        for sub in range(2):
            u = sub
            if LV < 2:
                continue
            b = tok_group(sub, 0, 512)
            act(P, sqt[u].t[:], b.t[:], AF.Square, [b], [sqt[u]])
            red(P, s8[u].t[:], sqt[u].t[:].rearrange("p (h d) -> p h d", h=8), ALU.add, [sqt[u]], [s8[u]])
            rstd_ops(cx, s8[u], 64, r8[u], NH, [])
            tt(P, "dve", t1[u].t[:].rearrange("p (h d) -> p h d", h=8), b.t[:].rearrange("p (h d) -> p h d", h=8),
               r8[u].t[:].unsqueeze(2).broadcast_to([128, 8, 64]), ALU.mult, [b, r8[u]], [t1[u]])
            tt(P, "pool", qn[u].t[:].rearrange("p (r g d) -> p g r d", r=4, g=2),
               t1[u].t[:].rearrange("p (g r d) -> p g r d", g=2, r=4),
               GQ.t[:].rearrange("p (g r d) -> p g r d", g=2, r=4), ALU.mult, [t1[u], GQ], [qn[u]])
            ssl = slice(sub * 128, (sub + 1) * 128)
            for r in range(4):
                tr(P, pq.t[:, r * 128:(r + 1) * 128], qn[u].t[:, r * 128:(r + 1) * 128],
                   ident.t[:], [qn[u], ident], [pq])
            cp(P, "act", QTt[s_].t[:, :, ssl], pq.t[:, 0:512].rearrange("p (r t) -> p r t", r=4), [pq], [QTt[s_]])
            if sub == 1:
                dma(P, "sp", A["QT"][:, :, tsl].rearrange("r p t -> p r t"), QTt[s_].t[:], [QTt[s_]], [], QT_g[s_])
            if LV < 3:
                continue
            b = tok_group(sub, 512, 256)
            act(P, sqt[u].t[:, 0:128], b.t[:, 0:128], AF.Square, [b], [sqt[u]])
            red(P, s2[u].t[:], sqt[u].t[:, 0:128].rearrange("p (h d) -> p h d", h=2), ALU.add, [sqt[u]], [s2[u]])
            rstd_ops(cx, s2[u], 64, r2[u], NH, [])
            tt(P, "dve", t1[u].t[:, 0:128].rearrange("p (h d) -> p h d", h=2),
               b.t[:, 0:128].rearrange("p (h d) -> p h d", h=2),
               r2[u].t[:].unsqueeze(2).broadcast_to([128, 2, 64]), ALU.mult, [b, r2[u]], [t1[u]])
            tt(P, "pool", kn[u].t[:], t1[u].t[:, 0:128], GK.t[:], ALU.mult, [t1[u], GK], [kn[u]])
            tr(P, pq.t[:, 512:640], kn[u].t[:], ident.t[:], [kn[u], ident], [pq])
            cp(P, "act", vb[s_].t[:, sub, :], b.t[:, 128:256], [b], [vb[s_]])
            cp(P, "dve", KTt[s_].t[:, ssl], pq.t[:, 512:640], [pq], [KTt[s_]])
            if sub == 1:
                dma(P, "sp", A["KT"][:, tsl], KTt[s_].t[:], [KTt[s_]], [], KT_g[s_])
                dma(P, "sp", A["V"][tsl, :].rearrange("(s p) d -> p s d", p=128), vb[s_].t[:], [vb[s_]], [], V_g[s_])
            if LV < 4:
                continue
            bw = tok_group(sub, 1280, 72)
            ts(P, "dve", zt[u].t[:], bw.t[:, 64:72], -IDX_SCALE, None, ALU.mult, ALU.bypass, [bw], [zt[u]])
            stt(P, absw[u].t[:], bw.t[:, 64:72], IDX_SCALE, zt[u].t[:], ALU.mult, ALU.max, [bw, zt[u]], [absw[u]])
            ts(P, "dve", sgn[s_].t[:, sub, :], bw.t[:, 64:72], 0.0, 2.0, ALU.is_ge, ALU.mult, [bw], [sgn[s_]])
            ts(P, "dve", sgn[s_].t[:, sub, :], sgn[s_].t[:, sub, :], -1.0, None, ALU.add, ALU.bypass, [sgn[s_]], [sgn[s_]])
            act(P, sqt[u].t[:, 0:64], bw.t[:, 0:64], AF.Square, [bw], [sqt[u], s1[u]], accum_out=s1[u].t[:])
            rstd_ops(cx, s1[u], 64, r1[u], NH, [])
            stt(P, ikn[u].t[:], bw.t[:, 0:64], r1[u].t[:], GI.t[:], ALU.mult, ALU.mult, [bw, r1[u], GI], [ikn[u]])
            b = tok_group(sub, 768, 512)
            tt(P, "dve", iqs[u].t[:].rearrange("p (h d) -> p h d", h=8), b.t[:].rearrange("p (h d) -> p h d", h=8),
               absw[u].t[:].unsqueeze(2).broadcast_to([128, 8, 64]), ALU.mult, [b, absw[u]], [iqs[u]])
            for r in range(4):
                tr(P, pi.t[:, r * 128:(r + 1) * 128], iqs[u].t[:, r * 128:(r + 1) * 128],
                   ident.t[:], [iqs[u], ident], [pi])
            tr(P, pi.t[0:64, 512:640], ikn[u].t[:], ident.t[:], [ikn[u], ident], [pi])
            cp(P, "act", IQTt[s_].t[:, :, ssl], pi.t[:, 0:512].rearrange("p (r t) -> p r t", r=4), [pi], [IQTt[s_]])
            cp(P, "dve", IKTt[s_].t[:, ssl], pi.t[0:64, 512:640], [pi], [IKTt[s_]])
            if sub == 1:
                dma(P, "sp", A["IQT"][:, :, tsl].rearrange("r p t -> p r t"), IQTt[s_].t[:], [IQTt[s_]], [], IQT_g[s_])
                dma(P, "sp", A["IKT"][:, tsl], IKTt[s_].t[:], [IKTt[s_]], [], IKT_g[s_])
                dma(P, "sp", A["SGN"][tsl, :].rearrange("(s p) h -> p s h", p=128), sgn[s_].t[:], [sgn[s_]], [], SGN_g[s_])
            if LV < 5:
                continue
            b = tok_group(sub, 1864, 512)
            cp(P, "dve", mvb[s_].t[:, sub, :], b.t[:], [b], [mvb[s_]])
            b = tok_group(sub, 2376, 8)
            tt(P, "dve", zt[u].t[:], b.t[:, 0:8], BIF.t[:], ALU.add, [b, BIF], [zt[u]])
            cp(P, "pool", lif[s_].t[:, sub, 0:4], zt[u].t[:, 0:4], [zt[u]], [lif[s_]])
            act(P, et[u].t[:], zt[u].t[:, 4:8], AF.Exp, [zt[u]], [et[u]], scale=-1.0)
            act(P, et[u].t[:], et[u].t[:], AF.Ln, [et[u]], [et[u]], bias=1.0)
            ts(P, "dve", lif[s_].t[:, sub, 4:8], et[u].t[:], -1.0, None, ALU.mult, ALU.bypass, [et[u]], [lif[s_]])
            b = tok_group(sub, 2384, 512)
            act(P, mob[s_].t[:, sub, :], b.t[:], AF.Sigmoid, [b], [mob[s_]])
        if LV < 6:
            return
        dma(P, "sp", A["MV"][tsl, :].rearrange("(s p) d -> p s d", p=128), mvb[s_].t[:], [mvb[s_]], [], MV_g[s_])
        dma(P, "sp", A["LIF"][tsl, :].rearrange("(s p) d -> p s d", p=128), lif[s_].t[:], [lif[s_]], [], LIF_g[s_])
        dma(P, "sp", A["MO"][tsl, :].rearrange("(s p) d -> p s d", p=128), mob[s_].t[:], [mob[s_]], [], MO_g[s_])

        if i + 1 < NT:
            nt(i + 1)
        def feat_pair(c0):
            b = bank()
            for c in range(2):
                for kc in range(8):
                    mm(P, b.t[:, c * TT:(c + 1) * TT], WIN.t[:, kc, c0 + c * 128:c0 + (c + 1) * 128], H.t[:, kc, :],
                       kc == 0, kc == 7, [H, WIN], [b])
            return b

        if LV < 7:
            return
        for (base, dst) in ((2896, gat[s_]), (3920, gbt[s_])):
            for pr in range(4):
                b = feat_pair(base + pr * 256)
                act(P, dst.t[:, 2 * pr:2 * pr + 2, :], b.t[:].rearrange("p (c t) -> p c t", c=2), AF.Sigmoid, [b], [dst])
        dma(P, "sp", A["GAT"][:, :, tsl].rearrange("c p t -> p c t"), gat[s_].t[:], [gat[s_]], [], GAT_g[s_])
        dma(P, "sp", A["GBT"][:, :, tsl].rearrange("c p t -> p c t"), gbt[s_].t[:], [gbt[s_]], [], GBT_g[s_])
        if LV < 8:
            return
        xm, xn = XM[s_], XM[1 - s_]
        for pr in range(2):
            b = feat_pair(1352 + pr * 256)
            cp(P, "dve", xm.t[:, 2 * pr:2 * pr + 2, 3:3 + TT], b.t[:].rearrange("p (c t) -> p c t", c=2), [b], [xm])
        cp(P, "pool", xn.t[:, :, 0:3], xm.t[:, :, TT:TT + 3], [xm], [xn])
        ac = acc[s_]
        for c in range(4):
            ts(P, "dve", ac.t[:, c, :], xm.t[:, c, 0:TT], CW.t[:, c, 0:1], CB.t[:, c:c + 1], ALU.mult, ALU.add,
               [xm, CW, CB], [ac])
            for j in range(1, 4):
                stt(P, ac.t[:, c, :], xm.t[:, c, j:j + TT], CW.t[:, c, j:j + 1], ac.t[:, c, :], ALU.mult, ALU.add,
                    [xm, CW, ac], [ac])
        act(P, sig[s_].t[:], ac.t[:], AF.Sigmoid, [ac], [sig[s_]])
        tt(P, "pool", xct[s_].t[:], ac.t[:], sig[s_].t[:], ALU.mult, [ac, sig[s_]], [xct[s_]])
        if LV < 9:
            return
        xc = xct[s_]
        for (wi, dst, scale, name, grp) in ((0, mqt[s_], MQ_SCALE, "MQT", MQT_g[s_]), (1, mkt[s_], 1.0, "MKT", MKT_g[s_])):
            for hp in range(2):
                b = bank()
                for c in range(2):
                    h = hp * 2 + c
                    mm(P, b.t[0:64, c * TT:(c + 1) * TT], WM.t[:, wi, h, :], xc.t[:, h, :], True, True, [WM, xc], [b])
                act(P, dst.t[:, 2 * hp:2 * hp + 2, :], b.t[0:64, :].rearrange("p (c t) -> p c t", c=2), AF.Copy,
                    [b], [dst], scale=scale)
            dma(P, "sp", A[name][:, :, tsl].rearrange("h d t -> d h t"), dst.t[:], [dst], [], grp)
        for sub in range(2):
            b = bank()
            for h in range(4):
                mm(P, b.t[:, h * 64:(h + 1) * 64], xc.t[:, h, sub * 128:(sub + 1) * 128], WM.t[:, 1, h, :], True, True,
                   [xc, WM], [b])
            cp(P, "dve", mkb[s_].t[:, sub, :], b.t[:, 0:256], [b], [mkb[s_]])
        dma(P, "sp", A["MK"][tsl, :].rearrange("(s p) d -> p s d", p=128), mkb[s_].t[:], [mkb[s_]], [], MK_g[s_])

    def nt(i):
        q_ = i % 2
        norm_transpose(cx, xt[q_], G, ident, NH, hb[q_], hT[q_], pT, ss[q_], rs[q_], junk)

    load(0)
    if NT > 1:
        load(1)
    nt(0)
    for i in range(NT):
        tile(i)
        if i + 2 < NT:
            load(i + 2)
    fin = QT_g + KT_g + IQT_g + IKT_g + V_g + SGN_g + MV_g + LIF_g + MO_g + GAT_g + GBT_g + MQT_g + MKT_g + MK_g
    return {"sp": fin}


NIT = 12


def dsa_phase(cx, S, A, Cd):
    nc, P = cx.nc, cx.P
    NB = S // 512
    NJ = S // 128
    cg = cx.group(final=True)
    ident = load_ident(cx, Cd, cg)
    IKbs = [cx.sb(f"IKb{i}", [128, 512], BF16) for i in range(2)]
    IK_g = [cx.group() for _ in range(2)]
    KT = cx.sb("KT", [128, S], BF16)
    dma(P, "sp", KT.t[:], A["KT"], [], [KT], cg)
    VA = cx.sb("VA", [128, NJ, 2, 65], BF16)
    mset(P, "pool", VA.t[:], 1.0, [VA])
    for g_ in range(2):
        dma(P, "sp", VA.t[:, :, g_, 0:64], A["V"][:, g_ * 64:(g_ + 1) * 64].rearrange("(j p) d -> p j d", p=128),
            [], [VA], cg)
    CM = cx.sb("CM", [128, 4, 512], BF16)
    dma(P, "sp", CM.t[:], Cd["c_cmask"].rearrange("i p s -> p i s"), [], [CM], cg)
    CROW = cx.sb("CROW", [128, NIT], F32)
    dma(P, "sp", CROW.t[:], Cd["c_steps"], [], [CROW], cg)
    ONES = cx.sb("ONES", [128, 64], F32)
    mset(P, "pool", ONES.t[:], 1.0, [ONES])

    SCs = [cx.sb(f"SC{i}", [128, S], F32) for i in range(2)]
    MK = cx.sb("MK", [128, S], BF16)
    MT = cx.sb("MT", [128, NJ, 512], BF16)
    IQb = cx.sb("IQb", [128, 4, 512], BF16)
    Qb = cx.sb("Qb", [128, 4, 512], BF16)
    SGb = cx.sb("SGb", [128, 4, 8], F32)
    ld_g1, ld_g2, ld_g3 = cx.group(), cx.group(), cx.group()
    DG = [cx.sb("DG0", [128, 8, 128], BF16)]
    Rt = [cx.sb(f"Rt{i}", [128, 512], BF16) for i in range(4)]
    Et = [cx.sb(f"Et{i}", [128, 512], BF16) for i in range(2)]
    Pt = [cx.sb(f"Pt{i}", [128, 512], BF16) for i in range(4)]
    M1 = cx.sb("M1", [128, 1], F32)
    LO = cx.sb("LO", [128, 1], F32)
    W0 = cx.sb("W0", [128, 1], F32)
    STEP = cx.sb("STEP", [128, NIT], F32)
    NSTEP = cx.sb("NSTEP", [128, NIT], F32)
    STEP2 = cx.sb("STEP2", [128, NIT], F32)
    MID = cx.sb("MID", [128, 1], F32)
    CNT = cx.sb("CNT", [128, 1], F32)
    DT = cx.sb("DT", [128, 1], F32)
    RC = cx.sb("RC", [128, 512], F32)
    OS = cx.sb("OS", [64, 512], BF16)
    YT = [cx.sb("YT0", [64, 512], BF16)] * 2
    YT_g = [cx.group() for _ in range(2)]
    pW = [cx.ps(f"pW{i}", [128, 512], F32) for i in range(4)]
    SCp = cx.ps("SCp", [128, 512], F32)
    pTr = cx.ps("pTr", [128, 1024], BF16)
    pOs = [cx.ps(f"pO{i}", [128, 512], F32) for i in range(2)]
    cnt = {"ik": 0, "l": 0, "r": 0, "s": 0, "e": 0, "p": 0, "y": 0, "d": 0, "m": 0}

    def nxt(key, lst):
        v = lst[cnt[key] % len(lst)]
        cnt[key] += 1
        return v

    tiles = [(b, ii) for b in range(NB) for ii in range(4)]
    dgs = {}

    def indexer(n):
        b, ii = tiles[n]
        bs = slice(b * 512, (b + 1) * 512)
        SC = SCs[n % 2]
        if ii == 0:
            dma(P, "sp", IQb.t[:], A["IQT"][:, :, bs].rearrange("r p t -> p r t"), [], [IQb], ld_g1)
            dma(P, "sp", SGb.t[:], A["SGN"][bs, :].rearrange("(i p) h -> p i h", p=128), [], [SGb], ld_g3)
        tq = slice(ii * 128, (ii + 1) * 128)
        dg = nxt("d", DG)
        for h in range(8):
            ts(P, "pool", dg.t[:, h, :], ident.t[:], SGb.t[:, ii, h:h + 1], 1.0, ALU.mult, ALU.mult,
               [ident, SGb], [dg])
        items = [(kb, pr) for kb in range(b + 1) for pr in range(4)]
        LA = 1
        hold = {}

        def front(it):
            kb, pr = it
            ks = slice(kb * 512, (kb + 1) * 512)
            if pr == 0:
                ikq = cnt["ik"] % 2
                cnt["ik"] += 1
                IKb = IKbs[ikq]
                dma(P, "sp", IKb.t[0:64, :], A["IKT"][:, ks], [], [IKb], IK_g[ikq])
                dma(P, "sp", IKb.t[64:128, :], A["IKT"][:, ks], [], [IKb], IK_g[ikq])
                hold[("ik", kb)] = IKb
            IKb = hold[("ik", kb)]
            Ls = []
            for half in range(2):
                ps_ = slice(half * 64, half * 64 + 64)
                L = nxt("l", pW)
                mm(P, L.t[:], IQb.t[ps_, pr, tq], IKb.t[ps_, :], True, True, [IQb, IKb], [L])
                Ls.append(L)
            Rs = []
            for half in range(2):
                R = nxt("r", Rt)
                act(P, R.t[:], Ls[half].t[:], AF.Relu, [Ls[half]], [R])
                Rs.append(R)
            hold[it] = Rs

        def back(it):
            kb, pr = it
            ks = slice(kb * 512, (kb + 1) * 512)
            Rs = hold.pop(it)
            for half in range(2):
                h = 2 * pr + half
                mm(P, SCp.t[:], dg.t[:, h, :], Rs[half].t[:], h == 0, h == 7, [dg, Rs[half]], [SCp])
            if pr == 3:
                cp(P, "act", SC.t[:, ks], SCp.t[:], [SCp], [SC])

        for idx in range(len(items) + LA):
            if idx < len(items):
                front(items[idx])
            if idx - LA >= 0:
                back(items[idx - LA])

    def post(n):
        b, ii = tiles[n]
        Nb = 512 * (b + 1)
        njb = 4 * (b + 1)
        bs = slice(b * 512, (b + 1) * 512)
        tq = slice(ii * 128, (ii + 1) * 128)
        SC = SCs[n % 2]
        red(P, M1.t[:], SC.t[:, 0:Nb], ALU.max, [SC], [M1], apply_absolute_value=True)
        tt(P, "pool", SC.t[:, bs], SC.t[:, bs], CM.t[:, ii, :], ALU.add, [SC, CM], [SC])
        ts(P, "dve", W0.t[:], M1.t[:], 2.002, 2e-6, ALU.mult, ALU.add, [M1], [W0])
        ts(P, "dve", STEP.t[:], CROW.t[:], W0.t[:], None, ALU.mult, ALU.bypass, [CROW, W0], [STEP])
        ts(P, "dve", NSTEP.t[:], STEP.t[:], -1.0, None, ALU.mult, ALU.bypass, [STEP], [NSTEP])
        ts(P, "dve", STEP2.t[:], STEP.t[:], 2.0, None, ALU.mult, ALU.bypass, [STEP], [STEP2])
        mset(P, "dve", MID.t[:], 0.0, [MID])
        for k in range(NIT):
            ts(P, "dve", MK.t[:, 0:Nb], SC.t[:, 0:Nb], MID.t[:], None, ALU.is_ge, ALU.add, [SC, MID], [MK, CNT],
               accum=CNT.t[:])
            if k + 1 < NIT:
                stt(P, DT.t[:], CNT.t[:], 255.5, STEP2.t[:, k + 1:k + 2], ALU.is_ge, ALU.mult, [CNT, STEP2], [DT])
                stt(P, MID.t[:], DT.t[:], NSTEP.t[:, k + 1:k + 2], MID.t[:], ALU.add, ALU.add, [DT, NSTEP, MID], [MID])
            else:
                stt(P, DT.t[:], CNT.t[:], 255.5, STEP.t[:, k:k + 1], ALU.is_lt, ALU.mult, [CNT, STEP], [DT])
                tt(P, "dve", LO.t[:], MID.t[:], DT.t[:], ALU.subtract, [MID, DT], [LO])
        ts(P, "dve", MK.t[:, 0:Nb], SC.t[:, 0:Nb], LO.t[:], None, ALU.is_ge, ALU.bypass, [SC, LO], [MK])
        for j0 in range(0, njb, 8):
            nn = min(8, njb - j0)
            for jj in range(nn):
                j = j0 + jj
                tr(P, pTr.t[:, jj * 128:(jj + 1) * 128], MK.t[:, j * 128:(j + 1) * 128], ident.t[:], [MK, ident], [pTr])
            cp(P, "act" if (cnt["m"] % 2 == 0) else "dve", MT.t[:, j0:j0 + nn, tq],
               pTr.t[:, 0:nn * 128].rearrange("p (j t) -> p j t", t=128), [pTr], [MT])
            cnt["m"] += 1

    def attention(b):
        njb = 4 * (b + 1)
        bs = slice(b * 512, (b + 1) * 512)
        dma(P, "sp", Qb.t[:], A["QT"][:, :, bs].rearrange("r p t -> p r t"), [], [Qb], ld_g2)
        items = [(r, j) for r in range(4) for j in range(njb)]
        LA = 1
        hold = {}

        def front(it):
            r, j = it
            STs = []
            for g in range(2):
                ps_ = slice(g * 64, g * 64 + 64)
                ST = nxt("l", pW)
                mm(P, ST.t[:], KT.t[ps_, j * 128:(j + 1) * 128], Qb.t[ps_, r, :], True, True, [KT, Qb], [ST])
                STs.append(ST)
            PTs = []
            for g in range(2):
                E = nxt("e", Et)
                act(P, E.t[:], STs[g].t[:], AF.Exp, [STs[g]], [E])
                PT = nxt("p", Pt)
                tt(P, "pool" if (cnt["p"] % 4 == 0) else "dve", PT.t[:], E.t[:], MT.t[:, j, :], ALU.mult, [E, MT], [PT])
                PTs.append(PT)
            hold[it] = PTs

        def back(it):
            r, j = it
            PTs = hold.pop(it)
            for g in range(2):
                pO = pOs[g]
                mm(P, pO.t[0:65, :], VA.t[:, j, g, :], PTs[g].t[:], j == 0, j == njb - 1, [VA, PTs[g]], [pO])
            if j == njb - 1:
                for g in range(2):
                    h = g * 4 + r
                    pO = pOs[g]
                    P.op("dve", lambda e, pO=pO: e.reciprocal(out=RC.t[64:65, :], in_=pO.t[64:65, :]), [pO.b], [RC.b])
                    BC = nxt("l", pW)
                    mm(P, BC.t[0:64, :], ONES.t[64:65, 0:64], RC.t[64:65, :], True, True, [ONES, RC], [BC])
                    cp(P, "act", OS.t[:], pO.t[0:64, :], [pO], [OS])
                    y = cnt["y"] % 2
                    cnt["y"] += 1
                    tt(P, "dve", YT[y].t[:], OS.t[:], BC.t[0:64, :], ALU.mult, [OS, BC], [YT[y]])
                    dma(P, "sp", A["YAT"][h, :, bs], YT[y].t[:], [YT[y]], [], YT_g[y])

        for idx in range(len(items) + LA):
            if idx < len(items):
                front(items[idx])
            if idx - LA >= 0:
                back(items[idx - LA])

    indexer(0)
    for n in range(len(tiles)):
        if n + 1 < len(tiles):
            indexer(n + 1)
        post(n)
        if tiles[n][1] == 3:
            attention(tiles[n][0])
    return {"sp": YT_g}


def mlstm_phase(cx, S, A, Wd, Cd):
    nc, P = cx.nc, cx.P
    NT = S // 128
    cg = cx.group(final=True)
    ident = load_ident(cx, Cd, cg)
    T2 = cx.sb("T2", [128, 128], F32)
    dma(P, "sp", T2.t[:], Cd["c_tri2"], [], [T2], cg)
    NEGM = cx.sb("NEGM", [128, 512], BF16)
    dma(P, "sp", NEGM.t[:], Cd["c_negm"], [], [NEGM], cg)
    HG = cx.sb("HG", [128, 512], F32)
    dma(P, "sp", HG.t[:], Wd["m_head_norm"].partition_broadcast(128), [], [HG], cg)
    NH = cx.sb("NH", [128, 8], F32)
    mset(P, "pool", NH.t[:], -0.5, [NH])
    CN = [cx.sb(f"CN{i}", [64, 4, 129], F32) for i in range(3)]
    mset(P, "pool", CN[0].t[:], 0.0, [CN[0]])
    R2 = range(2)
    LIF = [cx.sb(f"LIF{i}", [128, 8], F32) for i in R2]
    MQ = [cx.sb(f"MQ{i}", [64, 4, 128], BF16) for i in R2]
    MKt = [cx.sb(f"MKt{i}", [64, 4, 128], BF16) for i in R2]
    MKb = [cx.sb(f"MKb{i}", [128, 256], BF16) for i in R2]
    VA = [cx.sb(f"VA{i}", [128, 4, 129], BF16) for i in R2]
    MOt = [cx.sb(f"MOt{i}", [128, 512], BF16) for i in R2]
    lg = [[cx.group() for _ in range(6)] for _ in R2]
    for i in R2:
        mset(P, "pool", VA[i].t[:], 1.0, [VA[i]])
    LFm = cx.sb("LFm", [128, 4, 128], F32)
    bias = cx.sb("bias", [128, 4], F32)
    AT = cx.sb("AT", [128, 4, 128], F32)
    EB = cx.sb("EB", [128, 4, 128], F32)
    BL = cx.sb("BL", [128, 4], F32)
    WS = cx.sb("WS", [128, 4], F32)
    WT = cx.sb("WT", [128, 4, 128], BF16)
    QP = cx.sb("QP", [64, 4, 128], F32)
    KW = cx.sb("KW", [128, 4, 64], BF16)
    dn = cx.sb("dn", [128, 4], F32)
    rec = cx.sb("rec", [128, 4], F32)
    hh = cx.sb("hh", [128, 4, 128], F32)
    junk = cx.sb("junk", [128, 128], BF16)
    ssq = cx.sb("ssq", [128, 4], F32)
    rs4 = cx.sb("rs4", [128, 4], F32)
    yb = cx.sb("yb", [128, 512], BF16)
    YBt = [cx.sb(f"YBt{i}", [128, 4, 512], BF16) for i in R2]
    YB_g = [cx.group() for _ in R2]
    pX = cx.ps("pX", [128, 512], F32)
    pY = cx.ps("pY", [128, 512], F32)
    pZ = cx.ps("pZ", [128, 512], F32)
    pQ = cx.ps("pQ", [128, 512], F32)
    pD = cx.ps("pD", [128, 512], F32)
    pN = [cx.ps(f"pN{i}", [128, 512], F32) for i in R2]
    pTr = cx.ps("pTr", [128, 1024], BF16)

    def load(i):
        s_ = i % 2
        tsl = slice(i * 128, (i + 1) * 128)
        g = lg[s_]
        dma(P, "sp", LIF[s_].t[:], A["LIF"][tsl, :], [], [LIF[s_]], g[0])
        dma(P, "sp", MQ[s_].t[:], A["MQT"][:, :, tsl].rearrange("h d t -> d h t"), [], [MQ[s_]], g[1])
        dma(P, "sp", MKt[s_].t[:], A["MKT"][:, :, tsl].rearrange("h d t -> d h t"), [], [MKt[s_]], g[2])
        dma(P, "sp", MKb[s_].t[:], A["MK"][tsl, :], [], [MKb[s_]], g[3])
        dma(P, "sp", VA[s_].t[:, :, 0:128], A["MV"][tsl, :].rearrange("p (h d) -> p h d", h=4), [], [VA[s_]], g[4])
        dma(P, "sp", MOt[s_].t[:], A["MO"][tsl, :], [], [MOt[s_]], g[5])

    def tile(i):
        s_ = i % 2
        lif, mq, mkt, mkb, va, mo = LIF[s_], MQ[s_], MKt[s_], MKb[s_], VA[s_], MOt[s_]
        import os
        ML = int(os.environ.get('ML', '99'))
        cc, cm, cn = CN[(2 * i) % 3], CN[(2 * i + 1) % 3], CN[(2 * i + 2) % 3]
        cp(P, "pool", LFm.t[:], lif.t[:, 4:8].unsqueeze(2).broadcast_to([128, 4, 128]), [lif], [LFm])
        for h in range(4):
            mm(P, pX.t[:, h * 128:(h + 1) * 128], LFm.t[:, h, :], T2.t[:], True, True, [LFm, T2], [pX])
        mm(P, pY.t[:], ident.t[:], NEGM.t[:], True, False, [ident, NEGM], [pY])
        for h in range(4):
            mm(P, pY.t[:, h * 128:(h + 1) * 128], LFm.t[:, h, :], T2.t[:], False, h == 3, [LFm, T2], [pY])
        mm(P, pZ.t[:, 0:4], T2.t[:], lif.t[:, 4:8], True, True, [T2, lif], [pZ])
        tt(P, "dve", bias.t[:], lif.t[:, 0:4], pZ.t[:, 0:4], ALU.subtract, [lif, pZ], [bias])
        for h in range(4):
            act(P, AT.t[:, h, :], pY.t[:, h * 128:(h + 1) * 128], AF.Exp, [pY, bias], [AT], bias=bias.t[:, h:h + 1])
        act(P, EB.t[:], pX.t[:].rearrange("p (h t) -> p h t", h=4), AF.Exp, [pX], [EB])
        if ML < 2:
            return
        pXv = pX.t[:].rearrange("p (h t) -> p h t", h=4)
        cp(P, "dve", BL.t[0:64, :], pXv[0:64, :, 63], [pX], [BL])
        cp(P, "dve", BL.t[64:128, :], pXv[64:128, :, 127], [pX], [BL])
        tt(P, "dve", BL.t[:], BL.t[:], bias.t[:], ALU.add, [BL, bias], [BL])
        act(P, WS.t[:], BL.t[:], AF.Exp, [BL], [WS])
        for h in range(4):
            mm(P, pQ.t[:, h * 128:(h + 1) * 128], mkt.t[:, h, :], mq.t[:, h, :], True, True, [mkt, mq], [pQ])
        tt(P, "dve", WT.t[:], pQ.t[:].rearrange("p (h t) -> p h t", h=4), AT.t[:], ALU.mult, [pQ, AT], [WT])
        tt(P, "pool", QP.t[:], mq.t[:], EB.t[0:64, :, :], ALU.mult, [mq, EB], [QP])
        tt(P, "pool", KW.t[:], mkb.t[:].rearrange("p (h d) -> p h d", h=4),
           WS.t[:].unsqueeze(2).broadcast_to([128, 4, 64]), ALU.mult, [mkb, WS], [KW])
        if ML < 3:
            return
        for (rows, src, dst, col) in ((slice(0, 64), cc, cm, 63), (slice(64, 128), cm, cn, 127)):
            for hp in range(2):
                for c in range(2):
                    h = 2 * hp + c
                    mm(P, pD.t[0:64, c * 129:(c + 1) * 129], KW.t[rows, h, :], va.t[rows, h, :], True, True,
                       [KW, va], [pD])
                for c in range(2):
                    h = 2 * hp + c
                    stt(P, dst.t[:, h, :], src.t[:, h, :], EB.t[0:64, h, col:col + 1], pD.t[0:64, c * 129:(c + 1) * 129],
                        ALU.mult, ALU.add, [src, EB, pD], [dst])
        if ML < 4:
            return
        for h in range(4):
            bk = pN[h // 2]
            osl = slice((h % 2) * 129, (h % 2) * 129 + 129)
            mm(P, bk.t[:, osl], WT.t[:, h, :], va.t[:, h, :], True, False, [WT, va], [bk])
            mm(P, bk.t[0:64, osl], QP.t[:, h, 0:64], cc.t[:, h, :], False, False, [QP, cc], [bk])
            mm(P, bk.t[64:128, osl], QP.t[:, h, 64:128], cm.t[:, h, :], False, True, [QP, cm], [bk])
        if ML < 5:
            return
        for q in range(2):
            act(P, dn.t[:, 2 * q:2 * q + 2], pN[q].t[:, 0:258].rearrange("p (h c) -> p h c", c=129)[:, :, 128],
                AF.Abs, [pN[q]], [dn])
        ts(P, "dve", dn.t[:], dn.t[:], 1.0, None, ALU.max, ALU.bypass, [dn], [dn])
        P.op("dve", lambda e: e.reciprocal(out=rec.t[:], in_=dn.t[:]), [dn.b], [rec.b])
        for h in range(4):
            o0 = (h % 2) * 129
            act(P, hh.t[:, h, :], pN[h // 2].t[:, o0:o0 + 128], AF.Copy, [pN[h // 2], rec], [hh], scale=rec.t[:, h:h + 1])
        if ML < 6:
            return
        for h in range(4):
            act(P, junk.t[:], hh.t[:, h, :], AF.Square, [hh], [junk, ssq], accum_out=ssq.t[:, h:h + 1])
        rstd_ops(cx, ssq, 128, rs4, NH, [])
        if ML < 7:
            return
        tt(P, "dve", hh.t[:], hh.t[:], rs4.t[:].unsqueeze(2).broadcast_to([128, 4, 128]), ALU.mult, [hh, rs4], [hh])
        hf = hh.t[:].rearrange("p h d -> p (h d)")
        tt(P, "pool", hf, hf, HG.t[:], ALU.mult, [hh, HG], [hh])
        tt(P, "dve", yb.t[:], hf, mo.t[:], ALU.mult, [hh, mo], [yb])
        if ML < 8:
            return
        for c in range(4):
            tr(P, pTr.t[:, c * 128:(c + 1) * 128], yb.t[:, c * 128:(c + 1) * 128], ident.t[:], [yb, ident], [pTr])
        yt = YBt[(i // 4) % 2]
        cp(P, "act", yt.t[:, :, (i % 4) * 128:(i % 4 + 1) * 128], pTr.t[:, 0:512].rearrange("p (c t) -> p c t", c=4),
           [pTr], [yt])
        if i % 4 == 3:
            b0 = (i // 4) * 512
            dma(P, "sp", A["YBT"][:, :, b0:b0 + 512].rearrange("c p t -> p c t"), yt.t[:], [yt], [], YB_g[(i // 4) % 2])

    load(0)
    for i in range(NT):
        if i + 1 < NT:
            load(i + 1)
        tile(i)
    return {"sp": YB_g}


def merge_phase(cx, S, X1, X2, A, Wd, Cd):
    nc, P = cx.nc, cx.P
    TT = 256
    NT = S // TT
    WA = cx.sb("WA", [128, 4, D], BF16)
    WB = cx.sb("WB", [128, 4, D], BF16)
    WO = cx.sb("WO", [128, 8, D], BF16)
    stg = [cx.sb(f"stg{i}", [128, D], F32) for i in range(2)]
    stg_g = [cx.group() for _ in range(2)]
    k = 0
    for (src, dst, n) in ((Wd["w_proj_a"], WA, 4), (Wd["w_proj_b"], WB, 4), (Wd["w_out"], WO, 8)):
        for c in range(n):
            s_ = k % 2
            dma(P, "sp", stg[s_].t[:], src[c * 128:(c + 1) * 128, :], [], [stg[s_]], stg_g[s_])
            cp(P, ["act", "dve", "pool"][k % 3], dst.t[:, c, :], stg[s_].t[:], [stg[s_]], [dst])
            k += 1
    R2 = range(2)
    YA = [cx.sb(f"YA{i}", [128, 4, TT], BF16) for i in R2]
    YB = [cx.sb(f"YB{i}", [128, 4, TT], BF16) for i in R2]
    GA = [cx.sb(f"GA{i}", [128, 8, TT], BF16) for i in R2]
    GB = [cx.sb(f"GB{i}", [128, 8, TT], BF16) for i in R2]
    xt = [cx.sb(f"xt{i}", [128, 2, D], F32) for i in R2]
    lg = [[cx.group() for _ in range(5)] for _ in R2]
    st_g = [cx.group() for _ in R2]
    m1 = [cx.sb(f"m1{i}", [128, 512], F32) for i in R2]
    m2 = [cx.sb(f"m2{i}", [128, 512], F32) for i in R2]
    MG = [cx.sb(f"MG{i}", [128, 8, TT], BF16) for i in R2]
    pA = [cx.ps(f"pA{i}", [128, 512], F32) for i in R2]
    pB = [cx.ps(f"pB{i}", [128, 512], F32) for i in R2]
    pO = [cx.ps(f"pO{i}", [128, 512], F32) for i in R2]

    def load(i):
        s_ = i % 2
        tsl = slice(i * TT, (i + 1) * TT)
        g = lg[s_]
        dma(P, "sp", YA[s_].t[:], A["YAT"][:, :, tsl].rearrange("(c w) d t -> (w d) c t", w=2), [], [YA[s_]], g[0])
        dma(P, "sp", YB[s_].t[:], A["YBT"][:, :, tsl].rearrange("c p t -> p c t"), [], [YB[s_]], g[1])
        dma(P, "sp", GA[s_].t[:], A["GAT"][:, :, tsl].rearrange("c p t -> p c t"), [], [GA[s_]], g[2])
        dma(P, "sp", GB[s_].t[:], A["GBT"][:, :, tsl].rearrange("c p t -> p c t"), [], [GB[s_]], g[3])
        dma(P, "sp", xt[s_].t[:], X1[tsl, :].rearrange("(s p) d -> p s d", p=128), [], [xt[s_]], g[4])

    def tile(i):
        s_ = i % 2
        tsl = slice(i * TT, (i + 1) * TT)
        for dcp in range(4):
            q = dcp % 2
            for (W_, Y_, bk) in ((WA, YA[s_], pA[q]), (WB, YB[s_], pB[q])):
                for c2 in range(2):
                    dc = 2 * dcp + c2
                    for c in range(4):
                        mm(P, bk.t[:, c2 * TT:(c2 + 1) * TT], W_.t[:, c, dc * 128:(dc + 1) * 128], Y_.t[:, c, :],
                           c == 0, c == 3, [W_, Y_], [bk])
            gsl = slice(2 * dcp, 2 * dcp + 2)
            tt(P, "dve", m1[q].t[:].rearrange("p (c t) -> p c t", c=2), pA[q].t[:].rearrange("p (c t) -> p c t", c=2),
               GA[s_].t[:, gsl, :], ALU.mult, [pA[q], GA[s_]], [m1[q]])
            tt(P, "dve", m2[q].t[:].rearrange("p (c t) -> p c t", c=2), pB[q].t[:].rearrange("p (c t) -> p c t", c=2),
               GB[s_].t[:, gsl, :], ALU.mult, [pB[q], GB[s_]], [m2[q]])
            tt(P, "pool", MG[s_].t[:, gsl, :], m1[q].t[:].rearrange("p (c t) -> p c t", c=2),
               m2[q].t[:].rearrange("p (c t) -> p c t", c=2), ALU.add, [m1[q], m2[q]], [MG[s_]])
        n = 0
        for sub in range(2):
            for dh in range(2):
                bk = pO[n % 2]
                n += 1
                for dc in range(8):
                    mm(P, bk.t[:], MG[s_].t[:, dc, sub * 128:(sub + 1) * 128], WO.t[:, dc, dh * 512:(dh + 1) * 512],
                       dc == 0, dc == 7, [MG[s_], WO], [bk])
                xs = xt[s_].t[:, sub, dh * 512:(dh + 1) * 512]
                tt(P, "dve", xs, bk.t[:], xs, ALU.add, [bk, xt[s_]], [xt[s_]])
        dma(P, "pool", X2[tsl, :].rearrange("(s p) d -> p s d", p=128), xt[s_].t[:], [xt[s_]], [], st_g[s_])

    load(0)
    if NT > 1:
        load(1)
    for i in range(NT):
        tile(i)
        if i + 2 < NT:
            load(i + 2)
    return {"pool": st_g}


WSPEC = {
    "ffn1_norm": [1, D], "ffn1_w_gate": [D, DFF], "ffn1_w_up": [D, DFF], "ffn1_w_down": [DFF, D],
    "mix_norm": [1, D], "w_in": [D, DIN], "q_norm": [1, 64], "k_norm": [1, 64], "idx_k_norm": [1, 64],
    "conv_w": [128, 16], "conv_b": [128, 4], "w_mq": [128, 256], "w_mk": [128, 256], "b_i": [1, 4],
    "b_f": [1, 4], "m_head_norm": [1, 512], "w_proj_a": [512, D], "w_proj_b": [512, D], "w_out": [D, D],
    "ffn2_norm": [1, D], "ffn2_w_gate": [D, DFF], "ffn2_w_up": [D, DFF], "ffn2_w_down": [DFF, D],
}


def consts():
    c = {}
    c["c_ident"] = np.eye(128, dtype=np.float32).astype(ml_dtypes.bfloat16)
    cm = np.zeros((4, 128, 512), np.float32)
    for ii in range(4):
        for p in range(128):
            lim = ((ii * 128 + p) // 64 + 1) * 64
            cm[ii, p, lim:] = -1e30
    c["c_cmask"] = cm.astype(ml_dtypes.bfloat16)
    c["c_steps"] = np.tile((2.0 ** -(np.arange(NIT) + 1.0)).astype(np.float32)[None, :], (128, 1))
    j = np.arange(128)
    t2 = ((j[:, None] // 64 == j[None, :] // 64) & (j[:, None] <= j[None, :]))
    c["c_tri2"] = t2.astype(np.float32)
    c["c_negm"] = np.tile(np.where(t2, 0.0, NEG).astype(np.float32), (1, 4)).astype(ml_dtypes.bfloat16)
    return c


CSPEC = {"c_ident": ([128, 128], BF16), "c_cmask": ([4, 128, 512], BF16), "c_steps": ([128, NIT], F32),
         "c_tri2": ([128, 128], F32), "c_negm": ([128, 512], BF16)}


def build_nc(S, debug=False, upto=9):
    nc = bass.Bass("TRN2", target_bir_lowering=False)

    def din(name, shape, dt=F32):
        return nc.dram_tensor(name, list(shape), dt, kind="ExternalInput").ap()

    x = din("x", [S, D])
    Wd = {k: din(k, v) for k, v in WSPEC.items()}
    Cd = {k: din(k, v[0], v[1]) for k, v in CSPEC.items()}
    out = nc.dram_tensor("out", [S, D], F32, kind="ExternalOutput").ap()
    kind = "ExternalOutput" if debug else "Internal"

    def scr(name, shape, dt):
        return nc.dram_tensor(name, list(shape), dt, kind=kind).ap()

    X1 = scr("X1", [S, D], F32)
    X2 = scr("X2", [S, D], F32)
    A = {
        "QT": scr("QT", [4, 128, S], BF16), "KT": scr("KT", [128, S], BF16), "V": scr("V", [S, 128], BF16),
        "IQT": scr("IQT", [4, 128, S], BF16), "IKT": scr("IKT", [64, S], BF16), "SGN": scr("SGN", [S, 8], F32),
        "MV": scr("MV", [S, 512], BF16), "LIF": scr("LIF", [S, 8], F32), "MO": scr("MO", [S, 512], BF16),
        "GAT": scr("GAT", [8, 128, S], BF16), "GBT": scr("GBT", [8, 128, S], BF16),
        "MQT": scr("MQT", [4, 64, S], BF16), "MKT": scr("MKT", [4, 64, S], BF16), "MK": scr("MK", [S, 256], BF16),
        "YAT": scr("YAT", [8, 64, S], BF16), "YBT": scr("YBT", [4, 128, S], BF16),
    }
    ph = [
        ("f1", lambda cx: ffn_phase(cx, S, x, X1, Wd["ffn1_norm"], Wd["ffn1_w_gate"], Wd["ffn1_w_up"],
                                    Wd["ffn1_w_down"], Cd["c_ident"])),
        ("pj", lambda cx: proj_phase(cx, S, X1, A, Wd, Cd)),
        ("ds", lambda cx: dsa_phase(cx, S, A, Cd)),
        ("ml", lambda cx: mlstm_phase(cx, S, A, Wd, Cd)),
        ("mg", lambda cx: merge_phase(cx, S, X1, X2, A, Wd, Cd)),
        ("f2", lambda cx: ffn_phase(cx, S, X2, out, Wd["ffn2_norm"], Wd["ffn2_w_gate"], Wd["ffn2_w_up"],
                                    Wd["ffn2_w_down"], Cd["c_ident"])),
    ]
    import os
    lo_ = int(os.environ.get('FROM', '0'))
    for n, (tag, fn) in enumerate(ph):
        if lo_ <= n < upto:
            run_phase(nc, tag, fn, debug)
    return nc


def layout_weights(inputs):
    sh = {}
    for k, v in WSPEC.items():
        a = np.asarray(inputs[k], dtype=np.float32)[0]
        if k == "conv_w":
            a = a.reshape(4, 4, 128).transpose(2, 1, 0)
        elif k == "conv_b":
            a = a.reshape(4, 128).T
        elif k in ("w_mq", "w_mk"):
            a = a.transpose(1, 0, 2)
        sh[k] = np.ascontiguousarray(a).reshape(v)
    return sh


_NC_CACHE = {}


def kernel(**inputs):
    x = np.ascontiguousarray(np.asarray(inputs["x"], dtype=np.float32))
    B, S, _ = x.shape
    if S not in _NC_CACHE:
        _NC_CACHE[S] = build_nc(S)
    nc = _NC_CACHE[S]
    shared = layout_weights(inputs)
    shared.update(consts())
    in_maps = []
    for b in range(B):
        m = dict(shared)
        m["x"] = x[b]
        in_maps.append(m)
    res = run_bass_kernel_spmd(nc, in_maps, core_ids=list(range(B)))
    return np.stack([np.asarray(r["out"], dtype=np.float32) for r in res.results], axis=0)
```

```python
import numpy as np
from contextlib import ExitStack
import ml_dtypes
import concourse.bass as bass
import concourse.mybir as mybir
from concourse.bass_utils import run_bass_kernel_spmd

F32 = mybir.dt.float32
BF16 = mybir.dt.bfloat16
ALU = mybir.AluOpType
AF = mybir.ActivationFunctionType
AX = mybir.AxisListType

D = 1024
DFF = 2816
NFF = DFF // 128
DIN = 4944
EPS = 1e-6
NEG = -30000.0


class Buf:
    __slots__ = ("name", "w", "r", "psum")

    def __init__(self, name):
        self.name = name
        self.w = {}
        self.r = {}
        self.psum = False


class Group:
    def __init__(self, sem, final=False):
        self.sem = sem
        self.n = 0
        self.final = final


class Prog:
    ENG = ("pe", "act", "dve", "pool", "sp")

    def __init__(self, nc):
        self.nc = nc
        self.ops = {e: [] for e in self.ENG}

    def op(self, eng, fn, reads=(), writes=(), group=None):
        ops = self.ops[eng]
        idx = len(ops)
        rec = {"fn": fn, "waits": {}, "sig": False, "group": group}
        is_dma = group is not None

        def need(key, val, raw):
            if is_dma and key[0] == "g" and key[1] is group:
                return
            if key[0] == "e" and key[1] == eng and not is_dma:
                if eng == "pe" or not raw:
                    return
            w = rec["waits"]
            if w.get(key, -1) < val:
                w[key] = val
            if key[0] == "e":
                self.ops[key[1]][val]["sig"] = True

        for b in reads:
            for k, v in b.w.items():
                need(k, v, True)
            if b.psum:
                for k, v in b.r.items():
                    need(k, v, False)
        for b in writes:
            for k, v in b.w.items():
                need(k, v, False)
            for k, v in b.r.items():
                need(k, v, False)
        if is_dma:
            group.n += 1
            me = (("g", group), group.n)
        else:
            me = (("e", eng), idx)
        for b in reads:
            if b.r.get(me[0], -1) < me[1]:
                b.r[me[0]] = me[1]
        for b in writes:
            b.w = {me[0]: me[1]}
            b.r = {}
        ops.append(rec)
        return rec

    def emit(self, block, sems, final_waits, base=None):
        if base is None:
            base = {e: 0 for e in self.ENG}
        sigcount = {}
        for e in self.ENG:
            c = base[e]
            lst = []
            for rec in self.ops[e]:
                if rec["sig"]:
                    c += 1
                lst.append(c)
            sigcount[e] = lst

        def run(engname, engine):
            waited = {}
            for rec in self.ops[engname]:
                for key, val in rec["waits"].items():
                    if key[0] == "e":
                        sem = sems[key[1]]
                        v = sigcount[key[1]][val]
                    else:
                        g = key[1]
                        sem = g.sem
                        v = 16 * (g.n if g.final else val)
                    if waited.get(id(sem), -1) >= v:
                        continue
                    waited[id(sem)] = v
                    engine.wait_ge(sem, v)
                ins = rec["fn"](engine)
                if rec["group"] is not None:
                    ins.then_inc(rec["group"].sem, 16)
                elif rec["sig"]:
                    ins.then_inc(sems[engname], 1)
            for g in final_waits.get(engname, ()):
                engine.wait_ge(g.sem, 16 * g.n)

        @block.tensor
        def _(e):
            run("pe", e)

        @block.scalar
        def _(e):
            run("act", e)

        @block.vector
        def _(e):
            run("dve", e)

        @block.gpsimd
        def _(e):
            run("pool", e)

        @block.sync
        def _(e):
            run("sp", e)

        for e in self.ENG:
            base[e] = sigcount[e][-1] if sigcount[e] else base[e]


class _Alloc:
    def __init__(self, nc, es):
        self._nc = nc
        self._es = es

    def alloc_sbuf_tensor(self, name, shape, dt):
        return self._es.enter_context(self._nc.sbuf_tensor(name, list(shape), dt))

    def alloc_psum_tensor(self, name, shape, dt):
        return self._es.enter_context(self._nc.psum_tensor(name, list(shape), dt))


class TL:
    def __init__(self, t, name):
        self.t = t
        self.b = Buf(name)


class Ctx:
    def __init__(self, nc, es, tag, debug=False):
        self.rnc = nc
        self.nc = _Alloc(nc, es)
        self.es = es
        self.tag = tag
        self.P = Prog(nc)
        self.groups = []
        self.debug = debug

    def group(self, final=False):
        sem = self.rnc.alloc_semaphore(f"{self.tag}g{len(self.groups)}")
        g = Group(sem, final)
        self.groups.append(g)
        return g

    def sb(self, name, shape, dt):
        return TL(self.nc.alloc_sbuf_tensor(self.tag + name, list(shape), dt), name)

    def ps(self, name, shape, dt):
        t = TL(self.nc.alloc_psum_tensor(self.tag + name, list(shape), dt), name)
        t.b.psum = True
        return t


_GSTATE = {}


def run_phase(nc, tag, fn, debug=False):
    with ExitStack() as es:
        cx = Ctx(nc, es, tag, debug)
        import os
        for _i in range(int(os.environ.get('DUMMYSEM', '0'))):
            es.enter_context(nc.semaphore(f"{tag}dummy{_i}"))
        finals = fn(cx)
        st = _GSTATE.setdefault(id(nc), {})
        if "sems" not in st:
            st["sems"] = {e: nc.alloc_semaphore(f"s_{e}") for e in Prog.ENG}
            st["base"] = {e: 0 for e in Prog.ENG}
        with nc.Block() as block:
            cx.P.emit(block, st["sems"], finals, st["base"])


def _rr(lst, i):
    return lst[i % len(lst)]


def ffn_phase(cx, S, xin, xout, gvec, wg, wu, wd, identd):
    nc, P = cx.nc, cx.P
    tag = cx.tag
    xin_buf, xout_buf = Buf("xin"), Buf("xout")
    ident = nc.alloc_sbuf_tensor(f"{tag}ident", [128, 128], BF16)
    identb = Buf("ident")
    cg0 = cx.group(final=True)
    P.op("sp", lambda e: e.dma_start(out=ident[:], in_=identd), writes=[identb], group=cg0)
    TT = 256
    NT = S // TT
    WG = nc.alloc_sbuf_tensor(f"{tag}WG", [128, 8, DFF], BF16)
    WU = nc.alloc_sbuf_tensor(f"{tag}WU", [128, 8, DFF], BF16)
    WD = nc.alloc_sbuf_tensor(f"{tag}WD", [128, NFF, D], BF16)
    stg = [nc.alloc_sbuf_tensor(f"{tag}stg{i}", [128, DFF // 2], F32) for i in range(2)]
    stg_b = [Buf(f"stg{i}") for i in range(2)]
    stg_g = [cx.group() for _ in range(2)]
    wbuf = Buf("W")
    wgb = [Buf(f"WG{h}") for h in range(2)]
    wub = [Buf(f"WU{h}") for h in range(2)]
    wdb = [Buf(f"WD{f}") for f in range(NFF)]
    G = nc.alloc_sbuf_tensor(f"{tag}G", [128, D], F32)
    gb = Buf("G")
    cg = cx.group(final=True)
    P.op("sp", lambda e: e.dma_start(out=G[:], in_=gvec.partition_broadcast(128)), writes=[gb], group=cg)
    nhalf = nc.alloc_sbuf_tensor(f"{tag}nh", [128, 2], F32)
    nhb = Buf("nh")
    P.op("pool", lambda e: e.memset(nhalf[:], -0.5), writes=[nhb])

    conv_eng = ["act", "dve", "pool"]
    k = 0

    def load_conv(src_ap, dst_ap, width, wb):
        nonlocal k
        s = k % 2
        st, sb_, sg = stg[s], stg_b[s], stg_g[s]
        P.op("sp", lambda e: e.dma_start(out=st[:, 0:width], in_=src_ap), writes=[sb_], group=sg)
        ce = conv_eng[k % 3]
        if ce == "act":
            P.op("act", lambda e: e.copy(out=dst_ap, in_=st[:, 0:width]), reads=[sb_], writes=[wb])
        else:
            P.op(ce, lambda e: e.tensor_copy(out=dst_ap, in_=st[:, 0:width]), reads=[sb_], writes=[wb])
        k += 1

    HF = DFF // 2
    for hh in range(2):
        for kc in range(8):
            load_conv(wg[kc * 128:(kc + 1) * 128, hh * HF:(hh + 1) * HF], WG[:, kc, hh * HF:(hh + 1) * HF], HF, wgb[hh])
            load_conv(wu[kc * 128:(kc + 1) * 128, hh * HF:(hh + 1) * HF], WU[:, kc, hh * HF:(hh + 1) * HF], HF, wub[hh])
    for f in range(NFF):
        load_conv(wd[f * 128:(f + 1) * 128, :], WD[:, f, :], D, wdb[f])

    xt = [nc.alloc_sbuf_tensor(f"{tag}xt{i}", [128, 2, D], F32) for i in range(2)]
    xt_b = [Buf(f"xt{i}") for i in range(2)]
    xt_g = [cx.group() for _ in range(2)]
    st_g = [cx.group() for _ in range(2)]
    junk = nc.alloc_sbuf_tensor(f"{tag}junk", [128, D], BF16)
    junk_b = Buf("junk")
    ss = [nc.alloc_sbuf_tensor(f"{tag}ss{i}", [128, 2], F32) for i in range(2)]
    ss_b = [Buf(f"ss{i}") for i in range(2)]
    ms = [nc.alloc_sbuf_tensor(f"{tag}ms{i}", [128, 2], F32) for i in range(2)]
    ms_b = [Buf(f"ms{i}") for i in range(2)]
    rs = [nc.alloc_sbuf_tensor(f"{tag}rs{i}", [128, 2], F32) for i in range(2)]
    rs_b = [Buf(f"rs{i}") for i in range(2)]
    hb = [nc.alloc_sbuf_tensor(f"{tag}h{i}", [128, 2, D], BF16) for i in range(2)]
    hb_b = [Buf(f"h{i}") for i in range(2)]
    hT = [nc.alloc_sbuf_tensor(f"{tag}hT{i}", [128, 8, TT], BF16) for i in range(2)]
    hT_b = [Buf(f"hT{i}") for i in range(2)]
    actT = [nc.alloc_sbuf_tensor(f"{tag}actT{i}", [128, NFF, TT], BF16) for i in range(2)]
    actT_b = [Buf(f"actT{i}") for i in range(2)]
    sg = [nc.alloc_sbuf_tensor(f"{tag}sg{i}", [128, 512], F32) for i in range(2)]
    sg_b = [Buf(f"sg{i}") for i in range(2)]
    pT = [nc.alloc_psum_tensor(f"{tag}pT{i}", [128, 1024], BF16) for i in range(2)]
    pT_b = [Buf(f"pT{i}") for i in range(2)]
    pG = [nc.alloc_psum_tensor(f"{tag}pG{i}", [128, 512], F32) for i in range(2)]
    pG_b = [Buf(f"pG{i}") for i in range(2)]
    pU = [nc.alloc_psum_tensor(f"{tag}pU{i}", [128, 512], F32) for i in range(2)]
    pU_b = [Buf(f"pU{i}") for i in range(2)]
    pD = [nc.alloc_psum_tensor(f"{tag}pD{i}", [128, 512], F32) for i in range(2)]
    pD_b = [Buf(f"pD{i}") for i in range(2)]

    def stage_load(i):
        s = i % 2
        t0 = i * TT
        P.op("sp", lambda e: e.dma_start(
            out=xt[s][:], in_=xin[t0:t0 + TT, :].rearrange("(s p) d -> p s d", p=128)),
            reads=[xin_buf], writes=[xt_b[s]], group=xt_g[s])

    def stage_norm(i):
        s = i % 2
        for sub in range(2):
            P.op("act", lambda e, sub=sub: e.activation(
                out=junk[:], in_=xt[s][:, sub, :], func=AF.Square, accum_out=ss[s][:, sub:sub + 1]),
                reads=[xt_b[s]], writes=[junk_b, ss_b[s]])
        P.op("dve", lambda e: e.tensor_scalar(out=ms[s][:], in0=ss[s][:], scalar1=1.0 / D, scalar2=EPS,
                                              op0=ALU.mult, op1=ALU.add), reads=[ss_b[s]], writes=[ms_b[s]])
        P.op("pool", lambda e: e.tensor_tensor(out=rs[s][:], in0=ms[s][:], in1=nhalf[:], op=ALU.pow),
             reads=[ms_b[s], nhb], writes=[rs_b[s]])
        for sub in range(2):
            P.op("dve", lambda e, sub=sub: e.scalar_tensor_tensor(
                out=hb[s][:, sub, :], in0=xt[s][:, sub, :], scalar=rs[s][:, sub:sub + 1], in1=G[:],
                op0=ALU.mult, op1=ALU.mult), reads=[xt_b[s], rs_b[s], gb], writes=[hb_b[s]])

    def stage_T(i):
        s = i % 2
        for half in range(2):
            pb, pbb = pT[half], pT_b[half]
            for kk in range(4):
                kc = half * 4 + kk
                for sub in range(2):
                    P.op("pe", lambda e, kc=kc, sub=sub, kk=kk, pb=pb: e.transpose(
                        out=pb[:, kk * TT + sub * 128: kk * TT + (sub + 1) * 128],
                        in_=hb[s][:, sub, kc * 128:(kc + 1) * 128], identity=ident[:]),
                        reads=[hb_b[s], identb], writes=[pbb])
            if half == 0:
                P.op("act", lambda e, pb=pb: e.copy(out=hT[s][:, 0:4, :], in_=pb[:].rearrange("p (k t) -> p k t", k=4)),
                     reads=[pbb], writes=[hT_b[s]])
            else:
                P.op("dve", lambda e, pb=pb: e.tensor_copy(out=hT[s][:, 4:8, :], in_=pb[:].rearrange("p (k t) -> p k t", k=4)),
                     reads=[pbb], writes=[hT_b[s]])

    def stage_GU(i):
        s = i % 2
        for f2 in range(NFF // 2):
            q = f2 % 2
            for (W_, pp, ppb, wbs) in ((WG, pG[q], pG_b[q], wgb), (WU, pU[q], pU_b[q], wub)):
                for c in range(2):
                    f = 2 * f2 + c
                    for kc in range(8):
                        P.op("pe", lambda e, W_=W_, pp=pp, c=c, f=f, kc=kc: e.matmul(
                            out=pp[:, c * TT:(c + 1) * TT], lhsT=W_[:, kc, f * 128:(f + 1) * 128],
                            rhs=hT[s][:, kc, :], start=(kc == 0), stop=(kc == 7)),
                            reads=[wbs[0 if f < NFF // 2 else 1], hT_b[s]], writes=[ppb])
            P.op("act", lambda e, q=q: e.activation(out=sg[q][:], in_=pG[q][:], func=AF.Silu),
                 reads=[pG_b[q]], writes=[sg_b[q]])
            P.op("dve", lambda e, q=q, f2=f2: e.tensor_tensor(
                out=actT[s][:, 2 * f2:2 * f2 + 2, :], in0=sg[q][:].rearrange("p (c t) -> p c t", c=2),
                in1=pU[q][:].rearrange("p (c t) -> p c t", c=2), op=ALU.mult),
                reads=[sg_b[q], pU_b[q]], writes=[actT_b[s]])

    def stage_D(i):
        s = i % 2
        t0 = i * TT
        n = 0
        for sub in range(2):
            for dh in range(2):
                q = n % 2
                n += 1
                for f in range(NFF):
                    P.op("pe", lambda e, q=q, f=f, sub=sub, dh=dh: e.matmul(
                        out=pD[q][:], lhsT=actT[s][:, f, sub * 128:(sub + 1) * 128],
                        rhs=WD[:, f, dh * 512:(dh + 1) * 512], start=(f == 0), stop=(f == NFF - 1)),
                        reads=[wdb[f], actT_b[s]], writes=[pD_b[q]])
                P.op("dve", lambda e, q=q, sub=sub, dh=dh: e.scalar_tensor_tensor(
                    out=xt[s][:, sub, dh * 512:(dh + 1) * 512], in0=pD[q][:], scalar=0.5,
                    in1=xt[s][:, sub, dh * 512:(dh + 1) * 512], op0=ALU.mult, op1=ALU.add),
                    reads=[pD_b[q], xt_b[s]], writes=[xt_b[s]])
        P.op("pool", lambda e: e.dma_start(
            out=xout[t0:t0 + TT, :].rearrange("(s p) d -> p s d", p=128), in_=xt[s][:]),
            reads=[xt_b[s]], writes=[xout_buf], group=st_g[s])

    stage_load(0)
    if NT > 1:
        stage_load(1)
    stage_norm(0)
    stage_T(0)
    for i in range(NT):
        stage_GU(i)
        if i + 1 < NT:
            stage_norm(i + 1)
            stage_T(i + 1)
        stage_D(i)
        if i + 2 < NT:
            stage_load(i + 2)
    return {"pool": st_g}


def _b(L):
    return [x.b for x in L]


def mm(P, out, lhsT, rhs, st, sp, R, W):
    P.op("pe", lambda e: e.matmul(out=out, lhsT=lhsT, rhs=rhs, start=st, stop=sp), _b(R), _b(W))


def tr(P, out, in_, ident, R, W):
    P.op("pe", lambda e: e.transpose(out=out, in_=in_, identity=ident), _b(R), _b(W))


def act(P, out, in_, func, R, W, **kw):
    P.op("act", lambda e: e.activation(out=out, in_=in_, func=func, **kw), _b(R), _b(W))


def ts(P, eng, out, in0, s1, s2, op0, op1, R, W, accum=None):
    if accum is None:
        P.op(eng, lambda e: e.tensor_scalar(out=out, in0=in0, scalar1=s1, scalar2=s2, op0=op0, op1=op1), _b(R), _b(W))
    else:
        P.op(eng, lambda e: e.tensor_scalar(out=out, in0=in0, scalar1=s1, scalar2=s2, op0=op0, op1=op1,
                                            accum_out=accum), _b(R), _b(W))


def tt(P, eng, out, in0, in1, op, R, W):
    P.op(eng, lambda e: e.tensor_tensor(out=out, in0=in0, in1=in1, op=op), _b(R), _b(W))


def stt(P, out, in0, scalar, in1, op0, op1, R, W):
    P.op("dve", lambda e: e.scalar_tensor_tensor(out=out, in0=in0, scalar=scalar, in1=in1, op0=op0, op1=op1),
         _b(R), _b(W))


def cp(P, eng, out, in_, R, W):
    if eng == "act":
        P.op("act", lambda e: e.copy(out=out, in_=in_), _b(R), _b(W))
    else:
        P.op(eng, lambda e: e.tensor_copy(out=out, in_=in_), _b(R), _b(W))


def red(P, out, in_, op, R, W, **kw):
    P.op("dve", lambda e: e.tensor_reduce(out=out, in_=in_, axis=AX.X, op=op, **kw), _b(R), _b(W))


def dma(P, eng, out, in_, R, W, g, **kw):
    P.op(eng, lambda e: e.dma_start(out=out, in_=in_, **kw), _b(R), _b(W), group=g)


def mset(P, eng, ap, val, W):
    P.op(eng, lambda e: e.memset(ap, val), [], _b(W))


def rstd_ops(cx, ssq, n, rs, nh, R):
    P = cx.P
    ts(P, "dve", ssq.t[:], ssq.t[:], 1.0 / n, EPS, ALU.mult, ALU.add, R + [ssq], [ssq])
    tt(P, "pool", rs.t[:], ssq.t[:], nh.t[:, 0:ssq.t.shape[1]], ALU.pow, [ssq, nh], [rs])


IDX_SCALE = (64 ** -0.5) * (8 ** -0.5)
ATT_SCALE = 64 ** -0.5
MQ_SCALE = 64 ** -0.5


def load_ident(cx, Cd, cg):
    ident = cx.sb("ident", [128, 128], BF16)
    dma(cx.P, "sp", ident.t[:], Cd["c_ident"], [], [ident], cg)
    return ident


def norm_transpose(cx, xt, G, ident, nh, hb, hT, pT, ss, rs, junk, ntok_sub=2):
    P = cx.P
    for sub in range(ntok_sub):
        act(P, junk.t[:], xt.t[:, sub, :], AF.Square, [xt], [junk, ss], accum_out=ss.t[:, sub:sub + 1])
    rstd_ops(cx, ss, D, rs, nh, [])
    for sub in range(ntok_sub):
        stt(P, hb.t[:, sub, :], xt.t[:, sub, :], rs.t[:, sub:sub + 1], G.t[:], ALU.mult, ALU.mult,
            [xt, rs, G], [hb])
    for half in range(2):
        pb = pT[half]
        for kk in range(4):
            kc = half * 4 + kk
            for sub in range(ntok_sub):
                tr(P, pb.t[:, kk * 256 + sub * 128: kk * 256 + (sub + 1) * 128],
                   hb.t[:, sub, kc * 128:(kc + 1) * 128], ident.t[:], [hb, ident], [pb])
        cp(P, "act" if half == 0 else "dve", hT.t[:, half * 4:half * 4 + 4, :],
           pb.t[:].rearrange("p (k t) -> p k t", k=4), [pb], [hT])


def proj_phase(cx, S, X1, A, Wd, Cd):
    nc, P = cx.nc, cx.P
    TT = 256
    NT = S // TT
    cg = cx.group(final=True)
    ident = load_ident(cx, Cd, cg)
    WIN = cx.sb("WIN", [128, 8, DIN], BF16)
    stg = [cx.sb(f"stg{i}", [128, 824], F32) for i in range(2)]
    stg_g = [cx.group() for _ in range(2)]
    k = 0
    for kc in range(8):
        for part in range(6):
            s_ = k % 2
            c0 = part * 824
            dma(P, "sp", stg[s_].t[:], Wd["w_in"][kc * 128:(kc + 1) * 128, c0:c0 + 824], [], [stg[s_]], stg_g[s_])
            cp(P, ["act", "dve", "pool"][k % 3], WIN.t[:, kc, c0:c0 + 824], stg[s_].t[:], [stg[s_]], [WIN])
            k += 1
    import os
    SL = int(os.environ.get('SL', '99'))
    G = cx.sb("G", [128, D], F32)
    dma(P, "sp", G.t[:], Wd["mix_norm"].partition_broadcast(128), [], [G], cg)
    GQ = cx.sb("GQ", [128, 512], F32)
    for h in range(8):
        if SL >= 2:
            dma(P, "sp", GQ.t[:, h * 64:(h + 1) * 64], Wd["q_norm"].partition_broadcast(128), [], [GQ], cg)
    if SL >= 2:
        ts(P, "dve", GQ.t[:], GQ.t[:], ATT_SCALE, None, ALU.mult, ALU.bypass, [GQ], [GQ])
    GK = cx.sb("GK", [128, 128], F32)
    for h in range(2):
        if SL >= 3:
            dma(P, "sp", GK.t[:, h * 64:(h + 1) * 64], Wd["k_norm"].partition_broadcast(128), [], [GK], cg)
    GI = cx.sb("GI", [128, 64], F32)
    if SL >= 3:
        dma(P, "sp", GI.t[:], Wd["idx_k_norm"].partition_broadcast(128), [], [GI], cg)
    CW = cx.sb("CW", [128, 4, 4], F32)
    CB = cx.sb("CB", [128, 4], F32)
    if SL >= 4:
        dma(P, "sp", CB.t[:], Wd["conv_b"], [], [CB], cg)
    if SL >= 4:
      dma(P, "sp", CW.t[:], Wd["conv_w"].rearrange("p (c j) -> p c j", c=4), [], [CW], cg)
    wst = cx.sb("wst", [128, 2, 4, 64], F32)
    WM = cx.sb("WM", [128, 2, 4, 64], BF16)
    if SL >= 5:
        dma(P, "sp", wst.t[:, 0], Wd["w_mq"].rearrange("c (h d) -> c h d", h=4), [], [wst], cg)
        dma(P, "sp", wst.t[:, 1], Wd["w_mk"].rearrange("c (h d) -> c h d", h=4), [], [wst], cg)
        cp(P, "dve", WM.t[:], wst.t[:], [wst], [WM])
    BIF = cx.sb("BIF", [128, 8], F32)
    if SL >= 6:
        dma(P, "sp", BIF.t[:, 0:4], Wd["b_i"].partition_broadcast(128), [], [BIF], cg)
        dma(P, "sp", BIF.t[:, 4:8], Wd["b_f"].partition_broadcast(128), [], [BIF], cg)
    NH = cx.sb("NH", [128, 8], F32)
    mset(P, "pool", NH.t[:], -0.5, [NH])

    R2 = range(2)
    xt = [cx.sb(f"xt{i}", [128, 2, D], F32) for i in R2]
    xt_g = [cx.group() for _ in R2]
    junk = cx.sb("junk", [128, D], BF16)
    ss = [cx.sb(f"ss{i}", [128, 2], F32) for i in R2]
    rs = [cx.sb(f"rs{i}", [128, 2], F32) for i in R2]
    hb = [cx.sb("hb0", [128, 2, D], BF16)] * 2
    hT = [cx.sb(f"hT{i}", [128, 8, TT], BF16) for i in R2]
    pT = [cx.ps(f"pT{i}", [128, 1024], BF16) for i in R2]
    pX = [cx.ps(f"pX{i}", [128, 1024], BF16) for i in R2]
    pM = [cx.ps(f"pM{i}", [128, 512], F32) for i in range(4)]
    nbank = [0]

    def bank():
        b = pM[nbank[0] % 4]
        nbank[0] += 1
        return b

    sqt = [cx.sb(f"sqt{i}", [128, 512], F32) for i in R2]
    t1 = [cx.sb(f"t1{i}", [128, 512], F32) for i in R2]
    qn = [cx.sb(f"qn{i}", [128, 512], BF16) for i in R2]
    kn = [cx.sb(f"kn{i}", [128, 128], BF16) for i in R2]
    ikn = [cx.sb(f"ikn{i}", [128, 64], BF16) for i in R2]
    iqs = [cx.sb(f"iqs{i}", [128, 512], BF16) for i in R2]
    s8 = [cx.sb(f"s8{i}", [128, 8], F32) for i in R2]
    r8 = [cx.sb(f"r8{i}", [128, 8], F32) for i in R2]
    s2 = [cx.sb(f"s2{i}", [128, 2], F32) for i in R2]
    r2 = [cx.sb(f"r2{i}", [128, 2], F32) for i in R2]
    s1 = [cx.sb(f"s1{i}", [128, 1], F32) for i in R2]
    r1 = [cx.sb(f"r1{i}", [128, 1], F32) for i in R2]
    absw = [cx.sb(f"absw{i}", [128, 8], F32) for i in R2]
    zt = [cx.sb(f"zt{i}", [128, 8], F32) for i in R2]
    et = [cx.sb(f"et{i}", [128, 4], F32) for i in R2]
    def outs(name, shape, dt):
        return [cx.sb(f"{name}{i}", shape, dt) for i in R2], [cx.group() for _ in R2]
    QTt, QT_g = outs("QTt", [128, 4, TT], BF16)
    KTt, KT_g = outs("KTt", [128, TT], BF16)
    IQTt, IQT_g = outs("IQTt", [128, 4, TT], BF16)
    IKTt, IKT_g = outs("IKTt", [64, TT], BF16)
    vb, V_g = outs("vb", [128, 2, 128], BF16)
    sgn, SGN_g = outs("sgn", [128, 2, 8], F32)
    mvb, MV_g = outs("mvb", [128, 2, 512], BF16)
    lif, LIF_g = outs("lif", [128, 2, 8], F32)
    mob, MO_g = outs("mob", [128, 2, 512], BF16)
    gat, GAT_g = outs("gat", [128, 8, TT], BF16)
    gbt, GBT_g = outs("gbt", [128, 8, TT], BF16)
    mqt, MQT_g = outs("mqt", [64, 4, TT], BF16)
    mkt, MKT_g = outs("mkt", [64, 4, TT], BF16)
    mkb, MK_g = outs("mkb", [128, 2, 256], BF16)
    XM = [cx.sb(f"XM{i}", [128, 4, TT + 3], F32) for i in R2]
    mset(P, "pool", XM[0].t[:, :, 0:3], 0.0, [XM[0]])
    acc = [cx.sb("acc0", [128, 4, TT], F32)] * 2
    sig = [cx.sb("sig0", [128, 4, TT], F32)] * 2
    xct = [cx.sb(f"xct{i}", [128, 4, TT], BF16) for i in R2]
    X1b = TL(None, "X1")
    allg = []

    def load(i):
        s_ = i % 2
        t0 = i * TT
        dma(P, "sp", xt[s_].t[:], X1[t0:t0 + TT, :].rearrange("(s p) d -> p s d", p=128), [X1b], [xt[s_]], xt_g[s_])

    def tile(i):
        import os
        LV = int(os.environ.get('LV', '99'))
        if LV < 1:
            return
        s_ = i % 2
        t0 = i * TT
        tsl = slice(t0, t0 + TT)
        H = hT[s_]

        def tok_group(sub, c0, width):
            b = bank()
            for kc in range(8):
                mm(P, b.t[:, 0:width], H.t[:, kc, sub * 128:(sub + 1) * 128], WIN.t[:, kc, c0:c0 + width],
                   kc == 0, kc == 7, [H, WIN], [b])
            return b

        pq, pi = pX[0], pX[1]
        ssl = None
        for sub in range(2):
            u = sub
            if LV < 2:
                continue
            b = tok_group(sub, 0, 512)
            act(P, sqt[u].t[:], b.t[:], AF.Square, [b], [sqt[u]])
            red(P, s8[u].t[:], sqt[u].t[:].rearrange("p (h d) -> p h d", h=8), ALU.add, [sqt[u]], [s8[u]])
            rstd_ops(cx, s8[u], 64, r8[u], NH, [])
            tt(P, "dve", t1[u].t[:].rearrange("p (h d) -> p h d", h=8), b.t[:].rearrange("p (h d) -> p h d", h=8),
               r8[u].t[:].unsqueeze(2).broadcast_to([128, 8, 64]), ALU.mult, [b, r8[u]], [t1[u]])
            tt(P, "pool", qn[u].t[:].rearrange("p (r g d) -> p g r d", r=4, g=2),
               t1[u].t[:].rearrange("p (g r d) -> p g r d", g=2, r=4),
               GQ.t[:].rearrange("p (g r d) -> p g r d", g=2, r=4), ALU.mult, [t1[u], GQ], [qn[u]])
            ssl = slice(sub * 128, (sub + 1) * 128)
            for r in range(4):
                tr(P, pq.t[:, r * 128:(r + 1) * 128], qn[u].t[:, r * 128:(r + 1) * 128],
                   ident.t[:], [qn[u], ident], [pq])
            cp(P, "act", QTt[s_].t[:, :, ssl], pq.t[:, 0:512].rearrange("p (r t) -> p r t", r=4), [pq], [QTt[s_]])
            if sub == 1:
                dma(P, "sp", A["QT"][:, :, tsl].rearrange("r p t -> p r t"), QTt[s_].t[:], [QTt[s_]], [], QT_g[s_])
            if LV < 3:
                continue
            b = tok_group(sub, 512, 256)
            act(P, sqt[u].t[:, 0:128], b.t[:, 0:128], AF.Square, [b], [sqt[u]])
            red(P, s2[u].t[:], sqt[u].t[:, 0:128].rearrange("p (h d) -> p h d", h=2), ALU.add, [sqt[u]], [s2[u]])
            rstd_ops(cx, s2[u], 64, r2[u], NH, [])
            tt(P, "dve", t1[u].t[:, 0:128].rearrange("p (h d) -> p h d", h=2),
               b.t[:, 0:128].rearrange("p (h d) -> p h d", h=2),
               r2[u].t[:].unsqueeze(2).broadcast_to([128, 2, 64]), ALU.mult, [b, r2[u]], [t1[u]])
            tt(P, "pool", kn[u].t[:], t1[u].t[:, 0:128], GK.t[:], ALU.mult, [t1[u], GK], [kn[u]])
            tr(P, pq.t[:, 512:640], kn[u].t[:], ident.t[:], [kn[u], ident], [pq])
            cp(P, "act", vb[s_].t[:, sub, :], b.t[:, 128:256], [b], [vb[s_]])
            cp(P, "dve", KTt[s_].t[:, ssl], pq.t[:, 512:640], [pq], [KTt[s_]])
            if sub == 1:
                dma(P, "sp", A["KT"][:, tsl], KTt[s_].t[:], [KTt[s_]], [], KT_g[s_])
                dma(P, "sp", A["V"][tsl, :].rearrange("(s p) d -> p s d", p=128), vb[s_].t[:], [vb[s_]], [], V_g[s_])
            if LV < 4:
                continue
            bw = tok_group(sub, 1280, 72)
            ts(P, "dve", zt[u].t[:], bw.t[:, 64:72], -IDX_SCALE, None, ALU.mult, ALU.bypass, [bw], [zt[u]])
            stt(P, absw[u].t[:], bw.t[:, 64:72], IDX_SCALE, zt[u].t[:], ALU.mult, ALU.max, [bw, zt[u]], [absw[u]])
            ts(P, "dve", sgn[s_].t[:, sub, :], bw.t[:, 64:72], 0.0, 2.0, ALU.is_ge, ALU.mult, [bw], [sgn[s_]])
            ts(P, "dve", sgn[s_].t[:, sub, :], sgn[s_].t[:, sub, :], -1.0, None, ALU.add, ALU.bypass, [sgn[s_]], [sgn[s_]])
            act(P, sqt[u].t[:, 0:64], bw.t[:, 0:64], AF.Square, [bw], [sqt[u], s1[u]], accum_out=s1[u].t[:])
            rstd_ops(cx, s1[u], 64, r1[u], NH, [])
            stt(P, ikn[u].t[:], bw.t[:, 0:64], r1[u].t[:], GI.t[:], ALU.mult, ALU.mult, [bw, r1[u], GI], [ikn[u]])
            b = tok_group(sub, 768, 512)
            tt(P, "dve", iqs[u].t[:].rearrange("p (h d) -> p h d", h=8), b.t[:].rearrange("p (h d) -> p h d", h=8),
               absw[u].t[:].unsqueeze(2).broadcast_to([128, 8, 64]), ALU.mult, [b, absw[u]], [iqs[u]])
            for r in range(4):
                tr(P, pi.t[:, r * 128:(r + 1) * 128], iqs[u].t[:, r * 128:(r + 1) * 128],
                   ident.t[:], [iqs[u], ident], [pi])
            tr(P, pi.t[0:64, 512:640], ikn[u].t[:], ident.t[:], [ikn[u], ident], [pi])
            cp(P, "act", IQTt[s_].t[:, :, ssl], pi.t[:, 0:512].rearrange("p (r t) -> p r t", r=4), [pi], [IQTt[s_]])
            cp(P, "dve", IKTt[s_].t[:, ssl], pi.t[0:64, 512:640], [pi], [IKTt[s_]])
            if sub == 1:
                dma(P, "sp", A["IQT"][:, :, tsl].rearrange("r p t -> p r t"), IQTt[s_].t[:], [IQTt[s_]], [], IQT_g[s_])
                dma(P, "sp", A["IKT"][:, tsl], IKTt[s_].t[:], [IKTt[s_]], [], IKT_g[s_])
                dma(P, "sp", A["SGN"][tsl, :].rearrange("(s p) h -> p s h", p=128), sgn[s_].t[:], [sgn[s_]], [], SGN_g[s_])
            if LV < 5:
                continue
            b = tok_group(sub, 1864, 512)
            cp(P, "dve", mvb[s_].t[:, sub, :], b.t[:], [b], [mvb[s_]])
            b = tok_group(sub, 2376, 8)
            tt(P, "dve", zt[u].t[:], b.t[:, 0:8], BIF.t[:], ALU.add, [b, BIF], [zt[u]])
            cp(P, "pool", lif[s_].t[:, sub, 0:4], zt[u].t[:, 0:4], [zt[u]], [lif[s_]])
            act(P, et[u].t[:], zt[u].t[:, 4:8], AF.Exp, [zt[u]], [et[u]], scale=-1.0)
            act(P, et[u].t[:], et[u].t[:], AF.Ln, [et[u]], [et[u]], bias=1.0)
            ts(P, "dve", lif[s_].t[:, sub, 4:8], et[u].t[:], -1.0, None, ALU.mult, ALU.bypass, [et[u]], [lif[s_]])
            b = tok_group(sub, 2384, 512)
            act(P, mob[s_].t[:, sub, :], b.t[:], AF.Sigmoid, [b], [mob[s_]])
        if LV < 6:
            return
        dma(P, "sp", A["MV"][tsl, :].rearrange("(s p) d -> p s d", p=128), mvb[s_].t[:], [mvb[s_]], [], MV_g[s_])
        dma(P, "sp", A["LIF"][tsl, :].rearrange("(s p) d -> p s d", p=128), lif[s_].t[:], [lif[s_]], [], LIF_g[s_])
        dma(P, "sp", A["MO"][tsl, :].rearrange("(s p) d -> p s d", p=128), mob[s_].t[:], [mob[s_]], [], MO_g[s_])

        if i + 1 < NT:
            nt(i + 1)
        def feat_pair(c0):
            b = bank()
            for c in range(2):
                for kc in range(8):
                    mm(P, b.t[:, c * TT:(c + 1) * TT], WIN.t[:, kc, c0 + c * 128:c0 + (c + 1) * 128], H.t[:, kc, :],
                       kc == 0, kc == 7, [H, WIN], [b])
            return b

        if LV < 7:
            return
        for (base, dst) in ((2896, gat[s_]), (3920, gbt[s_])):
            for pr in range(4):
                b = feat_pair(base + pr * 256)
                act(P, dst.t[:, 2 * pr:2 * pr + 2, :], b.t[:].rearrange("p (c t) -> p c t", c=2), AF.Sigmoid, [b], [dst])
        dma(P, "sp", A["GAT"][:, :, tsl].rearrange("c p t -> p c t"), gat[s_].t[:], [gat[s_]], [], GAT_g[s_])
        dma(P, "sp", A["GBT"][:, :, tsl].rearrange("c p t -> p c t"), gbt[s_].t[:], [gbt[s_]], [], GBT_g[s_])
        if LV < 8:
            return
        xm, xn = XM[s_], XM[1 - s_]
        for pr in range(2):
            b = feat_pair(1352 + pr * 256)
            cp(P, "dve", xm.t[:, 2 * pr:2 * pr + 2, 3:3 + TT], b.t[:].rearrange("p (c t) -> p c t", c=2), [b], [xm])
        cp(P, "pool", xn.t[:, :, 0:3], xm.t[:, :, TT:TT + 3], [xm], [xn])
        ac = acc[s_]
        for c in range(4):
            ts(P, "dve", ac.t[:, c, :], xm.t[:, c, 0:TT], CW.t[:, c, 0:1], CB.t[:, c:c + 1], ALU.mult, ALU.add,
               [xm, CW, CB], [ac])
            for j in range(1, 4):
                stt(P, ac.t[:, c, :], xm.t[:, c, j:j + TT], CW.t[:, c, j:j + 1], ac.t[:, c, :], ALU.mult, ALU.add,
                    [xm, CW, ac], [ac])
        act(P, sig[s_].t[:], ac.t[:], AF.Sigmoid, [ac], [sig[s_]])
        tt(P, "pool", xct[s_].t[:], ac.t[:], sig[s_].t[:], ALU.mult, [ac, sig[s_]], [xct[s_]])
        if LV < 9:
            return
        xc = xct[s_]
        for (wi, dst, scale, name, grp) in ((0, mqt[s_], MQ_SCALE, "MQT", MQT_g[s_]), (1, mkt[s_], 1.0, "MKT", MKT_g[s_])):
            for hp in range(2):
                b = bank()
                for c in range(2):
                    h = hp * 2 + c
                    mm(P, b.t[0:64, c * TT:(c + 1) * TT], WM.t[:, wi, h, :], xc.t[:, h, :], True, True, [WM, xc], [b])
                act(P, dst.t[:, 2 * hp:2 * hp + 2, :], b.t[0:64, :].rearrange("p (c t) -> p c t", c=2), AF.Copy,
                    [b], [dst], scale=scale)
            dma(P, "sp", A[name][:, :, tsl].rearrange("h d t -> d h t"), dst.t[:], [dst], [], grp)
        for sub in range(2):
            b = bank()
            for h in range(4):
                mm(P, b.t[:, h * 64:(h + 1) * 64], xc.t[:, h, sub * 128:(sub + 1) * 128], WM.t[:, 1, h, :], True, True,
                   [xc, WM], [b])
            cp(P, "dve", mkb[s_].t[:, sub, :], b.t[:, 0:256], [b], [mkb[s_]])
        dma(P, "sp", A["MK"][tsl, :].rearrange("(s p) d -> p s d", p=128), mkb[s_].t[:], [mkb[s_]], [], MK_g[s_])

    def nt(i):
        q_ = i % 2
        norm_transpose(cx, xt[q_], G, ident, NH, hb[q_], hT[q_], pT, ss[q_], rs[q_], junk)

    load(0)
    if NT > 1:
        load(1)
    nt(0)
    for i in range(NT):
        tile(i)
        if i + 2 < NT:
            load(i + 2)
    fin = QT_g + KT_g + IQT_g + IKT_g + V_g + SGN_g + MV_g + LIF_g + MO_g + GAT_g + GBT_g + MQT_g + MKT_g + MK_g
    return {"sp": fin}


NIT = 11


def dsa_phase(cx, S, A, Cd):
    nc, P = cx.nc, cx.P
    NB = S // 512
    NJ = S // 128
    cg = cx.group(final=True)
    ident = load_ident(cx, Cd, cg)
    IKbs = [cx.sb(f"IKb{i}", [128, 512], BF16) for i in range(2)]
    IK_g = [cx.group() for _ in range(2)]
    KT = cx.sb("KT", [128, S], BF16)
    dma(P, "sp", KT.t[:], A["KT"], [], [KT], cg)
    VA = cx.sb("VA", [128, NJ, 2, 65], BF16)
    mset(P, "pool", VA.t[:], 1.0, [VA])
    for g_ in range(2):
        dma(P, "sp", VA.t[:, :, g_, 0:64], A["V"][:, g_ * 64:(g_ + 1) * 64].rearrange("(j p) d -> p j d", p=128),
            [], [VA], cg)
    CM = cx.sb("CM", [128, 4, 512], BF16)
    dma(P, "sp", CM.t[:], Cd["c_cmask"].rearrange("i p s -> p i s"), [], [CM], cg)
    CROW = cx.sb("CROW", [128, NIT], F32)
    dma(P, "sp", CROW.t[:], Cd["c_steps"], [], [CROW], cg)
    ONES = cx.sb("ONES", [128, 64], F32)
    mset(P, "pool", ONES.t[:], 1.0, [ONES])

    SCs = [cx.sb(f"SC{i}", [128, S], F32) for i in range(2)]
    MK = cx.sb("MK", [128, S], BF16)
    MT = cx.sb("MT", [128, NJ, 512], BF16)
    IQb = cx.sb("IQb", [128, 4, 512], BF16)
    Qb = cx.sb("Qb", [128, 4, 512], BF16)
    SGb = cx.sb("SGb", [128, 4, 8], F32)
    ld_g1, ld_g2, ld_g3 = cx.group(), cx.group(), cx.group()
    DG = [cx.sb("DG0", [128, 8, 128], BF16)]
    Rt = [cx.sb(f"Rt{i}", [128, 512], BF16) for i in range(4)]
    Et = [cx.sb(f"Et{i}", [128, 512], BF16) for i in range(2)]
    Pt = [cx.sb(f"Pt{i}", [128, 512], BF16) for i in range(4)]
    M1 = cx.sb("M1", [128, 1], F32)
    LO = cx.sb("LO", [128, 1], F32)
    W0 = cx.sb("W0", [128, 1], F32)
    STEP = cx.sb("STEP", [128, NIT], F32)
    NSTEP = cx.sb("NSTEP", [128, NIT], F32)
    STEP2 = cx.sb("STEP2", [128, NIT], F32)
    MID = cx.sb("MID", [128, 1], F32)
    CNT = cx.sb("CNT", [128, 1], F32)
    DT = cx.sb("DT", [128, 1], F32)
    RC = cx.sb("RC", [128, 512], F32)
    OS = cx.sb("OS", [64, 512], BF16)
    YT = [cx.sb("YT0", [64, 512], BF16)] * 2
    YT_g = [cx.group() for _ in range(2)]
    pW = [cx.ps(f"pW{i}", [128, 512], F32) for i in range(4)]
    SCp = cx.ps("SCp", [128, 512], F32)
    pTr = cx.ps("pTr", [128, 1024], BF16)
    pOs = [cx.ps(f"pO{i}", [128, 512], F32) for i in range(2)]
    cnt = {"ik": 0, "l": 0, "r": 0, "s": 0, "e": 0, "p": 0, "y": 0, "d": 0, "m": 0}

    def nxt(key, lst):
        v = lst[cnt[key] % len(lst)]
        cnt[key] += 1
        return v

    tiles = [(b, ii) for b in range(NB) for ii in range(4)]
    dgs = {}

    def indexer(n):
        b, ii = tiles[n]
        bs = slice(b * 512, (b + 1) * 512)
        SC = SCs[n % 2]
        if ii == 0:
            dma(P, "sp", IQb.t[:], A["IQT"][:, :, bs].rearrange("r p t -> p r t"), [], [IQb], ld_g1)
            dma(P, "sp", SGb.t[:], A["SGN"][bs, :].rearrange("(i p) h -> p i h", p=128), [], [SGb], ld_g3)
        tq = slice(ii * 128, (ii + 1) * 128)
        dg = nxt("d", DG)
        for h in range(8):
            ts(P, "pool", dg.t[:, h, :], ident.t[:], SGb.t[:, ii, h:h + 1], 1.0, ALU.mult, ALU.mult,
               [ident, SGb], [dg])
        items = [(kb, pr) for kb in range(b + 1) for pr in range(4)]
        LA = 1
        hold = {}

        def front(it):
            kb, pr = it
            ks = slice(kb * 512, (kb + 1) * 512)
            if pr == 0:
                ikq = cnt["ik"] % 2
                cnt["ik"] += 1
                IKb = IKbs[ikq]
                dma(P, "sp", IKb.t[0:64, :], A["IKT"][:, ks], [], [IKb], IK_g[ikq])
                dma(P, "sp", IKb.t[64:128, :], A["IKT"][:, ks], [], [IKb], IK_g[ikq])
                hold[("ik", kb)] = IKb
            IKb = hold[("ik", kb)]
            Ls = []
            for half in range(2):
                ps_ = slice(half * 64, half * 64 + 64)
                L = nxt("l", pW)
                mm(P, L.t[:], IQb.t[ps_, pr, tq], IKb.t[ps_, :], True, True, [IQb, IKb], [L])
                Ls.append(L)
            Rs = []
            for half in range(2):
                R = nxt("r", Rt)
                act(P, R.t[:], Ls[half].t[:], AF.Relu, [Ls[half]], [R])
                Rs.append(R)
            hold[it] = Rs

        def back(it):
            kb, pr = it
            ks = slice(kb * 512, (kb + 1) * 512)
            Rs = hold.pop(it)
            for half in range(2):
                h = 2 * pr + half
                mm(P, SCp.t[:], dg.t[:, h, :], Rs[half].t[:], h == 0, h == 7, [dg, Rs[half]], [SCp])
            if pr == 3:
                cp(P, "act", SC.t[:, ks], SCp.t[:], [SCp], [SC])

        for idx in range(len(items) + LA):
            if idx < len(items):
                front(items[idx])
            if idx - LA >= 0:
                back(items[idx - LA])

    def post(n):
        b, ii = tiles[n]
        Nb = 512 * (b + 1)
        njb = 4 * (b + 1)
        bs = slice(b * 512, (b + 1) * 512)
        tq = slice(ii * 128, (ii + 1) * 128)
        SC = SCs[n % 2]
        red(P, M1.t[:], SC.t[:, 0:Nb], ALU.max, [SC], [M1], apply_absolute_value=True)
        tt(P, "pool", SC.t[:, bs], SC.t[:, bs], CM.t[:, ii, :], ALU.add, [SC, CM], [SC])
        ts(P, "dve", W0.t[:], M1.t[:], 2.002, 2e-6, ALU.mult, ALU.add, [M1], [W0])
        ts(P, "dve", STEP.t[:], CROW.t[:], W0.t[:], None, ALU.mult, ALU.bypass, [CROW, W0], [STEP])
        ts(P, "dve", NSTEP.t[:], STEP.t[:], -1.0, None, ALU.mult, ALU.bypass, [STEP], [NSTEP])
        ts(P, "dve", STEP2.t[:], STEP.t[:], 2.0, None, ALU.mult, ALU.bypass, [STEP], [STEP2])
        mset(P, "dve", MID.t[:], 0.0, [MID])
        for k in range(NIT):
            ts(P, "dve", MK.t[:, 0:Nb], SC.t[:, 0:Nb], MID.t[:], None, ALU.is_ge, ALU.add, [SC, MID], [MK, CNT],
               accum=CNT.t[:])
            if k + 1 < NIT:
                stt(P, DT.t[:], CNT.t[:], 255.5, STEP2.t[:, k + 1:k + 2], ALU.is_ge, ALU.mult, [CNT, STEP2], [DT])
                stt(P, MID.t[:], DT.t[:], NSTEP.t[:, k + 1:k + 2], MID.t[:], ALU.add, ALU.add, [DT, NSTEP, MID], [MID])
            else:
                stt(P, DT.t[:], CNT.t[:], 255.5, STEP.t[:, k:k + 1], ALU.is_lt, ALU.mult, [CNT, STEP], [DT])
                tt(P, "dve", LO.t[:], MID.t[:], DT.t[:], ALU.subtract, [MID, DT], [LO])
        ts(P, "dve", MK.t[:, 0:Nb], SC.t[:, 0:Nb], LO.t[:], None, ALU.is_ge, ALU.bypass, [SC, LO], [MK])
        for j0 in range(0, njb, 8):
            nn = min(8, njb - j0)
            for jj in range(nn):
                j = j0 + jj
                tr(P, pTr.t[:, jj * 128:(jj + 1) * 128], MK.t[:, j * 128:(j + 1) * 128], ident.t[:], [MK, ident], [pTr])
            cp(P, "act" if (cnt["m"] % 2 == 0) else "dve", MT.t[:, j0:j0 + nn, tq],
               pTr.t[:, 0:nn * 128].rearrange("p (j t) -> p j t", t=128), [pTr], [MT])
            cnt["m"] += 1

    def attention(b):
        njb = 4 * (b + 1)
        bs = slice(b * 512, (b + 1) * 512)
        dma(P, "sp", Qb.t[:], A["QT"][:, :, bs].rearrange("r p t -> p r t"), [], [Qb], ld_g2)
        items = [(r, j) for r in range(4) for j in range(njb)]
        LA = 1
        hold = {}

        def front(it):
            r, j = it
            STs = []
            for g in range(2):
                ps_ = slice(g * 64, g * 64 + 64)
                ST = nxt("l", pW)
                mm(P, ST.t[:], KT.t[ps_, j * 128:(j + 1) * 128], Qb.t[ps_, r, :], True, True, [KT, Qb], [ST])
                STs.append(ST)
            PTs = []
            for g in range(2):
                E = nxt("e", Et)
                act(P, E.t[:], STs[g].t[:], AF.Exp, [STs[g]], [E])
                PT = nxt("p", Pt)
                tt(P, "pool" if (cnt["p"] % 4 == 0) else "dve", PT.t[:], E.t[:], MT.t[:, j, :], ALU.mult, [E, MT], [PT])
                PTs.append(PT)
            hold[it] = PTs

        def back(it):
            r, j = it
            PTs = hold.pop(it)
            for g in range(2):
                pO = pOs[g]
                mm(P, pO.t[0:65, :], VA.t[:, j, g, :], PTs[g].t[:], j == 0, j == njb - 1, [VA, PTs[g]], [pO])
            if j == njb - 1:
                for g in range(2):
                    h = g * 4 + r
                    pO = pOs[g]
                    P.op("dve", lambda e, pO=pO: e.reciprocal(out=RC.t[64:65, :], in_=pO.t[64:65, :]), [pO.b], [RC.b])
                    BC = nxt("l", pW)
                    mm(P, BC.t[0:64, :], ONES.t[64:65, 0:64], RC.t[64:65, :], True, True, [ONES, RC], [BC])
                    cp(P, "act", OS.t[:], pO.t[0:64, :], [pO], [OS])
                    y = cnt["y"] % 2
                    cnt["y"] += 1
                    tt(P, "dve", YT[y].t[:], OS.t[:], BC.t[0:64, :], ALU.mult, [OS, BC], [YT[y]])
                    dma(P, "sp", A["YAT"][h, :, bs], YT[y].t[:], [YT[y]], [], YT_g[y])

        for idx in range(len(items) + LA):
            if idx < len(items):
                front(items[idx])
            if idx - LA >= 0:
                back(items[idx - LA])

    indexer(0)
    for n in range(len(tiles)):
        if n + 1 < len(tiles):
            indexer(n + 1)
        post(n)
        if tiles[n][1] == 3:
            attention(tiles[n][0])
    return {"sp": YT_g}


def mlstm_phase(cx, S, A, Wd, Cd):
    nc, P = cx.nc, cx.P
    NT = S // 128
    cg = cx.group(final=True)
    ident = load_ident(cx, Cd, cg)
    T2 = cx.sb("T2", [128, 128], F32)
    dma(P, "sp", T2.t[:], Cd["c_tri2"], [], [T2], cg)
    NEGM = cx.sb("NEGM", [128, 512], BF16)
    dma(P, "sp", NEGM.t[:], Cd["c_negm"], [], [NEGM], cg)
    HG = cx.sb("HG", [128, 512], F32)
    dma(P, "sp", HG.t[:], Wd["m_head_norm"].partition_broadcast(128), [], [HG], cg)
    NH = cx.sb("NH", [128, 8], F32)
    mset(P, "pool", NH.t[:], -0.5, [NH])
    CN = [cx.sb(f"CN{i}", [64, 4, 129], F32) for i in range(3)]
    mset(P, "pool", CN[0].t[:], 0.0, [CN[0]])
    R2 = range(2)
    LIF = [cx.sb(f"LIF{i}", [128, 8], F32) for i in R2]
    MQ = [cx.sb(f"MQ{i}", [64, 4, 128], BF16) for i in R2]
    MKt = [cx.sb(f"MKt{i}", [64, 4, 128], BF16) for i in R2]
    MKb = [cx.sb(f"MKb{i}", [128, 256], BF16) for i in R2]
    VA = [cx.sb(f"VA{i}", [128, 4, 129], BF16) for i in R2]
    MOt = [cx.sb(f"MOt{i}", [128, 512], BF16) for i in R2]
    lg = [[cx.group() for _ in range(6)] for _ in R2]
    for i in R2:
        mset(P, "pool", VA[i].t[:], 1.0, [VA[i]])
    LFm = cx.sb("LFm", [128, 4, 128], F32)
    bias = cx.sb("bias", [128, 4], F32)
    AT = cx.sb("AT", [128, 4, 128], F32)
    EBs = [cx.sb(f"EB{i}", [128, 4, 128], F32) for i in range(2)]
    BL = cx.sb("BL", [128, 4], F32)
    WS = cx.sb("WS", [128, 4], F32)
    WTs = [cx.sb(f"WT{i}", [128, 4, 128], BF16) for i in range(2)]
    QPs = [cx.sb(f"QP{i}", [64, 4, 128], F32) for i in range(2)]
    KWs = [cx.sb(f"KW{i}", [128, 4, 64], BF16) for i in range(2)]
    dn = cx.sb("dn", [128, 4], F32)
    rec = cx.sb("rec", [128, 4], F32)
    hh = cx.sb("hh", [128, 4, 128], F32)
    junk = cx.sb("junk", [128, 128], BF16)
    ssq = cx.sb("ssq", [128, 4], F32)
    rs4 = cx.sb("rs4", [128, 4], F32)
    yb = cx.sb("yb", [128, 512], BF16)
    YBt = [cx.sb(f"YBt{i}", [128, 4, 512], BF16) for i in R2]
    YB_g = [cx.group() for _ in R2]
    pX = cx.ps("pX", [128, 512], F32)
    pY = cx.ps("pY", [128, 512], F32)
    pZ = cx.ps("pZ", [128, 512], F32)
    pQ = cx.ps("pQ", [128, 512], F32)
    pD = cx.ps("pD", [128, 512], F32)
    pN = [cx.ps(f"pN{i}", [128, 512], F32) for i in R2]
    pTr = cx.ps("pTr", [128, 1024], BF16)

    def load(i):
        s_ = i % 2
        tsl = slice(i * 128, (i + 1) * 128)
        g = lg[s_]
        dma(P, "sp", LIF[s_].t[:], A["LIF"][tsl, :], [], [LIF[s_]], g[0])
        dma(P, "sp", MQ[s_].t[:], A["MQT"][:, :, tsl].rearrange("h d t -> d h t"), [], [MQ[s_]], g[1])
        dma(P, "sp", MKt[s_].t[:], A["MKT"][:, :, tsl].rearrange("h d t -> d h t"), [], [MKt[s_]], g[2])
        dma(P, "sp", MKb[s_].t[:], A["MK"][tsl, :], [], [MKb[s_]], g[3])
        dma(P, "sp", VA[s_].t[:, :, 0:128], A["MV"][tsl, :].rearrange("p (h d) -> p h d", h=4), [], [VA[s_]], g[4])
        dma(P, "sp", MOt[s_].t[:], A["MO"][tsl, :], [], [MOt[s_]], g[5])

    def front(i):
        s_ = i % 2
        lif, mq, mkt, mkb, va, mo = LIF[s_], MQ[s_], MKt[s_], MKb[s_], VA[s_], MOt[s_]
        EB, WT, QP, KW = EBs[s_], WTs[s_], QPs[s_], KWs[s_]
        import os
        ML = int(os.environ.get('ML', '99'))
        cp(P, "pool", LFm.t[:], lif.t[:, 4:8].unsqueeze(2).broadcast_to([128, 4, 128]), [lif], [LFm])
        for h in range(4):
            mm(P, pX.t[:, h * 128:(h + 1) * 128], LFm.t[:, h, :], T2.t[:], True, True, [LFm, T2], [pX])
        mm(P, pY.t[:], ident.t[:], NEGM.t[:], True, False, [ident, NEGM], [pY])
        for h in range(4):
            mm(P, pY.t[:, h * 128:(h + 1) * 128], LFm.t[:, h, :], T2.t[:], False, h == 3, [LFm, T2], [pY])
        mm(P, pZ.t[:, 0:4], T2.t[:], lif.t[:, 4:8], True, True, [T2, lif], [pZ])
        tt(P, "dve", bias.t[:], lif.t[:, 0:4], pZ.t[:, 0:4], ALU.subtract, [lif, pZ], [bias])
        for h in range(4):
            act(P, AT.t[:, h, :], pY.t[:, h * 128:(h + 1) * 128], AF.Exp, [pY, bias], [AT], bias=bias.t[:, h:h + 1])
        act(P, EB.t[:], pX.t[:].rearrange("p (h t) -> p h t", h=4), AF.Exp, [pX], [EB])
        if ML < 2:
            return
        pXv = pX.t[:].rearrange("p (h t) -> p h t", h=4)
        cp(P, "dve", BL.t[0:64, :], pXv[0:64, :, 63], [pX], [BL])
        cp(P, "dve", BL.t[64:128, :], pXv[64:128, :, 127], [pX], [BL])
        tt(P, "dve", BL.t[:], BL.t[:], bias.t[:], ALU.add, [BL, bias], [BL])
        act(P, WS.t[:], BL.t[:], AF.Exp, [BL], [WS])
        for h in range(4):
            mm(P, pQ.t[:, h * 128:(h + 1) * 128], mkt.t[:, h, :], mq.t[:, h, :], True, True, [mkt, mq], [pQ])
        tt(P, "dve", WT.t[:], pQ.t[:].rearrange("p (h t) -> p h t", h=4), AT.t[:], ALU.mult, [pQ, AT], [WT])
        tt(P, "pool", QP.t[:], mq.t[:], EB.t[0:64, :, :], ALU.mult, [mq, EB], [QP])
        tt(P, "pool", KW.t[:], mkb.t[:].rearrange("p (h d) -> p h d", h=4),
           WS.t[:].unsqueeze(2).broadcast_to([128, 4, 64]), ALU.mult, [mkb, WS], [KW])

    def back(i):
        s_ = i % 2
        lif, mq, mkt, mkb, va, mo = LIF[s_], MQ[s_], MKt[s_], MKb[s_], VA[s_], MOt[s_]
        EB, WT, QP, KW = EBs[s_], WTs[s_], QPs[s_], KWs[s_]
        import os
        ML = int(os.environ.get('ML', '99'))
        cc, cm, cn = CN[(2 * i) % 3], CN[(2 * i + 1) % 3], CN[(2 * i + 2) % 3]
        if ML < 3:
            return
        for (rows, src, dst, col) in ((slice(0, 64), cc, cm, 63), (slice(64, 128), cm, cn, 127)):
            for hp in range(2):
                for c in range(2):
                    h = 2 * hp + c
                    mm(P, pD.t[0:64, c * 129:(c + 1) * 129], KW.t[rows, h, :], va.t[rows, h, :], True, True,
                       [KW, va], [pD])
                for c in range(2):
                    h = 2 * hp + c
                    stt(P, dst.t[:, h, :], src.t[:, h, :], EB.t[0:64, h, col:col + 1], pD.t[0:64, c * 129:(c + 1) * 129],
                        ALU.mult, ALU.add, [src, EB, pD], [dst])
        if ML < 4:
            return
        for h in range(4):
            bk = pN[h // 2]
            osl = slice((h % 2) * 129, (h % 2) * 129 + 129)
            mm(P, bk.t[:, osl], WT.t[:, h, :], va.t[:, h, :], True, False, [WT, va], [bk])
            mm(P, bk.t[0:64, osl], QP.t[:, h, 0:64], cc.t[:, h, :], False, False, [QP, cc], [bk])
            mm(P, bk.t[64:128, osl], QP.t[:, h, 64:128], cm.t[:, h, :], False, True, [QP, cm], [bk])
        if ML < 5:
            return
        for q in range(2):
            act(P, dn.t[:, 2 * q:2 * q + 2], pN[q].t[:, 0:258].rearrange("p (h c) -> p h c", c=129)[:, :, 128],
                AF.Abs, [pN[q]], [dn])
        ts(P, "dve", dn.t[:], dn.t[:], 1.0, None, ALU.max, ALU.bypass, [dn], [dn])
        P.op("dve", lambda e: e.reciprocal(out=rec.t[:], in_=dn.t[:]), [dn.b], [rec.b])
        for h in range(4):
            o0 = (h % 2) * 129
            act(P, hh.t[:, h, :], pN[h // 2].t[:, o0:o0 + 128], AF.Copy, [pN[h // 2], rec], [hh], scale=rec.t[:, h:h + 1])
        if ML < 6:
            return
        for h in range(4):
            act(P, junk.t[:], hh.t[:, h, :], AF.Square, [hh], [junk, ssq], accum_out=ssq.t[:, h:h + 1])
        rstd_ops(cx, ssq, 128, rs4, NH, [])
        if ML < 7:
            return
        tt(P, "dve", hh.t[:], hh.t[:], rs4.t[:].unsqueeze(2).broadcast_to([128, 4, 128]), ALU.mult, [hh, rs4], [hh])
        hf = hh.t[:].rearrange("p h d -> p (h d)")
        tt(P, "pool", hf, hf, HG.t[:], ALU.mult, [hh, HG], [hh])
        tt(P, "dve", yb.t[:], hf, mo.t[:], ALU.mult, [hh, mo], [yb])
        if ML < 8:
            return
        for c in range(4):
            tr(P, pTr.t[:, c * 128:(c + 1) * 128], yb.t[:, c * 128:(c + 1) * 128], ident.t[:], [yb, ident], [pTr])
        yt = YBt[(i // 4) % 2]
        cp(P, "act", yt.t[:, :, (i % 4) * 128:(i % 4 + 1) * 128], pTr.t[:, 0:512].rearrange("p (c t) -> p c t", c=4),
           [pTr], [yt])
        if i % 4 == 3:
            b0 = (i // 4) * 512
            dma(P, "sp", A["YBT"][:, :, b0:b0 + 512].rearrange("c p t -> p c t"), yt.t[:], [yt], [], YB_g[(i // 4) % 2])

    load(0)
    if NT > 1:
        load(1)
    front(0)
    for i in range(NT):
        if i + 1 < NT:
            front(i + 1)
        back(i)
        if i + 2 < NT:
            load(i + 2)
    return {"sp": YB_g}


def merge_phase(cx, S, X1, X2, A, Wd, Cd):
    nc, P = cx.nc, cx.P
    TT = 256
    NT = S // TT
    WA = cx.sb("WA", [128, 4, D], BF16)
    WB = cx.sb("WB", [128, 4, D], BF16)
    WO = cx.sb("WO", [128, 8, D], BF16)
    stg = [cx.sb(f"stg{i}", [128, D], F32) for i in range(2)]
    stg_g = [cx.group() for _ in range(2)]
    k = 0
    for (src, dst, n) in ((Wd["w_proj_a"], WA, 4), (Wd["w_proj_b"], WB, 4), (Wd["w_out"], WO, 8)):
        for c in range(n):
            s_ = k % 2
            dma(P, "sp", stg[s_].t[:], src[c * 128:(c + 1) * 128, :], [], [stg[s_]], stg_g[s_])
            cp(P, ["act", "dve", "pool"][k % 3], dst.t[:, c, :], stg[s_].t[:], [stg[s_]], [dst])
            k += 1
    R2 = range(2)
    YA = [cx.sb(f"YA{i}", [128, 4, TT], BF16) for i in R2]
    YB = [cx.sb(f"YB{i}", [128, 4, TT], BF16) for i in R2]
    GA = [cx.sb(f"GA{i}", [128, 8, TT], BF16) for i in R2]
    GB = [cx.sb(f"GB{i}", [128, 8, TT], BF16) for i in R2]
    xt = [cx.sb(f"xt{i}", [128, 2, D], F32) for i in R2]
    lg = [[cx.group() for _ in range(5)] for _ in R2]
    st_g = [cx.group() for _ in R2]
    m1 = [cx.sb(f"m1{i}", [128, 512], F32) for i in R2]
    m2 = [cx.sb(f"m2{i}", [128, 512], F32) for i in R2]
    MG = [cx.sb(f"MG{i}", [128, 8, TT], BF16) for i in R2]
    pA = [cx.ps(f"pA{i}", [128, 512], F32) for i in R2]
    pB = [cx.ps(f"pB{i}", [128, 512], F32) for i in R2]
    pO = [cx.ps(f"pO{i}", [128, 512], F32) for i in R2]

    def load(i):
        s_ = i % 2
        tsl = slice(i * TT, (i + 1) * TT)
        g = lg[s_]
        dma(P, "sp", YA[s_].t[:], A["YAT"][:, :, tsl].rearrange("(c w) d t -> (w d) c t", w=2), [], [YA[s_]], g[0])
        dma(P, "sp", YB[s_].t[:], A["YBT"][:, :, tsl].rearrange("c p t -> p c t"), [], [YB[s_]], g[1])
        dma(P, "sp", GA[s_].t[:], A["GAT"][:, :, tsl].rearrange("c p t -> p c t"), [], [GA[s_]], g[2])
        dma(P, "sp", GB[s_].t[:], A["GBT"][:, :, tsl].rearrange("c p t -> p c t"), [], [GB[s_]], g[3])
        dma(P, "sp", xt[s_].t[:], X1[tsl, :].rearrange("(s p) d -> p s d", p=128), [], [xt[s_]], g[4])

    def tile(i):
        s_ = i % 2
        tsl = slice(i * TT, (i + 1) * TT)
        for dcp in range(4):
            q = dcp % 2
            for (W_, Y_, bk) in ((WA, YA[s_], pA[q]), (WB, YB[s_], pB[q])):
                for c2 in range(2):
                    dc = 2 * dcp + c2
                    for c in range(4):
                        mm(P, bk.t[:, c2 * TT:(c2 + 1) * TT], W_.t[:, c, dc * 128:(dc + 1) * 128], Y_.t[:, c, :],
                           c == 0, c == 3, [W_, Y_], [bk])
            gsl = slice(2 * dcp, 2 * dcp + 2)
            tt(P, "dve", m1[q].t[:].rearrange("p (c t) -> p c t", c=2), pA[q].t[:].rearrange("p (c t) -> p c t", c=2),
               GA[s_].t[:, gsl, :], ALU.mult, [pA[q], GA[s_]], [m1[q]])
            tt(P, "dve", m2[q].t[:].rearrange("p (c t) -> p c t", c=2), pB[q].t[:].rearrange("p (c t) -> p c t", c=2),
               GB[s_].t[:, gsl, :], ALU.mult, [pB[q], GB[s_]], [m2[q]])
            tt(P, "pool", MG[s_].t[:, gsl, :], m1[q].t[:].rearrange("p (c t) -> p c t", c=2),
               m2[q].t[:].rearrange("p (c t) -> p c t", c=2), ALU.add, [m1[q], m2[q]], [MG[s_]])
        n = 0
        for sub in range(2):
            for dh in range(2):
                bk = pO[n % 2]
                n += 1
                for dc in range(8):
                    mm(P, bk.t[:], MG[s_].t[:, dc, sub * 128:(sub + 1) * 128], WO.t[:, dc, dh * 512:(dh + 1) * 512],
                       dc == 0, dc == 7, [MG[s_], WO], [bk])
                xs = xt[s_].t[:, sub, dh * 512:(dh + 1) * 512]
                tt(P, "dve", xs, bk.t[:], xs, ALU.add, [bk, xt[s_]], [xt[s_]])
        dma(P, "pool", X2[tsl, :].rearrange("(s p) d -> p s d", p=128), xt[s_].t[:], [xt[s_]], [], st_g[s_])

    load(0)
    if NT > 1:
        load(1)
    for i in range(NT):
        tile(i)
        if i + 2 < NT:
            load(i + 2)
    return {"pool": st_g}


WSPEC = {
    "ffn1_norm": [1, D], "ffn1_w_gate": [D, DFF], "ffn1_w_up": [D, DFF], "ffn1_w_down": [DFF, D],
    "mix_norm": [1, D], "w_in": [D, DIN], "q_norm": [1, 64], "k_norm": [1, 64], "idx_k_norm": [1, 64],
    "conv_w": [128, 16], "conv_b": [128, 4], "w_mq": [128, 256], "w_mk": [128, 256], "b_i": [1, 4],
    "b_f": [1, 4], "m_head_norm": [1, 512], "w_proj_a": [512, D], "w_proj_b": [512, D], "w_out": [D, D],
    "ffn2_norm": [1, D], "ffn2_w_gate": [D, DFF], "ffn2_w_up": [D, DFF], "ffn2_w_down": [DFF, D],
}


def consts():
    c = {}
    c["c_ident"] = np.eye(128, dtype=np.float32).astype(ml_dtypes.bfloat16)
    cm = np.zeros((4, 128, 512), np.float32)
    for ii in range(4):
        for p in range(128):
            lim = ((ii * 128 + p) // 64 + 1) * 64
            cm[ii, p, lim:] = -1e30
    c["c_cmask"] = cm.astype(ml_dtypes.bfloat16)
    c["c_steps"] = np.tile((2.0 ** -(np.arange(NIT) + 1.0)).astype(np.float32)[None, :], (128, 1))
    j = np.arange(128)
    t2 = ((j[:, None] // 64 == j[None, :] // 64) & (j[:, None] <= j[None, :]))
    c["c_tri2"] = t2.astype(np.float32)
    c["c_negm"] = np.tile(np.where(t2, 0.0, NEG).astype(np.float32), (1, 4)).astype(ml_dtypes.bfloat16)
    return c


CSPEC = {"c_ident": ([128, 128], BF16), "c_cmask": ([4, 128, 512], BF16), "c_steps": ([128, NIT], F32),
         "c_tri2": ([128, 128], F32), "c_negm": ([128, 512], BF16)}


def build_nc(S, debug=False, upto=9):
    nc = bass.Bass("TRN2", target_bir_lowering=False)

    def din(name, shape, dt=F32):
        return nc.dram_tensor(name, list(shape), dt, kind="ExternalInput").ap()

    x = din("x", [S, D])
    Wd = {k: din(k, v) for k, v in WSPEC.items()}
    Cd = {k: din(k, v[0], v[1]) for k, v in CSPEC.items()}
    out = nc.dram_tensor("out", [S, D], F32, kind="ExternalOutput").ap()
    kind = "ExternalOutput" if debug else "Internal"

    def scr(name, shape, dt):
        return nc.dram_tensor(name, list(shape), dt, kind=kind).ap()

    X1 = scr("X1", [S, D], F32)
    X2 = scr("X2", [S, D], F32)
    A = {
        "QT": scr("QT", [4, 128, S], BF16), "KT": scr("KT", [128, S], BF16), "V": scr("V", [S, 128], BF16),
        "IQT": scr("IQT", [4, 128, S], BF16), "IKT": scr("IKT", [64, S], BF16), "SGN": scr("SGN", [S, 8], F32),
        "MV": scr("MV", [S, 512], BF16), "LIF": scr("LIF", [S, 8], F32), "MO": scr("MO", [S, 512], BF16),
        "GAT": scr("GAT", [8, 128, S], BF16), "GBT": scr("GBT", [8, 128, S], BF16),
        "MQT": scr("MQT", [4, 64, S], BF16), "MKT": scr("MKT", [4, 64, S], BF16), "MK": scr("MK", [S, 256], BF16),
        "YAT": scr("YAT", [8, 64, S], BF16), "YBT": scr("YBT", [4, 128, S], BF16),
    }
    ph = [
        ("f1", lambda cx: ffn_phase(cx, S, x, X1, Wd["ffn1_norm"], Wd["ffn1_w_gate"], Wd["ffn1_w_up"],
                                    Wd["ffn1_w_down"], Cd["c_ident"])),
        ("pj", lambda cx: proj_phase(cx, S, X1, A, Wd, Cd)),
        ("ds", lambda cx: dsa_phase(cx, S, A, Cd)),
        ("ml", lambda cx: mlstm_phase(cx, S, A, Wd, Cd)),
        ("mg", lambda cx: merge_phase(cx, S, X1, X2, A, Wd, Cd)),
        ("f2", lambda cx: ffn_phase(cx, S, X2, out, Wd["ffn2_norm"], Wd["ffn2_w_gate"], Wd["ffn2_w_up"],
                                    Wd["ffn2_w_down"], Cd["c_ident"])),
    ]
    import os
    lo_ = int(os.environ.get('FROM', '0'))
    for n, (tag, fn) in enumerate(ph):
        if lo_ <= n < upto:
            run_phase(nc, tag, fn, debug)
    return nc


def layout_weights(inputs):
    sh = {}
    for k, v in WSPEC.items():
        a = np.asarray(inputs[k], dtype=np.float32)[0]
        if k == "conv_w":
            a = a.reshape(4, 4, 128).transpose(2, 1, 0)
        elif k == "conv_b":
            a = a.reshape(4, 128).T
        elif k in ("w_mq", "w_mk"):
            a = a.transpose(1, 0, 2)
        sh[k] = np.ascontiguousarray(a).reshape(v)
    return sh


_NC_CACHE = {}


def kernel(**inputs):
    x = np.ascontiguousarray(np.asarray(inputs["x"], dtype=np.float32))
    B, S, _ = x.shape
    if S not in _NC_CACHE:
        _NC_CACHE[S] = build_nc(S)
    nc = _NC_CACHE[S]
    shared = layout_weights(inputs)
    shared.update(consts())
    in_maps = []
    for b in range(B):
        m = dict(shared)
        m["x"] = x[b]
        in_maps.append(m)
    res = run_bass_kernel_spmd(nc, in_maps, core_ids=list(range(B)))
    return np.stack([np.asarray(r["out"], dtype=np.float32) for r in res.results], axis=0)
```

```python
import numpy as np
from contextlib import ExitStack
import ml_dtypes
import concourse.bass as bass
import concourse.mybir as mybir
from concourse.bass_utils import run_bass_kernel_spmd

F32 = mybir.dt.float32
BF16 = mybir.dt.bfloat16
ALU = mybir.AluOpType
AF = mybir.ActivationFunctionType
AX = mybir.AxisListType

D = 1024
DFF = 2816
NFF = DFF // 128
DIN = 4944
EPS = 1e-6
NEG = -30000.0


class Buf:
    __slots__ = ("name", "w", "r", "psum")

    def __init__(self, name):
        self.name = name
        self.w = {}
        self.r = {}
        self.psum = False


class Group:
    def __init__(self, sem, final=False):
        self.sem = sem
        self.n = 0
        self.final = final


class Prog:
    ENG = ("pe", "act", "dve", "pool", "sp")

    def __init__(self, nc):
        self.nc = nc
        self.ops = {e: [] for e in self.ENG}

    def op(self, eng, fn, reads=(), writes=(), group=None):
        ops = self.ops[eng]
        idx = len(ops)
        rec = {"fn": fn, "waits": {}, "sig": False, "group": group}
        is_dma = group is not None

        def need(key, val, raw):
            if is_dma and key[0] == "g" and key[1] is group:
                return
            if key[0] == "e" and key[1] == eng and not is_dma:
                if eng == "pe" or not raw:
                    return
            w = rec["waits"]
            if w.get(key, -1) < val:
                w[key] = val
            if key[0] == "e":
                self.ops[key[1]][val]["sig"] = True

        for b in reads:
            for k, v in b.w.items():
                need(k, v, True)
            if b.psum:
                for k, v in b.r.items():
                    need(k, v, False)
        for b in writes:
            for k, v in b.w.items():
                need(k, v, False)
            for k, v in b.r.items():
                need(k, v, False)
        if is_dma:
            group.n += 1
            me = (("g", group), group.n)
        else:
            me = (("e", eng), idx)
        for b in reads:
            if b.r.get(me[0], -1) < me[1]:
                b.r[me[0]] = me[1]
        for b in writes:
            b.w = {me[0]: me[1]}
            b.r = {}
        ops.append(rec)
        return rec

    def emit(self, block, sems, final_waits, base=None):
        if base is None:
            base = {e: 0 for e in self.ENG}
        sigcount = {}
        for e in self.ENG:
            c = base[e]
            lst = []
            for rec in self.ops[e]:
                if rec["sig"]:
                    c += 1
                lst.append(c)
            sigcount[e] = lst

        def run(engname, engine):
            waited = {}
            for rec in self.ops[engname]:
                for key, val in rec["waits"].items():
                    if key[0] == "e":
                        sem = sems[key[1]]
                        v = sigcount[key[1]][val]
                    else:
                        g = key[1]
                        sem = g.sem
                        v = 16 * (g.n if g.final else val)
                    if waited.get(id(sem), -1) >= v:
                        continue
                    waited[id(sem)] = v
                    engine.wait_ge(sem, v)
                ins = rec["fn"](engine)
                if rec["group"] is not None:
                    ins.then_inc(rec["group"].sem, 16)
                elif rec["sig"]:
                    ins.then_inc(sems[engname], 1)
            for g in final_waits.get(engname, ()):
                engine.wait_ge(g.sem, 16 * g.n)

        @block.tensor
        def _(e):
            run("pe", e)

        @block.scalar
        def _(e):
            run("act", e)

        @block.vector
        def _(e):
            run("dve", e)

        @block.gpsimd
        def _(e):
            run("pool", e)

        @block.sync
        def _(e):
            run("sp", e)

        for e in self.ENG:
            base[e] = sigcount[e][-1] if sigcount[e] else base[e]


class _Alloc:
    def __init__(self, nc, es):
        self._nc = nc
        self._es = es

    def alloc_sbuf_tensor(self, name, shape, dt):
        return self._es.enter_context(self._nc.sbuf_tensor(name, list(shape), dt))

    def alloc_psum_tensor(self, name, shape, dt):
        return self._es.enter_context(self._nc.psum_tensor(name, list(shape), dt))


class TL:
    def __init__(self, t, name):
        self.t = t
        self.b = Buf(name)


class Ctx:
    def __init__(self, nc, es, tag, debug=False):
        self.rnc = nc
        self.nc = _Alloc(nc, es)
        self.es = es
        self.tag = tag
        self.P = Prog(nc)
        self.groups = []
        self.debug = debug

    def group(self, final=False):
        sem = self.rnc.alloc_semaphore(f"{self.tag}g{len(self.groups)}")
        g = Group(sem, final)
        self.groups.append(g)
        return g

    def sb(self, name, shape, dt):
        return TL(self.nc.alloc_sbuf_tensor(self.tag + name, list(shape), dt), name)

    def ps(self, name, shape, dt):
        t = TL(self.nc.alloc_psum_tensor(self.tag + name, list(shape), dt), name)
        t.b.psum = True
        return t


_GSTATE = {}


def run_phase(nc, tag, fn, debug=False):
    with ExitStack() as es:
        cx = Ctx(nc, es, tag, debug)
        import os
        for _i in range(int(os.environ.get('DUMMYSEM', '0'))):
            es.enter_context(nc.semaphore(f"{tag}dummy{_i}"))
        finals = fn(cx)
        st = _GSTATE.setdefault(id(nc), {})
        if "sems" not in st:
            st["sems"] = {e: nc.alloc_semaphore(f"s_{e}") for e in Prog.ENG}
            st["base"] = {e: 0 for e in Prog.ENG}
        with nc.Block() as block:
            cx.P.emit(block, st["sems"], finals, st["base"])


def _rr(lst, i):
    return lst[i % len(lst)]


def ffn_phase(cx, S, xin, xout, gvec, wg, wu, wd, identd):
    nc, P = cx.nc, cx.P
    tag = cx.tag
    xin_buf, xout_buf = Buf("xin"), Buf("xout")
    ident = nc.alloc_sbuf_tensor(f"{tag}ident", [128, 128], BF16)
    identb = Buf("ident")
    cg0 = cx.group(final=True)
    P.op("sp", lambda e: e.dma_start(out=ident[:], in_=identd), writes=[identb], group=cg0)
    TT = 256
    NT = S // TT
    WG = nc.alloc_sbuf_tensor(f"{tag}WG", [128, 8, DFF], BF16)
    WU = nc.alloc_sbuf_tensor(f"{tag}WU", [128, 8, DFF], BF16)
    WD = nc.alloc_sbuf_tensor(f"{tag}WD", [128, NFF, D], BF16)
    stg = [nc.alloc_sbuf_tensor(f"{tag}stg{i}", [128, DFF // 2], F32) for i in range(2)]
    stg_b = [Buf(f"stg{i}") for i in range(2)]
    stg_g = [cx.group() for _ in range(2)]
    wbuf = Buf("W")
    wgb = [Buf(f"WG{h}") for h in range(2)]
    wub = [Buf(f"WU{h}") for h in range(2)]
    wdb = [Buf(f"WD{f}") for f in range(NFF)]
    G = nc.alloc_sbuf_tensor(f"{tag}G", [128, D], F32)
    gb = Buf("G")
    cg = cx.group(final=True)
    P.op("sp", lambda e: e.dma_start(out=G[:], in_=gvec.partition_broadcast(128)), writes=[gb], group=cg)
    nhalf = nc.alloc_sbuf_tensor(f"{tag}nh", [128, 2], F32)
    nhb = Buf("nh")
    P.op("pool", lambda e: e.memset(nhalf[:], -0.5), writes=[nhb])

    conv_eng = ["act", "dve", "pool"]
    k = 0

    def load_conv(src_ap, dst_ap, width, wb):
        nonlocal k
        s = k % 2
        st, sb_, sg = stg[s], stg_b[s], stg_g[s]
        P.op("sp", lambda e: e.dma_start(out=st[:, 0:width], in_=src_ap), writes=[sb_], group=sg)
        ce = conv_eng[k % 3]
        if ce == "act":
            P.op("act", lambda e: e.copy(out=dst_ap, in_=st[:, 0:width]), reads=[sb_], writes=[wb])
        else:
            P.op(ce, lambda e: e.tensor_copy(out=dst_ap, in_=st[:, 0:width]), reads=[sb_], writes=[wb])
        k += 1

    HF = DFF // 2
    for hh in range(2):
        for kc in range(8):
            load_conv(wg[kc * 128:(kc + 1) * 128, hh * HF:(hh + 1) * HF], WG[:, kc, hh * HF:(hh + 1) * HF], HF, wgb[hh])
            load_conv(wu[kc * 128:(kc + 1) * 128, hh * HF:(hh + 1) * HF], WU[:, kc, hh * HF:(hh + 1) * HF], HF, wub[hh])
    for f in range(NFF):
        load_conv(wd[f * 128:(f + 1) * 128, :], WD[:, f, :], D, wdb[f])

    xt = [nc.alloc_sbuf_tensor(f"{tag}xt{i}", [128, 2, D], F32) for i in range(2)]
    xt_b = [Buf(f"xt{i}") for i in range(2)]
    xt_g = [cx.group() for _ in range(2)]
    st_g = [cx.group() for _ in range(2)]
    junk = nc.alloc_sbuf_tensor(f"{tag}junk", [128, D], BF16)
    junk_b = Buf("junk")
    ss = [nc.alloc_sbuf_tensor(f"{tag}ss{i}", [128, 2], F32) for i in range(2)]
    ss_b = [Buf(f"ss{i}") for i in range(2)]
    ms = [nc.alloc_sbuf_tensor(f"{tag}ms{i}", [128, 2], F32) for i in range(2)]
    ms_b = [Buf(f"ms{i}") for i in range(2)]
    rs = [nc.alloc_sbuf_tensor(f"{tag}rs{i}", [128, 2], F32) for i in range(2)]
    rs_b = [Buf(f"rs{i}") for i in range(2)]
    hb = [nc.alloc_sbuf_tensor(f"{tag}h{i}", [128, 2, D], BF16) for i in range(2)]
    hb_b = [Buf(f"h{i}") for i in range(2)]
    hT = [nc.alloc_sbuf_tensor(f"{tag}hT{i}", [128, 8, TT], BF16) for i in range(2)]
    hT_b = [Buf(f"hT{i}") for i in range(2)]
    actT = [nc.alloc_sbuf_tensor(f"{tag}actT{i}", [128, NFF, TT], BF16) for i in range(2)]
    actT_b = [Buf(f"actT{i}") for i in range(2)]
    sg = [nc.alloc_sbuf_tensor(f"{tag}sg{i}", [128, 512], F32) for i in range(2)]
    sg_b = [Buf(f"sg{i}") for i in range(2)]
    pT = [nc.alloc_psum_tensor(f"{tag}pT{i}", [128, 1024], BF16) for i in range(2)]
    pT_b = [Buf(f"pT{i}") for i in range(2)]
    pG = [nc.alloc_psum_tensor(f"{tag}pG{i}", [128, 512], F32) for i in range(2)]
    pG_b = [Buf(f"pG{i}") for i in range(2)]
    pU = [nc.alloc_psum_tensor(f"{tag}pU{i}", [128, 512], F32) for i in range(2)]
    pU_b = [Buf(f"pU{i}") for i in range(2)]
    pD = [nc.alloc_psum_tensor(f"{tag}pD{i}", [128, 512], F32) for i in range(2)]
    pD_b = [Buf(f"pD{i}") for i in range(2)]

    def stage_load(i):
        s = i % 2
        t0 = i * TT
        P.op("sp", lambda e: e.dma_start(
            out=xt[s][:], in_=xin[t0:t0 + TT, :].rearrange("(s p) d -> p s d", p=128)),
            reads=[xin_buf], writes=[xt_b[s]], group=xt_g[s])

    def stage_norm(i):
        s = i % 2
        for sub in range(2):
            P.op("act", lambda e, sub=sub: e.activation(
                out=junk[:], in_=xt[s][:, sub, :], func=AF.Square, accum_out=ss[s][:, sub:sub + 1]),
                reads=[xt_b[s]], writes=[junk_b, ss_b[s]])
        P.op("dve", lambda e: e.tensor_scalar(out=ms[s][:], in0=ss[s][:], scalar1=1.0 / D, scalar2=EPS,
                                              op0=ALU.mult, op1=ALU.add), reads=[ss_b[s]], writes=[ms_b[s]])
        P.op("pool", lambda e: e.tensor_tensor(out=rs[s][:], in0=ms[s][:], in1=nhalf[:], op=ALU.pow),
             reads=[ms_b[s], nhb], writes=[rs_b[s]])
        for sub in range(2):
            P.op("dve", lambda e, sub=sub: e.scalar_tensor_tensor(
                out=hb[s][:, sub, :], in0=xt[s][:, sub, :], scalar=rs[s][:, sub:sub + 1], in1=G[:],
                op0=ALU.mult, op1=ALU.mult), reads=[xt_b[s], rs_b[s], gb], writes=[hb_b[s]])

    def stage_T(i):
        s = i % 2
        for half in range(2):
            pb, pbb = pT[half], pT_b[half]
            for kk in range(4):
                kc = half * 4 + kk
                for sub in range(2):
                    P.op("pe", lambda e, kc=kc, sub=sub, kk=kk, pb=pb: e.transpose(
                        out=pb[:, kk * TT + sub * 128: kk * TT + (sub + 1) * 128],
                        in_=hb[s][:, sub, kc * 128:(kc + 1) * 128], identity=ident[:]),
                        reads=[hb_b[s], identb], writes=[pbb])
            if half == 0:
                P.op("act", lambda e, pb=pb: e.copy(out=hT[s][:, 0:4, :], in_=pb[:].rearrange("p (k t) -> p k t", k=4)),
                     reads=[pbb], writes=[hT_b[s]])
            else:
                P.op("dve", lambda e, pb=pb: e.tensor_copy(out=hT[s][:, 4:8, :], in_=pb[:].rearrange("p (k t) -> p k t", k=4)),
                     reads=[pbb], writes=[hT_b[s]])

    def stage_GU(i):
        s = i % 2
        for f2 in range(NFF // 2):
            q = f2 % 2
            for (W_, pp, ppb, wbs) in ((WG, pG[q], pG_b[q], wgb), (WU, pU[q], pU_b[q], wub)):
                for c in range(2):
                    f = 2 * f2 + c
                    for kc in range(8):
                        P.op("pe", lambda e, W_=W_, pp=pp, c=c, f=f, kc=kc: e.matmul(
                            out=pp[:, c * TT:(c + 1) * TT], lhsT=W_[:, kc, f * 128:(f + 1) * 128],
                            rhs=hT[s][:, kc, :], start=(kc == 0), stop=(kc == 7)),
                            reads=[wbs[0 if f < NFF // 2 else 1], hT_b[s]], writes=[ppb])
            P.op("act", lambda e, q=q: e.activation(out=sg[q][:], in_=pG[q][:], func=AF.Silu),
                 reads=[pG_b[q]], writes=[sg_b[q]])
            P.op("dve", lambda e, q=q, f2=f2: e.tensor_tensor(
                out=actT[s][:, 2 * f2:2 * f2 + 2, :], in0=sg[q][:].rearrange("p (c t) -> p c t", c=2),
                in1=pU[q][:].rearrange("p (c t) -> p c t", c=2), op=ALU.mult),
                reads=[sg_b[q], pU_b[q]], writes=[actT_b[s]])

    def stage_D(i):
        s = i % 2
        t0 = i * TT
        n = 0
        for sub in range(2):
            for dh in range(2):
                q = n % 2
                n += 1
                for f in range(NFF):
                    P.op("pe", lambda e, q=q, f=f, sub=sub, dh=dh: e.matmul(
                        out=pD[q][:], lhsT=actT[s][:, f, sub * 128:(sub + 1) * 128],
                        rhs=WD[:, f, dh * 512:(dh + 1) * 512], start=(f == 0), stop=(f == NFF - 1)),
                        reads=[wdb[f], actT_b[s]], writes=[pD_b[q]])
                P.op("dve", lambda e, q=q, sub=sub, dh=dh: e.scalar_tensor_tensor(
                    out=xt[s][:, sub, dh * 512:(dh + 1) * 512], in0=pD[q][:], scalar=0.5,
                    in1=xt[s][:, sub, dh * 512:(dh + 1) * 512], op0=ALU.mult, op1=ALU.add),
                    reads=[pD_b[q], xt_b[s]], writes=[xt_b[s]])
        P.op("pool", lambda e: e.dma_start(
            out=xout[t0:t0 + TT, :].rearrange("(s p) d -> p s d", p=128), in_=xt[s][:]),
            reads=[xt_b[s]], writes=[xout_buf], group=st_g[s])

    stage_load(0)
    if NT > 1:
        stage_load(1)
    stage_norm(0)
    stage_T(0)
    for i in range(NT):
        stage_GU(i)
        if i + 1 < NT:
            stage_norm(i + 1)
            stage_T(i + 1)
        stage_D(i)
        if i + 2 < NT:
            stage_load(i + 2)
    return {"pool": st_g}


def _b(L):
    return [x.b for x in L]


def mm(P, out, lhsT, rhs, st, sp, R, W):
    P.op("pe", lambda e: e.matmul(out=out, lhsT=lhsT, rhs=rhs, start=st, stop=sp), _b(R), _b(W))


def tr(P, out, in_, ident, R, W):
    P.op("pe", lambda e: e.transpose(out=out, in_=in_, identity=ident), _b(R), _b(W))


def act(P, out, in_, func, R, W, **kw):
    P.op("act", lambda e: e.activation(out=out, in_=in_, func=func, **kw), _b(R), _b(W))


def ts(P, eng, out, in0, s1, s2, op0, op1, R, W, accum=None):
    if accum is None:
        P.op(eng, lambda e: e.tensor_scalar(out=out, in0=in0, scalar1=s1, scalar2=s2, op0=op0, op1=op1), _b(R), _b(W))
    else:
        P.op(eng, lambda e: e.tensor_scalar(out=out, in0=in0, scalar1=s1, scalar2=s2, op0=op0, op1=op1,
                                            accum_out=accum), _b(R), _b(W))


def tt(P, eng, out, in0, in1, op, R, W):
    P.op(eng, lambda e: e.tensor_tensor(out=out, in0=in0, in1=in1, op=op), _b(R), _b(W))


def stt(P, out, in0, scalar, in1, op0, op1, R, W):
    P.op("dve", lambda e: e.scalar_tensor_tensor(out=out, in0=in0, scalar=scalar, in1=in1, op0=op0, op1=op1),
         _b(R), _b(W))


def cp(P, eng, out, in_, R, W):
    if eng == "act":
        P.op("act", lambda e: e.copy(out=out, in_=in_), _b(R), _b(W))
    else:
        P.op(eng, lambda e: e.tensor_copy(out=out, in_=in_), _b(R), _b(W))


def red(P, out, in_, op, R, W, eng="dve", **kw):
    P.op(eng, lambda e: e.tensor_reduce(out=out, in_=in_, axis=AX.X, op=op, **kw), _b(R), _b(W))


def dma(P, eng, out, in_, R, W, g, **kw):
    P.op(eng, lambda e: e.dma_start(out=out, in_=in_, **kw), _b(R), _b(W), group=g)


def mset(P, eng, ap, val, W):
    P.op(eng, lambda e: e.memset(ap, val), [], _b(W))


def rstd_ops(cx, ssq, n, rs, nh, R):
    P = cx.P
    ts(P, "dve", ssq.t[:], ssq.t[:], 1.0 / n, EPS, ALU.mult, ALU.add, R + [ssq], [ssq])
    tt(P, "pool", rs.t[:], ssq.t[:], nh.t[:, 0:ssq.t.shape[1]], ALU.pow, [ssq, nh], [rs])


IDX_SCALE = (64 ** -0.5) * (8 ** -0.5)
ATT_SCALE = 64 ** -0.5
MQ_SCALE = 64 ** -0.5


def load_ident(cx, Cd, cg):
    ident = cx.sb("ident", [128, 128], BF16)
    dma(cx.P, "sp", ident.t[:], Cd["c_ident"], [], [ident], cg)
    return ident


def norm_transpose(cx, xt, G, ident, nh, hb, hT, pT, ss, rs, junk, ntok_sub=2):
    P = cx.P
    for sub in range(ntok_sub):
        act(P, junk.t[:], xt.t[:, sub, :], AF.Square, [xt], [junk, ss], accum_out=ss.t[:, sub:sub + 1])
    rstd_ops(cx, ss, D, rs, nh, [])
    for sub in range(ntok_sub):
        stt(P, hb.t[:, sub, :], xt.t[:, sub, :], rs.t[:, sub:sub + 1], G.t[:], ALU.mult, ALU.mult,
            [xt, rs, G], [hb])
    for half in range(2):
        pb = pT[half]
        for kk in range(4):
            kc = half * 4 + kk
            for sub in range(ntok_sub):
                tr(P, pb.t[:, kk * 256 + sub * 128: kk * 256 + (sub + 1) * 128],
                   hb.t[:, sub, kc * 128:(kc + 1) * 128], ident.t[:], [hb, ident], [pb])
        cp(P, "act" if half == 0 else "dve", hT.t[:, half * 4:half * 4 + 4, :],
           pb.t[:].rearrange("p (k t) -> p k t", k=4), [pb], [hT])


def proj_phase(cx, S, X1, A, Wd, Cd):
    nc, P = cx.nc, cx.P
    TT = 256
    NT = S // TT
    cg = cx.group(final=True)
    ident = load_ident(cx, Cd, cg)
    WIN = cx.sb("WIN", [128, 8, DIN], BF16)
    stg = [cx.sb(f"stg{i}", [128, 824], F32) for i in range(2)]
    stg_g = [cx.group() for _ in range(2)]
    k = 0
    for kc in range(8):
        for part in range(6):
            s_ = k % 2
            c0 = part * 824
            dma(P, "sp", stg[s_].t[:], Wd["w_in"][kc * 128:(kc + 1) * 128, c0:c0 + 824], [], [stg[s_]], stg_g[s_])
            cp(P, ["act", "dve", "pool"][k % 3], WIN.t[:, kc, c0:c0 + 824], stg[s_].t[:], [stg[s_]], [WIN])
            k += 1
    import os
    SL = int(os.environ.get('SL', '99'))
    G = cx.sb("G", [128, D], F32)
    dma(P, "sp", G.t[:], Wd["mix_norm"].partition_broadcast(128), [], [G], cg)
    GQ = cx.sb("GQ", [128, 512], F32)
    for h in range(8):
        if SL >= 2:
            dma(P, "sp", GQ.t[:, h * 64:(h + 1) * 64], Wd["q_norm"].partition_broadcast(128), [], [GQ], cg)
    if SL >= 2:
        ts(P, "dve", GQ.t[:], GQ.t[:], ATT_SCALE, None, ALU.mult, ALU.bypass, [GQ], [GQ])
    GK = cx.sb("GK", [128, 128], F32)
    for h in range(2):
        if SL >= 3:
            dma(P, "sp", GK.t[:, h * 64:(h + 1) * 64], Wd["k_norm"].partition_broadcast(128), [], [GK], cg)
    GI = cx.sb("GI", [128, 64], F32)
    if SL >= 3:
        dma(P, "sp", GI.t[:], Wd["idx_k_norm"].partition_broadcast(128), [], [GI], cg)
    CW = cx.sb("CW", [128, 4, 4], F32)
    CB = cx.sb("CB", [128, 4], F32)
    if SL >= 4:
        dma(P, "sp", CB.t[:], Wd["conv_b"], [], [CB], cg)
    if SL >= 4:
      dma(P, "sp", CW.t[:], Wd["conv_w"].rearrange("p (c j) -> p c j", c=4), [], [CW], cg)
    wst = cx.sb("wst", [128, 2, 4, 64], F32)
    WM = cx.sb("WM", [128, 2, 4, 64], BF16)
    if SL >= 5:
        dma(P, "sp", wst.t[:, 0], Wd["w_mq"].rearrange("c (h d) -> c h d", h=4), [], [wst], cg)
        dma(P, "sp", wst.t[:, 1], Wd["w_mk"].rearrange("c (h d) -> c h d", h=4), [], [wst], cg)
        cp(P, "dve", WM.t[:], wst.t[:], [wst], [WM])
    BIF = cx.sb("BIF", [128, 8], F32)
    if SL >= 6:
        dma(P, "sp", BIF.t[:, 0:4], Wd["b_i"].partition_broadcast(128), [], [BIF], cg)
        dma(P, "sp", BIF.t[:, 4:8], Wd["b_f"].partition_broadcast(128), [], [BIF], cg)
    NH = cx.sb("NH", [128, 8], F32)
    mset(P, "pool", NH.t[:], -0.5, [NH])

    R2 = range(2)
    xt = [cx.sb(f"xt{i}", [128, 2, D], F32) for i in R2]
    xt_g = [cx.group() for _ in R2]
    junk = cx.sb("junk", [128, D], BF16)
    ss = [cx.sb(f"ss{i}", [128, 2], F32) for i in R2]
    rs = [cx.sb(f"rs{i}", [128, 2], F32) for i in R2]
    hb = [cx.sb("hb0", [128, 2, D], BF16)] * 2
    hT = [cx.sb(f"hT{i}", [128, 8, TT], BF16) for i in R2]
    pT = [cx.ps(f"pT{i}", [128, 1024], BF16) for i in R2]
    pX = [cx.ps(f"pX{i}", [128, 1024], BF16) for i in R2]
    pM = [cx.ps(f"pM{i}", [128, 512], F32) for i in range(4)]
    nbank = [0]

    def bank():
        b = pM[nbank[0] % 4]
        nbank[0] += 1
        return b

    sqt = [cx.sb(f"sqt{i}", [128, 512], F32) for i in R2]
    t1 = [cx.sb(f"t1{i}", [128, 512], F32) for i in R2]
    qn = [cx.sb(f"qn{i}", [128, 512], BF16) for i in R2]
    kn = [cx.sb(f"kn{i}", [128, 128], BF16) for i in R2]
    ikn = [cx.sb(f"ikn{i}", [128, 64], BF16) for i in R2]
    iqs = [cx.sb(f"iqs{i}", [128, 512], BF16) for i in R2]
    s8 = [cx.sb(f"s8{i}", [128, 8], F32) for i in R2]
    r8 = [cx.sb(f"r8{i}", [128, 8], F32) for i in R2]
    s2 = [cx.sb(f"s2{i}", [128, 2], F32) for i in R2]
    r2 = [cx.sb(f"r2{i}", [128, 2], F32) for i in R2]
    s1 = [cx.sb(f"s1{i}", [128, 1], F32) for i in R2]
    r1 = [cx.sb(f"r1{i}", [128, 1], F32) for i in R2]
    absw = [cx.sb(f"absw{i}", [128, 8], F32) for i in R2]
    zt = [cx.sb(f"zt{i}", [128, 8], F32) for i in R2]
    et = [cx.sb(f"et{i}", [128, 4], F32) for i in R2]
    def outs(name, shape, dt):
        return [cx.sb(f"{name}{i}", shape, dt) for i in R2], [cx.group() for _ in R2]
    QTt, QT_g = outs("QTt", [128, 4, TT], BF16)
    KTt, KT_g = outs("KTt", [128, TT], BF16)
    IQTt, IQT_g = outs("IQTt", [128, 4, TT], BF16)
    IKTt, IKT_g = outs("IKTt", [64, TT], BF16)
    vb, V_g = outs("vb", [128, 2, 128], BF16)
    sgn, SGN_g = outs("sgn", [128, 2, 8], F32)
    mvb, MV_g = outs("mvb", [128, 2, 512], BF16)
    lif, LIF_g = outs("lif", [128, 2, 8], F32)
    mob, MO_g = outs("mob", [128, 2, 512], BF16)
    gat, GAT_g = outs("gat", [128, 8, TT], BF16)
    gbt, GBT_g = outs("gbt", [128, 8, TT], BF16)
    mqt, MQT_g = outs("mqt", [64, 4, TT], BF16)
    mkt, MKT_g = outs("mkt", [64, 4, TT], BF16)
    mkb, MK_g = outs("mkb", [128, 2, 256], BF16)
    XM = [cx.sb(f"XM{i}", [128, 4, TT + 3], F32) for i in R2]
    mset(P, "pool", XM[0].t[:, :, 0:3], 0.0, [XM[0]])
    acc = [cx.sb("acc0", [128, 4, TT], F32)] * 2
    sig = [cx.sb("sig0", [128, 4, TT], F32)] * 2
    xct = [cx.sb(f"xct{i}", [128, 4, TT], BF16) for i in R2]
    X1b = TL(None, "X1")
    allg = []

    def load(i):
        s_ = i % 2
        t0 = i * TT
        dma(P, "sp", xt[s_].t[:], X1[t0:t0 + TT, :].rearrange("(s p) d -> p s d", p=128), [X1b], [xt[s_]], xt_g[s_])

    def tile(i):
        import os
        LV = int(os.environ.get('LV', '99'))
        if LV < 1:
            return
        s_ = i % 2
        t0 = i * TT
        tsl = slice(t0, t0 + TT)
        H = hT[s_]

        def tok_group(sub, c0, width):
            b = bank()
            for kc in range(8):
                mm(P, b.t[:, 0:width], H.t[:, kc, sub * 128:(sub + 1) * 128], WIN.t[:, kc, c0:c0 + width],
                   kc == 0, kc == 7, [H, WIN], [b])
            return b

        pq, pi = pX[0], pX[1]
        ssl = None
        for sub in range(2):
            u = sub
            if LV < 2:
                continue
            b = tok_group(sub, 0, 512)
            act(P, sqt[u].t[:], b.t[:], AF.Square, [b], [sqt[u]])
            red(P, s8[u].t[:], sqt[u].t[:].rearrange("p (h d) -> p h d", h=8), ALU.add, [sqt[u]], [s8[u]])
            rstd_ops(cx, s8[u], 64, r8[u], NH, [])
            tt(P, "dve", t1[u].t[:].rearrange("p (h d) -> p h d", h=8), b.t[:].rearrange("p (h d) -> p h d", h=8),
               r8[u].t[:].unsqueeze(2).broadcast_to([128, 8, 64]), ALU.mult, [b, r8[u]], [t1[u]])
            tt(P, "pool", qn[u].t[:].rearrange("p (r g d) -> p g r d", r=4, g=2),
               t1[u].t[:].rearrange("p (g r d) -> p g r d", g=2, r=4),
               GQ.t[:].rearrange("p (g r d) -> p g r d", g=2, r=4), ALU.mult, [t1[u], GQ], [qn[u]])
            ssl = slice(sub * 128, (sub + 1) * 128)
            for r in range(4):
                tr(P, pq.t[:, r * 128:(r + 1) * 128], qn[u].t[:, r * 128:(r + 1) * 128],
                   ident.t[:], [qn[u], ident], [pq])
            cp(P, "act", QTt[s_].t[:, :, ssl], pq.t[:, 0:512].rearrange("p (r t) -> p r t", r=4), [pq], [QTt[s_]])
            if sub == 1:
                dma(P, "sp", A["QT"][:, :, tsl].rearrange("r p t -> p r t"), QTt[s_].t[:], [QTt[s_]], [], QT_g[s_])
            if LV < 3:
                continue
            b = tok_group(sub, 512, 256)
            act(P, sqt[u].t[:, 0:128], b.t[:, 0:128], AF.Square, [b], [sqt[u]])
            red(P, s2[u].t[:], sqt[u].t[:, 0:128].rearrange("p (h d) -> p h d", h=2), ALU.add, [sqt[u]], [s2[u]])
            rstd_ops(cx, s2[u], 64, r2[u], NH, [])
            tt(P, "dve", t1[u].t[:, 0:128].rearrange("p (h d) -> p h d", h=2),
               b.t[:, 0:128].rearrange("p (h d) -> p h d", h=2),
               r2[u].t[:].unsqueeze(2).broadcast_to([128, 2, 64]), ALU.mult, [b, r2[u]], [t1[u]])
            tt(P, "pool", kn[u].t[:], t1[u].t[:, 0:128], GK.t[:], ALU.mult, [t1[u], GK], [kn[u]])
            tr(P, pq.t[:, 512:640], kn[u].t[:], ident.t[:], [kn[u], ident], [pq])
            cp(P, "act", vb[s_].t[:, sub, :], b.t[:, 128:256], [b], [vb[s_]])
            cp(P, "dve", KTt[s_].t[:, ssl], pq.t[:, 512:640], [pq], [KTt[s_]])
            if sub == 1:
                dma(P, "sp", A["KT"][:, tsl], KTt[s_].t[:], [KTt[s_]], [], KT_g[s_])
                dma(P, "sp", A["V"][tsl, :].rearrange("(s p) d -> p s d", p=128), vb[s_].t[:], [vb[s_]], [], V_g[s_])
            if LV < 4:
                continue
            bw = tok_group(sub, 1280, 72)
            ts(P, "dve", zt[u].t[:], bw.t[:, 64:72], -IDX_SCALE, None, ALU.mult, ALU.bypass, [bw], [zt[u]])
            stt(P, absw[u].t[:], bw.t[:, 64:72], IDX_SCALE, zt[u].t[:], ALU.mult, ALU.max, [bw, zt[u]], [absw[u]])
            ts(P, "dve", sgn[s_].t[:, sub, :], bw.t[:, 64:72], 0.0, 2.0, ALU.is_ge, ALU.mult, [bw], [sgn[s_]])
            ts(P, "dve", sgn[s_].t[:, sub, :], sgn[s_].t[:, sub, :], -1.0, None, ALU.add, ALU.bypass, [sgn[s_]], [sgn[s_]])
            act(P, sqt[u].t[:, 0:64], bw.t[:, 0:64], AF.Square, [bw], [sqt[u], s1[u]], accum_out=s1[u].t[:])
            rstd_ops(cx, s1[u], 64, r1[u], NH, [])
            stt(P, ikn[u].t[:], bw.t[:, 0:64], r1[u].t[:], GI.t[:], ALU.mult, ALU.mult, [bw, r1[u], GI], [ikn[u]])
            b = tok_group(sub, 768, 512)
            tt(P, "dve", iqs[u].t[:].rearrange("p (h d) -> p h d", h=8), b.t[:].rearrange("p (h d) -> p h d", h=8),
               absw[u].t[:].unsqueeze(2).broadcast_to([128, 8, 64]), ALU.mult, [b, absw[u]], [iqs[u]])
            for r in range(4):
                tr(P, pi.t[:, r * 128:(r + 1) * 128], iqs[u].t[:, r * 128:(r + 1) * 128],
                   ident.t[:], [iqs[u], ident], [pi])
            tr(P, pi.t[0:64, 512:640], ikn[u].t[:], ident.t[:], [ikn[u], ident], [pi])
            cp(P, "act", IQTt[s_].t[:, :, ssl], pi.t[:, 0:512].rearrange("p (r t) -> p r t", r=4), [pi], [IQTt[s_]])
            cp(P, "dve", IKTt[s_].t[:, ssl], pi.t[0:64, 512:640], [pi], [IKTt[s_]])
            if sub == 1:
                dma(P, "sp", A["IQT"][:, :, tsl].rearrange("r p t -> p r t"), IQTt[s_].t[:], [IQTt[s_]], [], IQT_g[s_])
                dma(P, "sp", A["IKT"][:, tsl], IKTt[s_].t[:], [IKTt[s_]], [], IKT_g[s_])
                dma(P, "sp", A["SGN"][tsl, :].rearrange("(s p) h -> p s h", p=128), sgn[s_].t[:], [sgn[s_]], [], SGN_g[s_])
            if LV < 5:
                continue
            b = tok_group(sub, 1864, 512)
            cp(P, "dve", mvb[s_].t[:, sub, :], b.t[:], [b], [mvb[s_]])
            b = tok_group(sub, 2376, 8)
            tt(P, "dve", zt[u].t[:], b.t[:, 0:8], BIF.t[:], ALU.add, [b, BIF], [zt[u]])
            cp(P, "pool", lif[s_].t[:, sub, 0:4], zt[u].t[:, 0:4], [zt[u]], [lif[s_]])
            act(P, et[u].t[:], zt[u].t[:, 4:8], AF.Exp, [zt[u]], [et[u]], scale=-1.0)
            act(P, et[u].t[:], et[u].t[:], AF.Ln, [et[u]], [et[u]], bias=1.0)
            ts(P, "dve", lif[s_].t[:, sub, 4:8], et[u].t[:], -1.0, None, ALU.mult, ALU.bypass, [et[u]], [lif[s_]])
            b = tok_group(sub, 2384, 512)
            act(P, mob[s_].t[:, sub, :], b.t[:], AF.Sigmoid, [b], [mob[s_]])
        if LV < 6:
            return
        dma(P, "sp", A["MV"][tsl, :].rearrange("(s p) d -> p s d", p=128), mvb[s_].t[:], [mvb[s_]], [], MV_g[s_])
        dma(P, "sp", A["LIF"][tsl, :].rearrange("(s p) d -> p s d", p=128), lif[s_].t[:], [lif[s_]], [], LIF_g[s_])
        dma(P, "sp", A["MO"][tsl, :].rearrange("(s p) d -> p s d", p=128), mob[s_].t[:], [mob[s_]], [], MO_g[s_])

        if i + 1 < NT:
            nt(i + 1)
        def feat_pair(c0):
            b = bank()
            for c in range(2):
                for kc in range(8):
                    mm(P, b.t[:, c * TT:(c + 1) * TT], WIN.t[:, kc, c0 + c * 128:c0 + (c + 1) * 128], H.t[:, kc, :],
                       kc == 0, kc == 7, [H, WIN], [b])
            return b

        if LV < 7:
            return
        for (base, dst) in ((2896, gat[s_]), (3920, gbt[s_])):
            for pr in range(4):
                b = feat_pair(base + pr * 256)
                act(P, dst.t[:, 2 * pr:2 * pr + 2, :], b.t[:].rearrange("p (c t) -> p c t", c=2), AF.Sigmoid, [b], [dst])
        dma(P, "sp", A["GAT"][:, :, tsl].rearrange("c p t -> p c t"), gat[s_].t[:], [gat[s_]], [], GAT_g[s_])
        dma(P, "sp", A["GBT"][:, :, tsl].rearrange("c p t -> p c t"), gbt[s_].t[:], [gbt[s_]], [], GBT_g[s_])
        if LV < 8:
            return
        xm, xn = XM[s_], XM[1 - s_]
        for pr in range(2):
            b = feat_pair(1352 + pr * 256)
            cp(P, "dve", xm.t[:, 2 * pr:2 * pr + 2, 3:3 + TT], b.t[:].rearrange("p (c t) -> p c t", c=2), [b], [xm])
        cp(P, "pool", xn.t[:, :, 0:3], xm.t[:, :, TT:TT + 3], [xm], [xn])
        ac = acc[s_]
        for c in range(4):
            ts(P, "dve", ac.t[:, c, :], xm.t[:, c, 0:TT], CW.t[:, c, 0:1], CB.t[:, c:c + 1], ALU.mult, ALU.add,
               [xm, CW, CB], [ac])
            for j in range(1, 4):
                stt(P, ac.t[:, c, :], xm.t[:, c, j:j + TT], CW.t[:, c, j:j + 1], ac.t[:, c, :], ALU.mult, ALU.add,
                    [xm, CW, ac], [ac])
        act(P, sig[s_].t[:], ac.t[:], AF.Sigmoid, [ac], [sig[s_]])
        tt(P, "pool", xct[s_].t[:], ac.t[:], sig[s_].t[:], ALU.mult, [ac, sig[s_]], [xct[s_]])
        if LV < 9:
            return
        xc = xct[s_]
        for (wi, dst, scale, name, grp) in ((0, mqt[s_], MQ_SCALE, "MQT", MQT_g[s_]), (1, mkt[s_], 1.0, "MKT", MKT_g[s_])):
            for hp in range(2):
                b = bank()
                for c in range(2):
                    h = hp * 2 + c
                    mm(P, b.t[0:64, c * TT:(c + 1) * TT], WM.t[:, wi, h, :], xc.t[:, h, :], True, True, [WM, xc], [b])
                act(P, dst.t[:, 2 * hp:2 * hp + 2, :], b.t[0:64, :].rearrange("p (c t) -> p c t", c=2), AF.Copy,
                    [b], [dst], scale=scale)
            dma(P, "sp", A[name][:, :, tsl].rearrange("h d t -> d h t"), dst.t[:], [dst], [], grp)
        for sub in range(2):
            b = bank()
            for h in range(4):
                mm(P, b.t[:, h * 64:(h + 1) * 64], xc.t[:, h, sub * 128:(sub + 1) * 128], WM.t[:, 1, h, :], True, True,
                   [xc, WM], [b])
            cp(P, "dve", mkb[s_].t[:, sub, :], b.t[:, 0:256], [b], [mkb[s_]])
        dma(P, "sp", A["MK"][tsl, :].rearrange("(s p) d -> p s d", p=128), mkb[s_].t[:], [mkb[s_]], [], MK_g[s_])

    def nt(i):
        q_ = i % 2
        norm_transpose(cx, xt[q_], G, ident, NH, hb[q_], hT[q_], pT, ss[q_], rs[q_], junk)

    load(0)
    if NT > 1:
        load(1)
    nt(0)
    for i in range(NT):
        tile(i)
        if i + 2 < NT:
            load(i + 2)
    fin = QT_g + KT_g + IQT_g + IKT_g + V_g + SGN_g + MV_g + LIF_g + MO_g + GAT_g + GBT_g + MQT_g + MKT_g + MK_g
    return {"sp": fin}


NIT = 11


def dsa_phase(cx, S, A, Cd):
    nc, P = cx.nc, cx.P
    NB = S // 512
    NJ = S // 128
    cg = cx.group(final=True)
    ident = load_ident(cx, Cd, cg)
    IKbs = [cx.sb(f"IKb{i}", [128, 512], BF16) for i in range(2)]
    IK_g = [cx.group() for _ in range(2)]
    KT = cx.sb("KT", [128, S], BF16)
    dma(P, "sp", KT.t[:], A["KT"], [], [KT], cg)
    VA = cx.sb("VA", [128, NJ, 2, 65], BF16)
    mset(P, "pool", VA.t[:], 1.0, [VA])
    for g_ in range(2):
        dma(P, "sp", VA.t[:, :, g_, 0:64], A["V"][:, g_ * 64:(g_ + 1) * 64].rearrange("(j p) d -> p j d", p=128),
            [], [VA], cg)
    CM = cx.sb("CM", [128, 4, 512], BF16)
    dma(P, "sp", CM.t[:], Cd["c_cmask"].rearrange("i p s -> p i s"), [], [CM], cg)
    CROW = cx.sb("CROW", [128, NIT], F32)
    dma(P, "sp", CROW.t[:], Cd["c_steps"], [], [CROW], cg)
    ONES = cx.sb("ONES", [128, 64], F32)
    mset(P, "pool", ONES.t[:], 1.0, [ONES])

    SCs = [cx.sb(f"SC{i}", [128, S], F32) for i in range(2)]
    MK = cx.sb("MK", [128, S], BF16)
    MT = cx.sb("MT", [128, NJ, 512], BF16)
    IQb = cx.sb("IQb", [128, 4, 512], BF16)
    Qb = cx.sb("Qb", [128, 4, 512], BF16)
    SGb = cx.sb("SGb", [128, 4, 8], F32)
    ld_g1, ld_g2, ld_g3 = cx.group(), cx.group(), cx.group()
    DG = [cx.sb("DG0", [128, 8, 128], BF16)]
    Rt = [cx.sb(f"Rt{i}", [128, 512], BF16) for i in range(4)]
    Et = [cx.sb(f"Et{i}", [128, 512], BF16) for i in range(2)]
    Pt = [cx.sb(f"Pt{i}", [128, 512], BF16) for i in range(4)]
    M1 = cx.sb("M1", [128, 1], F32)
    LO = cx.sb("LO", [128, 1], F32)
    W0 = cx.sb("W0", [128, 1], F32)
    STEP = cx.sb("STEP", [128, NIT], F32)
    NSTEP = cx.sb("NSTEP", [128, NIT], F32)
    STEP2 = cx.sb("STEP2", [128, NIT], F32)
    MID = cx.sb("MID", [128, 1], F32)
    CNT = cx.sb("CNT", [128, 1], F32)
    DT = cx.sb("DT", [128, 1], F32)
    RC = cx.sb("RC", [128, 512], F32)
    OS = cx.sb("OS", [64, 512], BF16)
    YT = [cx.sb("YT0", [64, 512], BF16)] * 2
    YT_g = [cx.group() for _ in range(2)]
    pW = [cx.ps(f"pW{i}", [128, 512], F32) for i in range(4)]
    SCp = cx.ps("SCp", [128, 512], F32)
    pTr = cx.ps("pTr", [128, 1024], BF16)
    pOs = [cx.ps(f"pO{i}", [128, 512], F32) for i in range(2)]
    cnt = {"ik": 0, "l": 0, "r": 0, "s": 0, "e": 0, "p": 0, "y": 0, "d": 0, "m": 0}

    def nxt(key, lst):
        v = lst[cnt[key] % len(lst)]
        cnt[key] += 1
        return v

    tiles = [(b, ii) for b in range(NB) for ii in range(4)]
    dgs = {}

    def indexer(n):
        b, ii = tiles[n]
        bs = slice(b * 512, (b + 1) * 512)
        SC = SCs[n % 2]
        if ii == 0:
            dma(P, "sp", IQb.t[:], A["IQT"][:, :, bs].rearrange("r p t -> p r t"), [], [IQb], ld_g1)
            dma(P, "sp", SGb.t[:], A["SGN"][bs, :].rearrange("(i p) h -> p i h", p=128), [], [SGb], ld_g3)
        tq = slice(ii * 128, (ii + 1) * 128)
        dg = nxt("d", DG)
        for h in range(8):
            ts(P, "pool", dg.t[:, h, :], ident.t[:], SGb.t[:, ii, h:h + 1], 1.0, ALU.mult, ALU.mult,
               [ident, SGb], [dg])
        items = [(kb, pr) for kb in range(b + 1) for pr in range(4)]
        LA = 1
        hold = {}

        def front(it):
            kb, pr = it
            ks = slice(kb * 512, (kb + 1) * 512)
            if pr == 0:
                ikq = cnt["ik"] % 2
                cnt["ik"] += 1
                IKb = IKbs[ikq]
                dma(P, "sp", IKb.t[0:64, :], A["IKT"][:, ks], [], [IKb], IK_g[ikq])
                dma(P, "sp", IKb.t[64:128, :], A["IKT"][:, ks], [], [IKb], IK_g[ikq])
                hold[("ik", kb)] = IKb
            IKb = hold[("ik", kb)]
            Ls = []
            for half in range(2):
                ps_ = slice(half * 64, half * 64 + 64)
                L = nxt("l", pW)
                mm(P, L.t[:], IQb.t[ps_, pr, tq], IKb.t[ps_, :], True, True, [IQb, IKb], [L])
                Ls.append(L)
            Rs = []
            for half in range(2):
                R = nxt("r", Rt)
                act(P, R.t[:], Ls[half].t[:], AF.Relu, [Ls[half]], [R])
                Rs.append(R)
            hold[it] = Rs

        def back(it):
            kb, pr = it
            ks = slice(kb * 512, (kb + 1) * 512)
            Rs = hold.pop(it)
            for half in range(2):
                h = 2 * pr + half
                mm(P, SCp.t[:], dg.t[:, h, :], Rs[half].t[:], h == 0, h == 7, [dg, Rs[half]], [SCp])
            if pr == 3:
                cp(P, "act", SC.t[:, ks], SCp.t[:], [SCp], [SC])

        for idx in range(len(items) + LA):
            if idx < len(items):
                front(items[idx])
            if idx - LA >= 0:
                back(items[idx - LA])

    def post(n):
        b, ii = tiles[n]
        Nb = 512 * (b + 1)
        njb = 4 * (b + 1)
        bs = slice(b * 512, (b + 1) * 512)
        tq = slice(ii * 128, (ii + 1) * 128)
        SC = SCs[n % 2]
        Ne = 512 * b + 128 * (ii + 1)
        red(P, M1.t[:], SC.t[:, 0:Ne], ALU.max, [SC], [M1], apply_absolute_value=True)
        if Ne < Nb:
            mset(P, "pool", MK.t[:, Ne:Nb], 0.0, [MK])
        tt(P, "pool", SC.t[:, bs], SC.t[:, bs], CM.t[:, ii, :], ALU.add, [SC, CM], [SC])
        ts(P, "dve", W0.t[:], M1.t[:], 2.002, 2e-6, ALU.mult, ALU.add, [M1], [W0])
        ts(P, "dve", STEP.t[:], CROW.t[:], W0.t[:], None, ALU.mult, ALU.bypass, [CROW, W0], [STEP])
        ts(P, "dve", NSTEP.t[:], STEP.t[:], -1.0, None, ALU.mult, ALU.bypass, [STEP], [NSTEP])
        ts(P, "dve", STEP2.t[:], STEP.t[:], 2.0, None, ALU.mult, ALU.bypass, [STEP], [STEP2])
        mset(P, "dve", MID.t[:], 0.0, [MID])
        for k in range(NIT):
            ts(P, "dve", MK.t[:, 0:Ne], SC.t[:, 0:Ne], MID.t[:], None, ALU.is_ge, ALU.add, [SC, MID], [MK, CNT],
               accum=CNT.t[:])
            if k + 1 < NIT:
                stt(P, DT.t[:], CNT.t[:], 255.5, STEP2.t[:, k + 1:k + 2], ALU.is_ge, ALU.mult, [CNT, STEP2], [DT])
                stt(P, MID.t[:], DT.t[:], NSTEP.t[:, k + 1:k + 2], MID.t[:], ALU.add, ALU.add, [DT, NSTEP, MID], [MID])
            else:
                stt(P, DT.t[:], CNT.t[:], 255.5, STEP.t[:, k:k + 1], ALU.is_lt, ALU.mult, [CNT, STEP], [DT])
                tt(P, "dve", LO.t[:], MID.t[:], DT.t[:], ALU.subtract, [MID, DT], [LO])
        ts(P, "dve", MK.t[:, 0:Ne], SC.t[:, 0:Ne], LO.t[:], None, ALU.is_ge, ALU.bypass, [SC, LO], [MK])
        for j0 in range(0, njb, 8):
            nn = min(8, njb - j0)
            for jj in range(nn):
                j = j0 + jj
                tr(P, pTr.t[:, jj * 128:(jj + 1) * 128], MK.t[:, j * 128:(j + 1) * 128], ident.t[:], [MK, ident], [pTr])
            cp(P, "act" if (cnt["m"] % 2 == 0) else "dve", MT.t[:, j0:j0 + nn, tq],
               pTr.t[:, 0:nn * 128].rearrange("p (j t) -> p j t", t=128), [pTr], [MT])
            cnt["m"] += 1

    def attention(b):
        njb = 4 * (b + 1)
        bs = slice(b * 512, (b + 1) * 512)
        dma(P, "sp", Qb.t[:], A["QT"][:, :, bs].rearrange("r p t -> p r t"), [], [Qb], ld_g2)
        items = [(r, j) for r in range(4) for j in range(njb)]
        LA = 1
        hold = {}

        def front(it):
            r, j = it
            STs = []
            for g in range(2):
                ps_ = slice(g * 64, g * 64 + 64)
                ST = nxt("l", pW)
                mm(P, ST.t[:], KT.t[ps_, j * 128:(j + 1) * 128], Qb.t[ps_, r, :], True, True, [KT, Qb], [ST])
                STs.append(ST)
            PTs = []
            for g in range(2):
                E = nxt("e", Et)
                act(P, E.t[:], STs[g].t[:], AF.Exp, [STs[g]], [E])
                PT = nxt("p", Pt)
                tt(P, "pool" if (cnt["p"] % 4 == 0) else "dve", PT.t[:], E.t[:], MT.t[:, j, :], ALU.mult, [E, MT], [PT])
                PTs.append(PT)
            hold[it] = PTs

        def back(it):
            r, j = it
            PTs = hold.pop(it)
            for g in range(2):
                pO = pOs[g]
                mm(P, pO.t[0:65, :], VA.t[:, j, g, :], PTs[g].t[:], j == 0, j == njb - 1, [VA, PTs[g]], [pO])
            if j == njb - 1:
                for g in range(2):
                    h = g * 4 + r
                    pO = pOs[g]
                    P.op("dve", lambda e, pO=pO: e.reciprocal(out=RC.t[64:65, :], in_=pO.t[64:65, :]), [pO.b], [RC.b])
                    BC = nxt("l", pW)
                    mm(P, BC.t[0:64, :], ONES.t[64:65, 0:64], RC.t[64:65, :], True, True, [ONES, RC], [BC])
                    cp(P, "act", OS.t[:], pO.t[0:64, :], [pO], [OS])
                    y = cnt["y"] % 2
                    cnt["y"] += 1
                    tt(P, "dve", YT[y].t[:], OS.t[:], BC.t[0:64, :], ALU.mult, [OS, BC], [YT[y]])
                    dma(P, "sp", A["YAT"][h, :, bs], YT[y].t[:], [YT[y]], [], YT_g[y])

        for idx in range(len(items) + LA):
            if idx < len(items):
                front(items[idx])
            if idx - LA >= 0:
                back(items[idx - LA])

    indexer(0)
    for n in range(len(tiles)):
        if n + 1 < len(tiles):
            indexer(n + 1)
        post(n)
        if tiles[n][1] == 3:
            attention(tiles[n][0])
    return {"sp": YT_g}


def mlstm_phase(cx, S, A, Wd, Cd):
    nc, P = cx.nc, cx.P
    NT = S // 128
    cg = cx.group(final=True)
    ident = load_ident(cx, Cd, cg)
    T2 = cx.sb("T2", [128, 128], F32)
    dma(P, "sp", T2.t[:], Cd["c_tri2"], [], [T2], cg)
    NEGM = cx.sb("NEGM", [128, 512], BF16)
    dma(P, "sp", NEGM.t[:], Cd["c_negm"], [], [NEGM], cg)
    HG = cx.sb("HG", [128, 512], F32)
    dma(P, "sp", HG.t[:], Wd["m_head_norm"].partition_broadcast(128), [], [HG], cg)
    NH = cx.sb("NH", [128, 8], F32)
    mset(P, "pool", NH.t[:], -0.5, [NH])
    CN = [cx.sb(f"CN{i}", [64, 4, 129], F32) for i in range(3)]
    mset(P, "pool", CN[0].t[:], 0.0, [CN[0]])
    R2 = range(2)
    LIF = [cx.sb(f"LIF{i}", [128, 8], F32) for i in R2]
    MQ = [cx.sb(f"MQ{i}", [64, 4, 128], BF16) for i in R2]
    MKt = [cx.sb(f"MKt{i}", [64, 4, 128], BF16) for i in R2]
    MKb = [cx.sb(f"MKb{i}", [128, 256], BF16) for i in R2]
    VA = [cx.sb(f"VA{i}", [128, 4, 129], BF16) for i in R2]
    MOt = [cx.sb(f"MOt{i}", [128, 512], BF16) for i in R2]
    lg = [[cx.group() for _ in range(6)] for _ in R2]
    for i in R2:
        mset(P, "pool", VA[i].t[:], 1.0, [VA[i]])
    LFm = cx.sb("LFm", [128, 4, 128], F32)
    bias = cx.sb("bias", [128, 4], F32)
    AT = cx.sb("AT", [128, 4, 128], F32)
    EBs = [cx.sb(f"EB{i}", [128, 4, 128], F32) for i in range(2)]
    BL = cx.sb("BL", [128, 4], F32)
    WS = cx.sb("WS", [128, 4], F32)
    WTs = [cx.sb(f"WT{i}", [128, 4, 128], BF16) for i in range(2)]
    QPs = [cx.sb(f"QP{i}", [64, 4, 128], F32) for i in range(2)]
    KWs = [cx.sb(f"KW{i}", [128, 4, 64], BF16) for i in range(2)]
    dn = cx.sb("dn", [128, 4], F32)
    rec = cx.sb("rec", [128, 4], F32)
    hh = cx.sb("hh", [128, 4, 128], F32)
    junk = cx.sb("junk", [128, 128], BF16)
    ssq = cx.sb("ssq", [128, 4], F32)
    rs4 = cx.sb("rs4", [128, 4], F32)
    yb = cx.sb("yb", [128, 512], BF16)
    YBt = [cx.sb(f"YBt{i}", [128, 4, 512], BF16) for i in R2]
    YB_g = [cx.group() for _ in R2]
    pX = cx.ps("pX", [128, 512], F32)
    pY = cx.ps("pY", [128, 512], F32)
    pZ = cx.ps("pZ", [128, 512], F32)
    pQ = cx.ps("pQ", [128, 512], F32)
    pD = cx.ps("pD", [128, 512], F32)
    pN = [cx.ps(f"pN{i}", [128, 512], F32) for i in R2]
    pTr = cx.ps("pTr", [128, 1024], BF16)

    def load(i):
        s_ = i % 2
        tsl = slice(i * 128, (i + 1) * 128)
        g = lg[s_]
        dma(P, "sp", LIF[s_].t[:], A["LIF"][tsl, :], [], [LIF[s_]], g[0])
        dma(P, "sp", MQ[s_].t[:], A["MQT"][:, :, tsl].rearrange("h d t -> d h t"), [], [MQ[s_]], g[1])
        dma(P, "sp", MKt[s_].t[:], A["MKT"][:, :, tsl].rearrange("h d t -> d h t"), [], [MKt[s_]], g[2])
        dma(P, "sp", MKb[s_].t[:], A["MK"][tsl, :], [], [MKb[s_]], g[3])
        dma(P, "sp", VA[s_].t[:, :, 0:128], A["MV"][tsl, :].rearrange("p (h d) -> p h d", h=4), [], [VA[s_]], g[4])
        dma(P, "sp", MOt[s_].t[:], A["MO"][tsl, :], [], [MOt[s_]], g[5])

    def front(i):
        s_ = i % 2
        lif, mq, mkt, mkb, va, mo = LIF[s_], MQ[s_], MKt[s_], MKb[s_], VA[s_], MOt[s_]
        EB, WT, QP, KW = EBs[s_], WTs[s_], QPs[s_], KWs[s_]
        import os
        ML = int(os.environ.get('ML', '99'))
        cp(P, "pool", LFm.t[:], lif.t[:, 4:8].unsqueeze(2).broadcast_to([128, 4, 128]), [lif], [LFm])
        for h in range(4):
            mm(P, pX.t[:, h * 128:(h + 1) * 128], LFm.t[:, h, :], T2.t[:], True, True, [LFm, T2], [pX])
        mm(P, pY.t[:], ident.t[:], NEGM.t[:], True, False, [ident, NEGM], [pY])
        for h in range(4):
            mm(P, pY.t[:, h * 128:(h + 1) * 128], LFm.t[:, h, :], T2.t[:], False, h == 3, [LFm, T2], [pY])
        mm(P, pZ.t[:, 0:4], T2.t[:], lif.t[:, 4:8], True, True, [T2, lif], [pZ])
        tt(P, "dve", bias.t[:], lif.t[:, 0:4], pZ.t[:, 0:4], ALU.subtract, [lif, pZ], [bias])
        for h in range(4):
            act(P, AT.t[:, h, :], pY.t[:, h * 128:(h + 1) * 128], AF.Exp, [pY, bias], [AT], bias=bias.t[:, h:h + 1])
        act(P, EB.t[:], pX.t[:].rearrange("p (h t) -> p h t", h=4), AF.Exp, [pX], [EB])
        if ML < 2:
            return
        pXv = pX.t[:].rearrange("p (h t) -> p h t", h=4)
        cp(P, "dve", BL.t[0:64, :], pXv[0:64, :, 63], [pX], [BL])
        cp(P, "dve", BL.t[64:128, :], pXv[64:128, :, 127], [pX], [BL])
        tt(P, "dve", BL.t[:], BL.t[:], bias.t[:], ALU.add, [BL, bias], [BL])
        act(P, WS.t[:], BL.t[:], AF.Exp, [BL], [WS])
        for h in range(4):
            mm(P, pQ.t[:, h * 128:(h + 1) * 128], mkt.t[:, h, :], mq.t[:, h, :], True, True, [mkt, mq], [pQ])
        tt(P, "dve", WT.t[:], pQ.t[:].rearrange("p (h t) -> p h t", h=4), AT.t[:], ALU.mult, [pQ, AT], [WT])
        tt(P, "pool", QP.t[:], mq.t[:], EB.t[0:64, :, :], ALU.mult, [mq, EB], [QP])
        tt(P, "pool", KW.t[:], mkb.t[:].rearrange("p (h d) -> p h d", h=4),
           WS.t[:].unsqueeze(2).broadcast_to([128, 4, 64]), ALU.mult, [mkb, WS], [KW])

    def back(i):
        s_ = i % 2
        lif, mq, mkt, mkb, va, mo = LIF[s_], MQ[s_], MKt[s_], MKb[s_], VA[s_], MOt[s_]
        EB, WT, QP, KW = EBs[s_], WTs[s_], QPs[s_], KWs[s_]
        import os
        ML = int(os.environ.get('ML', '99'))
        cc, cm, cn = CN[(2 * i) % 3], CN[(2 * i + 1) % 3], CN[(2 * i + 2) % 3]
        if ML < 3:
            return
        for (rows, src, dst, col) in ((slice(0, 64), cc, cm, 63), (slice(64, 128), cm, cn, 127)):
            for hp in range(2):
                for c in range(2):
                    h = 2 * hp + c
                    mm(P, pD.t[0:64, c * 129:(c + 1) * 129], KW.t[rows, h, :], va.t[rows, h, :], True, True,
                       [KW, va], [pD])
                for c in range(2):
                    h = 2 * hp + c
                    stt(P, dst.t[:, h, :], src.t[:, h, :], EB.t[0:64, h, col:col + 1], pD.t[0:64, c * 129:(c + 1) * 129],
                        ALU.mult, ALU.add, [src, EB, pD], [dst])
        if ML < 4:
            return
        for h in range(4):
            bk = pN[h // 2]
            osl = slice((h % 2) * 129, (h % 2) * 129 + 129)
            mm(P, bk.t[:, osl], WT.t[:, h, :], va.t[:, h, :], True, False, [WT, va], [bk])
            mm(P, bk.t[0:64, osl], QP.t[:, h, 0:64], cc.t[:, h, :], False, False, [QP, cc], [bk])
            mm(P, bk.t[64:128, osl], QP.t[:, h, 64:128], cm.t[:, h, :], False, True, [QP, cm], [bk])
        if ML < 5:
            return
        for q in range(2):
            act(P, dn.t[:, 2 * q:2 * q + 2], pN[q].t[:, 0:258].rearrange("p (h c) -> p h c", c=129)[:, :, 128],
                AF.Abs, [pN[q]], [dn])
        ts(P, "dve", dn.t[:], dn.t[:], 1.0, None, ALU.max, ALU.bypass, [dn], [dn])
        P.op("dve", lambda e: e.reciprocal(out=rec.t[:], in_=dn.t[:]), [dn.b], [rec.b])
        for h in range(4):
            o0 = (h % 2) * 129
            act(P, hh.t[:, h, :], pN[h // 2].t[:, o0:o0 + 128], AF.Copy, [pN[h // 2], rec], [hh], scale=rec.t[:, h:h + 1])
        if ML < 6:
            return
        for h in range(4):
            act(P, junk.t[:], hh.t[:, h, :], AF.Square, [hh], [junk, ssq], accum_out=ssq.t[:, h:h + 1])
        rstd_ops(cx, ssq, 128, rs4, NH, [])
        if ML < 7:
            return
        tt(P, "dve", hh.t[:], hh.t[:], rs4.t[:].unsqueeze(2).broadcast_to([128, 4, 128]), ALU.mult, [hh, rs4], [hh])
        hf = hh.t[:].rearrange("p h d -> p (h d)")
        tt(P, "pool", hf, hf, HG.t[:], ALU.mult, [hh, HG], [hh])
        tt(P, "dve", yb.t[:], hf, mo.t[:], ALU.mult, [hh, mo], [yb])
        if ML < 8:
            return
        for c in range(4):
            tr(P, pTr.t[:, c * 128:(c + 1) * 128], yb.t[:, c * 128:(c + 1) * 128], ident.t[:], [yb, ident], [pTr])
        yt = YBt[(i // 4) % 2]
        cp(P, "act", yt.t[:, :, (i % 4) * 128:(i % 4 + 1) * 128], pTr.t[:, 0:512].rearrange("p (c t) -> p c t", c=4),
           [pTr], [yt])
        if i % 4 == 3:
            b0 = (i // 4) * 512
            dma(P, "sp", A["YBT"][:, :, b0:b0 + 512].rearrange("c p t -> p c t"), yt.t[:], [yt], [], YB_g[(i // 4) % 2])

    load(0)
    if NT > 1:
        load(1)
    front(0)
    for i in range(NT):
        if i + 1 < NT:
            front(i + 1)
        back(i)
        if i + 2 < NT:
            load(i + 2)
    return {"sp": YB_g}


def merge_phase(cx, S, X1, X2, A, Wd, Cd):
    nc, P = cx.nc, cx.P
    TT = 256
    NT = S // TT
    WA = cx.sb("WA", [128, 4, D], BF16)
    WB = cx.sb("WB", [128, 4, D], BF16)
    WO = cx.sb("WO", [128, 8, D], BF16)
    stg = [cx.sb(f"stg{i}", [128, D], F32) for i in range(2)]
    stg_g = [cx.group() for _ in range(2)]
    k = 0
    for (src, dst, n) in ((Wd["w_proj_a"], WA, 4), (Wd["w_proj_b"], WB, 4), (Wd["w_out"], WO, 8)):
        for c in range(n):
            s_ = k % 2
            dma(P, "sp", stg[s_].t[:], src[c * 128:(c + 1) * 128, :], [], [stg[s_]], stg_g[s_])
            cp(P, ["act", "dve", "pool"][k % 3], dst.t[:, c, :], stg[s_].t[:], [stg[s_]], [dst])
            k += 1
    R2 = range(2)
    YA = [cx.sb(f"YA{i}", [128, 4, TT], BF16) for i in R2]
    YB = [cx.sb(f"YB{i}", [128, 4, TT], BF16) for i in R2]
    GA = [cx.sb(f"GA{i}", [128, 8, TT], BF16) for i in R2]
    GB = [cx.sb(f"GB{i}", [128, 8, TT], BF16) for i in R2]
    xt = [cx.sb(f"xt{i}", [128, 2, D], F32) for i in R2]
    lg = [[cx.group() for _ in range(5)] for _ in R2]
    st_g = [cx.group() for _ in R2]
    m1 = [cx.sb(f"m1{i}", [128, 512], F32) for i in R2]
    m2 = [cx.sb(f"m2{i}", [128, 512], F32) for i in R2]
    MG = [cx.sb(f"MG{i}", [128, 8, TT], BF16) for i in R2]
    pA = [cx.ps(f"pA{i}", [128, 512], F32) for i in R2]
    pB = [cx.ps(f"pB{i}", [128, 512], F32) for i in R2]
    pO = [cx.ps(f"pO{i}", [128, 512], F32) for i in R2]

    def load(i):
        s_ = i % 2
        tsl = slice(i * TT, (i + 1) * TT)
        g = lg[s_]
        dma(P, "sp", YA[s_].t[:], A["YAT"][:, :, tsl].rearrange("(c w) d t -> (w d) c t", w=2), [], [YA[s_]], g[0])
        dma(P, "sp", YB[s_].t[:], A["YBT"][:, :, tsl].rearrange("c p t -> p c t"), [], [YB[s_]], g[1])
        dma(P, "sp", GA[s_].t[:], A["GAT"][:, :, tsl].rearrange("c p t -> p c t"), [], [GA[s_]], g[2])
        dma(P, "sp", GB[s_].t[:], A["GBT"][:, :, tsl].rearrange("c p t -> p c t"), [], [GB[s_]], g[3])
        dma(P, "sp", xt[s_].t[:], X1[tsl, :].rearrange("(s p) d -> p s d", p=128), [], [xt[s_]], g[4])

    def tile(i):
        s_ = i % 2
        tsl = slice(i * TT, (i + 1) * TT)
        for dcp in range(4):
            q = dcp % 2
            for (W_, Y_, bk) in ((WA, YA[s_], pA[q]), (WB, YB[s_], pB[q])):
                for c2 in range(2):
                    dc = 2 * dcp + c2
                    for c in range(4):
                        mm(P, bk.t[:, c2 * TT:(c2 + 1) * TT], W_.t[:, c, dc * 128:(dc + 1) * 128], Y_.t[:, c, :],
                           c == 0, c == 3, [W_, Y_], [bk])
            gsl = slice(2 * dcp, 2 * dcp + 2)
            tt(P, "dve", m1[q].t[:].rearrange("p (c t) -> p c t", c=2), pA[q].t[:].rearrange("p (c t) -> p c t", c=2),
               GA[s_].t[:, gsl, :], ALU.mult, [pA[q], GA[s_]], [m1[q]])
            tt(P, "dve", m2[q].t[:].rearrange("p (c t) -> p c t", c=2), pB[q].t[:].rearrange("p (c t) -> p c t", c=2),
               GB[s_].t[:, gsl, :], ALU.mult, [pB[q], GB[s_]], [m2[q]])
            tt(P, "pool", MG[s_].t[:, gsl, :], m1[q].t[:].rearrange("p (c t) -> p c t", c=2),
               m2[q].t[:].rearrange("p (c t) -> p c t", c=2), ALU.add, [m1[q], m2[q]], [MG[s_]])
        n = 0
        for sub in range(2):
            for dh in range(2):
                bk = pO[n % 2]
                n += 1
                for dc in range(8):
                    mm(P, bk.t[:], MG[s_].t[:, dc, sub * 128:(sub + 1) * 128], WO.t[:, dc, dh * 512:(dh + 1) * 512],
                       dc == 0, dc == 7, [MG[s_], WO], [bk])
                xs = xt[s_].t[:, sub, dh * 512:(dh + 1) * 512]
                tt(P, "dve", xs, bk.t[:], xs, ALU.add, [bk, xt[s_]], [xt[s_]])
        dma(P, "pool", X2[tsl, :].rearrange("(s p) d -> p s d", p=128), xt[s_].t[:], [xt[s_]], [], st_g[s_])

    load(0)
    if NT > 1:
        load(1)
    for i in range(NT):
        tile(i)
        if i + 2 < NT:
            load(i + 2)
    return {"pool": st_g}


WSPEC = {
    "ffn1_norm": [1, D], "ffn1_w_gate": [D, DFF], "ffn1_w_up": [D, DFF], "ffn1_w_down": [DFF, D],
    "mix_norm": [1, D], "w_in": [D, DIN], "q_norm": [1, 64], "k_norm": [1, 64], "idx_k_norm": [1, 64],
    "conv_w": [128, 16], "conv_b": [128, 4], "w_mq": [128, 256], "w_mk": [128, 256], "b_i": [1, 4],
    "b_f": [1, 4], "m_head_norm": [1, 512], "w_proj_a": [512, D], "w_proj_b": [512, D], "w_out": [D, D],
    "ffn2_norm": [1, D], "ffn2_w_gate": [D, DFF], "ffn2_w_up": [D, DFF], "ffn2_w_down": [DFF, D],
}


def consts():
    c = {}
    c["c_ident"] = np.eye(128, dtype=np.float32).astype(ml_dtypes.bfloat16)
    cm = np.zeros((4, 128, 512), np.float32)
    for ii in range(4):
        for p in range(128):
            lim = ((ii * 128 + p) // 64 + 1) * 64
            cm[ii, p, lim:] = -1e30
    c["c_cmask"] = cm.astype(ml_dtypes.bfloat16)
    c["c_steps"] = np.tile((2.0 ** -(np.arange(NIT) + 1.0)).astype(np.float32)[None, :], (128, 1))
    j = np.arange(128)
    t2 = ((j[:, None] // 64 == j[None, :] // 64) & (j[:, None] <= j[None, :]))
    c["c_tri2"] = t2.astype(np.float32)
    c["c_negm"] = np.tile(np.where(t2, 0.0, NEG).astype(np.float32), (1, 4)).astype(ml_dtypes.bfloat16)
    return c


CSPEC = {"c_ident": ([128, 128], BF16), "c_cmask": ([4, 128, 512], BF16), "c_steps": ([128, NIT], F32),
         "c_tri2": ([128, 128], F32), "c_negm": ([128, 512], BF16)}


def build_nc(S, debug=False, upto=9):
    nc = bass.Bass("TRN2", target_bir_lowering=False)

    def din(name, shape, dt=F32):
        return nc.dram_tensor(name, list(shape), dt, kind="ExternalInput").ap()

    x = din("x", [S, D])
    Wd = {k: din(k, v) for k, v in WSPEC.items()}
    Cd = {k: din(k, v[0], v[1]) for k, v in CSPEC.items()}
    out = nc.dram_tensor("out", [S, D], F32, kind="ExternalOutput").ap()
    kind = "ExternalOutput" if debug else "Internal"

    def scr(name, shape, dt):
        return nc.dram_tensor(name, list(shape), dt, kind=kind).ap()

    X1 = scr("X1", [S, D], F32)
    X2 = scr("X2", [S, D], F32)
    A = {
        "QT": scr("QT", [4, 128, S], BF16), "KT": scr("KT", [128, S], BF16), "V": scr("V", [S, 128], BF16),
        "IQT": scr("IQT", [4, 128, S], BF16), "IKT": scr("IKT", [64, S], BF16), "SGN": scr("SGN", [S, 8], F32),
        "MV": scr("MV", [S, 512], BF16), "LIF": scr("LIF", [S, 8], F32), "MO": scr("MO", [S, 512], BF16),
        "GAT": scr("GAT", [8, 128, S], BF16), "GBT": scr("GBT", [8, 128, S], BF16),
        "MQT": scr("MQT", [4, 64, S], BF16), "MKT": scr("MKT", [4, 64, S], BF16), "MK": scr("MK", [S, 256], BF16),
        "YAT": scr("YAT", [8, 64, S], BF16), "YBT": scr("YBT", [4, 128, S], BF16),
    }
    ph = [
        ("f1", lambda cx: ffn_phase(cx, S, x, X1, Wd["ffn1_norm"], Wd["ffn1_w_gate"], Wd["ffn1_w_up"],
                                    Wd["ffn1_w_down"], Cd["c_ident"])),
        ("pj", lambda cx: proj_phase(cx, S, X1, A, Wd, Cd)),
        ("ds", lambda cx: dsa_phase(cx, S, A, Cd)),
        ("ml", lambda cx: mlstm_phase(cx, S, A, Wd, Cd)),
        ("mg", lambda cx: merge_phase(cx, S, X1, X2, A, Wd, Cd)),
        ("f2", lambda cx: ffn_phase(cx, S, X2, out, Wd["ffn2_norm"], Wd["ffn2_w_gate"], Wd["ffn2_w_up"],
                                    Wd["ffn2_w_down"], Cd["c_ident"])),
    ]
    import os
    lo_ = int(os.environ.get('FROM', '0'))
    for n, (tag, fn) in enumerate(ph):
        if lo_ <= n < upto:
            run_phase(nc, tag, fn, debug)
    return nc


def layout_weights(inputs):
    sh = {}
    for k, v in WSPEC.items():
        a = np.asarray(inputs[k], dtype=np.float32)[0]
        if k == "conv_w":
            a = a.reshape(4, 4, 128).transpose(2, 1, 0)
        elif k == "conv_b":
            a = a.reshape(4, 128).T
        elif k in ("w_mq", "w_mk"):
            a = a.transpose(1, 0, 2)
        sh[k] = np.ascontiguousarray(a).reshape(v)
    return sh


_NC_CACHE = {}


def kernel(**inputs):
    x = np.ascontiguousarray(np.asarray(inputs["x"], dtype=np.float32))
    B, S, _ = x.shape
    if S not in _NC_CACHE:
        _NC_CACHE[S] = build_nc(S)
    nc = _NC_CACHE[S]
    shared = layout_weights(inputs)
    shared.update(consts())
    in_maps = []
    for b in range(B):
        m = dict(shared)
        m["x"] = x[b]
        in_maps.append(m)
    res = run_bass_kernel_spmd(nc, in_maps, core_ids=list(range(B)))
    return np.stack([np.asarray(r["out"], dtype=np.float32) for r in res.results], axis=0)
```

```python
import numpy as np
from contextlib import ExitStack
import ml_dtypes
import concourse.bass as bass
import concourse.mybir as mybir
from concourse.bass_utils import run_bass_kernel_spmd

F32 = mybir.dt.float32
BF16 = mybir.dt.bfloat16
ALU = mybir.AluOpType
AF = mybir.ActivationFunctionType
AX = mybir.AxisListType

D = 1024
DFF = 2816
NFF = DFF // 128
DIN = 4944
EPS = 1e-6
NEG = -30000.0


class Buf:
    __slots__ = ("name", "w", "r", "psum")

    def __init__(self, name):
        self.name = name
        self.w = {}
        self.r = {}
        self.psum = False


class Group:
    def __init__(self, sem, final=False):
        self.sem = sem
        self.n = 0
        self.final = final


class Prog:
    ENG = ("pe", "act", "dve", "pool", "sp")

    def __init__(self, nc):
        self.nc = nc
        self.ops = {e: [] for e in self.ENG}

    def op(self, eng, fn, reads=(), writes=(), group=None):
        ops = self.ops[eng]
        idx = len(ops)
        rec = {"fn": fn, "waits": {}, "sig": False, "group": group}
        is_dma = group is not None

        def need(key, val, raw):
            if is_dma and key[0] == "g" and key[1] is group:
                return
            if key[0] == "e" and key[1] == eng and not is_dma:
                if eng == "pe" or not raw:
                    return
            w = rec["waits"]
            if w.get(key, -1) < val:
                w[key] = val
            if key[0] == "e":
                self.ops[key[1]][val]["sig"] = True

        for b in reads:
            for k, v in b.w.items():
                need(k, v, True)
            if b.psum:
                for k, v in b.r.items():
                    need(k, v, False)
        for b in writes:
            for k, v in b.w.items():
                need(k, v, False)
            for k, v in b.r.items():
                need(k, v, False)
        if is_dma:
            group.n += 1
            me = (("g", group), group.n)
        else:
            me = (("e", eng), idx)
        for b in reads:
            if b.r.get(me[0], -1) < me[1]:
                b.r[me[0]] = me[1]
        for b in writes:
            b.w = {me[0]: me[1]}
            b.r = {}
        ops.append(rec)
        return rec

    def emit(self, block, sems, final_waits, base=None):
        if base is None:
            base = {e: 0 for e in self.ENG}
        sigcount = {}
        for e in self.ENG:
            c = base[e]
            lst = []
            for rec in self.ops[e]:
                if rec["sig"]:
                    c += 1
                lst.append(c)
            sigcount[e] = lst

        def run(engname, engine):
            waited = {}
            for rec in self.ops[engname]:
                for key, val in rec["waits"].items():
                    if key[0] == "e":
                        sem = sems[key[1]]
                        v = sigcount[key[1]][val]
                    else:
                        g = key[1]
                        sem = g.sem
                        v = 16 * (g.n if g.final else val)
                    if waited.get(id(sem), -1) >= v:
                        continue
                    waited[id(sem)] = v
                    engine.wait_ge(sem, v)
                ins = rec["fn"](engine)
                if rec["group"] is not None:
                    ins.then_inc(rec["group"].sem, 16)
                elif rec["sig"]:
                    ins.then_inc(sems[engname], 1)
            for g in final_waits.get(engname, ()):
                engine.wait_ge(g.sem, 16 * g.n)

        @block.tensor
        def _(e):
            run("pe", e)

        @block.scalar
        def _(e):
            run("act", e)

        @block.vector
        def _(e):
            run("dve", e)

        @block.gpsimd
        def _(e):
            run("pool", e)

        @block.sync
        def _(e):
            run("sp", e)

        for e in self.ENG:
            base[e] = sigcount[e][-1] if sigcount[e] else base[e]


class _Alloc:
    def __init__(self, nc, es):
        self._nc = nc
        self._es = es

    def alloc_sbuf_tensor(self, name, shape, dt):
        return self._es.enter_context(self._nc.sbuf_tensor(name, list(shape), dt))

    def alloc_psum_tensor(self, name, shape, dt):
        return self._es.enter_context(self._nc.psum_tensor(name, list(shape), dt))


class TL:
    def __init__(self, t, name):
        self.t = t
        self.b = Buf(name)


class Ctx:
    def __init__(self, nc, es, tag, debug=False):
        self.rnc = nc
        self.nc = _Alloc(nc, es)
        self.es = es
        self.tag = tag
        self.P = Prog(nc)
        self.groups = []
        self.debug = debug

    def group(self, final=False):
        sem = self.rnc.alloc_semaphore(f"{self.tag}g{len(self.groups)}")
        g = Group(sem, final)
        self.groups.append(g)
        return g

    def sb(self, name, shape, dt):
        return TL(self.nc.alloc_sbuf_tensor(self.tag + name, list(shape), dt), name)

    def ps(self, name, shape, dt):
        t = TL(self.nc.alloc_psum_tensor(self.tag + name, list(shape), dt), name)
        t.b.psum = True
        return t


_GSTATE = {}


def run_phase(nc, tag, fn, debug=False):
    with ExitStack() as es:
        cx = Ctx(nc, es, tag, debug)
        import os
        for _i in range(int(os.environ.get('DUMMYSEM', '0'))):
            es.enter_context(nc.semaphore(f"{tag}dummy{_i}"))
        finals = fn(cx)
        st = _GSTATE.setdefault(id(nc), {})
        if "sems" not in st:
            st["sems"] = {e: nc.alloc_semaphore(f"s_{e}") for e in Prog.ENG}
            st["base"] = {e: 0 for e in Prog.ENG}
        with nc.Block() as block:
            cx.P.emit(block, st["sems"], finals, st["base"])


def _rr(lst, i):
    return lst[i % len(lst)]


def ffn_phase(cx, S, xin, xout, gvec, wg, wu, wd, identd):
    nc, P = cx.nc, cx.P
    tag = cx.tag
    xin_buf, xout_buf = Buf("xin"), Buf("xout")
    ident = nc.alloc_sbuf_tensor(f"{tag}ident", [128, 128], BF16)
    identb = Buf("ident")
    cg0 = cx.group(final=True)
    P.op("sp", lambda e: e.dma_start(out=ident[:], in_=identd), writes=[identb], group=cg0)
    TT = 256
    NT = S // TT
    WG = nc.alloc_sbuf_tensor(f"{tag}WG", [128, 8, DFF], BF16)
    WU = nc.alloc_sbuf_tensor(f"{tag}WU", [128, 8, DFF], BF16)
    WD = nc.alloc_sbuf_tensor(f"{tag}WD", [128, NFF, D], BF16)
    stg = [nc.alloc_sbuf_tensor(f"{tag}stg{i}", [128, DFF // 2], F32) for i in range(2)]
    stg_b = [Buf(f"stg{i}") for i in range(2)]
    stg_g = [cx.group() for _ in range(2)]
    wbuf = Buf("W")
    wgb = [Buf(f"WG{h}") for h in range(2)]
    wub = [Buf(f"WU{h}") for h in range(2)]
    wdb = [Buf(f"WD{f}") for f in range(NFF)]
    G = nc.alloc_sbuf_tensor(f"{tag}G", [128, D], F32)
    gb = Buf("G")
    cg = cx.group(final=True)
    P.op("sp", lambda e: e.dma_start(out=G[:], in_=gvec.partition_broadcast(128)), writes=[gb], group=cg)
    nhalf = nc.alloc_sbuf_tensor(f"{tag}nh", [128, 2], F32)
    nhb = Buf("nh")
    P.op("pool", lambda e: e.memset(nhalf[:], -0.5), writes=[nhb])

    conv_eng = ["act", "dve", "pool"]
    k = 0

    def load_conv(src_ap, dst_ap, width, wb):
        nonlocal k
        s = k % 2
        st, sb_, sg = stg[s], stg_b[s], stg_g[s]
        P.op("sp", lambda e: e.dma_start(out=st[:, 0:width], in_=src_ap), writes=[sb_], group=sg)
        ce = conv_eng[k % 3]
        if ce == "act":
            P.op("act", lambda e: e.copy(out=dst_ap, in_=st[:, 0:width]), reads=[sb_], writes=[wb])
        else:
            P.op(ce, lambda e: e.tensor_copy(out=dst_ap, in_=st[:, 0:width]), reads=[sb_], writes=[wb])
        k += 1

    HF = DFF // 2
    for hh in range(2):
        for kc in range(8):
            load_conv(wg[kc * 128:(kc + 1) * 128, hh * HF:(hh + 1) * HF], WG[:, kc, hh * HF:(hh + 1) * HF], HF, wgb[hh])
            load_conv(wu[kc * 128:(kc + 1) * 128, hh * HF:(hh + 1) * HF], WU[:, kc, hh * HF:(hh + 1) * HF], HF, wub[hh])
    for f in range(NFF):
        load_conv(wd[f * 128:(f + 1) * 128, :], WD[:, f, :], D, wdb[f])

    xt = [nc.alloc_sbuf_tensor(f"{tag}xt{i}", [128, 2, D], F32) for i in range(2)]
    xt_b = [Buf(f"xt{i}") for i in range(2)]
    xt_g = [cx.group() for _ in range(2)]
    st_g = [cx.group() for _ in range(2)]
    junk = nc.alloc_sbuf_tensor(f"{tag}junk", [128, D], BF16)
    junk_b = Buf("junk")
    ss = [nc.alloc_sbuf_tensor(f"{tag}ss{i}", [128, 2], F32) for i in range(2)]
    ss_b = [Buf(f"ss{i}") for i in range(2)]
    ms = [nc.alloc_sbuf_tensor(f"{tag}ms{i}", [128, 2], F32) for i in range(2)]
    ms_b = [Buf(f"ms{i}") for i in range(2)]
    rs = [nc.alloc_sbuf_tensor(f"{tag}rs{i}", [128, 2], F32) for i in range(2)]
    rs_b = [Buf(f"rs{i}") for i in range(2)]
    hb = [nc.alloc_sbuf_tensor(f"{tag}h{i}", [128, 2, D], BF16) for i in range(2)]
    hb_b = [Buf(f"h{i}") for i in range(2)]
    hT = [nc.alloc_sbuf_tensor(f"{tag}hT{i}", [128, 8, TT], BF16) for i in range(2)]
    hT_b = [Buf(f"hT{i}") for i in range(2)]
    actT = [nc.alloc_sbuf_tensor(f"{tag}actT{i}", [128, NFF, TT], BF16) for i in range(2)]
    actT_b = [Buf(f"actT{i}") for i in range(2)]
    sg = [nc.alloc_sbuf_tensor(f"{tag}sg{i}", [128, 512], F32) for i in range(2)]
    sg_b = [Buf(f"sg{i}") for i in range(2)]
    pT = [nc.alloc_psum_tensor(f"{tag}pT{i}", [128, 1024], BF16) for i in range(2)]
    pT_b = [Buf(f"pT{i}") for i in range(2)]
    pG = [nc.alloc_psum_tensor(f"{tag}pG{i}", [128, 512], F32) for i in range(2)]
    pG_b = [Buf(f"pG{i}") for i in range(2)]
    pU = [nc.alloc_psum_tensor(f"{tag}pU{i}", [128, 512], F32) for i in range(2)]
    pU_b = [Buf(f"pU{i}") for i in range(2)]
    pD = [nc.alloc_psum_tensor(f"{tag}pD{i}", [128, 512], F32) for i in range(2)]
    pD_b = [Buf(f"pD{i}") for i in range(2)]

    def stage_load(i):
        s = i % 2
        t0 = i * TT
        P.op("sp", lambda e: e.dma_start(
            out=xt[s][:], in_=xin[t0:t0 + TT, :].rearrange("(s p) d -> p s d", p=128)),
            reads=[xin_buf], writes=[xt_b[s]], group=xt_g[s])

    def stage_norm(i):
        s = i % 2
        for sub in range(2):
            P.op("act", lambda e, sub=sub: e.activation(
                out=junk[:], in_=xt[s][:, sub, :], func=AF.Square, accum_out=ss[s][:, sub:sub + 1]),
                reads=[xt_b[s]], writes=[junk_b, ss_b[s]])
        P.op("dve", lambda e: e.tensor_scalar(out=ms[s][:], in0=ss[s][:], scalar1=1.0 / D, scalar2=EPS,
                                              op0=ALU.mult, op1=ALU.add), reads=[ss_b[s]], writes=[ms_b[s]])
        P.op("pool", lambda e: e.tensor_tensor(out=rs[s][:], in0=ms[s][:], in1=nhalf[:], op=ALU.pow),
             reads=[ms_b[s], nhb], writes=[rs_b[s]])
        for sub in range(2):
            P.op("dve", lambda e, sub=sub: e.scalar_tensor_tensor(
                out=hb[s][:, sub, :], in0=xt[s][:, sub, :], scalar=rs[s][:, sub:sub + 1], in1=G[:],
                op0=ALU.mult, op1=ALU.mult), reads=[xt_b[s], rs_b[s], gb], writes=[hb_b[s]])

    def stage_T(i):
        s = i % 2
        for half in range(2):
            pb, pbb = pT[half], pT_b[half]
            for kk in range(4):
                kc = half * 4 + kk
                for sub in range(2):
                    P.op("pe", lambda e, kc=kc, sub=sub, kk=kk, pb=pb: e.transpose(
                        out=pb[:, kk * TT + sub * 128: kk * TT + (sub + 1) * 128],
                        in_=hb[s][:, sub, kc * 128:(kc + 1) * 128], identity=ident[:]),
                        reads=[hb_b[s], identb], writes=[pbb])
            if half == 0:
                P.op("act", lambda e, pb=pb: e.copy(out=hT[s][:, 0:4, :], in_=pb[:].rearrange("p (k t) -> p k t", k=4)),
                     reads=[pbb], writes=[hT_b[s]])
            else:
                P.op("dve", lambda e, pb=pb: e.tensor_copy(out=hT[s][:, 4:8, :], in_=pb[:].rearrange("p (k t) -> p k t", k=4)),
                     reads=[pbb], writes=[hT_b[s]])

    def stage_GU(i):
        s = i % 2
        for f2 in range(NFF // 2):
            q = f2 % 2
            for (W_, pp, ppb, wbs) in ((WG, pG[q], pG_b[q], wgb), (WU, pU[q], pU_b[q], wub)):
                for c in range(2):
                    f = 2 * f2 + c
                    for kc in range(8):
                        P.op("pe", lambda e, W_=W_, pp=pp, c=c, f=f, kc=kc: e.matmul(
                            out=pp[:, c * TT:(c + 1) * TT], lhsT=W_[:, kc, f * 128:(f + 1) * 128],
                            rhs=hT[s][:, kc, :], start=(kc == 0), stop=(kc == 7)),
                            reads=[wbs[0 if f < NFF // 2 else 1], hT_b[s]], writes=[ppb])
            P.op("act", lambda e, q=q: e.activation(out=sg[q][:], in_=pG[q][:], func=AF.Silu),
                 reads=[pG_b[q]], writes=[sg_b[q]])
            P.op("dve", lambda e, q=q, f2=f2: e.tensor_tensor(
                out=actT[s][:, 2 * f2:2 * f2 + 2, :], in0=sg[q][:].rearrange("p (c t) -> p c t", c=2),
                in1=pU[q][:].rearrange("p (c t) -> p c t", c=2), op=ALU.mult),
                reads=[sg_b[q], pU_b[q]], writes=[actT_b[s]])

    def stage_D(i):
        s = i % 2
        t0 = i * TT
        n = 0
        for sub in range(2):
            for dh in range(2):
                q = n % 2
                n += 1
                for f in range(NFF):
                    P.op("pe", lambda e, q=q, f=f, sub=sub, dh=dh: e.matmul(
                        out=pD[q][:], lhsT=actT[s][:, f, sub * 128:(sub + 1) * 128],
                        rhs=WD[:, f, dh * 512:(dh + 1) * 512], start=(f == 0), stop=(f == NFF - 1)),
                        reads=[wdb[f], actT_b[s]], writes=[pD_b[q]])
                P.op("dve", lambda e, q=q, sub=sub, dh=dh: e.scalar_tensor_tensor(
                    out=xt[s][:, sub, dh * 512:(dh + 1) * 512], in0=pD[q][:], scalar=0.5,
                    in1=xt[s][:, sub, dh * 512:(dh + 1) * 512], op0=ALU.mult, op1=ALU.add),
                    reads=[pD_b[q], xt_b[s]], writes=[xt_b[s]])
        P.op("pool", lambda e: e.dma_start(
            out=xout[t0:t0 + TT, :].rearrange("(s p) d -> p s d", p=128), in_=xt[s][:]),
            reads=[xt_b[s]], writes=[xout_buf], group=st_g[s])

    stage_load(0)
    if NT > 1:
        stage_load(1)
    stage_norm(0)
    stage_T(0)
    for i in range(NT):
        stage_GU(i)
        if i + 1 < NT:
            stage_norm(i + 1)
            stage_T(i + 1)
        stage_D(i)
        if i + 2 < NT:
            stage_load(i + 2)
    return {"pool": st_g}


def _b(L):
    return [x.b for x in L]


def mm(P, out, lhsT, rhs, st, sp, R, W):
    P.op("pe", lambda e: e.matmul(out=out, lhsT=lhsT, rhs=rhs, start=st, stop=sp), _b(R), _b(W))


def tr(P, out, in_, ident, R, W):
    P.op("pe", lambda e: e.transpose(out=out, in_=in_, identity=ident), _b(R), _b(W))


def act(P, out, in_, func, R, W, **kw):
    P.op("act", lambda e: e.activation(out=out, in_=in_, func=func, **kw), _b(R), _b(W))


def ts(P, eng, out, in0, s1, s2, op0, op1, R, W, accum=None):
    if accum is None:
        P.op(eng, lambda e: e.tensor_scalar(out=out, in0=in0, scalar1=s1, scalar2=s2, op0=op0, op1=op1), _b(R), _b(W))
    else:
        P.op(eng, lambda e: e.tensor_scalar(out=out, in0=in0, scalar1=s1, scalar2=s2, op0=op0, op1=op1,
                                            accum_out=accum), _b(R), _b(W))


def tt(P, eng, out, in0, in1, op, R, W):
    P.op(eng, lambda e: e.tensor_tensor(out=out, in0=in0, in1=in1, op=op), _b(R), _b(W))


def stt(P, out, in0, scalar, in1, op0, op1, R, W):
    P.op("dve", lambda e: e.scalar_tensor_tensor(out=out, in0=in0, scalar=scalar, in1=in1, op0=op0, op1=op1),
         _b(R), _b(W))


def cp(P, eng, out, in_, R, W):
    if eng == "act":
        P.op("act", lambda e: e.copy(out=out, in_=in_), _b(R), _b(W))
    else:
        P.op(eng, lambda e: e.tensor_copy(out=out, in_=in_), _b(R), _b(W))


def red(P, out, in_, op, R, W, eng="dve", **kw):
    P.op(eng, lambda e: e.tensor_reduce(out=out, in_=in_, axis=AX.X, op=op, **kw), _b(R), _b(W))


def dma(P, eng, out, in_, R, W, g, **kw):
    P.op(eng, lambda e: e.dma_start(out=out, in_=in_, **kw), _b(R), _b(W), group=g)


def mset(P, eng, ap, val, W):
    P.op(eng, lambda e: e.memset(ap, val), [], _b(W))


def rstd_ops(cx, ssq, n, rs, nh, R):
    P = cx.P
    ts(P, "dve", ssq.t[:], ssq.t[:], 1.0 / n, EPS, ALU.mult, ALU.add, R + [ssq], [ssq])
    tt(P, "pool", rs.t[:], ssq.t[:], nh.t[:, 0:ssq.t.shape[1]], ALU.pow, [ssq, nh], [rs])


IDX_SCALE = (64 ** -0.5) * (8 ** -0.5)
ATT_SCALE = 64 ** -0.5
MQ_SCALE = 64 ** -0.5


def load_ident(cx, Cd, cg):
    ident = cx.sb("ident", [128, 128], BF16)
    dma(cx.P, "sp", ident.t[:], Cd["c_ident"], [], [ident], cg)
    return ident


def norm_transpose(cx, xt, G, ident, nh, hb, hT, pT, ss, rs, junk, ntok_sub=2):
    P = cx.P
    for sub in range(ntok_sub):
        act(P, junk.t[:], xt.t[:, sub, :], AF.Square, [xt], [junk, ss], accum_out=ss.t[:, sub:sub + 1])
    rstd_ops(cx, ss, D, rs, nh, [])
    for sub in range(ntok_sub):
        stt(P, hb.t[:, sub, :], xt.t[:, sub, :], rs.t[:, sub:sub + 1], G.t[:], ALU.mult, ALU.mult,
            [xt, rs, G], [hb])
    for half in range(2):
        pb = pT[half]
        for kk in range(4):
            kc = half * 4 + kk
            for sub in range(ntok_sub):
                tr(P, pb.t[:, kk * 256 + sub * 128: kk * 256 + (sub + 1) * 128],
                   hb.t[:, sub, kc * 128:(kc + 1) * 128], ident.t[:], [hb, ident], [pb])
        cp(P, "act" if half == 0 else "dve", hT.t[:, half * 4:half * 4 + 4, :],
           pb.t[:].rearrange("p (k t) -> p k t", k=4), [pb], [hT])


def proj_phase(cx, S, X1, A, Wd, Cd):
    nc, P = cx.nc, cx.P
    TT = 256
    NT = S // TT
    cg = cx.group(final=True)
    ident = load_ident(cx, Cd, cg)
    WIN = cx.sb("WIN", [128, 8, DIN], BF16)
    stg = [cx.sb(f"stg{i}", [128, 824], F32) for i in range(2)]
    stg_g = [cx.group() for _ in range(2)]
    k = 0
    for kc in range(8):
        for part in range(6):
            s_ = k % 2
            c0 = part * 824
            dma(P, "sp", stg[s_].t[:], Wd["w_in"][kc * 128:(kc + 1) * 128, c0:c0 + 824], [], [stg[s_]], stg_g[s_])
            cp(P, ["act", "dve", "pool"][k % 3], WIN.t[:, kc, c0:c0 + 824], stg[s_].t[:], [stg[s_]], [WIN])
            k += 1
    import os
    SL = int(os.environ.get('SL', '99'))
    G = cx.sb("G", [128, D], F32)
    dma(P, "sp", G.t[:], Wd["mix_norm"].partition_broadcast(128), [], [G], cg)
    GQ = cx.sb("GQ", [128, 512], F32)
    for h in range(8):
        if SL >= 2:
            dma(P, "sp", GQ.t[:, h * 64:(h + 1) * 64], Wd["q_norm"].partition_broadcast(128), [], [GQ], cg)
    if SL >= 2:
        ts(P, "dve", GQ.t[:], GQ.t[:], ATT_SCALE, None, ALU.mult, ALU.bypass, [GQ], [GQ])
    GK = cx.sb("GK", [128, 128], F32)
    for h in range(2):
        if SL >= 3:
            dma(P, "sp", GK.t[:, h * 64:(h + 1) * 64], Wd["k_norm"].partition_broadcast(128), [], [GK], cg)
    GI = cx.sb("GI", [128, 64], F32)
    if SL >= 3:
        dma(P, "sp", GI.t[:], Wd["idx_k_norm"].partition_broadcast(128), [], [GI], cg)
    CW = cx.sb("CW", [128, 4, 4], F32)
    CB = cx.sb("CB", [128, 4], F32)
    if SL >= 4:
        dma(P, "sp", CB.t[:], Wd["conv_b"], [], [CB], cg)
    if SL >= 4:
      dma(P, "sp", CW.t[:], Wd["conv_w"].rearrange("p (c j) -> p c j", c=4), [], [CW], cg)
    wst = cx.sb("wst", [128, 2, 4, 64], F32)
    WM = cx.sb("WM", [128, 2, 4, 64], BF16)
    if SL >= 5:
        dma(P, "sp", wst.t[:, 0], Wd["w_mq"].rearrange("c (h d) -> c h d", h=4), [], [wst], cg)
        dma(P, "sp", wst.t[:, 1], Wd["w_mk"].rearrange("c (h d) -> c h d", h=4), [], [wst], cg)
        cp(P, "dve", WM.t[:], wst.t[:], [wst], [WM])
    BIF = cx.sb("BIF", [128, 8], F32)
    if SL >= 6:
        dma(P, "sp", BIF.t[:, 0:4], Wd["b_i"].partition_broadcast(128), [], [BIF], cg)
        dma(P, "sp", BIF.t[:, 4:8], Wd["b_f"].partition_broadcast(128), [], [BIF], cg)
    NH = cx.sb("NH", [128, 8], F32)
    mset(P, "pool", NH.t[:], -0.5, [NH])

    R2 = range(2)
    xt = [cx.sb(f"xt{i}", [128, 2, D], F32) for i in R2]
    xt_g = [cx.group() for _ in R2]
    junk = cx.sb("junk", [128, D], BF16)
    ss = [cx.sb(f"ss{i}", [128, 2], F32) for i in R2]
    rs = [cx.sb(f"rs{i}", [128, 2], F32) for i in R2]
    hb = [cx.sb("hb0", [128, 2, D], BF16)] * 2
    hT = [cx.sb(f"hT{i}", [128, 8, TT], BF16) for i in R2]
    pT = [cx.ps(f"pT{i}", [128, 1024], BF16) for i in R2]
    pX = [cx.ps(f"pX{i}", [128, 1024], BF16) for i in R2]
    pM = [cx.ps(f"pM{i}", [128, 512], F32) for i in range(4)]
    nbank = [0]

    def bank():
        b = pM[nbank[0] % 4]
        nbank[0] += 1
        return b

    sqt = [cx.sb(f"sqt{i}", [128, 512], F32) for i in R2]
    t1 = [cx.sb(f"t1{i}", [128, 512], F32) for i in R2]
    qn = [cx.sb(f"qn{i}", [128, 512], BF16) for i in R2]
    kn = [cx.sb(f"kn{i}", [128, 128], BF16) for i in R2]
    ikn = [cx.sb(f"ikn{i}", [128, 64], BF16) for i in R2]
    iqs = [cx.sb(f"iqs{i}", [128, 512], BF16) for i in R2]
    s8 = [cx.sb(f"s8{i}", [128, 8], F32) for i in R2]
    r8 = [cx.sb(f"r8{i}", [128, 8], F32) for i in R2]
    s2 = [cx.sb(f"s2{i}", [128, 2], F32) for i in R2]
    r2 = [cx.sb(f"r2{i}", [128, 2], F32) for i in R2]
    s1 = [cx.sb(f"s1{i}", [128, 1], F32) for i in R2]
    r1 = [cx.sb(f"r1{i}", [128, 1], F32) for i in R2]
    absw = [cx.sb(f"absw{i}", [128, 8], F32) for i in R2]
    zt = [cx.sb(f"zt{i}", [128, 8], F32) for i in R2]
    et = [cx.sb(f"et{i}", [128, 4], F32) for i in R2]
    def outs(name, shape, dt):
        return [cx.sb(f"{name}{i}", shape, dt) for i in R2], [cx.group() for _ in R2]
    QTt, QT_g = outs("QTt", [128, 4, TT], BF16)
    KTt, KT_g = outs("KTt", [128, TT], BF16)
    IQTt, IQT_g = outs("IQTt", [128, 4, TT], BF16)
    IKTt, IKT_g = outs("IKTt", [64, TT], BF16)
    vb, V_g = outs("vb", [128, 2, 128], BF16)
    sgn, SGN_g = outs("sgn", [128, 2, 8], F32)
    mvb, MV_g = outs("mvb", [128, 2, 512], BF16)
    lif, LIF_g = outs("lif", [128, 2, 8], F32)
    mob, MO_g = outs("mob", [128, 2, 512], BF16)
    gat, GAT_g = outs("gat", [128, 8, TT], BF16)
    gbt, GBT_g = outs("gbt", [128, 8, TT], BF16)
    mqt, MQT_g = outs("mqt", [64, 4, TT], BF16)
    mkt, MKT_g = outs("mkt", [64, 4, TT], BF16)
    mkb, MK_g = outs("mkb", [128, 2, 256], BF16)
    XM = [cx.sb(f"XM{i}", [128, 4, TT + 3], F32) for i in R2]
    mset(P, "pool", XM[0].t[:, :, 0:3], 0.0, [XM[0]])
    acc = [cx.sb("acc0", [128, 4, TT], F32)] * 2
    sig = [cx.sb("sig0", [128, 4, TT], F32)] * 2
    xct = [cx.sb(f"xct{i}", [128, 4, TT], BF16) for i in R2]
    X1b = TL(None, "X1")
    allg = []

    def load(i):
        s_ = i % 2
        t0 = i * TT
        dma(P, "sp", xt[s_].t[:], X1[t0:t0 + TT, :].rearrange("(s p) d -> p s d", p=128), [X1b], [xt[s_]], xt_g[s_])

    def tile(i):
        import os
        LV = int(os.environ.get('LV', '99'))
        if LV < 1:
            return
        s_ = i % 2
        t0 = i * TT
        tsl = slice(t0, t0 + TT)
        H = hT[s_]

        def tok_group(sub, c0, width):
            b = bank()
            for kc in range(8):
                mm(P, b.t[:, 0:width], H.t[:, kc, sub * 128:(sub + 1) * 128], WIN.t[:, kc, c0:c0 + width],
                   kc == 0, kc == 7, [H, WIN], [b])
            return b

        pq, pi = pX[0], pX[1]
        ssl = None
        for sub in range(2):
            u = sub
            if LV < 2:
                continue
            b = tok_group(sub, 0, 512)
            act(P, sqt[u].t[:], b.t[:], AF.Square, [b], [sqt[u]])
            red(P, s8[u].t[:], sqt[u].t[:].rearrange("p (h d) -> p h d", h=8), ALU.add, [sqt[u]], [s8[u]])
            rstd_ops(cx, s8[u], 64, r8[u], NH, [])
            tt(P, "dve", t1[u].t[:].rearrange("p (h d) -> p h d", h=8), b.t[:].rearrange("p (h d) -> p h d", h=8),
               r8[u].t[:].unsqueeze(2).broadcast_to([128, 8, 64]), ALU.mult, [b, r8[u]], [t1[u]])
            tt(P, "pool", qn[u].t[:].rearrange("p (r g d) -> p g r d", r=4, g=2),
               t1[u].t[:].rearrange("p (g r d) -> p g r d", g=2, r=4),
               GQ.t[:].rearrange("p (g r d) -> p g r d", g=2, r=4), ALU.mult, [t1[u], GQ], [qn[u]])
            ssl = slice(sub * 128, (sub + 1) * 128)
            for r in range(4):
                tr(P, pq.t[:, r * 128:(r + 1) * 128], qn[u].t[:, r * 128:(r + 1) * 128],
                   ident.t[:], [qn[u], ident], [pq])
            cp(P, "act", QTt[s_].t[:, :, ssl], pq.t[:, 0:512].rearrange("p (r t) -> p r t", r=4), [pq], [QTt[s_]])
            if sub == 1:
                dma(P, "sp", A["QT"][:, :, tsl].rearrange("r p t -> p r t"), QTt[s_].t[:], [QTt[s_]], [], QT_g[s_])
            if LV < 3:
                continue
            b = tok_group(sub, 512, 256)
            act(P, sqt[u].t[:, 0:128], b.t[:, 0:128], AF.Square, [b], [sqt[u]])
            red(P, s2[u].t[:], sqt[u].t[:, 0:128].rearrange("p (h d) -> p h d", h=2), ALU.add, [sqt[u]], [s2[u]])
            rstd_ops(cx, s2[u], 64, r2[u], NH, [])
            tt(P, "dve", t1[u].t[:, 0:128].rearrange("p (h d) -> p h d", h=2),
               b.t[:, 0:128].rearrange("p (h d) -> p h d", h=2),
               r2[u].t[:].unsqueeze(2).broadcast_to([128, 2, 64]), ALU.mult, [b, r2[u]], [t1[u]])
            tt(P, "pool", kn[u].t[:], t1[u].t[:, 0:128], GK.t[:], ALU.mult, [t1[u], GK], [kn[u]])
            tr(P, pq.t[:, 512:640], kn[u].t[:], ident.t[:], [kn[u], ident], [pq])
            cp(P, "act", vb[s_].t[:, sub, :], b.t[:, 128:256], [b], [vb[s_]])
            cp(P, "dve", KTt[s_].t[:, ssl], pq.t[:, 512:640], [pq], [KTt[s_]])
            if sub == 1:
                dma(P, "sp", A["KT"][:, tsl], KTt[s_].t[:], [KTt[s_]], [], KT_g[s_])
                dma(P, "sp", A["V"][tsl, :].rearrange("(s p) d -> p s d", p=128), vb[s_].t[:], [vb[s_]], [], V_g[s_])
            if LV < 4:
                continue
            bw = tok_group(sub, 1280, 72)
            ts(P, "dve", zt[u].t[:], bw.t[:, 64:72], -IDX_SCALE, None, ALU.mult, ALU.bypass, [bw], [zt[u]])
            stt(P, absw[u].t[:], bw.t[:, 64:72], IDX_SCALE, zt[u].t[:], ALU.mult, ALU.max, [bw, zt[u]], [absw[u]])
            ts(P, "dve", sgn[s_].t[:, sub, :], bw.t[:, 64:72], 0.0, 2.0, ALU.is_ge, ALU.mult, [bw], [sgn[s_]])
            ts(P, "dve", sgn[s_].t[:, sub, :], sgn[s_].t[:, sub, :], -1.0, None, ALU.add, ALU.bypass, [sgn[s_]], [sgn[s_]])
            act(P, sqt[u].t[:, 0:64], bw.t[:, 0:64], AF.Square, [bw], [sqt[u], s1[u]], accum_out=s1[u].t[:])
            rstd_ops(cx, s1[u], 64, r1[u], NH, [])
            stt(P, ikn[u].t[:], bw.t[:, 0:64], r1[u].t[:], GI.t[:], ALU.mult, ALU.mult, [bw, r1[u], GI], [ikn[u]])
            b = tok_group(sub, 768, 512)
            tt(P, "dve", iqs[u].t[:].rearrange("p (h d) -> p h d", h=8), b.t[:].rearrange("p (h d) -> p h d", h=8),
               absw[u].t[:].unsqueeze(2).broadcast_to([128, 8, 64]), ALU.mult, [b, absw[u]], [iqs[u]])
            for r in range(4):
                tr(P, pi.t[:, r * 128:(r + 1) * 128], iqs[u].t[:, r * 128:(r + 1) * 128],
                   ident.t[:], [iqs[u], ident], [pi])
            tr(P, pi.t[0:64, 512:640], ikn[u].t[:], ident.t[:], [ikn[u], ident], [pi])
            cp(P, "act", IQTt[s_].t[:, :, ssl], pi.t[:, 0:512].rearrange("p (r t) -> p r t", r=4), [pi], [IQTt[s_]])
            cp(P, "dve", IKTt[s_].t[:, ssl], pi.t[0:64, 512:640], [pi], [IKTt[s_]])
            if sub == 1:
                dma(P, "sp", A["IQT"][:, :, tsl].rearrange("r p t -> p r t"), IQTt[s_].t[:], [IQTt[s_]], [], IQT_g[s_])
                dma(P, "sp", A["IKT"][:, tsl], IKTt[s_].t[:], [IKTt[s_]], [], IKT_g[s_])
                dma(P, "sp", A["SGN"][tsl, :].rearrange("(s p) h -> p s h", p=128), sgn[s_].t[:], [sgn[s_]], [], SGN_g[s_])
            if LV < 5:
                continue
            b = tok_group(sub, 1864, 512)
            cp(P, "dve", mvb[s_].t[:, sub, :], b.t[:], [b], [mvb[s_]])
            b = tok_group(sub, 2376, 8)
            tt(P, "dve", zt[u].t[:], b.t[:, 0:8], BIF.t[:], ALU.add, [b, BIF], [zt[u]])
            cp(P, "pool", lif[s_].t[:, sub, 0:4], zt[u].t[:, 0:4], [zt[u]], [lif[s_]])
            act(P, et[u].t[:], zt[u].t[:, 4:8], AF.Exp, [zt[u]], [et[u]], scale=-1.0)
            act(P, et[u].t[:], et[u].t[:], AF.Ln, [et[u]], [et[u]], bias=1.0)
            ts(P, "dve", lif[s_].t[:, sub, 4:8], et[u].t[:], -1.0, None, ALU.mult, ALU.bypass, [et[u]], [lif[s_]])
            b = tok_group(sub, 2384, 512)
            act(P, mob[s_].t[:, sub, :], b.t[:], AF.Sigmoid, [b], [mob[s_]])
        if LV < 6:
            return
        dma(P, "sp", A["MV"][tsl, :].rearrange("(s p) d -> p s d", p=128), mvb[s_].t[:], [mvb[s_]], [], MV_g[s_])
        dma(P, "sp", A["LIF"][tsl, :].rearrange("(s p) d -> p s d", p=128), lif[s_].t[:], [lif[s_]], [], LIF_g[s_])
        dma(P, "sp", A["MO"][tsl, :].rearrange("(s p) d -> p s d", p=128), mob[s_].t[:], [mob[s_]], [], MO_g[s_])

        if i + 1 < NT:
            nt(i + 1)
        def feat_pair(c0):
            b = bank()
            for c in range(2):
                for kc in range(8):
                    mm(P, b.t[:, c * TT:(c + 1) * TT], WIN.t[:, kc, c0 + c * 128:c0 + (c + 1) * 128], H.t[:, kc, :],
                       kc == 0, kc == 7, [H, WIN], [b])
            return b

        if LV < 7:
            return
        for (base, dst) in ((2896, gat[s_]), (3920, gbt[s_])):
            for pr in range(4):
                b = feat_pair(base + pr * 256)
                act(P, dst.t[:, 2 * pr:2 * pr + 2, :], b.t[:].rearrange("p (c t) -> p c t", c=2), AF.Sigmoid, [b], [dst])
        dma(P, "sp", A["GAT"][:, :, tsl].rearrange("c p t -> p c t"), gat[s_].t[:], [gat[s_]], [], GAT_g[s_])
        dma(P, "sp", A["GBT"][:, :, tsl].rearrange("c p t -> p c t"), gbt[s_].t[:], [gbt[s_]], [], GBT_g[s_])
        if LV < 8:
            return
        xm, xn = XM[s_], XM[1 - s_]
        for pr in range(2):
            b = feat_pair(1352 + pr * 256)
            cp(P, "dve", xm.t[:, 2 * pr:2 * pr + 2, 3:3 + TT], b.t[:].rearrange("p (c t) -> p c t", c=2), [b], [xm])
        cp(P, "pool", xn.t[:, :, 0:3], xm.t[:, :, TT:TT + 3], [xm], [xn])
        ac = acc[s_]
        for c in range(4):
            ts(P, "dve", ac.t[:, c, :], xm.t[:, c, 0:TT], CW.t[:, c, 0:1], CB.t[:, c:c + 1], ALU.mult, ALU.add,
               [xm, CW, CB], [ac])
            for j in range(1, 4):
                stt(P, ac.t[:, c, :], xm.t[:, c, j:j + TT], CW.t[:, c, j:j + 1], ac.t[:, c, :], ALU.mult, ALU.add,
                    [xm, CW, ac], [ac])
        act(P, sig[s_].t[:], ac.t[:], AF.Sigmoid, [ac], [sig[s_]])
        tt(P, "pool", xct[s_].t[:], ac.t[:], sig[s_].t[:], ALU.mult, [ac, sig[s_]], [xct[s_]])
        if LV < 9:
            return
        xc = xct[s_]
        for (wi, dst, scale, name, grp) in ((0, mqt[s_], MQ_SCALE, "MQT", MQT_g[s_]), (1, mkt[s_], 1.0, "MKT", MKT_g[s_])):
            for hp in range(2):
                b = bank()
                for c in range(2):
                    h = hp * 2 + c
                    mm(P, b.t[0:64, c * TT:(c + 1) * TT], WM.t[:, wi, h, :], xc.t[:, h, :], True, True, [WM, xc], [b])
                act(P, dst.t[:, 2 * hp:2 * hp + 2, :], b.t[0:64, :].rearrange("p (c t) -> p c t", c=2), AF.Copy,
                    [b], [dst], scale=scale)
            dma(P, "sp", A[name][:, :, tsl].rearrange("h d t -> d h t"), dst.t[:], [dst], [], grp)
        for sub in range(2):
            b = bank()
            for h in range(4):
                mm(P, b.t[:, h * 64:(h + 1) * 64], xc.t[:, h, sub * 128:(sub + 1) * 128], WM.t[:, 1, h, :], True, True,
                   [xc, WM], [b])
            cp(P, "dve", mkb[s_].t[:, sub, :], b.t[:, 0:256], [b], [mkb[s_]])
        dma(P, "sp", A["MK"][tsl, :].rearrange("(s p) d -> p s d", p=128), mkb[s_].t[:], [mkb[s_]], [], MK_g[s_])

    def nt(i):
        q_ = i % 2
        norm_transpose(cx, xt[q_], G, ident, NH, hb[q_], hT[q_], pT, ss[q_], rs[q_], junk)

    load(0)
    if NT > 1:
        load(1)
    nt(0)
    for i in range(NT):
        tile(i)
        if i + 2 < NT:
            load(i + 2)
    fin = QT_g + KT_g + IQT_g + IKT_g + V_g + SGN_g + MV_g + LIF_g + MO_g + GAT_g + GBT_g + MQT_g + MKT_g + MK_g
    return {"sp": fin}


NIT = 11


def dsa_phase(cx, S, A, Cd):
    nc, P = cx.nc, cx.P
    NB = S // 512
    NJ = S // 128
    cg = cx.group(final=True)
    ident = load_ident(cx, Cd, cg)
    IKbs = [cx.sb(f"IKb{i}", [128, 512], BF16) for i in range(2)]
    IK_g = [cx.group() for _ in range(2)]
    KT = cx.sb("KT", [128, S], BF16)
    dma(P, "sp", KT.t[:], A["KT"], [], [KT], cg)
    VA = cx.sb("VA", [128, NJ, 2, 65], BF16)
    mset(P, "pool", VA.t[:], 1.0, [VA])
    for g_ in range(2):
        dma(P, "sp", VA.t[:, :, g_, 0:64], A["V"][:, g_ * 64:(g_ + 1) * 64].rearrange("(j p) d -> p j d", p=128),
            [], [VA], cg)
    CM = cx.sb("CM", [128, 4, 512], BF16)
    dma(P, "sp", CM.t[:], Cd["c_cmask"].rearrange("i p s -> p i s"), [], [CM], cg)
    CROW = cx.sb("CROW", [128, NIT], F32)
    dma(P, "sp", CROW.t[:], Cd["c_steps"], [], [CROW], cg)
    ONES = cx.sb("ONES", [128, 64], F32)
    mset(P, "pool", ONES.t[:], 1.0, [ONES])

    SCs = [cx.sb(f"SC{i}", [128, S], F32) for i in range(2)]
    MK = cx.sb("MK", [128, S], BF16)
    MT = cx.sb("MT", [128, NJ, 512], BF16)
    IQb = cx.sb("IQb", [128, 4, 512], BF16)
    Qb = cx.sb("Qb", [128, 4, 512], BF16)
    SGb = cx.sb("SGb", [128, 4, 8], F32)
    ld_g1, ld_g2, ld_g3 = cx.group(), cx.group(), cx.group()
    DG = [cx.sb("DG0", [128, 8, 128], BF16)]
    Rt = [cx.sb(f"Rt{i}", [128, 512], BF16) for i in range(4)]
    Et = [cx.sb(f"Et{i}", [128, 512], BF16) for i in range(2)]
    Pt = [cx.sb(f"Pt{i}", [128, 512], BF16) for i in range(4)]
    M1 = cx.sb("M1", [128, 1], F32)
    LO = cx.sb("LO", [128, 1], F32)
    W0 = cx.sb("W0", [128, 1], F32)
    STEP = cx.sb("STEP", [128, NIT], F32)
    NSTEP = cx.sb("NSTEP", [128, NIT], F32)
    STEP2 = cx.sb("STEP2", [128, NIT], F32)
    MID = cx.sb("MID", [128, 1], F32)
    CNT = cx.sb("CNT", [128, 1], F32)
    DT = cx.sb("DT", [128, 1], F32)
    RC = cx.sb("RC", [128, 512], F32)
    OS = cx.sb("OS", [64, 512], BF16)
    YT = [cx.sb("YT0", [64, 512], BF16)] * 2
    YT_g = [cx.group() for _ in range(2)]
    pW = [cx.ps(f"pW{i}", [128, 512], F32) for i in range(4)]
    SCp = cx.ps("SCp", [128, 512], F32)
    pTr = cx.ps("pTr", [128, 1024], BF16)
    pOs = [cx.ps(f"pO{i}", [128, 512], F32) for i in range(2)]
    cnt = {"ik": 0, "l": 0, "r": 0, "s": 0, "e": 0, "p": 0, "y": 0, "d": 0, "m": 0}

    def nxt(key, lst):
        v = lst[cnt[key] % len(lst)]
        cnt[key] += 1
        return v

    tiles = [(b, ii) for b in range(NB) for ii in range(4)]
    dgs = {}

    def indexer(n):
        b, ii = tiles[n]
        bs = slice(b * 512, (b + 1) * 512)
        SC = SCs[n % 2]
        if ii == 0:
            dma(P, "sp", IQb.t[:], A["IQT"][:, :, bs].rearrange("r p t -> p r t"), [], [IQb], ld_g1)
            dma(P, "sp", SGb.t[:], A["SGN"][bs, :].rearrange("(i p) h -> p i h", p=128), [], [SGb], ld_g3)
        tq = slice(ii * 128, (ii + 1) * 128)
        dg = nxt("d", DG)
        for h in range(8):
            ts(P, "pool", dg.t[:, h, :], ident.t[:], SGb.t[:, ii, h:h + 1], 1.0, ALU.mult, ALU.mult,
               [ident, SGb], [dg])
        items = [(kb, pr) for kb in range(b + 1) for pr in range(4)]
        LA = 1
        hold = {}

        def front(it):
            kb, pr = it
            ks = slice(kb * 512, (kb + 1) * 512)
            if pr == 0:
                ikq = cnt["ik"] % 2
                cnt["ik"] += 1
                IKb = IKbs[ikq]
                dma(P, "sp", IKb.t[0:64, :], A["IKT"][:, ks], [], [IKb], IK_g[ikq])
                dma(P, "sp", IKb.t[64:128, :], A["IKT"][:, ks], [], [IKb], IK_g[ikq])
                hold[("ik", kb)] = IKb
            IKb = hold[("ik", kb)]
            Ls = []
            for half in range(2):
                ps_ = slice(half * 64, half * 64 + 64)
                L = nxt("l", pW)
                mm(P, L.t[:], IQb.t[ps_, pr, tq], IKb.t[ps_, :], True, True, [IQb, IKb], [L])
                Ls.append(L)
            Rs = []
            for half in range(2):
                R = nxt("r", Rt)
                act(P, R.t[:], Ls[half].t[:], AF.Relu, [Ls[half]], [R])
                Rs.append(R)
            hold[it] = Rs

        def back(it):
            kb, pr = it
            ks = slice(kb * 512, (kb + 1) * 512)
            Rs = hold.pop(it)
            for half in range(2):
                h = 2 * pr + half
                mm(P, SCp.t[:], dg.t[:, h, :], Rs[half].t[:], h == 0, h == 7, [dg, Rs[half]], [SCp])
            if pr == 3:
                cp(P, "act", SC.t[:, ks], SCp.t[:], [SCp], [SC])

        for idx in range(len(items) + LA):
            if idx < len(items):
                front(items[idx])
            if idx - LA >= 0:
                back(items[idx - LA])

    def post(n):
        b, ii = tiles[n]
        Nb = 512 * (b + 1)
        njb = 4 * (b + 1)
        bs = slice(b * 512, (b + 1) * 512)
        tq = slice(ii * 128, (ii + 1) * 128)
        SC = SCs[n % 2]
        Ne = 512 * b + 128 * (ii + 1)
        red(P, M1.t[:], SC.t[:, 0:Ne], ALU.max, [SC], [M1], apply_absolute_value=True)
        if Ne < Nb:
            mset(P, "pool", MK.t[:, Ne:Nb], 0.0, [MK])
        tt(P, "pool", SC.t[:, bs], SC.t[:, bs], CM.t[:, ii, :], ALU.add, [SC, CM], [SC])
        ts(P, "dve", W0.t[:], M1.t[:], 2.002, 2e-6, ALU.mult, ALU.add, [M1], [W0])
        ts(P, "dve", STEP.t[:], CROW.t[:], W0.t[:], None, ALU.mult, ALU.bypass, [CROW, W0], [STEP])
        ts(P, "dve", NSTEP.t[:], STEP.t[:], -1.0, None, ALU.mult, ALU.bypass, [STEP], [NSTEP])
        ts(P, "dve", STEP2.t[:], STEP.t[:], 2.0, None, ALU.mult, ALU.bypass, [STEP], [STEP2])
        mset(P, "dve", MID.t[:], 0.0, [MID])
        for k in range(NIT):
            ts(P, "dve", MK.t[:, 0:Ne], SC.t[:, 0:Ne], MID.t[:], None, ALU.is_ge, ALU.add, [SC, MID], [MK, CNT],
               accum=CNT.t[:])
            if k + 1 < NIT:
                stt(P, DT.t[:], CNT.t[:], 255.5, STEP2.t[:, k + 1:k + 2], ALU.is_ge, ALU.mult, [CNT, STEP2], [DT])
                stt(P, MID.t[:], DT.t[:], NSTEP.t[:, k + 1:k + 2], MID.t[:], ALU.add, ALU.add, [DT, NSTEP, MID], [MID])
            else:
                stt(P, DT.t[:], CNT.t[:], 255.5, STEP.t[:, k:k + 1], ALU.is_lt, ALU.mult, [CNT, STEP], [DT])
                tt(P, "dve", LO.t[:], MID.t[:], DT.t[:], ALU.subtract, [MID, DT], [LO])
        ts(P, "dve", MK.t[:, 0:Ne], SC.t[:, 0:Ne], LO.t[:], None, ALU.is_ge, ALU.bypass, [SC, LO], [MK])
        for j0 in range(0, njb, 8):
            nn = min(8, njb - j0)
            for jj in range(nn):
                j = j0 + jj
                tr(P, pTr.t[:, jj * 128:(jj + 1) * 128], MK.t[:, j * 128:(j + 1) * 128], ident.t[:], [MK, ident], [pTr])
            cp(P, "act" if (cnt["m"] % 2 == 0) else "dve", MT.t[:, j0:j0 + nn, tq],
               pTr.t[:, 0:nn * 128].rearrange("p (j t) -> p j t", t=128), [pTr], [MT])
            cnt["m"] += 1

    def attention(b):
        njb = 4 * (b + 1)
        bs = slice(b * 512, (b + 1) * 512)
        dma(P, "sp", Qb.t[:], A["QT"][:, :, bs].rearrange("r p t -> p r t"), [], [Qb], ld_g2)
        items = [(r, j) for r in range(4) for j in range(njb)]
        LA = 1
        hold = {}

        def front(it):
            r, j = it
            STs = []
            for g in range(2):
                ps_ = slice(g * 64, g * 64 + 64)
                ST = nxt("l", pW)
                mm(P, ST.t[:], KT.t[ps_, j * 128:(j + 1) * 128], Qb.t[ps_, r, :], True, True, [KT, Qb], [ST])
                STs.append(ST)
            PTs = []
            for g in range(2):
                E = nxt("e", Et)
                act(P, E.t[:], STs[g].t[:], AF.Exp, [STs[g]], [E])
                PT = nxt("p", Pt)
                tt(P, "dve", PT.t[:], E.t[:], MT.t[:, j, :], ALU.mult, [E, MT], [PT])
                PTs.append(PT)
            hold[it] = PTs

        def back(it):
            r, j = it
            PTs = hold.pop(it)
            for g in range(2):
                pO = pOs[g]
                mm(P, pO.t[0:65, :], VA.t[:, j, g, :], PTs[g].t[:], j == 0, j == njb - 1, [VA, PTs[g]], [pO])
            if j == njb - 1:
                for g in range(2):
                    h = g * 4 + r
                    pO = pOs[g]
                    P.op("dve", lambda e, pO=pO: e.reciprocal(out=RC.t[64:65, :], in_=pO.t[64:65, :]), [pO.b], [RC.b])
                    BC = nxt("l", pW)
                    mm(P, BC.t[0:64, :], ONES.t[64:65, 0:64], RC.t[64:65, :], True, True, [ONES, RC], [BC])
                    cp(P, "act", OS.t[:], pO.t[0:64, :], [pO], [OS])
                    y = cnt["y"] % 2
                    cnt["y"] += 1
                    tt(P, "dve", YT[y].t[:], OS.t[:], BC.t[0:64, :], ALU.mult, [OS, BC], [YT[y]])
                    dma(P, "sp", A["YAT"][h, :, bs], YT[y].t[:], [YT[y]], [], YT_g[y])

        for idx in range(len(items) + LA):
            if idx < len(items):
                front(items[idx])
            if idx - LA >= 0:
                back(items[idx - LA])

    indexer(0)
    for n in range(len(tiles)):
        if n + 1 < len(tiles):
            indexer(n + 1)
        post(n)
        if tiles[n][1] == 3:
            attention(tiles[n][0])
    return {"sp": YT_g}


def mlstm_phase(cx, S, A, Wd, Cd):
    nc, P = cx.nc, cx.P
    NT = S // 128
    cg = cx.group(final=True)
    ident = load_ident(cx, Cd, cg)
    T2 = cx.sb("T2", [128, 128], F32)
    dma(P, "sp", T2.t[:], Cd["c_tri2"], [], [T2], cg)
    NEGM = cx.sb("NEGM", [128, 512], BF16)
    dma(P, "sp", NEGM.t[:], Cd["c_negm"], [], [NEGM], cg)
    HG = cx.sb("HG", [128, 512], F32)
    dma(P, "sp", HG.t[:], Wd["m_head_norm"].partition_broadcast(128), [], [HG], cg)
    NH = cx.sb("NH", [128, 8], F32)
    mset(P, "pool", NH.t[:], -0.5, [NH])
    CN = [cx.sb(f"CN{i}", [64, 4, 129], F32) for i in range(3)]
    mset(P, "pool", CN[0].t[:], 0.0, [CN[0]])
    R2 = range(2)
    LIF = [cx.sb(f"LIF{i}", [128, 8], F32) for i in R2]
    MQ = [cx.sb(f"MQ{i}", [64, 4, 128], BF16) for i in R2]
    MKt = [cx.sb(f"MKt{i}", [64, 4, 128], BF16) for i in R2]
    MKb = [cx.sb(f"MKb{i}", [128, 256], BF16) for i in R2]
    VA = [cx.sb(f"VA{i}", [128, 4, 129], BF16) for i in R2]
    MOt = [cx.sb(f"MOt{i}", [128, 512], BF16) for i in R2]
    lg = [[cx.group() for _ in range(6)] for _ in R2]
    for i in R2:
        mset(P, "pool", VA[i].t[:], 1.0, [VA[i]])
    LFm = cx.sb("LFm", [128, 4, 128], F32)
    bias = cx.sb("bias", [128, 4], F32)
    AT = cx.sb("AT", [128, 4, 128], F32)
    EBs = [cx.sb(f"EB{i}", [128, 4, 128], F32) for i in range(2)]
    BL = cx.sb("BL", [128, 4], F32)
    WS = cx.sb("WS", [128, 4], F32)
    WTs = [cx.sb(f"WT{i}", [128, 4, 128], BF16) for i in range(2)]
    QPs = [cx.sb(f"QP{i}", [64, 4, 128], F32) for i in range(2)]
    KWs = [cx.sb(f"KW{i}", [128, 4, 64], BF16) for i in range(2)]
    dn = cx.sb("dn", [128, 4], F32)
    rec = cx.sb("rec", [128, 4], F32)
    hh = cx.sb("hh", [128, 4, 128], F32)
    junk = cx.sb("junk", [128, 128], BF16)
    ssq = cx.sb("ssq", [128, 4], F32)
    rs4 = cx.sb("rs4", [128, 4], F32)
    yb = cx.sb("yb", [128, 512], BF16)
    YBt = [cx.sb(f"YBt{i}", [128, 4, 512], BF16) for i in R2]
    YB_g = [cx.group() for _ in R2]
    pX = cx.ps("pX", [128, 512], F32)
    pY = cx.ps("pY", [128, 512], F32)
    pZ = cx.ps("pZ", [128, 512], F32)
    pQ = cx.ps("pQ", [128, 512], F32)
    pD = cx.ps("pD", [128, 512], F32)
    pN = [cx.ps(f"pN{i}", [128, 512], F32) for i in R2]
    pTr = cx.ps("pTr", [128, 1024], BF16)

    def load(i):
        s_ = i % 2
        tsl = slice(i * 128, (i + 1) * 128)
        g = lg[s_]
        dma(P, "sp", LIF[s_].t[:], A["LIF"][tsl, :], [], [LIF[s_]], g[0])
        dma(P, "sp", MQ[s_].t[:], A["MQT"][:, :, tsl].rearrange("h d t -> d h t"), [], [MQ[s_]], g[1])
        dma(P, "sp", MKt[s_].t[:], A["MKT"][:, :, tsl].rearrange("h d t -> d h t"), [], [MKt[s_]], g[2])
        dma(P, "sp", MKb[s_].t[:], A["MK"][tsl, :], [], [MKb[s_]], g[3])
        dma(P, "sp", VA[s_].t[:, :, 0:128], A["MV"][tsl, :].rearrange("p (h d) -> p h d", h=4), [], [VA[s_]], g[4])
        dma(P, "sp", MOt[s_].t[:], A["MO"][tsl, :], [], [MOt[s_]], g[5])

    def front(i):
        s_ = i % 2
        lif, mq, mkt, mkb, va, mo = LIF[s_], MQ[s_], MKt[s_], MKb[s_], VA[s_], MOt[s_]
        EB, WT, QP, KW = EBs[s_], WTs[s_], QPs[s_], KWs[s_]
        import os
        ML = int(os.environ.get('ML', '99'))
        cp(P, "pool", LFm.t[:], lif.t[:, 4:8].unsqueeze(2).broadcast_to([128, 4, 128]), [lif], [LFm])
        for h in range(4):
            mm(P, pX.t[:, h * 128:(h + 1) * 128], LFm.t[:, h, :], T2.t[:], True, True, [LFm, T2], [pX])
        mm(P, pY.t[:], ident.t[:], NEGM.t[:], True, False, [ident, NEGM], [pY])
        for h in range(4):
            mm(P, pY.t[:, h * 128:(h + 1) * 128], LFm.t[:, h, :], T2.t[:], False, h == 3, [LFm, T2], [pY])
        mm(P, pZ.t[:, 0:4], T2.t[:], lif.t[:, 4:8], True, True, [T2, lif], [pZ])
        tt(P, "dve", bias.t[:], lif.t[:, 0:4], pZ.t[:, 0:4], ALU.subtract, [lif, pZ], [bias])
        for h in range(4):
            act(P, AT.t[:, h, :], pY.t[:, h * 128:(h + 1) * 128], AF.Exp, [pY, bias], [AT], bias=bias.t[:, h:h + 1])
        act(P, EB.t[:], pX.t[:].rearrange("p (h t) -> p h t", h=4), AF.Exp, [pX], [EB])
        if ML < 2:
            return
        pXv = pX.t[:].rearrange("p (h t) -> p h t", h=4)
        cp(P, "dve", BL.t[0:64, :], pXv[0:64, :, 63], [pX], [BL])
        cp(P, "dve", BL.t[64:128, :], pXv[64:128, :, 127], [pX], [BL])
        tt(P, "dve", BL.t[:], BL.t[:], bias.t[:], ALU.add, [BL, bias], [BL])
        act(P, WS.t[:], BL.t[:], AF.Exp, [BL], [WS])
        for h in range(4):
            mm(P, pQ.t[:, h * 128:(h + 1) * 128], mkt.t[:, h, :], mq.t[:, h, :], True, True, [mkt, mq], [pQ])
        tt(P, "dve", WT.t[:], pQ.t[:].rearrange("p (h t) -> p h t", h=4), AT.t[:], ALU.mult, [pQ, AT], [WT])
        tt(P, "pool", QP.t[:], mq.t[:], EB.t[0:64, :, :], ALU.mult, [mq, EB], [QP])
        tt(P, "pool", KW.t[:], mkb.t[:].rearrange("p (h d) -> p h d", h=4),
           WS.t[:].unsqueeze(2).broadcast_to([128, 4, 64]), ALU.mult, [mkb, WS], [KW])

    def back(i):
        s_ = i % 2
        lif, mq, mkt, mkb, va, mo = LIF[s_], MQ[s_], MKt[s_], MKb[s_], VA[s_], MOt[s_]
        EB, WT, QP, KW = EBs[s_], WTs[s_], QPs[s_], KWs[s_]
        import os
        ML = int(os.environ.get('ML', '99'))
        cc, cm, cn = CN[(2 * i) % 3], CN[(2 * i + 1) % 3], CN[(2 * i + 2) % 3]
        if ML < 3:
            return
        for (rows, src, dst, col) in ((slice(0, 64), cc, cm, 63), (slice(64, 128), cm, cn, 127)):
            for hp in range(2):
                for c in range(2):
                    h = 2 * hp + c
                    mm(P, pD.t[0:64, c * 129:(c + 1) * 129], KW.t[rows, h, :], va.t[rows, h, :], True, True,
                       [KW, va], [pD])
                for c in range(2):
                    h = 2 * hp + c
                    stt(P, dst.t[:, h, :], src.t[:, h, :], EB.t[0:64, h, col:col + 1], pD.t[0:64, c * 129:(c + 1) * 129],
                        ALU.mult, ALU.add, [src, EB, pD], [dst])
        if ML < 4:
            return
        for h in range(4):
            bk = pN[h // 2]
            osl = slice((h % 2) * 129, (h % 2) * 129 + 129)
            mm(P, bk.t[:, osl], WT.t[:, h, :], va.t[:, h, :], True, False, [WT, va], [bk])
            mm(P, bk.t[0:64, osl], QP.t[:, h, 0:64], cc.t[:, h, :], False, False, [QP, cc], [bk])
            mm(P, bk.t[64:128, osl], QP.t[:, h, 64:128], cm.t[:, h, :], False, True, [QP, cm], [bk])
        if ML < 5:
            return
        for q in range(2):
            act(P, dn.t[:, 2 * q:2 * q + 2], pN[q].t[:, 0:258].rearrange("p (h c) -> p h c", c=129)[:, :, 128],
                AF.Abs, [pN[q]], [dn])
        ts(P, "dve", dn.t[:], dn.t[:], 1.0, None, ALU.max, ALU.bypass, [dn], [dn])
        P.op("dve", lambda e: e.reciprocal(out=rec.t[:], in_=dn.t[:]), [dn.b], [rec.b])
        for h in range(4):
            o0 = (h % 2) * 129
            act(P, hh.t[:, h, :], pN[h // 2].t[:, o0:o0 + 128], AF.Copy, [pN[h // 2], rec], [hh], scale=rec.t[:, h:h + 1])
        if ML < 6:
            return
        for h in range(4):
            act(P, junk.t[:], hh.t[:, h, :], AF.Square, [hh], [junk, ssq], accum_out=ssq.t[:, h:h + 1])
        rstd_ops(cx, ssq, 128, rs4, NH, [])
        if ML < 7:
            return
        tt(P, "dve", hh.t[:], hh.t[:], rs4.t[:].unsqueeze(2).broadcast_to([128, 4, 128]), ALU.mult, [hh, rs4], [hh])
        hf = hh.t[:].rearrange("p h d -> p (h d)")
        tt(P, "pool", hf, hf, HG.t[:], ALU.mult, [hh, HG], [hh])
        tt(P, "dve", yb.t[:], hf, mo.t[:], ALU.mult, [hh, mo], [yb])
        if ML < 8:
            return
        for c in range(4):
            tr(P, pTr.t[:, c * 128:(c + 1) * 128], yb.t[:, c * 128:(c + 1) * 128], ident.t[:], [yb, ident], [pTr])
        yt = YBt[(i // 4) % 2]
        cp(P, "act", yt.t[:, :, (i % 4) * 128:(i % 4 + 1) * 128], pTr.t[:, 0:512].rearrange("p (c t) -> p c t", c=4),
           [pTr], [yt])
        if i % 4 == 3:
            b0 = (i // 4) * 512
            dma(P, "sp", A["YBT"][:, :, b0:b0 + 512].rearrange("c p t -> p c t"), yt.t[:], [yt], [], YB_g[(i // 4) % 2])

    load(0)
    if NT > 1:
        load(1)
    front(0)
    for i in range(NT):
        if i + 1 < NT:
            front(i + 1)
        back(i)
        if i + 2 < NT:
            load(i + 2)
    return {"sp": YB_g}


def merge_phase(cx, S, X1, X2, A, Wd, Cd):
    nc, P = cx.nc, cx.P
    TT = 256
    NT = S // TT
    WA = cx.sb("WA", [128, 4, D], BF16)
    WB = cx.sb("WB", [128, 4, D], BF16)
    WO = cx.sb("WO", [128, 8, D], BF16)
    stg = [cx.sb(f"stg{i}", [128, D], F32) for i in range(2)]
    stg_g = [cx.group() for _ in range(2)]
    k = 0
    for (src, dst, n) in ((Wd["w_proj_a"], WA, 4), (Wd["w_proj_b"], WB, 4), (Wd["w_out"], WO, 8)):
        for c in range(n):
            s_ = k % 2
            dma(P, "sp", stg[s_].t[:], src[c * 128:(c + 1) * 128, :], [], [stg[s_]], stg_g[s_])
            cp(P, ["act", "dve", "pool"][k % 3], dst.t[:, c, :], stg[s_].t[:], [stg[s_]], [dst])
            k += 1
    R2 = range(2)
    YA = [cx.sb(f"YA{i}", [128, 4, TT], BF16) for i in R2]
    YB = [cx.sb(f"YB{i}", [128, 4, TT], BF16) for i in R2]
    GA = [cx.sb(f"GA{i}", [128, 8, TT], BF16) for i in R2]
    GB = [cx.sb(f"GB{i}", [128, 8, TT], BF16) for i in R2]
    xt = [cx.sb(f"xt{i}", [128, 2, D], F32) for i in R2]
    lg = [[cx.group() for _ in range(5)] for _ in R2]
    st_g = [cx.group() for _ in R2]
    m1 = [cx.sb(f"m1{i}", [128, 512], F32) for i in R2]
    m2 = [cx.sb(f"m2{i}", [128, 512], F32) for i in R2]
    MG = [cx.sb(f"MG{i}", [128, 8, TT], BF16) for i in R2]
    pA = [cx.ps(f"pA{i}", [128, 512], F32) for i in R2]
    pB = [cx.ps(f"pB{i}", [128, 512], F32) for i in R2]
    pO = [cx.ps(f"pO{i}", [128, 512], F32) for i in R2]

    def load(i):
        s_ = i % 2
        tsl = slice(i * TT, (i + 1) * TT)
        g = lg[s_]
        dma(P, "sp", YA[s_].t[:], A["YAT"][:, :, tsl].rearrange("(c w) d t -> (w d) c t", w=2), [], [YA[s_]], g[0])
        dma(P, "sp", YB[s_].t[:], A["YBT"][:, :, tsl].rearrange("c p t -> p c t"), [], [YB[s_]], g[1])
        dma(P, "sp", GA[s_].t[:], A["GAT"][:, :, tsl].rearrange("c p t -> p c t"), [], [GA[s_]], g[2])
        dma(P, "sp", GB[s_].t[:], A["GBT"][:, :, tsl].rearrange("c p t -> p c t"), [], [GB[s_]], g[3])
        dma(P, "sp", xt[s_].t[:], X1[tsl, :].rearrange("(s p) d -> p s d", p=128), [], [xt[s_]], g[4])

    def tile(i):
        s_ = i % 2
        tsl = slice(i * TT, (i + 1) * TT)
        for dcp in range(4):
            q = dcp % 2
            for (W_, Y_, bk) in ((WA, YA[s_], pA[q]), (WB, YB[s_], pB[q])):
                for c2 in range(2):
                    dc = 2 * dcp + c2
                    for c in range(4):
                        mm(P, bk.t[:, c2 * TT:(c2 + 1) * TT], W_.t[:, c, dc * 128:(dc + 1) * 128], Y_.t[:, c, :],
                           c == 0, c == 3, [W_, Y_], [bk])
            gsl = slice(2 * dcp, 2 * dcp + 2)
            tt(P, "dve", m1[q].t[:].rearrange("p (c t) -> p c t", c=2), pA[q].t[:].rearrange("p (c t) -> p c t", c=2),
               GA[s_].t[:, gsl, :], ALU.mult, [pA[q], GA[s_]], [m1[q]])
            tt(P, "dve", m2[q].t[:].rearrange("p (c t) -> p c t", c=2), pB[q].t[:].rearrange("p (c t) -> p c t", c=2),
               GB[s_].t[:, gsl, :], ALU.mult, [pB[q], GB[s_]], [m2[q]])
            tt(P, "pool", MG[s_].t[:, gsl, :], m1[q].t[:].rearrange("p (c t) -> p c t", c=2),
               m2[q].t[:].rearrange("p (c t) -> p c t", c=2), ALU.add, [m1[q], m2[q]], [MG[s_]])
        n = 0
        for sub in range(2):
            for dh in range(2):
                bk = pO[n % 2]
                n += 1
                for dc in range(8):
                    mm(P, bk.t[:], MG[s_].t[:, dc, sub * 128:(sub + 1) * 128], WO.t[:, dc, dh * 512:(dh + 1) * 512],
                       dc == 0, dc == 7, [MG[s_], WO], [bk])
                xs = xt[s_].t[:, sub, dh * 512:(dh + 1) * 512]
                tt(P, "dve", xs, bk.t[:], xs, ALU.add, [bk, xt[s_]], [xt[s_]])
        dma(P, "pool", X2[tsl, :].rearrange("(s p) d -> p s d", p=128), xt[s_].t[:], [xt[s_]], [], st_g[s_])

    load(0)
    if NT > 1:
        load(1)
    for i in range(NT):
        tile(i)
        if i + 2 < NT:
            load(i + 2)
    return {"pool": st_g}


WSPEC = {
    "ffn1_norm": [1, D], "ffn1_w_gate": [D, DFF], "ffn1_w_up": [D, DFF], "ffn1_w_down": [DFF, D],
    "mix_norm": [1, D], "w_in": [D, DIN], "q_norm": [1, 64], "k_norm": [1, 64], "idx_k_norm": [1, 64],
    "conv_w": [128, 16], "conv_b": [128, 4], "w_mq": [128, 256], "w_mk": [128, 256], "b_i": [1, 4],
    "b_f": [1, 4], "m_head_norm": [1, 512], "w_proj_a": [512, D], "w_proj_b": [512, D], "w_out": [D, D],
    "ffn2_norm": [1, D], "ffn2_w_gate": [D, DFF], "ffn2_w_up": [D, DFF], "ffn2_w_down": [DFF, D],
}


def consts():
    c = {}
    c["c_ident"] = np.eye(128, dtype=np.float32).astype(ml_dtypes.bfloat16)
    cm = np.zeros((4, 128, 512), np.float32)
    for ii in range(4):
        for p in range(128):
            lim = ((ii * 128 + p) // 64 + 1) * 64
            cm[ii, p, lim:] = -1e30
    c["c_cmask"] = cm.astype(ml_dtypes.bfloat16)
    c["c_steps"] = np.tile((2.0 ** -(np.arange(NIT) + 1.0)).astype(np.float32)[None, :], (128, 1))
    j = np.arange(128)
    t2 = ((j[:, None] // 64 == j[None, :] // 64) & (j[:, None] <= j[None, :]))
    c["c_tri2"] = t2.astype(np.float32)
    c["c_negm"] = np.tile(np.where(t2, 0.0, NEG).astype(np.float32), (1, 4)).astype(ml_dtypes.bfloat16)
    return c


CSPEC = {"c_ident": ([128, 128], BF16), "c_cmask": ([4, 128, 512], BF16), "c_steps": ([128, NIT], F32),
         "c_tri2": ([128, 128], F32), "c_negm": ([128, 512], BF16)}


def build_nc(S, debug=False, upto=9):
    nc = bass.Bass("TRN2", target_bir_lowering=False)

    def din(name, shape, dt=F32):
        return nc.dram_tensor(name, list(shape), dt, kind="ExternalInput").ap()

    x = din("x", [S, D])
    Wd = {k: din(k, v) for k, v in WSPEC.items()}
    Cd = {k: din(k, v[0], v[1]) for k, v in CSPEC.items()}
    out = nc.dram_tensor("out", [S, D], F32, kind="ExternalOutput").ap()
    kind = "ExternalOutput" if debug else "Internal"

    def scr(name, shape, dt):
        return nc.dram_tensor(name, list(shape), dt, kind=kind).ap()

    X1 = scr("X1", [S, D], F32)
    X2 = scr("X2", [S, D], F32)
    A = {
        "QT": scr("QT", [4, 128, S], BF16), "KT": scr("KT", [128, S], BF16), "V": scr("V", [S, 128], BF16),
        "IQT": scr("IQT", [4, 128, S], BF16), "IKT": scr("IKT", [64, S], BF16), "SGN": scr("SGN", [S, 8], F32),
        "MV": scr("MV", [S, 512], BF16), "LIF": scr("LIF", [S, 8], F32), "MO": scr("MO", [S, 512], BF16),
        "GAT": scr("GAT", [8, 128, S], BF16), "GBT": scr("GBT", [8, 128, S], BF16),
        "MQT": scr("MQT", [4, 64, S], BF16), "MKT": scr("MKT", [4, 64, S], BF16), "MK": scr("MK", [S, 256], BF16),
        "YAT": scr("YAT", [8, 64, S], BF16), "YBT": scr("YBT", [4, 128, S], BF16),
    }
    ph = [
        ("f1", lambda cx: ffn_phase(cx, S, x, X1, Wd["ffn1_norm"], Wd["ffn1_w_gate"], Wd["ffn1_w_up"],
                                    Wd["ffn1_w_down"], Cd["c_ident"])),
        ("pj", lambda cx: proj_phase(cx, S, X1, A, Wd, Cd)),
        ("ds", lambda cx: dsa_phase(cx, S, A, Cd)),
        ("ml", lambda cx: mlstm_phase(cx, S, A, Wd, Cd)),
        ("mg", lambda cx: merge_phase(cx, S, X1, X2, A, Wd, Cd)),
        ("f2", lambda cx: ffn_phase(cx, S, X2, out, Wd["ffn2_norm"], Wd["ffn2_w_gate"], Wd["ffn2_w_up"],
                                    Wd["ffn2_w_down"], Cd["c_ident"])),
    ]
    import os
    lo_ = int(os.environ.get('FROM', '0'))
    for n, (tag, fn) in enumerate(ph):
        if lo_ <= n < upto:
            run_phase(nc, tag, fn, debug)
    return nc


def layout_weights(inputs):
    sh = {}
    for k, v in WSPEC.items():
        a = np.asarray(inputs[k], dtype=np.float32)[0]
        if k == "conv_w":
            a = a.reshape(4, 4, 128).transpose(2, 1, 0)
        elif k == "conv_b":
            a = a.reshape(4, 128).T
        elif k in ("w_mq", "w_mk"):
            a = a.transpose(1, 0, 2)
        sh[k] = np.ascontiguousarray(a).reshape(v)
    return sh


_NC_CACHE = {}


def kernel(**inputs):
    x = np.ascontiguousarray(np.asarray(inputs["x"], dtype=np.float32))
    B, S, _ = x.shape
    if S not in _NC_CACHE:
        _NC_CACHE[S] = build_nc(S)
    nc = _NC_CACHE[S]
    shared = layout_weights(inputs)
    shared.update(consts())
    in_maps = []
    for b in range(B):
        m = dict(shared)
        m["x"] = x[b]
        in_maps.append(m)
    res = run_bass_kernel_spmd(nc, in_maps, core_ids=list(range(B)))
    return np.stack([np.asarray(r["out"], dtype=np.float32) for r in res.results], axis=0)
```
